# Optimizing a Trainium2 kernel written in Bass

```python
import math
import jax, jax.numpy as jnp
from jax import lax
import numpy as np

D_MODEL = 1024
BATCH = 8
SEQ = 2048
DEPTH = 4
DEC_BATCH = 128
DEC_SEQ = 4
PAST_LEN = 16384
PAGE_SIZE = 128

D_MIX = 2 * D_MODEL
POOL_WIDTH = D_MIX // 4
POOL_WINDOWS = (2, 4, 8, 16)
POOL_GROUPS = len(POOL_WINDOWS)
POOL_GDIM = POOL_WIDTH // POOL_GROUPS
POOL_BUF = max(POOL_WINDOWS) - 1
SSD_WIDTH = D_MIX // 2
SSD_HEAD_DIM = 64
SSD_HEADS = SSD_WIDTH // SSD_HEAD_DIM
SSD_GROUPS = 2
SSD_HPG = SSD_HEADS // SSD_GROUPS
SSD_STATE = 128
SSD_CHUNK = 128
SSD_GN = SSD_GROUPS * SSD_STATE
SSD_CONV_DIM = SSD_WIDTH + 2 * SSD_GN
CONV_WIDTH = 4
LRU_WIDTH = D_MIX // 4
LRU_HEADS = 8
LRU_HDIM = LRU_WIDTH // LRU_HEADS
LRU_C = 8.0
N_MEM = 256
MEM_HEADS = 4
MEM_HDIM = D_MODEL // MEM_HEADS
D_FF = -(-8 * D_MODEL // (3 * 256)) * 256
EPS = 1e-6
OFF_POOL = 0
OFF_Z = OFF_POOL + POOL_WIDTH
OFF_XBC = OFF_Z + SSD_WIDTH
OFF_DT = OFF_XBC + SSD_CONV_DIM
OFF_GATE = OFF_DT + SSD_HEADS
OFF_LRU = OFF_GATE + LRU_WIDTH
N_IN = OFF_LRU + LRU_WIDTH

kernel_name = "hybrid_pool_ssd_rglru_memxattn_step"

F32 = jnp.float32


def rmsnorm(x, g):
    xf = x.astype(F32)
    var = jnp.mean(xf * xf, axis=-1, keepdims=True)
    return (xf * lax.rsqrt(var + EPS) * g.astype(F32)).astype(x.dtype)


def causal_dwconv(x, buf, w, b):
    L = x.shape[1]
    xp = jnp.concatenate([buf.astype(x.dtype), x], axis=1)
    y = b
    for k in range(CONV_WIDTH):
        y = y + w[k] * xp[:, k:k + L]
    return y, xp[:, -(CONV_WIDTH - 1):]


def pool_mixer(u, buf, pos0, w_grp, scale):
    B, L, _ = u.shape
    up = jnp.concatenate([buf.astype(u.dtype), u], axis=1).astype(F32)
    cs = jnp.pad(jnp.cumsum(up, axis=1), ((0, 0), (1, 0), (0, 0)))
    end = cs[:, POOL_BUF + 1:]
    pos = pos0 + jnp.arange(L)
    outs = []
    for g, w in enumerate(POOL_WINDOWS):
        sl = slice(g * POOL_GDIM, (g + 1) * POOL_GDIM)
        start = cs[:, POOL_BUF + 1 - w:POOL_BUF + 1 - w + L, sl]
        cnt = jnp.minimum(pos + 1, w).astype(F32)[None, :, None]
        outs.append((end[..., sl] - start) / cnt)
    pooled = jnp.concatenate(outs, axis=-1) - up[:, POOL_BUF:]
    pooled = pooled.reshape(B, L, POOL_GROUPS, POOL_GDIM)
    y = jnp.einsum('blgc,gcd->blgd', pooled, w_grp.astype(F32)).reshape(B, L, POOL_WIDTH)
    y = y * scale.astype(F32)
    return y.astype(u.dtype), up[:, -POOL_BUF:].astype(u.dtype)


def ssd_mixer(z, xbc, dt_raw, conv_buf, h0, conv_w, conv_b, dt_bias, a_log, d_skip, norm_g):
    B, L, _ = z.shape
    G, K, P, N = SSD_GROUPS, SSD_HPG, SSD_HEAD_DIM, SSD_STATE
    xbc, new_conv = causal_dwconv(xbc, conv_buf, conv_w, conv_b)
    xbc = jax.nn.silu(xbc).astype(F32)
    xs = xbc[..., :SSD_WIDTH].reshape(B, L, G, K, P)
    Bm = xbc[..., SSD_WIDTH:SSD_WIDTH + SSD_GN].reshape(B, L, G, N)
    Cm = xbc[..., SSD_WIDTH + SSD_GN:].reshape(B, L, G, N)
    dt = jax.nn.softplus(dt_raw.astype(F32) + dt_bias.astype(F32)).reshape(B, L, G, K)
    A = -jnp.exp(a_log.astype(F32)).reshape(G, K)
    Q = min(SSD_CHUNK, L)
    nC = -(-L // Q)
    pad = nC * Q - L

    def chunk(t):
        t = jnp.pad(t, ((0, 0), (0, pad)) + ((0, 0),) * (t.ndim - 2))
        return t.reshape((B, nC, Q) + t.shape[2:])

    xs, Bm, Cm, dt = chunk(xs), chunk(Bm), chunk(Cm), chunk(dt)
    acs = jnp.cumsum(dt * A, axis=2)
    xdt = xs * dt[..., None]
    acs_t = jnp.einsum('bcqgk->bcgkq', acs)
    seg = acs_t[..., :, None] - acs_t[..., None, :]
    mask = jnp.tril(jnp.ones((Q, Q), dtype=bool))
    Lmat = jnp.where(mask, jnp.exp(jnp.where(mask, seg, 0.0)), 0.0)
    CB = jnp.einsum('bclgn,bcsgn->bcgls', Cm, Bm)
    y_diag = jnp.einsum('bcgls,bcgkls,bcsgkp->bclgkp', CB, Lmat, xdt)
    decay_st = jnp.exp(acs[:, :, -1:] - acs)
    states = jnp.einsum('bcsgn,bcsgk,bcsgkp->bcgkpn', Bm, decay_st, xdt)
    chunk_decay = jnp.exp(acs[:, :, -1])

    def step(h, inp):
        st, dec = inp
        return h * dec[..., None, None] + st, h

    h_init = h0.astype(F32).reshape(B, G, K, P, N)
    hT, starts = lax.scan(step, h_init, (jnp.moveaxis(states, 1, 0), jnp.moveaxis(chunk_decay, 1, 0)))
    starts = jnp.moveaxis(starts, 0, 1)
    y_off = jnp.einsum('bclgn,bcgkpn,bclgk->bclgkp', Cm, starts, jnp.exp(acs))
    y = y_diag + y_off + xs * d_skip.astype(F32).reshape(G, K)[..., None]
    y = y.reshape(B, nC * Q, SSD_WIDTH)[:, :L]
    y = rmsnorm(y * jax.nn.silu(z.astype(F32)), norm_g)
    return y.astype(z.dtype), new_conv, hT.reshape(B, SSD_HEADS, P, N).astype(z.dtype)


def rglru_mixer(gate_in, xr, conv_buf, h0, pos0, conv_w, conv_b, wa, ba, wx, bx, lam):
    B, L, _ = xr.shape
    xc, new_conv = causal_dwconv(xr, conv_buf, conv_w, conv_b)
    xf = xc.astype(F32)
    xh = xf.reshape(B, L, LRU_HEADS, LRU_HDIM)
    r = jax.nn.sigmoid(jnp.einsum('blhi,hij->blhj', xh, wa.astype(F32)) + ba.astype(F32)).reshape(B, L, LRU_WIDTH)
    i = jax.nn.sigmoid(jnp.einsum('blhi,hij->blhj', xh, wx.astype(F32)) + bx.astype(F32)).reshape(B, L, LRU_WIDTH)
    log_a = -LRU_C * r * jax.nn.softplus(-lam.astype(F32))
    a = jnp.exp(log_a)
    pos = pos0 + jnp.arange(L)
    mult = jnp.where((pos == 0)[None, :, None], 1.0, jnp.sqrt(-jnp.expm1(2.0 * log_a)))
    b_in = mult * i * xf
    b_in = b_in.at[:, 0].add(a[:, 0] * h0.astype(F32))

    def comb(lhs, rhs):
        a1, b1 = lhs
        a2, b2 = rhs
        return a1 * a2, a2 * b1 + b2

    _, h = lax.associative_scan(comb, (a, b_in), axis=1)
    y = h * jax.nn.gelu(gate_in.astype(F32))
    return y.astype(xr.dtype), new_conv, h[:, -1].astype(xr.dtype)


def mem_attention(u, k, v, wq, wo):
    B, L, _ = u.shape
    q = (u @ wq).reshape(B, L, MEM_HEADS, MEM_HDIM)
    s = jnp.einsum('blhd,bmhd->bhlm', q, k).astype(F32) * (MEM_HDIM ** -0.5)
    p = jax.nn.softmax(s, axis=-1).astype(v.dtype)
    o = jnp.einsum('bhlm,bmhd->blhd', p, v).reshape(B, L, D_MODEL)
    return o @ wo


def run_trunk(x, pos0, pool_s, sconv_s, ssd_s, lconv_s, lru_s, mem_k, mem_v, W):
    h = x
    n_pool, n_sconv, n_ssd, n_lconv, n_lru = [], [], [], [], []
    for l in range(DEPTH):
        u = rmsnorm(h, W['norm_mix'][l])
        proj = jnp.einsum('bld,de->ble', u, W['w_in'][l])
        ya, s1 = pool_mixer(proj[..., OFF_POOL:OFF_Z], pool_s[l], pos0, W['pool_w'][l], W['pool_scale'][l])
        yb, s2, s3 = ssd_mixer(proj[..., OFF_Z:OFF_XBC], proj[..., OFF_XBC:OFF_DT], proj[..., OFF_DT:OFF_GATE],
                               sconv_s[l], ssd_s[l], W['ssd_conv_w'][l], W['ssd_conv_b'][l], W['ssd_dt_bias'][l],
                               W['ssd_a_log'][l], W['ssd_d'][l], W['ssd_norm'][l])
        yc, s4, s5 = rglru_mixer(proj[..., OFF_GATE:OFF_LRU], proj[..., OFF_LRU:N_IN], lconv_s[l], lru_s[l], pos0,
                                 W['lru_conv_w'][l], W['lru_conv_b'][l], W['lru_wa'][l], W['lru_ba'][l],
                                 W['lru_wx'][l], W['lru_bx'][l], W['lru_lambda'][l])
        h = h + jnp.concatenate([ya, yb, yc], axis=-1) @ W['w_out'][l]
        h = h + mem_attention(rmsnorm(h, W['norm_mem'][l]), mem_k[l], mem_v[l], W['w_mem_q'][l], W['w_mem_o'][l])
        u = rmsnorm(h, W['norm_ffn'][l])
        h = h + (jax.nn.silu(u @ W['w_ffn_gate'][l]) * (u @ W['w_ffn_up'][l])) @ W['w_ffn_down'][l]
        n_pool.append(s1); n_sconv.append(s2); n_ssd.append(s3); n_lconv.append(s4); n_lru.append(s5)
    y = rmsnorm(h, W['norm_final'])
    return (y, jnp.stack(n_pool), jnp.stack(n_sconv), jnp.stack(n_ssd), jnp.stack(n_lconv), jnp.stack(n_lru))


def setup_inputs(seed: int = 0) -> dict:
    key = jax.random.key(seed)
    ks = iter(jax.random.split(key, 64))

    def nrm(shape, scale=1.0):
        return jax.random.normal(next(ks), shape, F32) * scale

    def gain(shape):
        return 1.0 + nrm(shape, 0.02)

    dt0 = jnp.exp(jax.random.uniform(next(ks), (DEPTH, SSD_HEADS), F32, math.log(1e-3), math.log(1e-1)))
    base = jax.random.uniform(next(ks), (DEPTH, LRU_WIDTH), F32, 0.9, 0.999) ** (1.0 / LRU_C)
    return {
        "x_prompt": nrm((BATCH, SEQ, D_MODEL)),
        "x_sample": nrm((DEC_BATCH, DEC_SEQ, D_MODEL)),
        "mem_prompt": nrm((BATCH, N_MEM, D_MODEL)),
        "state_pool": nrm((DEPTH, DEC_BATCH, POOL_BUF, POOL_WIDTH)),
        "state_ssd_conv": nrm((DEPTH, DEC_BATCH, CONV_WIDTH - 1, SSD_CONV_DIM)),
        "state_ssd": nrm((DEPTH, DEC_BATCH, SSD_HEADS, SSD_HEAD_DIM, SSD_STATE), 0.1),
        "state_lru_conv": nrm((DEPTH, DEC_BATCH, CONV_WIDTH - 1, LRU_WIDTH)),
        "state_lru": nrm((DEPTH, DEC_BATCH, LRU_WIDTH)),
        "cache_mem_k": nrm((DEPTH, DEC_BATCH, N_MEM, MEM_HEADS, MEM_HDIM)),
        "cache_mem_v": nrm((DEPTH, DEC_BATCH, N_MEM, MEM_HEADS, MEM_HDIM)),
        "norm_mix": gain((DEPTH, D_MODEL)),
        "w_in": nrm((DEPTH, D_MODEL, N_IN), D_MODEL ** -0.5),
        "pool_w": nrm((DEPTH, POOL_GROUPS, POOL_GDIM, POOL_GDIM), POOL_GDIM ** -0.5),
        "pool_scale": gain((DEPTH, POOL_WIDTH)),
        "ssd_conv_w": nrm((DEPTH, CONV_WIDTH, SSD_CONV_DIM), CONV_WIDTH ** -0.5),
        "ssd_conv_b": nrm((DEPTH, SSD_CONV_DIM), 0.01),
        "ssd_dt_bias": dt0 + jnp.log(-jnp.expm1(-dt0)),
        "ssd_a_log": jnp.log(jax.random.uniform(next(ks), (DEPTH, SSD_HEADS), F32, 1.0, 16.0)),
        "ssd_d": gain((DEPTH, SSD_HEADS)),
        "ssd_norm": gain((DEPTH, SSD_WIDTH)),
        "lru_conv_w": nrm((DEPTH, CONV_WIDTH, LRU_WIDTH), CONV_WIDTH ** -0.5),
        "lru_conv_b": nrm((DEPTH, LRU_WIDTH), 0.01),
        "lru_wa": nrm((DEPTH, LRU_HEADS, LRU_HDIM, LRU_HDIM), LRU_HDIM ** -0.5),
        "lru_ba": nrm((DEPTH, LRU_HEADS, LRU_HDIM), 0.01),
        "lru_wx": nrm((DEPTH, LRU_HEADS, LRU_HDIM, LRU_HDIM), LRU_HDIM ** -0.5),
        "lru_bx": nrm((DEPTH, LRU_HEADS, LRU_HDIM), 0.01),
        "lru_lambda": jnp.log(base / (1.0 - base)),
        "w_out": nrm((DEPTH, D_MIX, D_MODEL), D_MIX ** -0.5),
        "norm_mem": gain((DEPTH, D_MODEL)),
        "w_mem_q": nrm((DEPTH, D_MODEL, D_MODEL), D_MODEL ** -0.5),
        "w_mem_k": nrm((DEPTH, D_MODEL, D_MODEL), D_MODEL ** -0.5),
        "w_mem_v": nrm((DEPTH, D_MODEL, D_MODEL), D_MODEL ** -0.5),
        "w_mem_o": nrm((DEPTH, D_MODEL, D_MODEL), D_MODEL ** -0.5),
        "norm_ffn": gain((DEPTH, D_MODEL)),
        "w_ffn_gate": nrm((DEPTH, D_MODEL, D_FF), D_MODEL ** -0.5),
        "w_ffn_up": nrm((DEPTH, D_MODEL, D_FF), D_MODEL ** -0.5),
        "w_ffn_down": nrm((DEPTH, D_FF, D_MODEL), D_FF ** -0.5),
        "norm_final": gain((D_MODEL,)),
    }


def reference(x_prompt, x_sample, mem_prompt, state_pool, state_ssd_conv, state_ssd, state_lru_conv, state_lru,
              cache_mem_k, cache_mem_v, norm_mix, w_in, pool_w, pool_scale, ssd_conv_w, ssd_conv_b, ssd_dt_bias,
              ssd_a_log, ssd_d, ssd_norm, lru_conv_w, lru_conv_b, lru_wa, lru_ba, lru_wx, lru_bx, lru_lambda,
              w_out, norm_mem, w_mem_q, w_mem_k, w_mem_v, w_mem_o, norm_ffn, w_ffn_gate, w_ffn_up, w_ffn_down,
              norm_final):
    W = dict(norm_mix=norm_mix, w_in=w_in, pool_w=pool_w, pool_scale=pool_scale, ssd_conv_w=ssd_conv_w,
             ssd_conv_b=ssd_conv_b, ssd_dt_bias=ssd_dt_bias, ssd_a_log=ssd_a_log, ssd_d=ssd_d, ssd_norm=ssd_norm,
             lru_conv_w=lru_conv_w, lru_conv_b=lru_conv_b, lru_wa=lru_wa, lru_ba=lru_ba, lru_wx=lru_wx,
             lru_bx=lru_bx, lru_lambda=lru_lambda, w_out=w_out, norm_mem=norm_mem, w_mem_q=w_mem_q,
             w_mem_o=w_mem_o, norm_ffn=norm_ffn, w_ffn_gate=w_ffn_gate, w_ffn_up=w_ffn_up,
             w_ffn_down=w_ffn_down, norm_final=norm_final)
    dt = x_prompt.dtype
    p_mem_k = jnp.einsum('bmd,lde->lbme', mem_prompt, w_mem_k).reshape(DEPTH, BATCH, N_MEM, MEM_HEADS, MEM_HDIM)
    p_mem_v = jnp.einsum('bmd,lde->lbme', mem_prompt, w_mem_v).reshape(DEPTH, BATCH, N_MEM, MEM_HEADS, MEM_HDIM)
    z_pool = jnp.zeros((DEPTH, BATCH, POOL_BUF, POOL_WIDTH), dt)
    z_sconv = jnp.zeros((DEPTH, BATCH, CONV_WIDTH - 1, SSD_CONV_DIM), dt)
    z_ssd = jnp.zeros((DEPTH, BATCH, SSD_HEADS, SSD_HEAD_DIM, SSD_STATE), dt)
    z_lconv = jnp.zeros((DEPTH, BATCH, CONV_WIDTH - 1, LRU_WIDTH), dt)
    z_lru = jnp.zeros((DEPTH, BATCH, LRU_WIDTH), dt)
    y_prompt, p_pool, p_sconv, p_ssd, p_lconv, p_lru = run_trunk(
        x_prompt, 0, z_pool, z_sconv, z_ssd, z_lconv, z_lru, p_mem_k, p_mem_v, W)
    y_sample, s_pool, s_sconv, s_ssd, s_lconv, s_lru = run_trunk(
        x_sample, PAST_LEN, state_pool, state_ssd_conv, state_ssd, state_lru_conv, state_lru,
        cache_mem_k, cache_mem_v, W)
    return (y_prompt, y_sample, p_pool, p_sconv, p_ssd, p_lconv, p_lru, p_mem_k, p_mem_v,
            s_pool, s_sconv, s_ssd, s_lconv, s_lru)
```

```python
import contextlib
import numpy as np
import concourse.bass as bass
import concourse.mybir as mybir
from concourse.bass_utils import run_bass_kernel_spmd

F32 = mybir.dt.float32
BF16 = mybir.dt.bfloat16
AF = mybir.ActivationFunctionType
ALU = mybir.AluOpType

ENGS = ("pe", "act", "dve", "pool", "sp")
NRING = 8

D = 1024
KC = 8
NPT = 2048
NSQ = 16
NS = 64
DEPTH = 4
NPASS = 2
NPP = NPT // NPASS
NT = NPP + NS
DFF = 2816
FC = 22
OFF_POOL, OFF_Z, OFF_XBC, OFF_DT, OFF_GATE, OFF_LRU, N_IN = 0, 512, 1536, 3072, 3088, 3600, 4112
NEG = -30000.0
EPS = 1e-6
NCST = 1104
WBUF = 4096
N_CORES = 8


def _isz(dt):
    return mybir.dt.size(dt)


def _region(ap):
    t = ap.tensor
    pairs = [tuple(x) for x in ap.ap]
    off = int(ap.offset)
    isz = _isz(ap.dtype)
    if type(t).__name__ == "DRamTensorHandle":
        ext = 0
        for st, cnt in pairs:
            ext += (cnt - 1) * abs(st)
        return (t.name, 0, 1, off * isz, (off + ext + 1) * isz)
    pst, pcnt = pairs[0]
    if pst == 0:
        pst = 1 << 40
    p_lo = off // pst
    f_lo = off % pst
    ext = 0
    for st, cnt in pairs[1:]:
        ext += (cnt - 1) * abs(st)
    b_lo, b_hi = f_lo * isz, (f_lo + ext + 1) * isz
    if type(t).__name__ == "PSumTensorHandle":
        b_lo = (b_lo // 2048) * 2048
        b_hi = ((b_hi + 2047) // 2048) * 2048
        return (t.name, (p_lo // 32) * 32, ((p_lo + pcnt + 31) // 32) * 32, b_lo, b_hi)
    return (t.name, p_lo, p_lo + pcnt, b_lo, b_hi)


class Op:
    __slots__ = ("eng", "idx", "fn", "deps", "is_dma", "needed", "signal", "clock", "ring")

    def __init__(self, eng, idx, fn, is_dma):
        self.eng = eng
        self.idx = idx
        self.fn = fn
        self.deps = []
        self.is_dma = is_dma
        self.needed = False
        self.signal = None
        self.clock = None
        self.ring = None


class Prog:
    def __init__(self, nc):
        self.nc = nc
        self.ops = {e: [] for e in ENGS}
        self.recs = {}
        self.ndma = {e: 0 for e in ENGS}
        self.dma_seq = {e: [] for e in ENGS}
        self.waited_dma = {e: set() for e in ENGS}
        self.out_dmas = []

    def op(self, eng, fn, reads=(), writes=(), is_dma=False, extra_deps=()):
        lst = self.ops[eng]
        o = Op(eng, len(lst), fn, is_dma)
        deps = {}
        ops = self.ops

        def add_dep(e2, i2):
            if e2 == eng and eng == "pe":
                return
            od = ops[e2][i2]
            if od.is_dma:
                deps[(e2, i2)] = True
            else:
                k = deps.get(e2)
                if k is None or i2 > k:
                    deps[e2] = i2

        rregs = [_region(ap) for ap in reads]
        wregs = [_region(ap) for ap in writes]
        for (name, pl, ph, fl, fh) in rregs:
            for r in self.recs.get(name, ()):
                if r[4] and r[0] < ph and pl < r[1] and r[2] < fh and fl < r[3]:
                    add_dep(r[5], r[6])
        for (name, pl, ph, fl, fh) in wregs:
            for r in self.recs.get(name, ()):
                if r[0] < ph and pl < r[1] and r[2] < fh and fl < r[3]:
                    add_dep(r[5], r[6])
        for (e2, i2) in extra_deps:
            add_dep(e2, i2)
        prev = lst[-1].clock if lst else {}
        clock = dict(prev)
        final = []
        for k in deps:
            if isinstance(k, tuple):
                if k in self.waited_dma[eng]:
                    continue
                final.append(k)
            else:
                i2 = deps[k]
                if clock.get(k, -1) >= i2:
                    continue
                final.append((k, i2))
        for (e2, i2) in final:
            od = ops[e2][i2]
            if od.is_dma:
                self.waited_dma[eng].add((e2, i2))
            else:
                if clock.get(e2, -1) < i2:
                    clock[e2] = i2
            for k2, v2 in od.clock.items():
                if clock.get(k2, -1) < v2:
                    clock[k2] = v2
        if is_dma:
            n = self.ndma[eng]
            o.ring = n
            self.ndma[eng] = n + 1
            if n >= NRING:
                pd = self.dma_seq[eng][n - NRING]
                if (eng, pd.idx) not in self.waited_dma[eng]:
                    final.append((eng, pd.idx))
                    self.waited_dma[eng].add((eng, pd.idx))
            self.dma_seq[eng].append(o)
        o.deps = final
        o.clock = clock
        lst.append(o)
        for (name, pl, ph, fl, fh) in rregs:
            L = self.recs.setdefault(name, [])
            if not is_dma:
                L[:] = [r for r in L if not ((not r[4]) and r[5] == eng and (not r[7]) and pl <= r[0] and r[1] <= ph and fl <= r[2] and r[3] <= fh)]
            L.append([pl, ph, fl, fh, False, eng, o.idx, is_dma])
        for (name, pl, ph, fl, fh) in wregs:
            L = self.recs.setdefault(name, [])
            L[:] = [r for r in L if not (pl <= r[0] and r[1] <= ph and fl <= r[2] and r[3] <= fh)]
            L.append([pl, ph, fl, fh, True, eng, o.idx, is_dma])
        return o

    def mm(self, out, lhsT, rhs, start=True, stop=True):
        return self.op("pe", lambda e: e.matmul(out, lhsT, rhs, start=start, stop=stop), [lhsT, rhs], [out])

    def tr(self, out, in_, ident):
        return self.op("pe", lambda e: e.transpose(out, in_, ident), [in_, ident], [out])

    def act(self, out, in_, func, bias=None, scale=None):
        reads = [in_]
        kw = {}
        if bias is not None:
            kw["bias"] = bias
            if not isinstance(bias, (int, float)):
                reads.append(bias)
        if scale is not None:
            kw["scale"] = scale
            if not isinstance(scale, (int, float)):
                reads.append(scale)
        return self.op("act", lambda e: e.activation(out, in_, func, **kw), reads, [out])

    def tt(self, out, a, b, op, eng="dve"):
        return self.op(eng, lambda e: e.tensor_tensor(out, a, b, op), [a, b], [out])

    def ts(self, out, a, s1, op0, s2=None, op1=None, eng="dve"):
        reads = [a]
        if not isinstance(s1, (int, float)):
            reads.append(s1)
        if s2 is not None and not isinstance(s2, (int, float)):
            reads.append(s2)
        if op1 is None:
            return self.op(eng, lambda e: e.tensor_scalar(out, a, s1, None, op0), reads, [out])
        return self.op(eng, lambda e: e.tensor_scalar(out, a, s1, s2, op0, op1), reads, [out])

    def stt(self, out, a, s, b, op0, op1):
        reads = [a, b]
        if not isinstance(s, (int, float)):
            reads.append(s)
        return self.op("dve", lambda e: e.scalar_tensor_tensor(out, a, s, b, op0, op1), reads, [out])

    def scan(self, out, d0, d1, init, op0, op1):
        reads = [d0, d1]
        if not isinstance(init, (int, float)):
            reads.append(init)
        return self.op("dve", lambda e: e.tensor_tensor_scan(out, d0, d1, init, op0, op1), reads, [out])

    def copy(self, out, in_, eng="dve"):
        if eng == "act":
            return self.op("act", lambda e: e.copy(out, in_), [in_], [out])
        return self.op(eng, lambda e: e.tensor_copy(out, in_), [in_], [out])

    def memset(self, ap, val, eng="dve"):
        return self.op(eng, lambda e: e.memset(ap, val), [], [ap])

    def recip(self, out, in_):
        return self.op("dve", lambda e: e.reciprocal(out, in_), [in_], [out])

    def dma(self, out, in_, q="sp", is_output=False, **kw):
        o = self.op(q, lambda e: e.dma_start(out=out, in_=in_, **kw), [in_], [out], is_dma=True)
        if is_output:
            self.out_dmas.append((q, o.idx))
        return o

    def emit(self):
        nc = self.nc
        self.op("sp", None, extra_deps=list(self.out_dmas))
        for e in ENGS:
            for o in self.ops[e]:
                for (e2, i2) in o.deps:
                    self.ops[e2][i2].needed = True
        for e in ENGS:
            c = 0
            for o in self.ops[e]:
                if o.needed and not o.is_dma:
                    c += 1
                    o.signal = c
        with contextlib.ExitStack() as st:
            sems = {e: st.enter_context(nc.semaphore("s_" + e)) for e in ENGS}
            rings = {e: [st.enter_context(nc.semaphore("r_%s_%d" % (e, i))) for i in range(NRING)] for e in ENGS if self.ndma[e]}
            block = st.enter_context(nc.Block())

            def run(engname, eng):
                for o in self.ops[engname]:
                    for (e2, i2) in o.deps:
                        od = self.ops[e2][i2]
                        if od.is_dma:
                            eng.wait_ge(rings[e2][od.ring % NRING], 16 * (od.ring // NRING + 1))
                        else:
                            eng.wait_ge(sems[e2], od.signal)
                    if o.fn is None:
                        continue
                    inst = o.fn(eng)
                    if o.is_dma:
                        inst.then_inc(rings[engname][o.ring % NRING], 16)
                    elif o.signal is not None:
                        inst.then_inc(sems[engname], 1)

            @block.tensor
            def _(eng):
                run("pe", eng)

            @block.scalar
            def _(eng):
                run("act", eng)

            @block.vector
            def _(eng):
                run("dve", eng)

            @block.gpsimd
            def _(eng):
                run("pool", eng)

            @block.sync
            def _(eng):
                run("sp", eng)


class Rot:
    def __init__(self, items):
        self.items = list(items)
        self.i = 0

    def next(self):
        x = self.items[self.i % len(self.items)]
        self.i += 1
        return x


class Arena:
    def __init__(self, tens, nwords):
        self.t = tens
        self.n = nwords
        self.off = 0

    def reset(self):
        self.off = 0

    def get(self, shape, dtype=F32, parts=128):
        n = 1
        for s in shape:
            n *= s
        words = (n * _isz(dtype) + 3) // 4
        words += words & 1
        assert self.off + words <= self.n, ("arena overflow", self.off, words, self.n)
        v = self.t[0:parts, self.off:self.off + words]
        self.off += words
        if dtype != F32:
            v = v.bitcast(dtype)
        v = v[:, 0:n]
        if len(shape) == 2:
            v = v.rearrange("p (a b) -> p a b", a=shape[0])
        elif len(shape) == 3:
            v = v.rearrange("p (a b c) -> p a b c", a=shape[0], b=shape[1])
        return v


def bcast(ap, shape):
    return ap.broadcast_to(list(shape))


def build_consts():
    c = np.zeros((128, NCST), np.float32)
    r = np.arange(128)
    c[:, 0:128] = np.eye(128)
    U = (r[:, None] <= r[None, :]).astype(np.float32)
    c[:, 128:256] = U
    c[:, 256:384] = -U
    sel = np.zeros((128, 128), np.float32)
    sel[127, :] = 1.0
    c[:, 384:512] = sel
    c[:, 512:640] = np.where(r[None, :] < r[:, None], NEG, 0.0)
    r64 = np.arange(64)
    same = (r64[:, None] // 4) == (r64[None, :] // 4)
    Us = (same & (r64[:, None] <= r64[None, :])).astype(np.float32)
    c[0:64, 640:704] = Us
    c[0:64, 704:768] = -Us
    sels = np.zeros((64, 64), np.float32)
    for s in range(64):
        sels[4 * (s // 4) + 3, s] = 1.0
    c[0:64, 768:832] = sels
    c[0:64, 832:896] = np.where(same & (r64[None, :] >= r64[:, None]), 0.0, NEG)
    bi = np.zeros((64, 16), np.float32)
    bi[r64, r64 // 4] = 1.0
    c[0:64, 896:912] = bi
    for g, w in enumerate((2, 4, 8, 16)):
        for t in range(16):
            c[:, 912 + g * 16 + t] = 1.0 / min(t + 1, w)
    c[:, 976:1104] = 1.0
    return c


IN_SPECS = [
    ("x_p", (NPT, D)), ("x_s", (NS, D)), ("mem", (256, D)),
    ("st_pool", (DEPTH, NSQ, 15, 512)), ("st_sconv", (DEPTH, NSQ, 3, 1536)), ("st_ssd", (DEPTH, NSQ, 16, 64, 128)),
    ("st_lconv", (DEPTH, NSQ, 3, 512)), ("st_lru", (DEPTH, NSQ, 512)),
    ("ck", (DEPTH, NSQ, 256, D)), ("cv", (DEPTH, NSQ, 256, D)),
    ("norm_mix", (DEPTH, D)), ("w_in", (DEPTH, D, N_IN)), ("pool_w", (DEPTH, 4, 128, 128)), ("pool_scale", (DEPTH, 512)),
    ("ssd_conv_w", (DEPTH, 4, 1536)), ("ssd_conv_b", (DEPTH, 1536)), ("ssd_dt_bias", (DEPTH, 16)), ("ssd_a_log", (DEPTH, 16)),
    ("ssd_d", (DEPTH, 16)), ("ssd_norm", (DEPTH, D)), ("lru_conv_w", (DEPTH, 4, 512)), ("lru_conv_b", (DEPTH, 512)),
    ("lru_wa", (DEPTH, 8, 64, 64)), ("lru_ba", (DEPTH, 8, 64)), ("lru_wx", (DEPTH, 8, 64, 64)), ("lru_bx", (DEPTH, 8, 64)),
    ("lru_lambda", (DEPTH, 512)), ("w_out", (DEPTH, 2048, D)), ("norm_mem", (DEPTH, D)), ("w_mem_q", (DEPTH, D, D)),
    ("w_mem_k", (DEPTH, D, D)), ("w_mem_v", (DEPTH, D, D)), ("w_mem_o", (DEPTH, D, D)), ("norm_ffn", (DEPTH, D)),
    ("w_ffn_gate", (DEPTH, D, DFF)), ("w_ffn_up", (DEPTH, D, DFF)), ("w_ffn_down", (DEPTH, DFF, D)), ("norm_final", (D,)),
    ("cst", (128, NCST)), ("cst2", (128, 1024)),
]
OUT_SPECS = [
    ("y_p", (NPT, D)), ("y_s", (NS, D)), ("p_pool", (DEPTH, 15, 512)), ("p_sconv", (DEPTH, 3, 1536)),
    ("p_ssd", (DEPTH, 16, 64, 128)), ("p_lconv", (DEPTH, 3, 512)), ("p_lru", (DEPTH, 512)),
    ("p_mk", (DEPTH, 256, D)), ("p_mv", (DEPTH, 256, D)),
    ("s_pool", (DEPTH, NSQ, 15, 512)), ("s_sconv", (DEPTH, NSQ, 3, 1536)), ("s_ssd", (DEPTH, NSQ, 16, 64, 128)),
    ("s_lconv", (DEPTH, NSQ, 3, 512)), ("s_lru", (DEPTH, NSQ, 512)),
]


class _Stop(Exception):
    pass


def build_program(n_layers=DEPTH, dbg=False, stop_at=None):
    nc = bass.Bass("TRN2", target_bir_lowering=False)
    I = {n: nc.dram_tensor(n, list(s), F32, kind="ExternalInput").ap() for n, s in IN_SPECS}
    O = {n: nc.dram_tensor(n, list(s), F32, kind="ExternalOutput").ap() for n, s in OUT_SPECS}
    spill = nc.dram_tensor("ssd_spill", [DEPTH, 128, D], F32, kind="Internal").ap()
    if dbg:
        O["dbg_h"] = nc.dram_tensor("dbg_h", [NPASS, DEPTH, 4, 128, KC * NT], F32, kind="ExternalOutput").ap()
    P = Prog(nc)
    with contextlib.ExitStack() as st:
        def sb(name, shape, dt=F32):
            return st.enter_context(nc.sbuf_tensor(name, list(shape), dt))

        def pst(name, shape, dt=F32):
            return st.enter_context(nc.psum_tensor(name, list(shape), dt))

        hT = sb("hT", [128, KC, NT])
        uT = sb("uT", [128, KC, NT], BF16)
        A = sb("A", [128, FC, NT], BF16)
        wbufs = [sb("wb%d" % i, [128, WBUF], BF16) for i in range(2)]
        cst_f = sb("cst_f", [128, NCST])
        cst_b = sb("cst_b", [128, NCST], BF16)
        PAR = sb("PAR", [128, 640])
        RB = sb("RB", [128, 3 * 64])
        DCOL = sb("DCOL", [128, DEPTH, 8])
        C1 = sb("C1", [128, 16])
        memT = sb("memT", [128, KC, 256], BF16)
        POOLW = sb("POOLW", [128, 4, 128], BF16)
        WA = sb("WA", [128, 4, 128], BF16)
        WX = sb("WX", [128, 4, 128], BF16)
        WDT = sb("WDT", [128, KC, 16], BF16)
        pool_tail = sb("pool_tail", [128, DEPTH, 4, 15])
        sconv_tail = sb("sconv_tail", [128, DEPTH, 12, 3])
        lconv_tail = sb("lconv_tail", [128, DEPTH, 4, 3])
        lru_h = sb("lru_h", [128, DEPTH, 4])
        ssdT = sb("ssdT", [128, D])
        ssdT_b = sb("ssdT_b", [128, D], BF16)
        ARW = 15360
        TMP = sb("TMP", [128, ARW])
        ar = Arena(TMP, ARW)

        psA = pst("psA", [128, 1024])
        pbs = [pst("pb%d" % i, [128, 512]) for i in range(6)]
        mmrot = Rot(pbs[0:3])
        smrot = Rot(pbs[3:5])
        psY = pbs[5]
        wrot = Rot(wbufs)

        ident_f = cst_f[:, 0:128]
        ident_b = cst_b[:, 0:128]
        ones_f = cst_f[:, 976:1104]
        ones_b = cst_b[:, 976:1104]
        CP = dict(U=cst_f[:, 128:256], negU=cst_f[:, 256:384], sel=cst_f[:, 384:512], negm=cst_b[:, 512:640], L=128)
        CS = dict(U=cst_f[0:64, 640:704], negU=cst_f[0:64, 704:768], sel=cst_f[0:64, 768:832], negm=cst_b[0:64, 832:896], L=64)
        blockind_f = cst_f[0:64, 896:912]
        blockind_b = cst_b[0:64, 896:912]
        rc_tab = cst_f[:, 912:976].rearrange("p (g t) -> p g t", g=4)

        P.dma(cst_f[:], I["cst"], q="sp")
        P.dma(cst_b[:], I["cst"], q="pool")
        BMASK = sb("BMASK", [128, 16, 64], BF16)
        P.dma(BMASK[:], I["cst2"].rearrange("p (b l) -> p b l", b=16), q="pool")
        prow = {}
        plist = [
            ("nm", I["norm_mix"].rearrange("l (j p) -> (l j) p", p=128)),
            ("nmem", I["norm_mem"].rearrange("l (j p) -> (l j) p", p=128)),
            ("nffn", I["norm_ffn"].rearrange("l (j p) -> (l j) p", p=128)),
            ("nfin", I["norm_final"].rearrange("(j p) -> j p", p=128)),
            ("pscale", I["pool_scale"].rearrange("l (j p) -> (l j) p", p=128)),
            ("scw", I["ssd_conv_w"].rearrange("l k (j p) -> (l k j) p", p=128)),
            ("scb", I["ssd_conv_b"].rearrange("l (j p) -> (l j) p", p=128)),
            ("snorm", I["ssd_norm"].rearrange("l (j p) -> (l j) p", p=128)),
            ("lcw", I["lru_conv_w"].rearrange("l k (j p) -> (l k j) p", p=128)),
            ("lcb", I["lru_conv_b"].rearrange("l (j p) -> (l j) p", p=128)),
            ("lam", I["lru_lambda"].rearrange("l (j p) -> (l j) p", p=128)),
            ("ba", I["lru_ba"].rearrange("l h i -> l (h i)").rearrange("l (j p) -> (l j) p", p=128)),
            ("bx", I["lru_bx"].rearrange("l h i -> l (h i)").rearrange("l (j p) -> (l j) p", p=128)),
        ]
        PST = ar.get([5, 128])
        P.memset(PST, 0.0)
        r0 = 0
        for name, ap2 in plist:
            nr = ap2.shape[0]
            prow[name] = r0
            done = 0
            while done < nr:
                t, rr = divmod(r0 + done, 128)
                n = min(nr - done, 128 - rr)
                P.dma(PST[rr:rr + n, t, :], ap2[done:done + n, :], q="sp")
                done += n
            r0 += nr
        assert r0 <= 640
        for t in range(5):
            ps = smrot.next()
            P.tr(ps[:, 0:128], PST[:, t, :], ident_f)
            P.copy(PAR[:, t * 128:(t + 1) * 128], ps[:, 0:128], eng="act")

        def pcol(name, idx):
            c = prow[name] + idx
            return PAR[:, c:c + 1]

        P.dma(RB[:, 0:64], I["ssd_dt_bias"].rearrange("l h -> (l h)").partition_broadcast(128), q="sp")
        P.dma(RB[:, 64:128], I["ssd_a_log"].rearrange("l h -> (l h)").partition_broadcast(128), q="sp")
        P.dma(RB[:, 128:192], I["ssd_d"].rearrange("l h -> (l h)").partition_broadcast(128), q="sp")
        P.act(RB[:, 64:128], RB[:, 64:128], AF.Exp)
        P.ts(RB[:, 64:128], RB[:, 64:128], -1.0, ALU.mult)
        dview = RB[:, 128:192].rearrange("p (l hp two) -> p l hp two", l=DEPTH, two=2)
        P.copy(DCOL[0:64, :, :], dview[0:64, :, :, 0])
        P.copy(DCOL[64:128, :, :], dview[64:128, :, :, 1])
        lam0 = prow["lam"]
        P.act(C1[:], PAR[:, lam0:lam0 + 16], AF.Exp, scale=-1.0)
        P.act(C1[:], C1[:], AF.Ln, bias=1.0)
        P.ts(C1[:], C1[:], -LRU_C, ALU.mult)
        P.memset(WA[:], 0.0)
        P.memset(WX[:], 0.0)
        for mc in range(2):
            mt = ar.get([D])
            P.dma(mt, I["mem"][mc * 128:(mc + 1) * 128, :], q="sp")
            for k in range(KC):
                P.tr(psA[:, k * 128:(k + 1) * 128], mt[:, k * 128:(k + 1) * 128], ident_f)
            P.copy(memT[:, 0:4, mc * 128:(mc + 1) * 128], psA[:, 0:512].rearrange("p (k n) -> p k n", k=4), eng="act")
            P.copy(memT[:, 4:8, mc * 128:(mc + 1) * 128], psA[:, 512:1024].rearrange("p (k n) -> p k n", k=4), eng="dve")
        for tl in (pool_tail, sconv_tail, lconv_tail, lru_h):
            P.memset(tl[:], 0.0)
        if dbg:
            P.memset(hT[:], 0.0)

        def stream(groups):
            for g in groups:
                wb = wrot.next()
                for dst_fn, src in g["dmas"]:
                    P.dma(dst_fn(wb), src, q="pool")
                g["body"](wb)

        def wview(wb, koff, kc, MG):
            return wb[:, koff * MG:(koff + kc) * MG].rearrange("p (k m) -> p k m", k=kc)

        def linear(wsrcs, m_total, MG, tiles, rhs_fn, consume):
            KCt = sum(kc for _, kc in wsrcs)
            groups = []
            ng = (m_total + MG - 1) // MG
            for gi in range(ng):
                mg = min(MG, m_total - gi * MG)

                def body(wb, gi=gi, mg=mg):
                    wv = wview(wb, 0, KCt, mg)
                    for mc in range(mg // 128):
                        mglob = gi * (MG // 128) + mc
                        for ti, (n0, n1) in enumerate(tiles):
                            ps = mmrot.next()
                            for k in range(KCt):
                                P.mm(ps[:, 0:n1 - n0], wv[:, k, mc * 128:(mc + 1) * 128], rhs_fn(k, n0, n1),
                                     start=(k == 0), stop=(k == KCt - 1))
                            consume(mglob, ti, n0, n1, ps)

                dmas = []
                koff = 0
                for src, kc in wsrcs:
                    dmas.append((lambda wb, koff=koff, kc=kc, mg=mg: wview(wb, koff, kc, mg),
                                 src[:, gi * MG:gi * MG + mg].rearrange("(k p) m -> p k m", p=128)))
                    koff += kc
                groups.append(dict(dmas=dmas, body=body))
            stream(groups)

        def rmsnorm(gname, l, tiles):
            for (n0, n1) in tiles:
                n = n1 - n0
                ps = mmrot.next()
                for k in range(KC):
                    sq = sqrot.next()
                    P.act(sq[:, 0:n], hT[:, k, n0:n1], AF.Square)
                    P.mm(ps[:, 0:n], ones_b, sq[:, 0:n], start=(k == 0), stop=(k == KC - 1))
                P.act(rstd[:, 0:n], ps[:, 0:n], AF.Sqrt, scale=1.0 / D, bias=epsc[:, 0:1])
                P.recip(rstd[:, 0:n], rstd[:, 0:n])
                for k in range(KC):
                    P.stt(uT[:, k, n0:n1], hT[:, k, n0:n1], pcol(gname, l * KC + k), rstd[:, 0:n], ALU.mult, ALU.mult)

        def add_resid(m, ti, n0, n1, ps):
            P.tt(hT[:, m, n0:n1], hT[:, m, n0:n1], ps[:, 0:n1 - n0], ALU.add)

        epsc = sb("epsc", [128, 1])
        P.memset(epsc[:], EPS)

        def stage(name):
            if stop_at == name:
                raise _Stop()

        stage("setup")
        try:
          for pas in range(NPASS):
              t0 = pas * NPP
              has_s = pas == NPASS - 1
              first = pas == 0
              last = pas == NPASS - 1
              ncol = NPP + (NS if has_s else 0)
              tiles = [(i * 512, (i + 1) * 512) for i in range(NPP // 512)] + ([(NPP, NT)] if has_s else [])
              nchunk = NPP // 128

              ar.reset()
              xrot = Rot([ar.get([D]) for _ in range(2)])
              for blk in range(nchunk):
                  xt = xrot.next()
                  P.dma(xt, I["x_p"][t0 + blk * 128:t0 + (blk + 1) * 128, :], q="sp")
                  for k in range(KC):
                      P.tr(psA[:, k * 128:(k + 1) * 128], xt[:, k * 128:(k + 1) * 128], ident_f)
                  P.copy(hT[:, 0:4, blk * 128:(blk + 1) * 128], psA[:, 0:512].rearrange("p (k n) -> p k n", k=4), eng="act")
                  P.copy(hT[:, 4:8, blk * 128:(blk + 1) * 128], psA[:, 512:1024].rearrange("p (k n) -> p k n", k=4), eng="dve")
              if has_s:
                  xt = xrot.next()
                  P.dma(xt[0:64, :], I["x_s"], q="sp")
                  for k in range(KC):
                      P.tr(psA[:, k * 64:(k + 1) * 64], xt[0:64, k * 128:(k + 1) * 128], ident_f[0:64, 0:64])
                  P.copy(hT[:, :, NPP:NT], psA[:, 0:512].rearrange("p (k n) -> p k n", k=8), eng="act")
              stage("loadx")

              for l in range(n_layers):
                  ar.reset()
                  sqrot = Rot([ar.get([512], BF16) for _ in range(2)])
                  rstd = ar.get([512])
                  arena_base = ar.off
                  rmsnorm("nm", l, tiles)
                  stage("norm1")
                  P.dma(POOLW[:], I["pool_w"][l].rearrange("g c d -> c g d"), q="pool")
                  for h2 in range(2):
                      P.dma(WA[h2 * 64:(h2 + 1) * 64, :, h2 * 64:(h2 + 1) * 64],
                            I["lru_wa"][l].rearrange("(j two) i o -> two i j o", two=2)[h2], q="pool")
                      P.dma(WX[h2 * 64:(h2 + 1) * 64, :, h2 * 64:(h2 + 1) * 64],
                            I["lru_wx"][l].rearrange("(j two) i o -> two i j o", two=2)[h2], q="pool")
                  P.dma(WDT[:], I["w_in"][l][:, OFF_DT:OFF_DT + 16].rearrange("(k p) m -> p k m", p=128), q="pool")
                  w_in_l = I["w_in"][l]
                  w_out_l = I["w_out"][l]
                  rhs_u = lambda k, n0, n1: uT[:, k, n0:n1]
                  deferred = []

                  def flush():
                      while deferred:
                          deferred.pop(0)()

                  if has_s:
                      SPin = ar.get([2, 512], parts=120)
                      P.dma(SPin, I["st_pool"][l].rearrange("(h b) r c -> (b r) h c", h=2), q="sp")
                      SPout = ar.get([2, 512], parts=120)
                      LCin = ar.get([512], parts=48)
                      P.dma(LCin, I["st_lconv"][l].rearrange("b r c -> (b r) c"), q="sp")
                      LCout = ar.get([512], parts=48)
                      LRin = ar.get([512], parts=16)
                      P.dma(LRin, I["st_lru"][l], q="sp")
                      LRout = ar.get([512], parts=16)
                      lru_h0 = ar.get([4, 16])
                  sbase = ar.off

                  RAWP = 15 + NPP + 16 * 19 + 2
                  rawrot = Rot([ar.get([RAWP]) for _ in range(2)])
                  tA = ar.get([RAWP])
                  tB = ar.get([RAWP])
                  plrot = Rot([ar.get([NT], BF16) for _ in range(2)])
                  tmp15 = ar.get([16])
                  tmp240 = ar.get([240])
                  cur_raw = [None]
                  SOFFP = 15 + NPP
                  ya = A[:, 0:4, :]

                  def svw(t, off, nb, w):
                      return t[:, off:off + nb * w].rearrange("p (b r) -> p b r", b=nb)

                  def pool_consume(j, ti, n0, n1, ps):
                      if ti == 0:
                          cur_raw[0] = rawrot.next()
                          raw = cur_raw[0]
                          P.copy(raw[:, 0:15], pool_tail[:, l, j, :], eng="pool")
                          if has_s:
                              pp = smrot.next()
                              for h in range(2):
                                  P.tr(pp[:, h * 120:(h + 1) * 120], SPin[:, h, j * 128:(j + 1) * 128], ident_f[0:120, 0:120])
                              P.copy(svw(raw, SOFFP, 16, 19)[:, :, 0:15], pp[:, 0:240].rearrange("p (b r) -> p b r", b=16), eng="dve")
                      raw = cur_raw[0]
                      if n0 < NPP:
                          P.copy(raw[:, 15 + n0:15 + n1], ps[:, 0:n1 - n0], eng="act")
                      else:
                          P.copy(svw(raw, SOFFP, 16, 19)[:, :, 15:19], ps[:, 0:64].rearrange("p (b r) -> p b r", b=16), eng="act")
                      if ti != len(tiles) - 1:
                          return
                      flush()
                      w = 2 << j
                      Wd = 15 + NPP
                      cur = raw
                      bufs = [tA, tB]
                      for lev in range(j + 1):
                          sh = 1 << lev
                          lo = (2 << lev) - 1
                          dst = bufs[lev % 2]
                          P.tt(dst[:, lo:Wd], cur[:, lo:Wd], cur[:, lo - sh:Wd - sh], ALU.add)
                          if has_s:
                              P.tt(svw(dst, SOFFP, 16, 19)[:, :, lo:19], svw(cur, SOFFP, 16, 19)[:, :, lo:19],
                                   svw(cur, SOFFP, 16, 19)[:, :, lo - sh:19 - sh], ALU.add)
                          cur = dst
                      pl = plrot.next()
                      P.stt(pl[:, 0:NPP], cur[:, 15:15 + NPP], 1.0 / w, raw[:, 15:15 + NPP], ALU.mult, ALU.subtract)
                      if first:
                          P.tt(tmp15[:, 0:15], cur[:, 15:30], rc_tab[:, j, 0:15], ALU.mult)
                          P.tt(pl[:, 0:15], tmp15[:, 0:15], raw[:, 15:30], ALU.subtract)
                      if has_s:
                          P.stt(pl[:, NPP:NT].rearrange("p (b r) -> p b r", b=16), svw(cur, SOFFP, 16, 19)[:, :, 15:19], 1.0 / w,
                                svw(raw, SOFFP, 16, 19)[:, :, 15:19], ALU.mult, ALU.subtract)
                      P.copy(pool_tail[:, l, j, :], raw[:, NPP:NPP + 15], eng="pool")
                      if has_s:
                          P.copy(tmp240.rearrange("p (b r) -> p b r", b=16), svw(raw, SOFFP, 16, 19)[:, :, 4:19], eng="pool")

                      def pe_part(j=j, pl=pl):
                          for (m0, m1) in tiles:
                              ps2 = mmrot.next()
                              P.mm(ps2[:, 0:m1 - m0], POOLW[:, j, :], pl[:, m0:m1])
                              P.act(ya[:, j, m0:m1], ps2[:, 0:m1 - m0], AF.Identity, scale=pcol("pscale", l * 4 + j))
                          if has_s:
                              pp = smrot.next()
                              for h in range(2):
                                  P.tr(pp[0:120, h * 128:(h + 1) * 128], tmp240[:, h * 120:(h + 1) * 120], ident_f)
                              P.copy(SPout[:, :, j * 128:(j + 1) * 128], pp[0:120, 0:256].rearrange("p (h c) -> p h c", h=2), eng="act")

                      deferred.append(pe_part)

                  linear([(w_in_l[:, OFF_POOL:OFF_POOL + 512], KC)], 512, 512, tiles, rhs_u, pool_consume)
                  flush()
                  stage("pool")
                  if has_s:
                      P.dma(O["s_pool"][l].rearrange("(h b) r c -> (b r) h c", h=2), SPout, q="sp", is_output=True)

                  ar.off = sbase
                  RAWL = 3 + NPP + 16 * 7 + 1
                  SOFFL = 3 + NPP
                  rawrot = Rot([ar.get([RAWL]) for _ in range(2)])
                  acc = ar.get([NT])
                  xcb = ar.get([NT], BF16)
                  rr_ = ar.get([NT])
                  ii_ = ar.get([NT])
                  mm_ = ar.get([NT])
                  tmp48 = ar.get([48])
                  tmp16 = ar.get([16])
                  gel = A[:, 8:12, :]
                  yc = A[:, 4:8, :]
                  if has_s:
                      pp = smrot.next()
                      for j in range(4):
                          P.tr(pp[:, j * 16:(j + 1) * 16], LRin[:, j * 128:(j + 1) * 128], ident_f[0:16, 0:16])
                      P.copy(lru_h0, pp[:, 0:64].rearrange("p (j b) -> p j b", j=4), eng="act")

                  def lru_consume(m, ti, n0, n1, ps):
                      n = n1 - n0
                      if m < 4:
                          P.act(gel[:, m, n0:n1], ps[:, 0:n], AF.Gelu_apprx_tanh)
                          return
                      j = m - 4
                      if ti == 0:
                          cur_raw[0] = rawrot.next()
                          raw = cur_raw[0]
                          P.copy(raw[:, 0:3], lconv_tail[:, l, j, :], eng="pool")
                          if has_s:
                              pp = smrot.next()
                              P.tr(pp[:, 0:48], LCin[:, j * 128:(j + 1) * 128], ident_f[0:48, 0:48])
                              P.copy(svw(raw, SOFFL, 16, 7)[:, :, 0:3], pp[:, 0:48].rearrange("p (b r) -> p b r", b=16), eng="dve")
                      raw = cur_raw[0]
                      if n0 < NPP:
                          P.copy(raw[:, 3 + n0:3 + n1], ps[:, 0:n], eng="act")
                      else:
                          P.copy(svw(raw, SOFFL, 16, 7)[:, :, 3:7], ps[:, 0:64].rearrange("p (b r) -> p b r", b=16), eng="act")
                      if ti != len(tiles) - 1:
                          return
                      flush()
                      for k in range(4):
                          wk = pcol("lcw", (l * 4 + k) * 4 + j)
                          if k == 0:
                              P.ts(acc[:, 0:NPP], raw[:, 0:NPP], wk, ALU.mult, pcol("lcb", l * 4 + j), ALU.add)
                              if has_s:
                                  P.ts(acc[:, NPP:NT].rearrange("p (b r) -> p b r", b=16), svw(raw, SOFFL, 16, 7)[:, :, 0:4], wk, ALU.mult,
                                       pcol("lcb", l * 4 + j), ALU.add)
                          else:
                              P.stt(acc[:, 0:NPP], raw[:, k:k + NPP], wk, acc[:, 0:NPP], ALU.mult, ALU.add)
                              if has_s:
                                  av = acc[:, NPP:NT].rearrange("p (b r) -> p b r", b=16)
                                  P.stt(av, svw(raw, SOFFL, 16, 7)[:, :, k:k + 4], wk, av, ALU.mult, ALU.add)
                      P.copy(lconv_tail[:, l, j, :], raw[:, NPP:NPP + 3], eng="pool")
                      if has_s:
                          P.copy(tmp48.rearrange("p (b r) -> p b r", b=16), svw(raw, SOFFL, 16, 7)[:, :, 4:7], eng="pool")
                      P.copy(xcb[:, 0:ncol], acc[:, 0:ncol], eng="act")

                      def pe_part(j=j):
                          for (m0, m1) in tiles:
                              psr = mmrot.next()
                              P.mm(psr[:, 0:m1 - m0], WA[:, j, :], xcb[:, m0:m1])
                              P.act(rr_[:, m0:m1], psr[:, 0:m1 - m0], AF.Sigmoid, bias=pcol("ba", l * 4 + j))
                              psi = mmrot.next()
                              P.mm(psi[:, 0:m1 - m0], WX[:, j, :], xcb[:, m0:m1])
                              P.act(ii_[:, m0:m1], psi[:, 0:m1 - m0], AF.Sigmoid, bias=pcol("bx", l * 4 + j))
                          if has_s:
                              pp = smrot.next()
                              P.tr(pp[0:48, 0:128], tmp48[:, 0:48], ident_f)
                              P.copy(LCout[:, j * 128:(j + 1) * 128], pp[0:48, 0:128], eng="act")
                          nn = ncol
                          P.act(rr_[:, 0:nn], rr_[:, 0:nn], AF.Exp, scale=C1[:, l * 4 + j:l * 4 + j + 1])
                          P.tt(mm_[:, 0:nn], rr_[:, 0:nn], rr_[:, 0:nn], ALU.mult)
                          P.act(mm_[:, 0:nn], mm_[:, 0:nn], AF.Sqrt, scale=-1.0, bias=1.0)
                          if first:
                              P.memset(mm_[:, 0:1], 1.0)
                          P.tt(ii_[:, 0:nn], ii_[:, 0:nn], mm_[:, 0:nn], ALU.mult)
                          P.tt(ii_[:, 0:nn], ii_[:, 0:nn], acc[:, 0:nn], ALU.mult)
                          if has_s:
                              a_s = rr_[:, NPP:NT].rearrange("p (b r) -> p b r", b=16)
                              b_s = ii_[:, NPP:NT].rearrange("p (b r) -> p b r", b=16)
                              t16 = tmp16.rearrange("p (b o) -> p b o", o=1)
                              P.tt(t16, a_s[:, :, 0:1], lru_h0[:, j, :].rearrange("p (b o) -> p b o", o=1), ALU.mult)
                              P.tt(b_s[:, :, 0:1], b_s[:, :, 0:1], t16, ALU.add)
                              P.memset(a_s[:, :, 0:1], 0.0)
                          P.scan(mm_[:, 0:NPP], rr_[:, 0:NPP], ii_[:, 0:NPP], lru_h[:, l, j:j + 1], ALU.mult, ALU.add)
                          if has_s:
                              P.scan(mm_[:, NPP:NT], rr_[:, NPP:NT], ii_[:, NPP:NT], 0.0, ALU.mult, ALU.add)
                          P.copy(lru_h[:, l, j:j + 1], mm_[:, NPP - 1:NPP], eng="pool")
                          if has_s:
                              P.copy(tmp16.rearrange("p (b o) -> p b o", o=1), mm_[:, NPP:NT].rearrange("p (b r) -> p b r", b=16)[:, :, 3:4], eng="pool")
                              pp = smrot.next()
                              P.tr(pp[0:16, 0:128], tmp16[:, 0:16], ident_f)
                              P.copy(LRout[:, j * 128:(j + 1) * 128], pp[0:16, 0:128], eng="act")
                          P.tt(yc[:, j, 0:nn], mm_[:, 0:nn], gel[:, j, 0:nn], ALU.mult)

                      deferred.append(pe_part)

                  linear([(w_in_l[:, OFF_GATE:OFF_GATE + 1024], KC)], 1024, 512, tiles, rhs_u, lru_consume)
                  flush()
                  stage("lru")
                  if has_s:
                      P.dma(O["s_lconv"][l].rearrange("b r c -> (b r) c"), LCout, q="sp", is_output=True)
                      P.dma(O["s_lru"][l], LRout, q="sp", is_output=True)

                  linear([(w_out_l[0:512, :], 4), (w_out_l[1536:2048, :], 4)], D, 512, tiles,
                         lambda k, n0, n1: A[:, k, n0:n1], add_resid)
                  stage("wout1")

                  ar.off = arena_base
                  xbc = A[:, 0:12, :]
                  zs = A[:, 12:20, :]
                  dtb = RB[:, l * 16:(l + 1) * 16]
                  Abc = RB[:, 64 + l * 16:64 + (l + 1) * 16]
                  dt_all = ar.get([nchunk + 1, 16])
                  dA_all = ar.get([nchunk + 1, 16])
                  acs_tok = ar.get([16])
                  decst = ar.get([16])
                  eacs = ar.get([16])
                  cdb = ar.get([16])
                  dtdec = ar.get([16])
                  Rr = Rot([ar.get([8, 128]) for _ in range(2)])
                  Er = Rot([ar.get([8, 128], BF16) for _ in range(2)])
                  Mr = Rot([ar.get([8, 128], BF16) for _ in range(2)])
                  xdtr = Rot([ar.get([512], BF16) for _ in range(2)])
                  xddr = Rot([ar.get([512], BF16) for _ in range(2)])
                  yofr = Rot([ar.get([512], BF16) for _ in range(2)])
                  Btr = Rot([ar.get([128], BF16) for _ in range(2)])
                  CBr = Rot([ar.get([128], BF16) for _ in range(2)])

                  sbase3 = ar.off
                  if has_s:
                      SCin = ar.get([1536], parts=48)
                      P.dma(SCin, I["st_sconv"][l].rearrange("b r c -> (b r) c"), q="sp")
                      SCout = ar.get([1536], parts=48)
                  sbase2 = ar.off
                  RAWS = 3 + NPP + 16 * 7 + 1
                  rawrot = Rot([ar.get([RAWS]) for _ in range(2)])
                  acc = ar.get([NT])
                  tmp48 = ar.get([48])

                  def xbc_consume(j, ti, n0, n1, ps):
                      n = n1 - n0
                      if ti == 0:
                          cur_raw[0] = rawrot.next()
                          raw = cur_raw[0]
                          P.copy(raw[:, 0:3], sconv_tail[:, l, j, :], eng="pool")
                          if has_s:
                              pp = smrot.next()
                              P.tr(pp[:, 0:48], SCin[:, j * 128:(j + 1) * 128], ident_f[0:48, 0:48])
                              P.copy(svw(raw, SOFFL, 16, 7)[:, :, 0:3], pp[:, 0:48].rearrange("p (b r) -> p b r", b=16), eng="dve")
                      raw = cur_raw[0]
                      if n0 < NPP:
                          P.copy(raw[:, 3 + n0:3 + n1], ps[:, 0:n], eng="act")
                      else:
                          P.copy(svw(raw, SOFFL, 16, 7)[:, :, 3:7], ps[:, 0:64].rearrange("p (b r) -> p b r", b=16), eng="act")
                      if ti != len(tiles) - 1:
                          return
                      for k in range(4):
                          wk = pcol("scw", (l * 4 + k) * 12 + j)
                          if k == 0:
                              P.ts(acc[:, 0:NPP], raw[:, 0:NPP], wk, ALU.mult, pcol("scb", l * 12 + j), ALU.add)
                              if has_s:
                                  P.ts(acc[:, NPP:NT].rearrange("p (b r) -> p b r", b=16), svw(raw, SOFFL, 16, 7)[:, :, 0:4], wk, ALU.mult,
                                       pcol("scb", l * 12 + j), ALU.add)
                          else:
                              P.stt(acc[:, 0:NPP], raw[:, k:k + NPP], wk, acc[:, 0:NPP], ALU.mult, ALU.add)
                              if has_s:
                                  av = acc[:, NPP:NT].rearrange("p (b r) -> p b r", b=16)
                                  P.stt(av, svw(raw, SOFFL, 16, 7)[:, :, k:k + 4], wk, av, ALU.mult, ALU.add)
                      P.copy(sconv_tail[:, l, j, :], raw[:, NPP:NPP + 3], eng="pool")
                      P.act(xbc[:, j, 0:ncol], acc[:, 0:ncol], AF.Silu)
                      if has_s:
                          P.copy(tmp48.rearrange("p (b r) -> p b r", b=16), svw(raw, SOFFL, 16, 7)[:, :, 4:7], eng="pool")
                          pp = smrot.next()
                          P.tr(pp[0:48, 0:128], tmp48[:, 0:48], ident_f)
                          P.copy(SCout[:, j * 128:(j + 1) * 128], pp[0:48, 0:128], eng="act")

                  linear([(w_in_l[:, OFF_XBC:OFF_XBC + 1536], KC)], 1536, 512, tiles, rhs_u, xbc_consume)
                  if has_s:
                      P.dma(O["s_sconv"][l].rearrange("b r c -> (b r) c"), SCout, q="sp", is_output=True)

                  def z_consume(m, ti, n0, n1, ps):
                      P.act(zs[:, m, n0:n1], ps[:, 0:n1 - n0], AF.Silu)

                  linear([(w_in_l[:, OFF_Z:OFF_Z + 1024], KC)], 1024, 512, tiles, rhs_u, z_consume)
                  stage("xbcz")

                  psd = smrot.next()
                  for c in range(nchunk):
                      for k in range(KC):
                          P.mm(psd[:, c * 16:(c + 1) * 16], uT[:, k, c * 128:(c + 1) * 128], WDT[:, k, :], start=(k == 0), stop=(k == KC - 1))
                  P.tt(dt_all[:, 0:nchunk, :], psd[:, 0:nchunk * 16].rearrange("p (c h) -> p c h", h=16),
                       bcast(dtb.rearrange("p (o h) -> p o h", o=1), [128, nchunk, 16]), ALU.add)
                  if has_s:
                      psd2 = smrot.next()
                      for k in range(KC):
                          P.mm(psd2[0:64, 0:16], uT[:, k, NPP:NT], WDT[:, k, :], start=(k == 0), stop=(k == KC - 1))
                      P.memset(dt_all[:, nchunk, :], 0.0)
                      P.tt(dt_all[0:64, nchunk, :], psd2[0:64, 0:16], dtb[0:64, :], ALU.add)
                  nch_all = nchunk + (1 if has_s else 0)
                  P.act(dt_all[:, 0:nch_all, :], dt_all[:, 0:nch_all, :], AF.Exp)
                  P.act(dt_all[:, 0:nch_all, :], dt_all[:, 0:nch_all, :], AF.Ln, bias=1.0)
                  P.tt(dA_all[:, 0:nch_all, :], dt_all[:, 0:nch_all, :], bcast(Abc.rearrange("p (o h) -> p o h", o=1), [128, nch_all, 16]), ALU.mult)

                  if first:
                      P.memset(ssdT[:], 0.0)
                      P.memset(ssdT_b[:], 0.0)
                  else:
                      P.dma(ssdT[:], spill[l], q="sp")
                      P.copy(ssdT_b[:], ssdT[:], eng="act")
                  stage("dt")

                  def chunk_common(c, cols, K):
                      L = K["L"]
                      dA_c = dA_all[0:L, c, :]
                      ps1 = smrot.next()
                      P.mm(ps1[0:L, 0:16], K["U"], dA_c)
                      P.copy(acs_tok[0:L, :], ps1[0:L, 0:16], eng="dve")
                      P.mm(ps1[0:L, 16:32], K["sel"], acs_tok[0:L, :])
                      P.tt(decst[0:L, :], ps1[0:L, 16:32], acs_tok[0:L, :], ALU.subtract)
                      P.act(decst[0:L, :], decst[0:L, :], AF.Exp)
                      P.act(eacs[0:L, :], acs_tok[0:L, :], AF.Exp)
                      if L == 128:
                          P.mm(ps1[:, 32:48], ones_f, dA_c)
                          P.copy(cdb[:], ps1[:, 32:48], eng="dve")
                          P.act(cdb[:], cdb[:], AF.Exp)
                      P.tt(dtdec[0:L, :], dt_all[0:L, c, :], decst[0:L, :], ALU.mult)

                  def unit_pre(c, g, cols, K):
                      L = K["L"]
                      c0, c1 = cols
                      dA_g = dA_all[0:L, c, 8 * g:8 * g + 8]
                      pxt = smrot.next()
                      pxb = pxt[:, 0:256].bitcast(BF16)
                      for j in range(4):
                          P.tr(pxb[0:L, j * 128:(j + 1) * 128], xbc[:, 4 * g + j, c0:c1], ident_b)
                      xdt = xdtr.next()
                      xdd = xddr.next()
                      pv = pxb[0:L, :].rearrange("p (h q) -> p h q", h=8)
                      P.tt(xdt[0:L, :].rearrange("p (h q) -> p h q", h=8), pv,
                           bcast(dt_all[0:L, c, 8 * g:8 * g + 8].rearrange("p (h o) -> p h o", o=1), [L, 8, 64]), ALU.mult)
                      P.tt(xdd[0:L, :].rearrange("p (h q) -> p h q", h=8), pv,
                           bcast(dtdec[0:L, 8 * g:8 * g + 8].rearrange("p (h o) -> p h o", o=1), [L, 8, 64]), ALU.mult)
                      pbt = smrot.next()
                      pbb = pbt[:, 0:256].bitcast(BF16)
                      P.tr(pbb[0:L, 0:128], xbc[:, 8 + g, c0:c1], ident_b)
                      Bt = Btr.next()
                      P.copy(Bt[0:L, :], pbb[0:L, 0:128], eng="act")
                      R = Rr.next()
                      Rv = R[0:L, :, 0:L]
                      P.tt(Rv, bcast(K["U"].rearrange("p (o l) -> p o l", o=1), [L, 8, L]),
                           bcast(dA_g.rearrange("p (h o) -> p h o", o=1), [L, 8, L]), ALU.mult)
                      E = Er.next()
                      nh = 512 // L
                      for half in range(8 // nh):
                          seg = psA[0:L, half * 512:half * 512 + nh * L].rearrange("p (h l) -> p h l", h=nh)
                          hs = slice(half * nh, (half + 1) * nh)
                          P.mm(seg, ones_f[0:L, 0:L], Rv[:, hs, :], start=True, stop=False)
                          P.mm(seg, K["negU"], bcast(dA_g[:, hs].rearrange("p (h o) -> p h o", o=1), [L, nh, L]), start=False, stop=False)
                          P.mm(seg, ident_b[0:L, 0:L], bcast(K["negm"].rearrange("p (o l) -> p o l", o=1), [L, nh, L]), start=False, stop=True)
                          P.act(E[0:L, hs, 0:L], seg, AF.Exp)
                      pcb = smrot.next()
                      P.mm(pcb[0:L, 0:L], xbc[:, 8 + g, c0:c1], xbc[:, 10 + g, c0:c1])
                      CBs = CBr.next()
                      P.copy(CBs[0:L, 0:L], pcb[0:L, 0:L], eng="act")
                      M = Mr.next()
                      P.tt(M[0:L, :, 0:L], E[0:L, :, 0:L], bcast(CBs[0:L, 0:L].rearrange("p (o l) -> p o l", o=1), [L, 8, L]), ALU.mult)
                      return dict(xdt=xdt, xdd=xdd, Bt=Bt, M=M)

                  def unit_y(c, g, cols, K, H, pyo):
                      L = K["L"]
                      c0, c1 = cols
                      yof = yofr.next()
                      P.tt(yof[0:L, :].rearrange("p (h q) -> p h q", h=8), pyo[0:L, 0:512].rearrange("p (h q) -> p h q", h=8),
                           bcast(eacs[0:L, 8 * g:8 * g + 8].rearrange("p (h o) -> p h o", o=1), [L, 8, 64]), ALU.mult)
                      Y = psY[:, 0:4 * L].rearrange("p (j l) -> p j l", j=4)
                      for k in range(8):
                          out = Y[64 * (k % 2):64 * (k % 2) + 64, k // 2, :]
                          P.mm(out, H["xdt"][0:L, 64 * k:64 * k + 64], H["M"][0:L, k, 0:L], start=True, stop=False)
                          P.mm(out, yof[0:L, 64 * k:64 * k + 64], ident_b[0:L, 0:L], start=False, stop=True)
                      for j in range(4):
                          P.stt(xbc[:, 4 * g + j, c0:c1], xbc[:, 4 * g + j, c0:c1], DCOL[:, l, 4 * g + j:4 * g + j + 1], Y[:, j, :], ALU.mult, ALU.add)

                  for c in range(nchunk):
                      cols = (c * 128, (c + 1) * 128)
                      chunk_common(c, cols, CP)
                      for g in range(2):
                          H = unit_pre(c, g, cols, CP)
                          pyo = mmrot.next()
                          P.mm(pyo[:, 0:512], xbc[:, 10 + g, cols[0]:cols[1]], ssdT_b[:, 512 * g:512 * (g + 1)])
                          unit_y(c, g, cols, CP, H, pyo)
                          pdl = mmrot.next()
                          P.mm(pdl[:, 0:512], H["Bt"][:, :], H["xdd"][:, :])
                          sv = ssdT[:, 512 * g:512 * (g + 1)].rearrange("p (h q) -> p h q", h=8)
                          P.tt(sv, sv, bcast(cdb[:, 8 * g:8 * g + 8].rearrange("p (h o) -> p h o", o=1), [128, 8, 64]), ALU.mult)
                          P.tt(ssdT[:, 512 * g:512 * (g + 1)], ssdT[:, 512 * g:512 * (g + 1)], pdl[:, 0:512], ALU.add)
                          P.copy(ssdT_b[:, 512 * g:512 * (g + 1)], ssdT[:, 512 * g:512 * (g + 1)], eng="act")
                      stage("scan1")
                  if not last:
                      P.dma(spill[l], ssdT[:], q="sp")
                      stage("scanp")
                  else:
                      for half in range(2):
                          for j in range(4):
                              P.tr(psA[:, j * 128:(j + 1) * 128], ssdT[:, (half * 4 + j) * 128:(half * 4 + j + 1) * 128], ident_f)
                          so = ar.get([4, 128])
                          P.copy(so, psA[:, 0:512].rearrange("p (j n) -> p j n", j=4), eng="act")
                          P.dma(O["p_ssd"][l].rearrange("(hp two) q n -> (two q) hp n", two=2)[:, half * 4:(half + 1) * 4, :], so, q="sp", is_output=True)

                  if has_s:
                      ar.off = sbase3
                      c = nchunk
                      cols = (NPP, NT)
                      chunk_common(c, cols, CS)
                      Hs = [unit_pre(c, g, cols, CS) for g in range(2)]
                      rblk = ar.get([16, 16], parts=64)
                      P.tt(rblk, bcast(dA_all[0:64, c, :].rearrange("p (o h) -> p o h", o=1), [64, 16, 16]),
                           bcast(blockind_f.rearrange("p (b o) -> p b o", o=1), [64, 16, 16]), ALU.mult)
                      pda = smrot.next()
                      P.mm(pda[:, 0:256], ones_f[0:64, :], rblk.rearrange("p b h -> p (b h)"))
                      dec_all = ar.get([16, 16])
                      P.act(dec_all.rearrange("p b h -> p (b h)"), pda[:, 0:256], AF.Exp)
                      Cm = []
                      Bblk = []
                      for g in range(2):
                          cm = ar.get([16, 64], BF16)
                          P.tt(cm, bcast(xbc[:, 10 + g, NPP:NT].rearrange("p (o l) -> p o l", o=1), [128, 16, 64]), BMASK[:], ALU.mult)
                          Cm.append(cm)
                          bb = ar.get([16, 128], BF16, parts=64)
                          P.tt(bb, bcast(Hs[g]["Bt"][0:64, :].rearrange("p (o n) -> p o n", o=1), [64, 16, 128]),
                               bcast(blockind_b.rearrange("p (b o) -> p b o", o=1), [64, 16, 128]), ALU.mult)
                          Bblk.append(bb)
                      S0r = Rot([ar.get([8, 128]) for _ in range(2)])
                      h0Tr = Rot([ar.get([D], BF16) for _ in range(2)])
                      pyo = [mmrot.next(), mmrot.next()]
                      for b in range(NSQ):
                          S0 = S0r.next()
                          P.dma(S0, I["st_ssd"][l, b].rearrange("(hp two) q n -> (two q) hp n", two=2), q="sp")
                          h0T = h0Tr.next()
                          for half in range(2):
                              for j in range(4):
                                  P.tr(psA[:, j * 128:(j + 1) * 128] if half == 0 else psA[:, 512 + j * 128:512 + (j + 1) * 128],
                                       S0[:, half * 4 + j, :], ident_f)
                          P.copy(h0T[:, 0:512], psA[:, 0:512], eng="act")
                          P.copy(h0T[:, 512:1024], psA[:, 512:1024], eng="dve")
                          for g in range(2):
                              P.mm(pyo[g][0:64, 0:512], Cm[g][:, b, :], h0T[:, 512 * g:512 * (g + 1)], start=(b == 0), stop=(b == NSQ - 1))
                          hn = S0
                          for h2 in range(2):
                              dsl = dec_all[h2 * 64:(h2 + 1) * 64, b, :].rearrange("p (hp two) -> p hp two", two=2)[:, :, h2:h2 + 1]
                              P.tt(hn[h2 * 64:(h2 + 1) * 64, :, :], S0[h2 * 64:(h2 + 1) * 64, :, :], bcast(dsl, [64, 8, 128]), ALU.mult)
                          for half in range(2):
                              pdl = smrot.next()
                              for j in range(4):
                                  hp = half * 4 + j
                                  g = hp // 4
                                  P.mm(pdl[:, j * 128:(j + 1) * 128], Hs[g]["xdd"][0:64, (hp % 4) * 128:(hp % 4 + 1) * 128], Bblk[g][:, b, :])
                              P.tt(hn[:, half * 4:(half + 1) * 4, :], hn[:, half * 4:(half + 1) * 4, :],
                                   pdl[:, 0:512].rearrange("p (j n) -> p j n", j=4), ALU.add)
                          P.dma(O["s_ssd"][l, b].rearrange("(hp two) q n -> (two q) hp n", two=2), hn, q="sp", is_output=True)
                      for g in range(2):
                          unit_y(c, g, cols, CS, Hs[g], pyo[g])

                  ar.off = arena_base
                  yg = A[:, 0:8, :]
                  for (n0, n1) in tiles:
                      n = n1 - n0
                      ps = mmrot.next()
                      for k in range(KC):
                          P.tt(yg[:, k, n0:n1], yg[:, k, n0:n1], zs[:, k, n0:n1], ALU.mult)
                          sq = sqrot.next()
                          P.act(sq[:, 0:n], yg[:, k, n0:n1], AF.Square)
                          P.mm(ps[:, 0:n], ones_b, sq[:, 0:n], start=(k == 0), stop=(k == KC - 1))
                      P.act(rstd[:, 0:n], ps[:, 0:n], AF.Sqrt, scale=1.0 / D, bias=epsc[:, 0:1])
                      P.recip(rstd[:, 0:n], rstd[:, 0:n])
                      for k in range(KC):
                          P.stt(yg[:, k, n0:n1], yg[:, k, n0:n1], pcol("snorm", l * KC + k), rstd[:, 0:n], ALU.mult, ALU.mult)
                  linear([(w_out_l[512:1536, :], 8)], D, 512, tiles, lambda k, n0, n1: A[:, k, n0:n1], add_resid)
                  stage("gate")
                  if dbg:
                      P.dma(O["dbg_h"][pas, l, 0], hT[:].rearrange("p k n -> p (k n)"), q="sp", is_output=True)

                  ar.off = arena_base
                  rmsnorm("nmem", l, tiles)
                  stage("norm2")
                  qT = A[:, 0:8, :]
                  oT = A[:, 8:16, :]
                  kvst = Rot([ar.get([512]) for _ in range(2)])
                  KT = ar.get([KC, 256], BF16)
                  VT = ar.get([2, D], BF16)

                  def kt_consume(m, ti, n0, n1, ps):
                      P.copy(KT[:, m, :], ps[:, 0:256], eng="act")

                  linear([(I["w_mem_k"][l], KC)], D, 512, [(0, 256)], lambda k, n0, n1: memT[:, k, n0:n1], kt_consume)
                  stage("kt")

                  def tokmajor_proj(wname, oname, keep):
                      groups = []
                      for gi in range(2):
                          def body(wb, gi=gi):
                              wv = wview(wb, 0, KC, 512)
                              for mc in range(2):
                                  ps = mmrot.next()
                                  for k in range(KC):
                                      P.mm(ps[:, 0:512], memT[:, k, mc * 128:(mc + 1) * 128], wv[:, k, :], start=(k == 0), stop=(k == KC - 1))
                                  if first:
                                      stg = kvst.next()
                                      P.copy(stg, ps[:, 0:512], eng="dve")
                                      if keep is not None:
                                          P.copy(keep[:, mc, gi * 512:(gi + 1) * 512], stg, eng="act")
                                      P.dma(O[oname][l, mc * 128:(mc + 1) * 128, gi * 512:(gi + 1) * 512], stg, q="sp", is_output=True)
                                  elif keep is not None:
                                      P.copy(keep[:, mc, gi * 512:(gi + 1) * 512], ps[:, 0:512], eng="act")
                          groups.append(dict(dmas=[(lambda wb: wview(wb, 0, KC, 512), I[wname][l][:, gi * 512:(gi + 1) * 512].rearrange("(k p) m -> p k m", p=128))], body=body))
                      stream(groups)

                  if first:
                      tokmajor_proj("w_mem_k", "p_mk", None)
                      stage("tmk")
                  tokmajor_proj("w_mem_v", "p_mv", VT)
                  stage("attnkv")

                  def q_consume(m, ti, n0, n1, ps):
                      P.copy(qT[:, m, n0:n1], ps[:, 0:n1 - n0], eng="act")

                  linear([(I["w_mem_q"][l], KC)], D, 512, tiles, rhs_u, q_consume)
                  stage("attnq")
                  SCL = 1.0 / 16.0
                  ptr = Rot([ar.get([2, 512], BF16) for _ in range(2)])
                  rcp = ar.get([512])
                  for (n0, n1) in tiles:
                      if n0 >= NPP:
                          continue
                      for h in range(4):
                          pt = ptr.next()
                          for mc in range(2):
                              sc = psA[:, mc * 512:(mc + 1) * 512]
                              for dc in range(2):
                                  P.mm(sc, KT[:, 2 * h + dc, mc * 128:(mc + 1) * 128], qT[:, 2 * h + dc, n0:n1], start=(dc == 0), stop=(dc == 1))
                              P.act(pt[:, mc, :], sc, AF.Exp, scale=SCL)
                          pss = smrot.next()
                          for mc in range(2):
                              P.mm(pss[:, 0:512], ones_b, pt[:, mc, :], start=(mc == 0), stop=(mc == 1))
                          P.recip(rcp, pss[:, 0:512])
                          for dc in range(2):
                              po = mmrot.next()
                              for mc in range(2):
                                  P.mm(po[:, 0:512], VT[:, mc, h * 256 + dc * 128:h * 256 + (dc + 1) * 128], pt[:, mc, :], start=(mc == 0), stop=(mc == 1))
                              P.tt(oT[:, 2 * h + dc, n0:n1], po[:, 0:512], rcp, ALU.mult)
                  if has_s:
                      kbr = Rot([ar.get([2, D], BF16) for _ in range(2)])
                      vbr = kbr
                      ktr = Rot([ar.get([KC, 256], BF16) for _ in range(2)])
                      pts = ar.get([NSQ, 2, 16], BF16)
                      rcs = ar.get([NSQ * 16])
                      vbs = []
                      pscs = mmrot.next()
                      scv = pscs[:, 0:512].rearrange("p (b m x) -> p b m x", b=NSQ, m=2)
                      for b in range(NSQ):
                          kb = kbr.next()
                          P.dma(kb, I["ck"][l, b].rearrange("(mc p) e -> p mc e", p=128), q="pool")
                          ktb = ktr.next()
                          for half in range(2):
                              pkb = psA[:, half * 512:(half + 1) * 512].bitcast(BF16)
                              for ee in range(4):
                                  e = half * 4 + ee
                                  for mc in range(2):
                                      P.tr(pkb[:, ee * 256 + mc * 128:ee * 256 + (mc + 1) * 128], kb[:, mc, e * 128:(e + 1) * 128], ident_b)
                              P.copy(ktb[:, half * 4:(half + 1) * 4, :], pkb.rearrange("p (e m) -> p e m", e=4), eng=("act" if half == 0 else "dve"))
                          for h in range(4):
                              for mc in range(2):
                                  for dc in range(2):
                                      P.mm(scv[:, b, mc, 4 * h:4 * h + 4], ktb[:, 2 * h + dc, mc * 128:(mc + 1) * 128],
                                           qT[:, 2 * h + dc, NPP + 4 * b:NPP + 4 * b + 4], start=(dc == 0), stop=(dc == 1))
                      P.act(pts.rearrange("p b m x -> p (b m x)"), pscs[:, 0:512], AF.Exp, scale=SCL)
                      pos = mmrot.next()
                      pss = smrot.next()
                      for b in range(NSQ):
                          vb = vbr.next()
                          P.dma(vb, I["cv"][l, b].rearrange("(mc p) e -> p mc e", p=128), q="pool")
                          for mc in range(2):
                              P.mm(pss[:, b * 16:(b + 1) * 16], ones_b, pts[:, b, mc, :], start=(mc == 0), stop=(mc == 1))
                          for e in range(KC):
                              h = e // 2
                              for mc in range(2):
                                  P.mm(pos[:, e * 64 + 4 * b:e * 64 + 4 * b + 4], vb[:, mc, e * 128:(e + 1) * 128], pts[:, b, mc, 4 * h:4 * h + 4],
                                       start=(mc == 0), stop=(mc == 1))
                      P.recip(rcs, pss[:, 0:256])
                      rv = rcs.rearrange("p (b h r) -> p b h r", b=NSQ, h=4)
                      for h in range(4):
                          for dc in range(2):
                              e = 2 * h + dc
                              P.tt(oT[:, e, NPP:NT].rearrange("p (b r) -> p b r", b=NSQ), pos[:, e * 64:(e + 1) * 64].rearrange("p (b r) -> p b r", b=NSQ),
                                   rv[:, :, h, :], ALU.mult)
                  linear([(I["w_mem_o"][l], KC)], D, 512, tiles, lambda k, n0, n1: oT[:, k, n0:n1], add_resid)
                  stage("attn")
                  if dbg:
                      P.dma(O["dbg_h"][pas, l, 1], hT[:].rearrange("p k n -> p (k n)"), q="sp", is_output=True)

                  ar.off = arena_base
                  rmsnorm("nffn", l, tiles)
                  sgr = Rot([ar.get([512]) for _ in range(2)])
                  groups = []
                  for gi in range(FC // 2):
                      def body(wb, gi=gi):
                          wv = wview(wb, 0, KC, 512)
                          for fi in range(2):
                              f = gi * 2 + fi
                              for (n0, n1) in tiles:
                                  n = n1 - n0
                                  pg = mmrot.next()
                                  for k in range(KC):
                                      P.mm(pg[:, 0:n], wv[:, k, fi * 128:(fi + 1) * 128], uT[:, k, n0:n1], start=(k == 0), stop=(k == KC - 1))
                                  pu = mmrot.next()
                                  for k in range(KC):
                                      P.mm(pu[:, 0:n], wv[:, k, 256 + fi * 128:256 + (fi + 1) * 128], uT[:, k, n0:n1], start=(k == 0), stop=(k == KC - 1))
                                  sg = sgr.next()
                                  P.act(sg[:, 0:n], pg[:, 0:n], AF.Silu)
                                  P.tt(A[:, f, n0:n1], sg[:, 0:n], pu[:, 0:n], ALU.mult)
                      dmas = [
                          (lambda wb: wview(wb, 0, KC, 512)[:, :, 0:256], I["w_ffn_gate"][l][:, gi * 256:(gi + 1) * 256].rearrange("(k p) m -> p k m", p=128)),
                          (lambda wb: wview(wb, 0, KC, 512)[:, :, 256:512], I["w_ffn_up"][l][:, gi * 256:(gi + 1) * 256].rearrange("(k p) m -> p k m", p=128)),
                      ]
                      groups.append(dict(dmas=dmas, body=body))
                  stream(groups)
                  linear([(I["w_ffn_down"][l], FC)], D, 128, tiles, lambda k, n0, n1: A[:, k, n0:n1], add_resid)
                  stage("ffn")
                  if dbg:
                      P.dma(O["dbg_h"][pas, l, 2], hT[:].rearrange("p k n -> p (k n)"), q="sp", is_output=True)

                  if last:
                      ar.off = arena_base
                      o1 = ar.get([512], parts=15)
                      pp = smrot.next()
                      for j in range(4):
                          P.tr(pp[0:15, j * 128:(j + 1) * 128], pool_tail[:, l, j, :], ident_f)
                      P.copy(o1, pp[0:15, 0:512], eng="act")
                      P.dma(O["p_pool"][l], o1, q="sp", is_output=True)
                      o2 = ar.get([1536], parts=3)
                      for q3 in range(3):
                          pp = smrot.next()
                          for j in range(4):
                              P.tr(pp[0:3, j * 128:(j + 1) * 128], sconv_tail[:, l, q3 * 4 + j, :], ident_f)
                          P.copy(o2[:, q3 * 512:(q3 + 1) * 512], pp[0:3, 0:512], eng="act")
                      P.dma(O["p_sconv"][l], o2, q="sp", is_output=True)
                      o3 = ar.get([512], parts=3)
                      pp = smrot.next()
                      for j in range(4):
                          P.tr(pp[0:3, j * 128:(j + 1) * 128], lconv_tail[:, l, j, :], ident_f)
                      P.copy(o3, pp[0:3, 0:512], eng="act")
                      P.dma(O["p_lconv"][l], o3, q="sp", is_output=True)
                      o4 = ar.get([512], parts=1)
                      pp = smrot.next()
                      for j in range(4):
                          P.tr(pp[0:1, j * 128:(j + 1) * 128], lru_h[:, l, j:j + 1], ident_f)
                      P.copy(o4, pp[0:1, 0:512], eng="act")
                      P.dma(O["p_lru"][l:l + 1, :], o4, q="sp", is_output=True)

              ar.reset()
              sqrot = Rot([ar.get([512], BF16) for _ in range(2)])
              rstd = ar.get([512])
              ytr = Rot([ar.get([KC, 128]) for _ in range(2)])
              yor = Rot([ar.get([D]) for _ in range(2)])
              for (n0, n1) in tiles:
                  n = n1 - n0
                  ps = mmrot.next()
                  for k in range(KC):
                      sq = sqrot.next()
                      P.act(sq[:, 0:n], hT[:, k, n0:n1], AF.Square)
                      P.mm(ps[:, 0:n], ones_b, sq[:, 0:n], start=(k == 0), stop=(k == KC - 1))
                  P.act(rstd[:, 0:n], ps[:, 0:n], AF.Sqrt, scale=1.0 / D, bias=epsc[:, 0:1])
                  P.recip(rstd[:, 0:n], rstd[:, 0:n])
                  bw = 128 if n0 < NPP else 64
                  for bi in range(n // bw):
                      yt = ytr.next()
                      for k in range(KC):
                          P.stt(yt[:, k, 0:bw], hT[:, k, n0 + bi * bw:n0 + (bi + 1) * bw], pcol("nfin", k), rstd[:, bi * bw:(bi + 1) * bw], ALU.mult, ALU.mult)
                      for k in range(KC):
                          P.tr(psA[0:bw, k * 128:(k + 1) * 128], yt[:, k, 0:bw], ident_f)
                      yo = yor.next()
                      P.copy(yo[0:bw, 0:512], psA[0:bw, 0:512], eng="act")
                      P.copy(yo[0:bw, 512:1024], psA[0:bw, 512:1024], eng="dve")
                      if n0 < NPP:
                          P.dma(O["y_p"][t0 + n0 + bi * 128:t0 + n0 + (bi + 1) * 128, :], yo, q="sp", is_output=True)
                      else:
                          P.dma(O["y_s"], yo[0:64, :], q="sp", is_output=True)
        except _Stop:
            pass
        P.emit()
    return nc


LRU_C = 8.0
_NC_CACHE = {}


def make_in_maps(inputs, cores):
    cst = build_consts()
    cst2 = np.zeros((128, 16, 64), np.float32)
    for b in range(16):
        cst2[:, b, 4 * b:4 * b + 4] = 1.0
    cst2 = cst2.reshape(128, 1024)
    maps = []
    for i in cores:
        s = slice(NSQ * i, NSQ * (i + 1))
        m = {
            "x_p": inputs["x_prompt"][i], "x_s": inputs["x_sample"][s].reshape(NS, D), "mem": inputs["mem_prompt"][i],
            "st_pool": inputs["state_pool"][:, s], "st_sconv": inputs["state_ssd_conv"][:, s], "st_ssd": inputs["state_ssd"][:, s],
            "st_lconv": inputs["state_lru_conv"][:, s], "st_lru": inputs["state_lru"][:, s],
            "ck": inputs["cache_mem_k"][:, s].reshape(DEPTH, NSQ, 256, D), "cv": inputs["cache_mem_v"][:, s].reshape(DEPTH, NSQ, 256, D),
            "cst": cst, "cst2": cst2,
        }
        for n, _ in IN_SPECS:
            if n not in m:
                m[n] = inputs[n]
        maps.append({k: np.ascontiguousarray(np.asarray(v, dtype=np.float32)) for k, v in m.items()})
    return maps


def kernel(**inputs):
    inputs = {k: np.asarray(v) for k, v in inputs.items()}
    if "nc" not in _NC_CACHE:
        _NC_CACHE["nc"] = build_program()
    nc = _NC_CACHE["nc"]
    maps = make_in_maps(inputs, list(range(N_CORES)))
    res = run_bass_kernel_spmd(nc, maps, core_ids=list(range(N_CORES)))
    R = res.results

    def cat(name, axis):
        return np.concatenate([np.asarray(r[name]) for r in R], axis=axis)

    y_p = np.stack([np.asarray(r["y_p"]) for r in R], 0)
    y_s = np.concatenate([np.asarray(r["y_s"]).reshape(NSQ, 4, D) for r in R], 0)
    outs = [y_p, y_s]
    for n in ("p_pool", "p_sconv", "p_ssd", "p_lconv", "p_lru"):
        outs.append(np.stack([np.asarray(r[n]) for r in R], 1))
    for n in ("p_mk", "p_mv"):
        outs.append(np.stack([np.asarray(r[n]).reshape(DEPTH, 256, 4, 256) for r in R], 1))
    for n in ("s_pool", "s_sconv", "s_ssd", "s_lconv", "s_lru"):
        outs.append(cat(n, 1))
    return tuple(np.ascontiguousarray(o.astype(np.float32)) for o in outs)
```

```python
import contextlib
import numpy as np
import concourse.bass as bass
import concourse.mybir as mybir
from concourse.bass_utils import run_bass_kernel_spmd

F32 = mybir.dt.float32
BF16 = mybir.dt.bfloat16
AF = mybir.ActivationFunctionType
ALU = mybir.AluOpType

ENGS = ("pe", "act", "dve", "pool", "sp")
NRING = 8

D = 1024
KC = 8
NPT = 2048
NSQ = 16
NS = 64
DEPTH = 4
NPASS = 2
NPP = NPT // NPASS
NT = NPP + NS
DFF = 2816
FC = 22
OFF_POOL, OFF_Z, OFF_XBC, OFF_DT, OFF_GATE, OFF_LRU, N_IN = 0, 512, 1536, 3072, 3088, 3600, 4112
NEG = -30000.0
EPS = 1e-6
NCST = 1104
WBUF = 4096
N_CORES = 8


def _isz(dt):
    return mybir.dt.size(dt)


def _region(ap):
    t = ap.tensor
    pairs = [tuple(x) for x in ap.ap]
    off = int(ap.offset)
    isz = _isz(ap.dtype)
    if type(t).__name__ == "DRamTensorHandle":
        ext = 0
        for st, cnt in pairs:
            ext += (cnt - 1) * abs(st)
        return (t.name, 0, 1, off * isz, (off + ext + 1) * isz)
    pst, pcnt = pairs[0]
    if pst == 0:
        pst = 1 << 40
    p_lo = off // pst
    f_lo = off % pst
    ext = 0
    for st, cnt in pairs[1:]:
        ext += (cnt - 1) * abs(st)
    b_lo, b_hi = f_lo * isz, (f_lo + ext + 1) * isz
    if type(t).__name__ == "PSumTensorHandle":
        b_lo = (b_lo // 2048) * 2048
        b_hi = ((b_hi + 2047) // 2048) * 2048
        return (t.name, (p_lo // 32) * 32, ((p_lo + pcnt + 31) // 32) * 32, b_lo, b_hi)
    return (t.name, p_lo, p_lo + pcnt, b_lo, b_hi)


class Op:
    __slots__ = ("eng", "idx", "fn", "deps", "is_dma", "needed", "signal", "clock", "ring", "tag")

    def __init__(self, eng, idx, fn, is_dma):
        self.eng = eng
        self.idx = idx
        self.fn = fn
        self.deps = []
        self.is_dma = is_dma
        self.needed = False
        self.signal = None
        self.clock = None
        self.ring = None


class Prog:
    def __init__(self, nc):
        self.nc = nc
        self.ops = {e: [] for e in ENGS}
        self.recs = {}
        self.ndma = {e: 0 for e in ENGS}
        self.dma_seq = {e: [] for e in ENGS}
        self.waited_dma = {e: set() for e in ENGS}
        self.out_dmas = []
        self.tag = ""

    def op(self, eng, fn, reads=(), writes=(), is_dma=False, extra_deps=()):
        lst = self.ops[eng]
        o = Op(eng, len(lst), fn, is_dma)
        o.tag = self.tag
        deps = {}
        ops = self.ops

        def add_dep(e2, i2):
            if e2 == eng and eng == "pe":
                return
            od = ops[e2][i2]
            if od.is_dma:
                deps[(e2, i2)] = True
            else:
                k = deps.get(e2)
                if k is None or i2 > k:
                    deps[e2] = i2

        rregs = [_region(ap) for ap in reads]
        wregs = [_region(ap) for ap in writes]
        for (name, pl, ph, fl, fh) in rregs:
            for r in self.recs.get(name, ()):
                if r[4] and r[0] < ph and pl < r[1] and r[2] < fh and fl < r[3]:
                    add_dep(r[5], r[6])
        for (name, pl, ph, fl, fh) in wregs:
            for r in self.recs.get(name, ()):
                if r[0] < ph and pl < r[1] and r[2] < fh and fl < r[3]:
                    add_dep(r[5], r[6])
        for (e2, i2) in extra_deps:
            add_dep(e2, i2)
        prev = lst[-1].clock if lst else {}
        clock = dict(prev)
        final = []
        for k in deps:
            if isinstance(k, tuple):
                if k in self.waited_dma[eng]:
                    continue
                final.append(k)
            else:
                i2 = deps[k]
                if clock.get(k, -1) >= i2:
                    continue
                final.append((k, i2))
        for (e2, i2) in final:
            od = ops[e2][i2]
            if od.is_dma:
                self.waited_dma[eng].add((e2, i2))
            else:
                if clock.get(e2, -1) < i2:
                    clock[e2] = i2
            for k2, v2 in od.clock.items():
                if clock.get(k2, -1) < v2:
                    clock[k2] = v2
        if is_dma:
            n = self.ndma[eng]
            o.ring = n
            self.ndma[eng] = n + 1
            if n >= NRING:
                pd = self.dma_seq[eng][n - NRING]
                if (eng, pd.idx) not in self.waited_dma[eng]:
                    final.append((eng, pd.idx))
                    self.waited_dma[eng].add((eng, pd.idx))
            self.dma_seq[eng].append(o)
        o.deps = final
        o.clock = clock
        lst.append(o)
        for (name, pl, ph, fl, fh) in rregs:
            L = self.recs.setdefault(name, [])
            if not is_dma:
                L[:] = [r for r in L if not ((not r[4]) and r[5] == eng and (not r[7]) and pl <= r[0] and r[1] <= ph and fl <= r[2] and r[3] <= fh)]
            L.append([pl, ph, fl, fh, False, eng, o.idx, is_dma])
        for (name, pl, ph, fl, fh) in wregs:
            L = self.recs.setdefault(name, [])
            L[:] = [r for r in L if not (pl <= r[0] and r[1] <= ph and fl <= r[2] and r[3] <= fh)]
            L.append([pl, ph, fl, fh, True, eng, o.idx, is_dma])
        return o

    def mm(self, out, lhsT, rhs, start=True, stop=True):
        return self.op("pe", lambda e: e.matmul(out, lhsT, rhs, start=start, stop=stop), [lhsT, rhs], [out])

    def tr(self, out, in_, ident):
        return self.op("pe", lambda e: e.transpose(out, in_, ident), [in_, ident], [out])

    def act(self, out, in_, func, bias=None, scale=None):
        reads = [in_]
        kw = {}
        if bias is not None:
            kw["bias"] = bias
            if not isinstance(bias, (int, float)):
                reads.append(bias)
        if scale is not None:
            kw["scale"] = scale
            if not isinstance(scale, (int, float)):
                reads.append(scale)
        return self.op("act", lambda e: e.activation(out, in_, func, **kw), reads, [out])

    def tt(self, out, a, b, op, eng="dve"):
        return self.op(eng, lambda e: e.tensor_tensor(out, a, b, op), [a, b], [out])

    def ts(self, out, a, s1, op0, s2=None, op1=None, eng="dve"):
        reads = [a]
        if not isinstance(s1, (int, float)):
            reads.append(s1)
        if s2 is not None and not isinstance(s2, (int, float)):
            reads.append(s2)
        if op1 is None:
            return self.op(eng, lambda e: e.tensor_scalar(out, a, s1, None, op0), reads, [out])
        return self.op(eng, lambda e: e.tensor_scalar(out, a, s1, s2, op0, op1), reads, [out])

    def stt(self, out, a, s, b, op0, op1):
        reads = [a, b]
        if not isinstance(s, (int, float)):
            reads.append(s)
        return self.op("dve", lambda e: e.scalar_tensor_tensor(out, a, s, b, op0, op1), reads, [out])

    def scan(self, out, d0, d1, init, op0, op1):
        reads = [d0, d1]
        if not isinstance(init, (int, float)):
            reads.append(init)
        return self.op("dve", lambda e: e.tensor_tensor_scan(out, d0, d1, init, op0, op1), reads, [out])

    def copy(self, out, in_, eng="dve"):
        if eng == "act":
            return self.op("act", lambda e: e.copy(out, in_), [in_], [out])
        return self.op(eng, lambda e: e.tensor_copy(out, in_), [in_], [out])

    def memset(self, ap, val, eng="dve"):
        return self.op(eng, lambda e: e.memset(ap, val), [], [ap])

    def recip(self, out, in_):
        return self.op("dve", lambda e: e.reciprocal(out, in_), [in_], [out])

    def dma(self, out, in_, q="sp", is_output=False, **kw):
        o = self.op(q, lambda e: e.dma_start(out=out, in_=in_, **kw), [in_], [out], is_dma=True)
        if is_output:
            self.out_dmas.append((q, o.idx))
        return o

    def emit(self):
        nc = self.nc
        self.op("sp", None, extra_deps=list(self.out_dmas))
        for e in ENGS:
            for o in self.ops[e]:
                for (e2, i2) in o.deps:
                    self.ops[e2][i2].needed = True
        for e in ENGS:
            c = 0
            for o in self.ops[e]:
                if o.needed and not o.is_dma:
                    c += 1
                    o.signal = c
        with contextlib.ExitStack() as st:
            sems = {e: st.enter_context(nc.semaphore("s_" + e)) for e in ENGS}
            rings = {e: [st.enter_context(nc.semaphore("r_%s_%d" % (e, i))) for i in range(NRING)] for e in ENGS if self.ndma[e]}
            block = st.enter_context(nc.Block())

            def run(engname, eng):
                for o in self.ops[engname]:
                    for (e2, i2) in o.deps:
                        od = self.ops[e2][i2]
                        if od.is_dma:
                            eng.wait_ge(rings[e2][od.ring % NRING], 16 * (od.ring // NRING + 1))
                        else:
                            eng.wait_ge(sems[e2], od.signal)
                    if o.fn is None:
                        continue
                    inst = o.fn(eng)
                    if o.is_dma:
                        inst.then_inc(rings[engname][o.ring % NRING], 16)
                    elif o.signal is not None:
                        inst.then_inc(sems[engname], 1)

            @block.tensor
            def _(eng):
                run("pe", eng)

            @block.scalar
            def _(eng):
                run("act", eng)

            @block.vector
            def _(eng):
                run("dve", eng)

            @block.gpsimd
            def _(eng):
                run("pool", eng)

            @block.sync
            def _(eng):
                run("sp", eng)


class Rot:
    def __init__(self, items):
        self.items = list(items)
        self.i = 0

    def next(self):
        x = self.items[self.i % len(self.items)]
        self.i += 1
        return x


class Arena:
    def __init__(self, tens, nwords):
        self.t = tens
        self.n = nwords
        self.off = 0

    def reset(self):
        self.off = 0

    def get(self, shape, dtype=F32, parts=128):
        n = 1
        for s in shape:
            n *= s
        words = (n * _isz(dtype) + 3) // 4
        words += words & 1
        assert self.off + words <= self.n, ("arena overflow", self.off, words, self.n)
        v = self.t[0:parts, self.off:self.off + words]
        self.off += words
        if dtype != F32:
            v = v.bitcast(dtype)
        v = v[:, 0:n]
        if len(shape) == 2:
            v = v.rearrange("p (a b) -> p a b", a=shape[0])
        elif len(shape) == 3:
            v = v.rearrange("p (a b c) -> p a b c", a=shape[0], b=shape[1])
        return v


def bcast(ap, shape):
    return ap.broadcast_to(list(shape))


def build_consts():
    c = np.zeros((128, NCST), np.float32)
    r = np.arange(128)
    c[:, 0:128] = np.eye(128)
    U = (r[:, None] <= r[None, :]).astype(np.float32)
    c[:, 128:256] = U
    c[:, 256:384] = -U
    sel = np.zeros((128, 128), np.float32)
    sel[127, :] = 1.0
    c[:, 384:512] = sel
    c[:, 512:640] = np.where(r[None, :] < r[:, None], NEG, 0.0)
    r64 = np.arange(64)
    same = (r64[:, None] // 4) == (r64[None, :] // 4)
    Us = (same & (r64[:, None] <= r64[None, :])).astype(np.float32)
    c[0:64, 640:704] = Us
    c[0:64, 704:768] = -Us
    sels = np.zeros((64, 64), np.float32)
    for s in range(64):
        sels[4 * (s // 4) + 3, s] = 1.0
    c[0:64, 768:832] = sels
    c[0:64, 832:896] = np.where(same & (r64[None, :] >= r64[:, None]), 0.0, NEG)
    bi = np.zeros((64, 16), np.float32)
    bi[r64, r64 // 4] = 1.0
    c[0:64, 896:912] = bi
    for g, w in enumerate((2, 4, 8, 16)):
        for t in range(16):
            c[:, 912 + g * 16 + t] = 1.0 / min(t + 1, w)
    c[:, 976:1104] = 1.0
    return c


IN_SPECS = [
    ("x_p", (NPT, D)), ("x_s", (NS, D)), ("mem", (256, D)),
    ("st_pool", (DEPTH, NSQ, 15, 512)), ("st_sconv", (DEPTH, NSQ, 3, 1536)), ("st_ssd", (DEPTH, NSQ, 16, 64, 128)),
    ("st_lconv", (DEPTH, NSQ, 3, 512)), ("st_lru", (DEPTH, NSQ, 512)),
    ("ck", (DEPTH, NSQ, 256, D)), ("cv", (DEPTH, NSQ, 256, D)),
    ("norm_mix", (DEPTH, D)), ("w_in", (DEPTH, D, N_IN)), ("pool_w", (DEPTH, 4, 128, 128)), ("pool_scale", (DEPTH, 512)),
    ("ssd_conv_w", (DEPTH, 4, 1536)), ("ssd_conv_b", (DEPTH, 1536)), ("ssd_dt_bias", (DEPTH, 16)), ("ssd_a_log", (DEPTH, 16)),
    ("ssd_d", (DEPTH, 16)), ("ssd_norm", (DEPTH, D)), ("lru_conv_w", (DEPTH, 4, 512)), ("lru_conv_b", (DEPTH, 512)),
    ("lru_wa", (DEPTH, 8, 64, 64)), ("lru_ba", (DEPTH, 8, 64)), ("lru_wx", (DEPTH, 8, 64, 64)), ("lru_bx", (DEPTH, 8, 64)),
    ("lru_lambda", (DEPTH, 512)), ("w_out", (DEPTH, 2048, D)), ("norm_mem", (DEPTH, D)), ("w_mem_q", (DEPTH, D, D)),
    ("w_mem_k", (DEPTH, D, D)), ("w_mem_v", (DEPTH, D, D)), ("w_mem_o", (DEPTH, D, D)), ("norm_ffn", (DEPTH, D)),
    ("w_ffn_gate", (DEPTH, D, DFF)), ("w_ffn_up", (DEPTH, D, DFF)), ("w_ffn_down", (DEPTH, DFF, D)), ("norm_final", (D,)),
    ("cst", (128, NCST)), ("cst2", (128, 1024)),
]
OUT_SPECS = [
    ("y_p", (NPT, D)), ("y_s", (NS, D)), ("p_pool", (DEPTH, 15, 512)), ("p_sconv", (DEPTH, 3, 1536)),
    ("p_ssd", (DEPTH, 16, 64, 128)), ("p_lconv", (DEPTH, 3, 512)), ("p_lru", (DEPTH, 512)),
    ("p_mk", (DEPTH, 256, D)), ("p_mv", (DEPTH, 256, D)),
    ("s_pool", (DEPTH, NSQ, 15, 512)), ("s_sconv", (DEPTH, NSQ, 3, 1536)), ("s_ssd", (DEPTH, NSQ, 16, 64, 128)),
    ("s_lconv", (DEPTH, NSQ, 3, 512)), ("s_lru", (DEPTH, NSQ, 512)),
]


class _Stop(Exception):
    pass


def build_program(n_layers=DEPTH, dbg=False, stop_at=None):
    nc = bass.Bass("TRN2", target_bir_lowering=False)
    I = {n: nc.dram_tensor(n, list(s), F32, kind="ExternalInput").ap() for n, s in IN_SPECS}
    O = {n: nc.dram_tensor(n, list(s), F32, kind="ExternalOutput").ap() for n, s in OUT_SPECS}
    spill = nc.dram_tensor("ssd_spill", [DEPTH, 128, D], F32, kind="Internal").ap()
    if dbg:
        O["dbg_h"] = nc.dram_tensor("dbg_h", [NPASS, DEPTH, 4, 128, KC * NT], F32, kind="ExternalOutput").ap()
    P = Prog(nc)
    with contextlib.ExitStack() as st:
        def sb(name, shape, dt=F32):
            return st.enter_context(nc.sbuf_tensor(name, list(shape), dt))

        def pst(name, shape, dt=F32):
            return st.enter_context(nc.psum_tensor(name, list(shape), dt))

        hT = sb("hT", [128, KC, NT])
        uT = sb("uT", [128, KC, NT], BF16)
        A = sb("A", [128, FC, NT], BF16)
        wbufs = [sb("wb%d" % i, [128, WBUF], BF16) for i in range(2)]
        cst_f = sb("cst_f", [128, NCST])
        cst_b = sb("cst_b", [128, NCST], BF16)
        PAR = sb("PAR", [128, 640])
        RB = sb("RB", [128, 3 * 64])
        DCOL = sb("DCOL", [128, DEPTH, 8])
        C1 = sb("C1", [128, 16])
        memT = sb("memT", [128, KC, 256], BF16)
        POOLW = sb("POOLW", [128, 4, 128], BF16)
        WA = sb("WA", [128, 4, 128], BF16)
        WX = sb("WX", [128, 4, 128], BF16)
        WDT = sb("WDT", [128, KC, 16], BF16)
        pool_tail = sb("pool_tail", [128, DEPTH, 4, 15])
        sconv_tail = sb("sconv_tail", [128, DEPTH, 12, 3])
        lconv_tail = sb("lconv_tail", [128, DEPTH, 4, 3])
        lru_h = sb("lru_h", [128, DEPTH, 4])
        ssdT = sb("ssdT", [128, D])
        ssdT_b = sb("ssdT_b", [128, D], BF16)
        ARW = 15360
        TMP = sb("TMP", [128, ARW])
        ar = Arena(TMP, ARW)

        psA = pst("psA", [128, 1024])
        pbs = [pst("pb%d" % i, [128, 512]) for i in range(6)]
        mmrot = Rot(pbs[0:3])
        smrot = Rot(pbs[3:5])
        psY = pbs[5]
        wrot = Rot(wbufs)

        ident_f = cst_f[:, 0:128]
        ident_b = cst_b[:, 0:128]
        ones_f = cst_f[:, 976:1104]
        ones_b = cst_b[:, 976:1104]
        CP = dict(U=cst_f[:, 128:256], negU=cst_f[:, 256:384], sel=cst_f[:, 384:512], negm=cst_b[:, 512:640], L=128)
        CS = dict(U=cst_f[0:64, 640:704], negU=cst_f[0:64, 704:768], sel=cst_f[0:64, 768:832], negm=cst_b[0:64, 832:896], L=64)
        blockind_f = cst_f[0:64, 896:912]
        blockind_b = cst_b[0:64, 896:912]
        rc_tab = cst_f[:, 912:976].rearrange("p (g t) -> p g t", g=4)

        P.dma(cst_f[:], I["cst"], q="sp")
        P.dma(cst_b[:], I["cst"], q="pool")
        BMASK = sb("BMASK", [128, 16, 64], BF16)
        P.dma(BMASK[:], I["cst2"].rearrange("p (b l) -> p b l", b=16), q="pool")
        prow = {}
        plist = [
            ("nm", I["norm_mix"].rearrange("l (j p) -> (l j) p", p=128)),
            ("nmem", I["norm_mem"].rearrange("l (j p) -> (l j) p", p=128)),
            ("nffn", I["norm_ffn"].rearrange("l (j p) -> (l j) p", p=128)),
            ("nfin", I["norm_final"].rearrange("(j p) -> j p", p=128)),
            ("pscale", I["pool_scale"].rearrange("l (j p) -> (l j) p", p=128)),
            ("scw", I["ssd_conv_w"].rearrange("l k (j p) -> (l k j) p", p=128)),
            ("scb", I["ssd_conv_b"].rearrange("l (j p) -> (l j) p", p=128)),
            ("snorm", I["ssd_norm"].rearrange("l (j p) -> (l j) p", p=128)),
            ("lcw", I["lru_conv_w"].rearrange("l k (j p) -> (l k j) p", p=128)),
            ("lcb", I["lru_conv_b"].rearrange("l (j p) -> (l j) p", p=128)),
            ("lam", I["lru_lambda"].rearrange("l (j p) -> (l j) p", p=128)),
            ("ba", I["lru_ba"].rearrange("l h i -> l (h i)").rearrange("l (j p) -> (l j) p", p=128)),
            ("bx", I["lru_bx"].rearrange("l h i -> l (h i)").rearrange("l (j p) -> (l j) p", p=128)),
        ]
        PST = ar.get([5, 128])
        P.memset(PST, 0.0)
        r0 = 0
        for name, ap2 in plist:
            nr = ap2.shape[0]
            prow[name] = r0
            done = 0
            while done < nr:
                t, rr = divmod(r0 + done, 128)
                n = min(nr - done, 128 - rr)
                P.dma(PST[rr:rr + n, t, :], ap2[done:done + n, :], q="sp")
                done += n
            r0 += nr
        assert r0 <= 640
        for t in range(5):
            ps = smrot.next()
            P.tr(ps[:, 0:128], PST[:, t, :], ident_f)
            P.copy(PAR[:, t * 128:(t + 1) * 128], ps[:, 0:128], eng="act")

        def pcol(name, idx):
            c = prow[name] + idx
            return PAR[:, c:c + 1]

        P.dma(RB[:, 0:64], I["ssd_dt_bias"].rearrange("l h -> (l h)").partition_broadcast(128), q="sp")
        P.dma(RB[:, 64:128], I["ssd_a_log"].rearrange("l h -> (l h)").partition_broadcast(128), q="sp")
        P.dma(RB[:, 128:192], I["ssd_d"].rearrange("l h -> (l h)").partition_broadcast(128), q="sp")
        P.act(RB[:, 64:128], RB[:, 64:128], AF.Exp)
        P.ts(RB[:, 64:128], RB[:, 64:128], -1.0, ALU.mult)
        dview = RB[:, 128:192].rearrange("p (l hp two) -> p l hp two", l=DEPTH, two=2)
        P.copy(DCOL[0:64, :, :], dview[0:64, :, :, 0])
        P.copy(DCOL[64:128, :, :], dview[64:128, :, :, 1])
        lam0 = prow["lam"]
        P.act(C1[:], PAR[:, lam0:lam0 + 16], AF.Exp, scale=-1.0)
        P.act(C1[:], C1[:], AF.Ln, bias=1.0)
        P.ts(C1[:], C1[:], -LRU_C, ALU.mult)
        P.memset(WA[:], 0.0)
        P.memset(WX[:], 0.0)
        for mc in range(2):
            mt = ar.get([D])
            P.dma(mt, I["mem"][mc * 128:(mc + 1) * 128, :], q="sp")
            for k in range(KC):
                P.tr(psA[:, k * 128:(k + 1) * 128], mt[:, k * 128:(k + 1) * 128], ident_f)
            P.copy(memT[:, 0:4, mc * 128:(mc + 1) * 128], psA[:, 0:512].rearrange("p (k n) -> p k n", k=4), eng="act")
            P.copy(memT[:, 4:8, mc * 128:(mc + 1) * 128], psA[:, 512:1024].rearrange("p (k n) -> p k n", k=4), eng="dve")
        for tl in (pool_tail, sconv_tail, lconv_tail, lru_h):
            P.memset(tl[:], 0.0)
        if dbg:
            P.memset(hT[:], 0.0)

        def stream(groups):
            for g in groups:
                wb = wrot.next()
                for dst_fn, src in g["dmas"]:
                    P.dma(dst_fn(wb), src, q="pool")
                g["body"](wb)

        def wview(wb, koff, kc, MG):
            return wb[:, koff * MG:(koff + kc) * MG].rearrange("p (k m) -> p k m", k=kc)

        def linear(wsrcs, m_total, MG, tiles, rhs_fn, consume):
            KCt = sum(kc for _, kc in wsrcs)
            groups = []
            ng = (m_total + MG - 1) // MG
            for gi in range(ng):
                mg = min(MG, m_total - gi * MG)

                def body(wb, gi=gi, mg=mg):
                    wv = wview(wb, 0, KCt, mg)
                    for mc in range(mg // 128):
                        mglob = gi * (MG // 128) + mc
                        for ti, (n0, n1) in enumerate(tiles):
                            ps = mmrot.next()
                            for k in range(KCt):
                                P.mm(ps[:, 0:n1 - n0], wv[:, k, mc * 128:(mc + 1) * 128], rhs_fn(k, n0, n1),
                                     start=(k == 0), stop=(k == KCt - 1))
                            consume(mglob, ti, n0, n1, ps)

                dmas = []
                koff = 0
                for src, kc in wsrcs:
                    dmas.append((lambda wb, koff=koff, kc=kc, mg=mg: wview(wb, koff, kc, mg),
                                 src[:, gi * MG:gi * MG + mg].rearrange("(k p) m -> p k m", p=128)))
                    koff += kc
                groups.append(dict(dmas=dmas, body=body))
            stream(groups)

        def rmsnorm(gname, l, tiles):
            for (n0, n1) in tiles:
                n = n1 - n0
                ps = mmrot.next()
                for k in range(KC):
                    sq = sqrot.next()
                    P.act(sq[:, 0:n], hT[:, k, n0:n1], AF.Square)
                    P.mm(ps[:, 0:n], ones_b, sq[:, 0:n], start=(k == 0), stop=(k == KC - 1))
                P.act(rstd[:, 0:n], ps[:, 0:n], AF.Sqrt, scale=1.0 / D, bias=epsc[:, 0:1])
                P.recip(rstd[:, 0:n], rstd[:, 0:n])
                for k in range(KC):
                    P.stt(uT[:, k, n0:n1], hT[:, k, n0:n1], pcol(gname, l * KC + k), rstd[:, 0:n], ALU.mult, ALU.mult)

        def add_resid(m, ti, n0, n1, ps):
            P.tt(hT[:, m, n0:n1], hT[:, m, n0:n1], ps[:, 0:n1 - n0], ALU.add)

        epsc = sb("epsc", [128, 1])
        P.memset(epsc[:], EPS)

        def stage(name):
            P.tag = name
            if stop_at == name:
                raise _Stop()

        stage("setup")
        try:
          for pas in range(NPASS):
              t0 = pas * NPP
              has_s = pas == NPASS - 1
              first = pas == 0
              last = pas == NPASS - 1
              ncol = NPP + (NS if has_s else 0)
              tiles = [(i * 512, (i + 1) * 512) for i in range(NPP // 512)] + ([(NPP, NT)] if has_s else [])
              nchunk = NPP // 128

              ar.reset()
              xrot = Rot([ar.get([D]) for _ in range(2)])
              for blk in range(nchunk):
                  xt = xrot.next()
                  P.dma(xt, I["x_p"][t0 + blk * 128:t0 + (blk + 1) * 128, :], q="sp")
                  for k in range(KC):
                      P.tr(psA[:, k * 128:(k + 1) * 128], xt[:, k * 128:(k + 1) * 128], ident_f)
                  P.copy(hT[:, 0:4, blk * 128:(blk + 1) * 128], psA[:, 0:512].rearrange("p (k n) -> p k n", k=4), eng="act")
                  P.copy(hT[:, 4:8, blk * 128:(blk + 1) * 128], psA[:, 512:1024].rearrange("p (k n) -> p k n", k=4), eng="dve")
              if has_s:
                  xt = xrot.next()
                  P.dma(xt[0:64, :], I["x_s"], q="sp")
                  for k in range(KC):
                      P.tr(psA[:, k * 64:(k + 1) * 64], xt[0:64, k * 128:(k + 1) * 128], ident_f[0:64, 0:64])
                  P.copy(hT[:, :, NPP:NT], psA[:, 0:512].rearrange("p (k n) -> p k n", k=8), eng="act")
              stage("loadx")

              for l in range(n_layers):
                  ar.reset()
                  sqrot = Rot([ar.get([512], BF16) for _ in range(2)])
                  rstd = ar.get([512])
                  arena_base = ar.off
                  rmsnorm("nm", l, tiles)
                  stage("norm1")
                  P.dma(POOLW[:], I["pool_w"][l].rearrange("g c d -> c g d"), q="pool")
                  for h2 in range(2):
                      P.dma(WA[h2 * 64:(h2 + 1) * 64, :, h2 * 64:(h2 + 1) * 64],
                            I["lru_wa"][l].rearrange("(j two) i o -> two i j o", two=2)[h2], q="pool")
                      P.dma(WX[h2 * 64:(h2 + 1) * 64, :, h2 * 64:(h2 + 1) * 64],
                            I["lru_wx"][l].rearrange("(j two) i o -> two i j o", two=2)[h2], q="pool")
                  P.dma(WDT[:], I["w_in"][l][:, OFF_DT:OFF_DT + 16].rearrange("(k p) m -> p k m", p=128), q="pool")
                  w_in_l = I["w_in"][l]
                  w_out_l = I["w_out"][l]
                  rhs_u = lambda k, n0, n1: uT[:, k, n0:n1]
                  deferred = []

                  def flush():
                      while deferred:
                          deferred.pop(0)()

                  if has_s:
                      SPin = ar.get([2, 512], parts=120)
                      P.dma(SPin, I["st_pool"][l].rearrange("(h b) r c -> (b r) h c", h=2), q="sp")
                      SPout = ar.get([2, 512], parts=120)
                      LCin = ar.get([512], parts=48)
                      P.dma(LCin, I["st_lconv"][l].rearrange("b r c -> (b r) c"), q="sp")
                      LCout = ar.get([512], parts=48)
                      LRin = ar.get([512], parts=16)
                      P.dma(LRin, I["st_lru"][l], q="sp")
                      LRout = ar.get([512], parts=16)
                      lru_h0 = ar.get([4, 16])
                  sbase = ar.off

                  RAWP = 15 + NPP + 16 * 19 + 2
                  rawrot = Rot([ar.get([RAWP]) for _ in range(2)])
                  tA = ar.get([RAWP])
                  tB = ar.get([RAWP])
                  plrot = Rot([ar.get([NT], BF16) for _ in range(2)])
                  tmp15 = ar.get([16])
                  tmp240 = ar.get([240])
                  cur_raw = [None]
                  SOFFP = 15 + NPP
                  ya = A[:, 0:4, :]

                  def svw(t, off, nb, w):
                      return t[:, off:off + nb * w].rearrange("p (b r) -> p b r", b=nb)

                  def pool_consume(j, ti, n0, n1, ps):
                      if ti == 0:
                          cur_raw[0] = rawrot.next()
                          raw = cur_raw[0]
                          P.copy(raw[:, 0:15], pool_tail[:, l, j, :], eng="pool")
                          if has_s:
                              pp = smrot.next()
                              for h in range(2):
                                  P.tr(pp[:, h * 120:(h + 1) * 120], SPin[:, h, j * 128:(j + 1) * 128], ident_f[0:120, 0:120])
                              P.copy(svw(raw, SOFFP, 16, 19)[:, :, 0:15], pp[:, 0:240].rearrange("p (b r) -> p b r", b=16), eng="dve")
                      raw = cur_raw[0]
                      if n0 < NPP:
                          P.copy(raw[:, 15 + n0:15 + n1], ps[:, 0:n1 - n0], eng="act")
                      else:
                          P.copy(svw(raw, SOFFP, 16, 19)[:, :, 15:19], ps[:, 0:64].rearrange("p (b r) -> p b r", b=16), eng="act")
                      if ti != len(tiles) - 1:
                          return
                      flush()
                      w = 2 << j
                      Wd = 15 + NPP
                      cur = raw
                      bufs = [tA, tB]
                      for lev in range(j + 1):
                          sh = 1 << lev
                          lo = (2 << lev) - 1
                          dst = bufs[lev % 2]
                          P.tt(dst[:, lo:Wd], cur[:, lo:Wd], cur[:, lo - sh:Wd - sh], ALU.add)
                          if has_s:
                              P.tt(svw(dst, SOFFP, 16, 19)[:, :, lo:19], svw(cur, SOFFP, 16, 19)[:, :, lo:19],
                                   svw(cur, SOFFP, 16, 19)[:, :, lo - sh:19 - sh], ALU.add)
                          cur = dst
                      pl = plrot.next()
                      P.stt(pl[:, 0:NPP], cur[:, 15:15 + NPP], 1.0 / w, raw[:, 15:15 + NPP], ALU.mult, ALU.subtract)
                      if first:
                          P.tt(tmp15[:, 0:15], cur[:, 15:30], rc_tab[:, j, 0:15], ALU.mult)
                          P.tt(pl[:, 0:15], tmp15[:, 0:15], raw[:, 15:30], ALU.subtract)
                      if has_s:
                          P.stt(pl[:, NPP:NT].rearrange("p (b r) -> p b r", b=16), svw(cur, SOFFP, 16, 19)[:, :, 15:19], 1.0 / w,
                                svw(raw, SOFFP, 16, 19)[:, :, 15:19], ALU.mult, ALU.subtract)
                      P.copy(pool_tail[:, l, j, :], raw[:, NPP:NPP + 15], eng="pool")
                      if has_s:
                          P.copy(tmp240.rearrange("p (b r) -> p b r", b=16), svw(raw, SOFFP, 16, 19)[:, :, 4:19], eng="pool")

                      def pe_part(j=j, pl=pl):
                          for (m0, m1) in tiles:
                              ps2 = mmrot.next()
                              P.mm(ps2[:, 0:m1 - m0], POOLW[:, j, :], pl[:, m0:m1])
                              P.act(ya[:, j, m0:m1], ps2[:, 0:m1 - m0], AF.Identity, scale=pcol("pscale", l * 4 + j))
                          if has_s:
                              pp = smrot.next()
                              for h in range(2):
                                  P.tr(pp[0:120, h * 128:(h + 1) * 128], tmp240[:, h * 120:(h + 1) * 120], ident_f)
                              P.copy(SPout[:, :, j * 128:(j + 1) * 128], pp[0:120, 0:256].rearrange("p (h c) -> p h c", h=2), eng="act")

                      deferred.append(pe_part)

                  linear([(w_in_l[:, OFF_POOL:OFF_POOL + 512], KC)], 512, 512, tiles, rhs_u, pool_consume)
                  flush()
                  stage("pool")
                  if has_s:
                      P.dma(O["s_pool"][l].rearrange("(h b) r c -> (b r) h c", h=2), SPout, q="sp", is_output=True)

                  ar.off = sbase
                  RAWL = 3 + NPP + 16 * 7 + 1
                  SOFFL = 3 + NPP
                  rawrot = Rot([ar.get([RAWL]) for _ in range(2)])
                  acc = ar.get([NT])
                  xcb = ar.get([NT], BF16)
                  rr_ = ar.get([NT])
                  ii_ = ar.get([NT])
                  mm_ = ar.get([NT])
                  tmp48 = ar.get([48])
                  tmp16 = ar.get([16])
                  gel = A[:, 8:12, :]
                  yc = A[:, 4:8, :]
                  if has_s:
                      pp = smrot.next()
                      for j in range(4):
                          P.tr(pp[:, j * 16:(j + 1) * 16], LRin[:, j * 128:(j + 1) * 128], ident_f[0:16, 0:16])
                      P.copy(lru_h0, pp[:, 0:64].rearrange("p (j b) -> p j b", j=4), eng="act")

                  def lru_consume(m, ti, n0, n1, ps):
                      n = n1 - n0
                      if m < 4:
                          P.act(gel[:, m, n0:n1], ps[:, 0:n], AF.Gelu_apprx_tanh)
                          return
                      j = m - 4
                      if ti == 0:
                          cur_raw[0] = rawrot.next()
                          raw = cur_raw[0]
                          P.copy(raw[:, 0:3], lconv_tail[:, l, j, :], eng="pool")
                          if has_s:
                              pp = smrot.next()
                              P.tr(pp[:, 0:48], LCin[:, j * 128:(j + 1) * 128], ident_f[0:48, 0:48])
                              P.copy(svw(raw, SOFFL, 16, 7)[:, :, 0:3], pp[:, 0:48].rearrange("p (b r) -> p b r", b=16), eng="dve")
                      raw = cur_raw[0]
                      if n0 < NPP:
                          P.copy(raw[:, 3 + n0:3 + n1], ps[:, 0:n], eng="act")
                      else:
                          P.copy(svw(raw, SOFFL, 16, 7)[:, :, 3:7], ps[:, 0:64].rearrange("p (b r) -> p b r", b=16), eng="act")
                      if ti != len(tiles) - 1:
                          return
                      flush()
                      for k in range(4):
                          wk = pcol("lcw", (l * 4 + k) * 4 + j)
                          if k == 0:
                              P.ts(acc[:, 0:NPP], raw[:, 0:NPP], wk, ALU.mult, pcol("lcb", l * 4 + j), ALU.add)
                              if has_s:
                                  P.ts(acc[:, NPP:NT].rearrange("p (b r) -> p b r", b=16), svw(raw, SOFFL, 16, 7)[:, :, 0:4], wk, ALU.mult,
                                       pcol("lcb", l * 4 + j), ALU.add)
                          else:
                              P.stt(acc[:, 0:NPP], raw[:, k:k + NPP], wk, acc[:, 0:NPP], ALU.mult, ALU.add)
                              if has_s:
                                  av = acc[:, NPP:NT].rearrange("p (b r) -> p b r", b=16)
                                  P.stt(av, svw(raw, SOFFL, 16, 7)[:, :, k:k + 4], wk, av, ALU.mult, ALU.add)
                      P.copy(lconv_tail[:, l, j, :], raw[:, NPP:NPP + 3], eng="pool")
                      if has_s:
                          P.copy(tmp48.rearrange("p (b r) -> p b r", b=16), svw(raw, SOFFL, 16, 7)[:, :, 4:7], eng="pool")
                      P.copy(xcb[:, 0:ncol], acc[:, 0:ncol], eng="act")

                      def pe_part(j=j):
                          for (m0, m1) in tiles:
                              psr = mmrot.next()
                              P.mm(psr[:, 0:m1 - m0], WA[:, j, :], xcb[:, m0:m1])
                              P.act(rr_[:, m0:m1], psr[:, 0:m1 - m0], AF.Sigmoid, bias=pcol("ba", l * 4 + j))
                              psi = mmrot.next()
                              P.mm(psi[:, 0:m1 - m0], WX[:, j, :], xcb[:, m0:m1])
                              P.act(ii_[:, m0:m1], psi[:, 0:m1 - m0], AF.Sigmoid, bias=pcol("bx", l * 4 + j))
                          if has_s:
                              pp = smrot.next()
                              P.tr(pp[0:48, 0:128], tmp48[:, 0:48], ident_f)
                              P.copy(LCout[:, j * 128:(j + 1) * 128], pp[0:48, 0:128], eng="act")
                          nn = ncol
                          P.act(rr_[:, 0:nn], rr_[:, 0:nn], AF.Exp, scale=C1[:, l * 4 + j:l * 4 + j + 1])
                          P.tt(mm_[:, 0:nn], rr_[:, 0:nn], rr_[:, 0:nn], ALU.mult)
                          P.act(mm_[:, 0:nn], mm_[:, 0:nn], AF.Sqrt, scale=-1.0, bias=1.0)
                          if first:
                              P.memset(mm_[:, 0:1], 1.0)
                          P.tt(ii_[:, 0:nn], ii_[:, 0:nn], mm_[:, 0:nn], ALU.mult)
                          P.tt(ii_[:, 0:nn], ii_[:, 0:nn], acc[:, 0:nn], ALU.mult)
                          if has_s:
                              a_s = rr_[:, NPP:NT].rearrange("p (b r) -> p b r", b=16)
                              b_s = ii_[:, NPP:NT].rearrange("p (b r) -> p b r", b=16)
                              t16 = tmp16.rearrange("p (b o) -> p b o", o=1)
                              P.tt(t16, a_s[:, :, 0:1], lru_h0[:, j, :].rearrange("p (b o) -> p b o", o=1), ALU.mult)
                              P.tt(b_s[:, :, 0:1], b_s[:, :, 0:1], t16, ALU.add)
                              P.memset(a_s[:, :, 0:1], 0.0)
                          P.scan(mm_[:, 0:NPP], rr_[:, 0:NPP], ii_[:, 0:NPP], lru_h[:, l, j:j + 1], ALU.mult, ALU.add)
                          if has_s:
                              P.scan(mm_[:, NPP:NT], rr_[:, NPP:NT], ii_[:, NPP:NT], 0.0, ALU.mult, ALU.add)
                          P.copy(lru_h[:, l, j:j + 1], mm_[:, NPP - 1:NPP], eng="pool")
                          if has_s:
                              P.copy(tmp16.rearrange("p (b o) -> p b o", o=1), mm_[:, NPP:NT].rearrange("p (b r) -> p b r", b=16)[:, :, 3:4], eng="pool")
                              pp = smrot.next()
                              P.tr(pp[0:16, 0:128], tmp16[:, 0:16], ident_f)
                              P.copy(LRout[:, j * 128:(j + 1) * 128], pp[0:16, 0:128], eng="act")
                          P.tt(yc[:, j, 0:nn], mm_[:, 0:nn], gel[:, j, 0:nn], ALU.mult)

                      deferred.append(pe_part)

                  linear([(w_in_l[:, OFF_GATE:OFF_GATE + 1024], KC)], 1024, 512, tiles, rhs_u, lru_consume)
                  flush()
                  stage("lru")
                  if has_s:
                      P.dma(O["s_lconv"][l].rearrange("b r c -> (b r) c"), LCout, q="sp", is_output=True)
                      P.dma(O["s_lru"][l], LRout, q="sp", is_output=True)

                  zs = A[:, 12:20, :]

                  def z_consume(m, ti, n0, n1, ps):
                      P.act(zs[:, m, n0:n1], ps[:, 0:n1 - n0], AF.Silu)

                  linear([(w_in_l[:, OFF_Z:OFF_Z + 1024], KC)], 1024, 512, tiles, rhs_u, z_consume)

                  linear([(w_out_l[0:512, :], 4), (w_out_l[1536:2048, :], 4)], D, 512, tiles,
                         lambda k, n0, n1: A[:, k, n0:n1], add_resid)
                  stage("wout1")

                  ar.off = arena_base
                  xbc = A[:, 0:12, :]
                  zs = A[:, 12:20, :]
                  dtb = RB[:, l * 16:(l + 1) * 16]
                  Abc = RB[:, 64 + l * 16:64 + (l + 1) * 16]
                  dt_all = ar.get([nchunk + 1, 16])
                  dA_all = ar.get([nchunk + 1, 16])
                  ccr = Rot([dict(acs_tok=ar.get([16]), decst=ar.get([16]), eacs=ar.get([16]), cdb=ar.get([16]), dtdec=ar.get([16]))
                             for _ in range(2)])
                  Rr = Rot([ar.get([8, 128]) for _ in range(2)])
                  Er = Rot([ar.get([8, 128], BF16) for _ in range(2)])
                  Mr = Rot([ar.get([8, 128], BF16) for _ in range(2)])
                  xdtr = Rot([ar.get([512], BF16) for _ in range(2)])
                  xddr = Rot([ar.get([512], BF16) for _ in range(2)])
                  yofr = Rot([ar.get([512], BF16) for _ in range(2)])
                  Btr = Rot([ar.get([128], BF16) for _ in range(2)])
                  CBr = Rot([ar.get([128], BF16) for _ in range(2)])

                  sbase3 = ar.off
                  if has_s:
                      SCin = ar.get([1536], parts=48)
                      P.dma(SCin, I["st_sconv"][l].rearrange("b r c -> (b r) c"), q="sp")
                      SCout = ar.get([1536], parts=48)
                  sbase2 = ar.off
                  RAWS = 3 + NPP + 16 * 7 + 1
                  rawrot = Rot([ar.get([RAWS]) for _ in range(2)])
                  accr = Rot([ar.get([NT]) for _ in range(2)])
                  tmp48 = ar.get([48])

                  def xbc_consume(j, ti, n0, n1, ps):
                      n = n1 - n0
                      if ti == 0:
                          cur_raw[0] = rawrot.next()
                          raw = cur_raw[0]
                          P.copy(raw[:, 0:3], sconv_tail[:, l, j, :], eng="pool")
                          if has_s:
                              pp = smrot.next()
                              P.tr(pp[:, 0:48], SCin[:, j * 128:(j + 1) * 128], ident_f[0:48, 0:48])
                              P.copy(svw(raw, SOFFL, 16, 7)[:, :, 0:3], pp[:, 0:48].rearrange("p (b r) -> p b r", b=16), eng="dve")
                      raw = cur_raw[0]
                      if n0 < NPP:
                          P.copy(raw[:, 3 + n0:3 + n1], ps[:, 0:n], eng="act")
                      else:
                          P.copy(svw(raw, SOFFL, 16, 7)[:, :, 3:7], ps[:, 0:64].rearrange("p (b r) -> p b r", b=16), eng="act")
                      if ti != len(tiles) - 1:
                          return
                      flush()
                      acc = accr.next()
                      for k in range(4):
                          wk = pcol("scw", (l * 4 + k) * 12 + j)
                          if k == 0:
                              P.ts(acc[:, 0:NPP], raw[:, 0:NPP], wk, ALU.mult, pcol("scb", l * 12 + j), ALU.add)
                              if has_s:
                                  P.ts(acc[:, NPP:NT].rearrange("p (b r) -> p b r", b=16), svw(raw, SOFFL, 16, 7)[:, :, 0:4], wk, ALU.mult,
                                       pcol("scb", l * 12 + j), ALU.add)
                          else:
                              P.stt(acc[:, 0:NPP], raw[:, k:k + NPP], wk, acc[:, 0:NPP], ALU.mult, ALU.add)
                              if has_s:
                                  av = acc[:, NPP:NT].rearrange("p (b r) -> p b r", b=16)
                                  P.stt(av, svw(raw, SOFFL, 16, 7)[:, :, k:k + 4], wk, av, ALU.mult, ALU.add)
                      P.copy(sconv_tail[:, l, j, :], raw[:, NPP:NPP + 3], eng="pool")
                      deferred.append(lambda j=j, acc=acc: P.act(xbc[:, j, 0:ncol], acc[:, 0:ncol], AF.Silu))
                      if has_s:
                          P.copy(tmp48.rearrange("p (b r) -> p b r", b=16), svw(raw, SOFFL, 16, 7)[:, :, 4:7], eng="pool")
                          pp = smrot.next()
                          P.tr(pp[0:48, 0:128], tmp48[:, 0:48], ident_f)
                          P.copy(SCout[:, j * 128:(j + 1) * 128], pp[0:48, 0:128], eng="act")

                  linear([(w_in_l[:, OFF_XBC:OFF_XBC + 1536], KC)], 1536, 512, tiles, rhs_u, xbc_consume)
                  flush()
                  if has_s:
                      P.dma(O["s_sconv"][l].rearrange("b r c -> (b r) c"), SCout, q="sp", is_output=True)

                  stage("xbcz")

                  psd = smrot.next()
                  for c in range(nchunk):
                      for k in range(KC):
                          P.mm(psd[:, c * 16:(c + 1) * 16], uT[:, k, c * 128:(c + 1) * 128], WDT[:, k, :], start=(k == 0), stop=(k == KC - 1))
                  P.tt(dt_all[:, 0:nchunk, :], psd[:, 0:nchunk * 16].rearrange("p (c h) -> p c h", h=16),
                       bcast(dtb.rearrange("p (o h) -> p o h", o=1), [128, nchunk, 16]), ALU.add)
                  if has_s:
                      psd2 = smrot.next()
                      for k in range(KC):
                          P.mm(psd2[0:64, 0:16], uT[:, k, NPP:NT], WDT[:, k, :], start=(k == 0), stop=(k == KC - 1))
                      P.memset(dt_all[:, nchunk, :], 0.0)
                      P.tt(dt_all[0:64, nchunk, :], psd2[0:64, 0:16], dtb[0:64, :], ALU.add)
                  nch_all = nchunk + (1 if has_s else 0)
                  P.act(dt_all[:, 0:nch_all, :], dt_all[:, 0:nch_all, :], AF.Exp)
                  P.act(dt_all[:, 0:nch_all, :], dt_all[:, 0:nch_all, :], AF.Ln, bias=1.0)
                  P.tt(dA_all[:, 0:nch_all, :], dt_all[:, 0:nch_all, :], bcast(Abc.rearrange("p (o h) -> p o h", o=1), [128, nch_all, 16]), ALU.mult)

                  if first:
                      P.memset(ssdT[:], 0.0)
                      P.memset(ssdT_b[:], 0.0)
                  else:
                      P.dma(ssdT[:], spill[l], q="sp")
                      P.copy(ssdT_b[:], ssdT[:], eng="act")
                  stage("dt")

                  def chunk_common(c, cols, K):
                      L = K["L"]
                      CC = ccr.next()
                      acs_tok, decst, eacs, cdb, dtdec = CC["acs_tok"], CC["decst"], CC["eacs"], CC["cdb"], CC["dtdec"]
                      dA_c = dA_all[0:L, c, :]
                      ps1 = smrot.next()
                      P.mm(ps1[0:L, 0:16], K["U"], dA_c)
                      P.copy(acs_tok[0:L, :], ps1[0:L, 0:16], eng="dve")
                      P.mm(ps1[0:L, 16:32], K["sel"], acs_tok[0:L, :])
                      P.tt(decst[0:L, :], ps1[0:L, 16:32], acs_tok[0:L, :], ALU.subtract)
                      P.act(decst[0:L, :], decst[0:L, :], AF.Exp)
                      P.act(eacs[0:L, :], acs_tok[0:L, :], AF.Exp)
                      if L == 128:
                          P.mm(ps1[:, 32:48], ones_f, dA_c)
                          P.copy(cdb[:], ps1[:, 32:48], eng="dve")
                          P.act(cdb[:], cdb[:], AF.Exp)
                      P.tt(dtdec[0:L, :], dt_all[0:L, c, :], decst[0:L, :], ALU.mult)
                      return CC

                  def unit_pre(c, g, cols, K, CC):
                      L = K["L"]
                      dtdec = CC["dtdec"]
                      c0, c1 = cols
                      dA_g = dA_all[0:L, c, 8 * g:8 * g + 8]
                      pxt = smrot.next()
                      pxb = pxt[:, 0:256].bitcast(BF16)
                      for j in range(4):
                          P.tr(pxb[0:L, j * 128:(j + 1) * 128], xbc[:, 4 * g + j, c0:c1], ident_b)
                      xdt = xdtr.next()
                      xdd = xddr.next()
                      pv = pxb[0:L, :].rearrange("p (h q) -> p h q", h=8)
                      P.tt(xdt[0:L, :].rearrange("p (h q) -> p h q", h=8), pv,
                           bcast(dt_all[0:L, c, 8 * g:8 * g + 8].rearrange("p (h o) -> p h o", o=1), [L, 8, 64]), ALU.mult)
                      P.tt(xdd[0:L, :].rearrange("p (h q) -> p h q", h=8), pv,
                           bcast(dtdec[0:L, 8 * g:8 * g + 8].rearrange("p (h o) -> p h o", o=1), [L, 8, 64]), ALU.mult)
                      pbt = smrot.next()
                      pbb = pbt[:, 0:256].bitcast(BF16)
                      P.tr(pbb[0:L, 0:128], xbc[:, 8 + g, c0:c1], ident_b)
                      Bt = Btr.next()
                      P.copy(Bt[0:L, :], pbb[0:L, 0:128], eng="act")
                      R = Rr.next()
                      Rv = R[0:L, :, 0:L]
                      P.tt(Rv, bcast(K["U"].rearrange("p (o l) -> p o l", o=1), [L, 8, L]),
                           bcast(dA_g.rearrange("p (h o) -> p h o", o=1), [L, 8, L]), ALU.mult)
                      E = Er.next()
                      nh = 512 // L
                      for half in range(8 // nh):
                          seg = psA[0:L, half * 512:half * 512 + nh * L].rearrange("p (h l) -> p h l", h=nh)
                          hs = slice(half * nh, (half + 1) * nh)
                          P.mm(seg, ones_f[0:L, 0:L], Rv[:, hs, :], start=True, stop=False)
                          P.mm(seg, K["negU"], bcast(dA_g[:, hs].rearrange("p (h o) -> p h o", o=1), [L, nh, L]), start=False, stop=False)
                          P.mm(seg, ident_b[0:L, 0:L], bcast(K["negm"].rearrange("p (o l) -> p o l", o=1), [L, nh, L]), start=False, stop=True)
                          P.act(E[0:L, hs, 0:L], seg, AF.Exp)
                      pcb = smrot.next()
                      P.mm(pcb[0:L, 0:L], xbc[:, 8 + g, c0:c1], xbc[:, 10 + g, c0:c1])
                      CBs = CBr.next()
                      P.copy(CBs[0:L, 0:L], pcb[0:L, 0:L], eng="act")
                      M = Mr.next()
                      return dict(xdt=xdt, xdd=xdd, Bt=Bt, M=M, E=E, CBs=CBs, CC=CC)

                  def unit_m(H, K):
                      L = K["L"]
                      P.tt(H["M"][0:L, :, 0:L], H["E"][0:L, :, 0:L], bcast(H["CBs"][0:L, 0:L].rearrange("p (o l) -> p o l", o=1), [L, 8, L]), ALU.mult)

                  def unit_y(c, g, cols, K, H, pyo):
                      L = K["L"]
                      c0, c1 = cols
                      eacs = H["CC"]["eacs"]
                      yof = yofr.next()
                      P.tt(yof[0:L, :].rearrange("p (h q) -> p h q", h=8), pyo[0:L, 0:512].rearrange("p (h q) -> p h q", h=8),
                           bcast(eacs[0:L, 8 * g:8 * g + 8].rearrange("p (h o) -> p h o", o=1), [L, 8, 64]), ALU.mult)
                      Y = psY[:, 0:4 * L].rearrange("p (j l) -> p j l", j=4)
                      for k in range(8):
                          out = Y[64 * (k % 2):64 * (k % 2) + 64, k // 2, :]
                          P.mm(out, H["xdt"][0:L, 64 * k:64 * k + 64], H["M"][0:L, k, 0:L], start=True, stop=False)
                          P.mm(out, yof[0:L, 64 * k:64 * k + 64], ident_b[0:L, 0:L], start=False, stop=True)
                      for j in range(4):
                          P.stt(xbc[:, 4 * g + j, c0:c1], xbc[:, 4 * g + j, c0:c1], DCOL[:, l, 4 * g + j:4 * g + j + 1], Y[:, j, :], ALU.mult, ALU.add)

                  units = [(c, g) for c in range(nchunk) for g in range(2)]
                  HH = {}
                  ccs = {}

                  def pre_a(u):
                      c, g = units[u]
                      cols = (c * 128, (c + 1) * 128)
                      if g == 0:
                          ccs[c] = chunk_common(c, cols, CP)
                      HH[u] = unit_pre(c, g, cols, CP, ccs[c])

                  def y_part(u):
                      c, g = units[u]
                      cols = (c * 128, (c + 1) * 128)
                      H = HH.pop(u)
                      cdb = H["CC"]["cdb"]
                      pyo = mmrot.next()
                      P.mm(pyo[:, 0:512], xbc[:, 10 + g, cols[0]:cols[1]], ssdT_b[:, 512 * g:512 * (g + 1)])
                      unit_y(c, g, cols, CP, H, pyo)
                      pdl = mmrot.next()
                      P.mm(pdl[:, 0:512], H["Bt"][:, :], H["xdd"][:, :])
                      sv = ssdT[:, 512 * g:512 * (g + 1)].rearrange("p (h q) -> p h q", h=8)
                      P.tt(sv, sv, bcast(cdb[:, 8 * g:8 * g + 8].rearrange("p (h o) -> p h o", o=1), [128, 8, 64]), ALU.mult)
                      P.tt(ssdT[:, 512 * g:512 * (g + 1)], ssdT[:, 512 * g:512 * (g + 1)], pdl[:, 0:512], ALU.add)
                      P.copy(ssdT_b[:, 512 * g:512 * (g + 1)], ssdT[:, 512 * g:512 * (g + 1)], eng="act")

                  pre_a(0)
                  unit_m(HH[0], CP)
                  for u in range(len(units)):
                      if u + 1 < len(units):
                          pre_a(u + 1)
                      y_part(u)
                      if u + 1 < len(units):
                          unit_m(HH[u + 1], CP)
                      stage("scan1")
                  if not last:
                      P.dma(spill[l], ssdT[:], q="sp")
                      stage("scanp")
                  else:
                      ar.off = sbase3
                      for half in range(2):
                          for j in range(4):
                              P.tr(psA[:, j * 128:(j + 1) * 128], ssdT[:, (half * 4 + j) * 128:(half * 4 + j + 1) * 128], ident_f)
                          so = ar.get([4, 128])
                          P.copy(so, psA[:, 0:512].rearrange("p (j n) -> p j n", j=4), eng="act")
                          P.dma(O["p_ssd"][l].rearrange("(hp two) q n -> (two q) hp n", two=2)[:, half * 4:(half + 1) * 4, :], so, q="sp", is_output=True)

                  if has_s:
                      ar.off = sbase3
                      c = nchunk
                      cols = (NPP, NT)
                      CCs = chunk_common(c, cols, CS)
                      Hs = [unit_pre(c, g, cols, CS, CCs) for g in range(2)]
                      for g in range(2):
                          unit_m(Hs[g], CS)
                      rblk = ar.get([16, 16], parts=64)
                      P.tt(rblk, bcast(dA_all[0:64, c, :].rearrange("p (o h) -> p o h", o=1), [64, 16, 16]),
                           bcast(blockind_f.rearrange("p (b o) -> p b o", o=1), [64, 16, 16]), ALU.mult)
                      pda = smrot.next()
                      P.mm(pda[:, 0:256], ones_f[0:64, :], rblk.rearrange("p b h -> p (b h)"))
                      dec_all = ar.get([16, 16])
                      P.act(dec_all.rearrange("p b h -> p (b h)"), pda[:, 0:256], AF.Exp)
                      Cm = []
                      Bblk = []
                      for g in range(2):
                          cm = ar.get([16, 64], BF16)
                          P.tt(cm, bcast(xbc[:, 10 + g, NPP:NT].rearrange("p (o l) -> p o l", o=1), [128, 16, 64]), BMASK[:], ALU.mult)
                          Cm.append(cm)
                          bb = ar.get([16, 128], BF16, parts=64)
                          P.tt(bb, bcast(Hs[g]["Bt"][0:64, :].rearrange("p (o n) -> p o n", o=1), [64, 16, 128]),
                               bcast(blockind_b.rearrange("p (b o) -> p b o", o=1), [64, 16, 128]), ALU.mult)
                          Bblk.append(bb)
                      S0r = Rot([ar.get([8, 128]) for _ in range(2)])
                      h0Tr = Rot([ar.get([D], BF16) for _ in range(2)])
                      pyo = [mmrot.next(), mmrot.next()]
                      for b in range(NSQ):
                          S0 = S0r.next()
                          P.dma(S0, I["st_ssd"][l, b].rearrange("(hp two) q n -> (two q) hp n", two=2), q="sp")
                          h0T = h0Tr.next()
                          for half in range(2):
                              for j in range(4):
                                  P.tr(psA[:, j * 128:(j + 1) * 128] if half == 0 else psA[:, 512 + j * 128:512 + (j + 1) * 128],
                                       S0[:, half * 4 + j, :], ident_f)
                          P.copy(h0T[:, 0:512], psA[:, 0:512], eng="act")
                          P.copy(h0T[:, 512:1024], psA[:, 512:1024], eng="dve")
                          for g in range(2):
                              P.mm(pyo[g][0:64, 0:512], Cm[g][:, b, :], h0T[:, 512 * g:512 * (g + 1)], start=(b == 0), stop=(b == NSQ - 1))
                          hn = S0
                          for h2 in range(2):
                              dsl = dec_all[h2 * 64:(h2 + 1) * 64, b, :].rearrange("p (hp two) -> p hp two", two=2)[:, :, h2:h2 + 1]
                              P.tt(hn[h2 * 64:(h2 + 1) * 64, :, :], S0[h2 * 64:(h2 + 1) * 64, :, :], bcast(dsl, [64, 8, 128]), ALU.mult)
                          for half in range(2):
                              pdl = smrot.next()
                              for j in range(4):
                                  hp = half * 4 + j
                                  g = hp // 4
                                  P.mm(pdl[:, j * 128:(j + 1) * 128], Hs[g]["xdd"][0:64, (hp % 4) * 128:(hp % 4 + 1) * 128], Bblk[g][:, b, :])
                              P.tt(hn[:, half * 4:(half + 1) * 4, :], hn[:, half * 4:(half + 1) * 4, :],
                                   pdl[:, 0:512].rearrange("p (j n) -> p j n", j=4), ALU.add)
                          P.dma(O["s_ssd"][l, b].rearrange("(hp two) q n -> (two q) hp n", two=2), hn, q="sp", is_output=True)
                      for g in range(2):
                          unit_y(c, g, cols, CS, Hs[g], pyo[g])

                  ar.off = arena_base
                  yg = A[:, 0:8, :]
                  for (n0, n1) in tiles:
                      n = n1 - n0
                      ps = mmrot.next()
                      for k in range(KC):
                          P.tt(yg[:, k, n0:n1], yg[:, k, n0:n1], zs[:, k, n0:n1], ALU.mult)
                          sq = sqrot.next()
                          P.act(sq[:, 0:n], yg[:, k, n0:n1], AF.Square)
                          P.mm(ps[:, 0:n], ones_b, sq[:, 0:n], start=(k == 0), stop=(k == KC - 1))
                      P.act(rstd[:, 0:n], ps[:, 0:n], AF.Sqrt, scale=1.0 / D, bias=epsc[:, 0:1])
                      P.recip(rstd[:, 0:n], rstd[:, 0:n])
                      for k in range(KC):
                          P.stt(yg[:, k, n0:n1], yg[:, k, n0:n1], pcol("snorm", l * KC + k), rstd[:, 0:n], ALU.mult, ALU.mult)
                  linear([(w_out_l[512:1536, :], 8)], D, 512, tiles, lambda k, n0, n1: A[:, k, n0:n1], add_resid)
                  stage("gate")
                  if dbg:
                      P.dma(O["dbg_h"][pas, l, 0], hT[:].rearrange("p k n -> p (k n)"), q="sp", is_output=True)

                  ar.off = arena_base
                  rmsnorm("nmem", l, tiles)
                  stage("norm2")
                  qT = A[:, 0:8, :]
                  oT = A[:, 8:16, :]
                  kvst = Rot([ar.get([512]) for _ in range(2)])
                  KT = ar.get([KC, 256], BF16)
                  VT = ar.get([2, D], BF16)

                  def kt_consume(m, ti, n0, n1, ps):
                      P.copy(KT[:, m, :], ps[:, 0:256], eng="act")

                  linear([(I["w_mem_k"][l], KC)], D, 512, [(0, 256)], lambda k, n0, n1: memT[:, k, n0:n1], kt_consume)
                  stage("kt")

                  def tokmajor_proj(wname, oname, keep):
                      groups = []
                      for gi in range(2):
                          def body(wb, gi=gi):
                              wv = wview(wb, 0, KC, 512)
                              for mc in range(2):
                                  ps = mmrot.next()
                                  for k in range(KC):
                                      P.mm(ps[:, 0:512], memT[:, k, mc * 128:(mc + 1) * 128], wv[:, k, :], start=(k == 0), stop=(k == KC - 1))
                                  if first:
                                      stg = kvst.next()
                                      P.copy(stg, ps[:, 0:512], eng="dve")
                                      if keep is not None:
                                          P.copy(keep[:, mc, gi * 512:(gi + 1) * 512], stg, eng="act")
                                      P.dma(O[oname][l, mc * 128:(mc + 1) * 128, gi * 512:(gi + 1) * 512], stg, q="sp", is_output=True)
                                  elif keep is not None:
                                      P.copy(keep[:, mc, gi * 512:(gi + 1) * 512], ps[:, 0:512], eng="act")
                          groups.append(dict(dmas=[(lambda wb: wview(wb, 0, KC, 512), I[wname][l][:, gi * 512:(gi + 1) * 512].rearrange("(k p) m -> p k m", p=128))], body=body))
                      stream(groups)

                  if first:
                      tokmajor_proj("w_mem_k", "p_mk", None)
                      stage("tmk")
                  tokmajor_proj("w_mem_v", "p_mv", VT)
                  stage("attnkv")

                  def q_consume(m, ti, n0, n1, ps):
                      P.copy(qT[:, m, n0:n1], ps[:, 0:n1 - n0], eng="act")

                  linear([(I["w_mem_q"][l], KC)], D, 512, tiles, rhs_u, q_consume)
                  stage("attnq")
                  SCL = 1.0 / 16.0
                  ptr = Rot([ar.get([2, 512], BF16) for _ in range(2)])
                  rcp = ar.get([512])
                  for (n0, n1) in tiles:
                      if n0 >= NPP:
                          continue
                      for h in range(4):
                          pt = ptr.next()
                          for mc in range(2):
                              sc = psA[:, mc * 512:(mc + 1) * 512]
                              for dc in range(2):
                                  P.mm(sc, KT[:, 2 * h + dc, mc * 128:(mc + 1) * 128], qT[:, 2 * h + dc, n0:n1], start=(dc == 0), stop=(dc == 1))
                              P.act(pt[:, mc, :], sc, AF.Exp, scale=SCL)
                          pss = smrot.next()
                          for mc in range(2):
                              P.mm(pss[:, 0:512], ones_b, pt[:, mc, :], start=(mc == 0), stop=(mc == 1))
                          P.recip(rcp, pss[:, 0:512])
                          for dc in range(2):
                              po = mmrot.next()
                              for mc in range(2):
                                  P.mm(po[:, 0:512], VT[:, mc, h * 256 + dc * 128:h * 256 + (dc + 1) * 128], pt[:, mc, :], start=(mc == 0), stop=(mc == 1))
                              P.tt(oT[:, 2 * h + dc, n0:n1], po[:, 0:512], rcp, ALU.mult)
                  if has_s:
                      kbr = Rot([ar.get([2, D], BF16) for _ in range(5)])
                      vbr = kbr
                      ktr = Rot([ar.get([KC, 256], BF16) for _ in range(2)])
                      pts = ar.get([NSQ, 2, 16], BF16)
                      rcs = ar.get([NSQ * 16])
                      vbs = []
                      pscs = mmrot.next()
                      scv = pscs[:, 0:512].rearrange("p (b m x) -> p b m x", b=NSQ, m=2)
                      for b in range(NSQ):
                          kb = kbr.next()
                          P.dma(kb, I["ck"][l, b].rearrange("(mc p) e -> p mc e", p=128), q="pool")
                          ktb = ktr.next()
                          for half in range(2):
                              pkb = psA[:, half * 512:(half + 1) * 512].bitcast(BF16)
                              for ee in range(4):
                                  e = half * 4 + ee
                                  for mc in range(2):
                                      P.tr(pkb[:, ee * 256 + mc * 128:ee * 256 + (mc + 1) * 128], kb[:, mc, e * 128:(e + 1) * 128], ident_b)
                              P.copy(ktb[:, half * 4:(half + 1) * 4, :], pkb.rearrange("p (e m) -> p e m", e=4), eng=("act" if half == 0 else "dve"))
                          for h in range(4):
                              for mc in range(2):
                                  for dc in range(2):
                                      P.mm(scv[:, b, mc, 4 * h:4 * h + 4], ktb[:, 2 * h + dc, mc * 128:(mc + 1) * 128],
                                           qT[:, 2 * h + dc, NPP + 4 * b:NPP + 4 * b + 4], start=(dc == 0), stop=(dc == 1))
                      P.act(pts.rearrange("p b m x -> p (b m x)"), pscs[:, 0:512], AF.Exp, scale=SCL)
                      pos = mmrot.next()
                      pss = smrot.next()
                      for b in range(NSQ):
                          vb = vbr.next()
                          P.dma(vb, I["cv"][l, b].rearrange("(mc p) e -> p mc e", p=128), q="pool")
                          for mc in range(2):
                              P.mm(pss[:, b * 16:(b + 1) * 16], ones_b, pts[:, b, mc, :], start=(mc == 0), stop=(mc == 1))
                          for e in range(KC):
                              h = e // 2
                              for mc in range(2):
                                  P.mm(pos[:, e * 64 + 4 * b:e * 64 + 4 * b + 4], vb[:, mc, e * 128:(e + 1) * 128], pts[:, b, mc, 4 * h:4 * h + 4],
                                       start=(mc == 0), stop=(mc == 1))
                      P.recip(rcs, pss[:, 0:256])
                      rv = rcs.rearrange("p (b h r) -> p b h r", b=NSQ, h=4)
                      for h in range(4):
                          for dc in range(2):
                              e = 2 * h + dc
                              P.tt(oT[:, e, NPP:NT].rearrange("p (b r) -> p b r", b=NSQ), pos[:, e * 64:(e + 1) * 64].rearrange("p (b r) -> p b r", b=NSQ),
                                   rv[:, :, h, :], ALU.mult)
                  linear([(I["w_mem_o"][l], KC)], D, 512, tiles, lambda k, n0, n1: oT[:, k, n0:n1], add_resid)
                  stage("attn")
                  if dbg:
                      P.dma(O["dbg_h"][pas, l, 1], hT[:].rearrange("p k n -> p (k n)"), q="sp", is_output=True)

                  ar.off = arena_base
                  rmsnorm("nffn", l, tiles)
                  sgr = Rot([ar.get([512]) for _ in range(2)])
                  groups = []
                  for gi in range(FC // 2):
                      def body(wb, gi=gi):
                          wv = wview(wb, 0, KC, 512)
                          for fi in range(2):
                              f = gi * 2 + fi
                              for (n0, n1) in tiles:
                                  n = n1 - n0
                                  pg = mmrot.next()
                                  for k in range(KC):
                                      P.mm(pg[:, 0:n], wv[:, k, fi * 128:(fi + 1) * 128], uT[:, k, n0:n1], start=(k == 0), stop=(k == KC - 1))
                                  pu = mmrot.next()
                                  for k in range(KC):
                                      P.mm(pu[:, 0:n], wv[:, k, 256 + fi * 128:256 + (fi + 1) * 128], uT[:, k, n0:n1], start=(k == 0), stop=(k == KC - 1))
                                  sg = sgr.next()
                                  P.act(sg[:, 0:n], pg[:, 0:n], AF.Silu)
                                  P.tt(A[:, f, n0:n1], sg[:, 0:n], pu[:, 0:n], ALU.mult)
                      dmas = [
                          (lambda wb: wview(wb, 0, KC, 512)[:, :, 0:256], I["w_ffn_gate"][l][:, gi * 256:(gi + 1) * 256].rearrange("(k p) m -> p k m", p=128)),
                          (lambda wb: wview(wb, 0, KC, 512)[:, :, 256:512], I["w_ffn_up"][l][:, gi * 256:(gi + 1) * 256].rearrange("(k p) m -> p k m", p=128)),
                      ]
                      groups.append(dict(dmas=dmas, body=body))
                  stream(groups)
                  linear([(I["w_ffn_down"][l], FC)], D, 128, tiles, lambda k, n0, n1: A[:, k, n0:n1], add_resid)
                  stage("ffn")
                  if dbg:
                      P.dma(O["dbg_h"][pas, l, 2], hT[:].rearrange("p k n -> p (k n)"), q="sp", is_output=True)

                  if last:
                      ar.off = arena_base
                      o1 = ar.get([512], parts=15)
                      pp = smrot.next()
                      for j in range(4):
                          P.tr(pp[0:15, j * 128:(j + 1) * 128], pool_tail[:, l, j, :], ident_f)
                      P.copy(o1, pp[0:15, 0:512], eng="act")
                      P.dma(O["p_pool"][l], o1, q="sp", is_output=True)
                      o2 = ar.get([1536], parts=3)
                      for q3 in range(3):
                          pp = smrot.next()
                          for j in range(4):
                              P.tr(pp[0:3, j * 128:(j + 1) * 128], sconv_tail[:, l, q3 * 4 + j, :], ident_f)
                          P.copy(o2[:, q3 * 512:(q3 + 1) * 512], pp[0:3, 0:512], eng="act")
                      P.dma(O["p_sconv"][l], o2, q="sp", is_output=True)
                      o3 = ar.get([512], parts=3)
                      pp = smrot.next()
                      for j in range(4):
                          P.tr(pp[0:3, j * 128:(j + 1) * 128], lconv_tail[:, l, j, :], ident_f)
                      P.copy(o3, pp[0:3, 0:512], eng="act")
                      P.dma(O["p_lconv"][l], o3, q="sp", is_output=True)
                      o4 = ar.get([512], parts=1)
                      pp = smrot.next()
                      for j in range(4):
                          P.tr(pp[0:1, j * 128:(j + 1) * 128], lru_h[:, l, j:j + 1], ident_f)
                      P.copy(o4, pp[0:1, 0:512], eng="act")
                      P.dma(O["p_lru"][l:l + 1, :], o4, q="sp", is_output=True)

              ar.reset()
              sqrot = Rot([ar.get([512], BF16) for _ in range(2)])
              rstd = ar.get([512])
              ytr = Rot([ar.get([KC, 128]) for _ in range(2)])
              yor = Rot([ar.get([D]) for _ in range(2)])
              for (n0, n1) in tiles:
                  n = n1 - n0
                  ps = mmrot.next()
                  for k in range(KC):
                      sq = sqrot.next()
                      P.act(sq[:, 0:n], hT[:, k, n0:n1], AF.Square)
                      P.mm(ps[:, 0:n], ones_b, sq[:, 0:n], start=(k == 0), stop=(k == KC - 1))
                  P.act(rstd[:, 0:n], ps[:, 0:n], AF.Sqrt, scale=1.0 / D, bias=epsc[:, 0:1])
                  P.recip(rstd[:, 0:n], rstd[:, 0:n])
                  bw = 128 if n0 < NPP else 64
                  for bi in range(n // bw):
                      yt = ytr.next()
                      for k in range(KC):
                          P.stt(yt[:, k, 0:bw], hT[:, k, n0 + bi * bw:n0 + (bi + 1) * bw], pcol("nfin", k), rstd[:, bi * bw:(bi + 1) * bw], ALU.mult, ALU.mult)
                      for k in range(KC):
                          P.tr(psA[0:bw, k * 128:(k + 1) * 128], yt[:, k, 0:bw], ident_f)
                      yo = yor.next()
                      P.copy(yo[0:bw, 0:512], psA[0:bw, 0:512], eng="act")
                      P.copy(yo[0:bw, 512:1024], psA[0:bw, 512:1024], eng="dve")
                      if n0 < NPP:
                          P.dma(O["y_p"][t0 + n0 + bi * 128:t0 + n0 + (bi + 1) * 128, :], yo, q="sp", is_output=True)
                      else:
                          P.dma(O["y_s"], yo[0:64, :], q="sp", is_output=True)
        except _Stop:
            pass
        P.emit()
    return nc


LRU_C = 8.0
_NC_CACHE = {}


def make_in_maps(inputs, cores):
    cst = build_consts()
    cst2 = np.zeros((128, 16, 64), np.float32)
    for b in range(16):
        cst2[:, b, 4 * b:4 * b + 4] = 1.0
    cst2 = cst2.reshape(128, 1024)
    maps = []
    for i in cores:
        s = slice(NSQ * i, NSQ * (i + 1))
        m = {
            "x_p": inputs["x_prompt"][i], "x_s": inputs["x_sample"][s].reshape(NS, D), "mem": inputs["mem_prompt"][i],
            "st_pool": inputs["state_pool"][:, s], "st_sconv": inputs["state_ssd_conv"][:, s], "st_ssd": inputs["state_ssd"][:, s],
            "st_lconv": inputs["state_lru_conv"][:, s], "st_lru": inputs["state_lru"][:, s],
            "ck": inputs["cache_mem_k"][:, s].reshape(DEPTH, NSQ, 256, D), "cv": inputs["cache_mem_v"][:, s].reshape(DEPTH, NSQ, 256, D),
            "cst": cst, "cst2": cst2,
        }
        for n, _ in IN_SPECS:
            if n not in m:
                m[n] = inputs[n]
        maps.append({k: np.ascontiguousarray(np.asarray(v, dtype=np.float32)) for k, v in m.items()})
    return maps


def kernel(**inputs):
    inputs = {k: np.asarray(v) for k, v in inputs.items()}
    if "nc" not in _NC_CACHE:
        _NC_CACHE["nc"] = build_program()
    nc = _NC_CACHE["nc"]
    maps = make_in_maps(inputs, list(range(N_CORES)))
    res = run_bass_kernel_spmd(nc, maps, core_ids=list(range(N_CORES)))
    R = res.results

    def cat(name, axis):
        return np.concatenate([np.asarray(r[name]) for r in R], axis=axis)

    y_p = np.stack([np.asarray(r["y_p"]) for r in R], 0)
    y_s = np.concatenate([np.asarray(r["y_s"]).reshape(NSQ, 4, D) for r in R], 0)
    outs = [y_p, y_s]
    for n in ("p_pool", "p_sconv", "p_ssd", "p_lconv", "p_lru"):
        outs.append(np.stack([np.asarray(r[n]) for r in R], 1))
    for n in ("p_mk", "p_mv"):
        outs.append(np.stack([np.asarray(r[n]).reshape(DEPTH, 256, 4, 256) for r in R], 1))
    for n in ("s_pool", "s_sconv", "s_ssd", "s_lconv", "s_lru"):
        outs.append(cat(n, 1))
    return tuple(np.ascontiguousarray(o.astype(np.float32)) for o in outs)
```

```python
import contextlib
import numpy as np
import concourse.bass as bass
import concourse.mybir as mybir
from concourse.bass_utils import run_bass_kernel_spmd

F32 = mybir.dt.float32
BF16 = mybir.dt.bfloat16
AF = mybir.ActivationFunctionType
ALU = mybir.AluOpType

ENGS = ("pe", "act", "dve", "pool", "sp")
NRING = 8

D = 1024
KC = 8
NPT = 2048
NSQ = 16
NS = 64
DEPTH = 4
NPASS = 2
NPP = NPT // NPASS
NT = NPP + NS
DFF = 2816
FC = 22
OFF_POOL, OFF_Z, OFF_XBC, OFF_DT, OFF_GATE, OFF_LRU, N_IN = 0, 512, 1536, 3072, 3088, 3600, 4112
NEG = -30000.0
EPS = 1e-6
NCST = 1104
WBUF = 4096
N_CORES = 8


def _isz(dt):
    return mybir.dt.size(dt)


def _region(ap):
    t = ap.tensor
    pairs = [tuple(x) for x in ap.ap]
    off = int(ap.offset)
    isz = _isz(ap.dtype)
    if type(t).__name__ == "DRamTensorHandle":
        ext = 0
        for st, cnt in pairs:
            ext += (cnt - 1) * abs(st)
        return (t.name, 0, 1, off * isz, (off + ext + 1) * isz)
    pst, pcnt = pairs[0]
    if pst == 0:
        pst = 1 << 40
    p_lo = off // pst
    f_lo = off % pst
    ext = 0
    for st, cnt in pairs[1:]:
        ext += (cnt - 1) * abs(st)
    b_lo, b_hi = f_lo * isz, (f_lo + ext + 1) * isz
    if type(t).__name__ == "PSumTensorHandle":
        b_lo = (b_lo // 2048) * 2048
        b_hi = ((b_hi + 2047) // 2048) * 2048
        return (t.name, (p_lo // 32) * 32, ((p_lo + pcnt + 31) // 32) * 32, b_lo, b_hi)
    return (t.name, p_lo, p_lo + pcnt, b_lo, b_hi)


class Op:
    __slots__ = ("eng", "idx", "fn", "deps", "is_dma", "needed", "signal", "clock", "ring", "tag")

    def __init__(self, eng, idx, fn, is_dma):
        self.eng = eng
        self.idx = idx
        self.fn = fn
        self.deps = []
        self.is_dma = is_dma
        self.needed = False
        self.signal = None
        self.clock = None
        self.ring = None


class Prog:
    def __init__(self, nc):
        self.nc = nc
        self.ops = {e: [] for e in ENGS}
        self.recs = {}
        self.ndma = {e: 0 for e in ENGS}
        self.dma_seq = {e: [] for e in ENGS}
        self.waited_dma = {e: set() for e in ENGS}
        self.out_dmas = []
        self.tag = ""

    def op(self, eng, fn, reads=(), writes=(), is_dma=False, extra_deps=()):
        lst = self.ops[eng]
        o = Op(eng, len(lst), fn, is_dma)
        o.tag = self.tag
        deps = {}
        ops = self.ops

        def add_dep(e2, i2):
            if e2 == eng and eng == "pe":
                return
            od = ops[e2][i2]
            if od.is_dma:
                deps[(e2, i2)] = True
            else:
                k = deps.get(e2)
                if k is None or i2 > k:
                    deps[e2] = i2

        rregs = [_region(ap) for ap in reads]
        wregs = [_region(ap) for ap in writes]
        for (name, pl, ph, fl, fh) in rregs:
            for r in self.recs.get(name, ()):
                if r[4] and r[0] < ph and pl < r[1] and r[2] < fh and fl < r[3]:
                    add_dep(r[5], r[6])
        for (name, pl, ph, fl, fh) in wregs:
            for r in self.recs.get(name, ()):
                if r[0] < ph and pl < r[1] and r[2] < fh and fl < r[3]:
                    add_dep(r[5], r[6])
        for (e2, i2) in extra_deps:
            add_dep(e2, i2)
        prev = lst[-1].clock if lst else {}
        clock = dict(prev)
        final = []
        for k in deps:
            if isinstance(k, tuple):
                if k in self.waited_dma[eng]:
                    continue
                final.append(k)
            else:
                i2 = deps[k]
                if clock.get(k, -1) >= i2:
                    continue
                final.append((k, i2))
        for (e2, i2) in final:
            od = ops[e2][i2]
            if od.is_dma:
                self.waited_dma[eng].add((e2, i2))
            else:
                if clock.get(e2, -1) < i2:
                    clock[e2] = i2
            for k2, v2 in od.clock.items():
                if clock.get(k2, -1) < v2:
                    clock[k2] = v2
        if is_dma:
            n = self.ndma[eng]
            o.ring = n
            self.ndma[eng] = n + 1
            if n >= NRING:
                pd = self.dma_seq[eng][n - NRING]
                if (eng, pd.idx) not in self.waited_dma[eng]:
                    final.append((eng, pd.idx))
                    self.waited_dma[eng].add((eng, pd.idx))
            self.dma_seq[eng].append(o)
        o.deps = final
        o.clock = clock
        lst.append(o)
        for (name, pl, ph, fl, fh) in rregs:
            L = self.recs.setdefault(name, [])
            if not is_dma:
                L[:] = [r for r in L if not ((not r[4]) and r[5] == eng and (not r[7]) and pl <= r[0] and r[1] <= ph and fl <= r[2] and r[3] <= fh)]
            L.append([pl, ph, fl, fh, False, eng, o.idx, is_dma])
        for (name, pl, ph, fl, fh) in wregs:
            L = self.recs.setdefault(name, [])
            L[:] = [r for r in L if not (pl <= r[0] and r[1] <= ph and fl <= r[2] and r[3] <= fh)]
            L.append([pl, ph, fl, fh, True, eng, o.idx, is_dma])
        return o

    def mm(self, out, lhsT, rhs, start=True, stop=True):
        return self.op("pe", lambda e: e.matmul(out, lhsT, rhs, start=start, stop=stop), [lhsT, rhs], [out])

    def tr(self, out, in_, ident):
        return self.op("pe", lambda e: e.transpose(out, in_, ident), [in_, ident], [out])

    def act(self, out, in_, func, bias=None, scale=None):
        reads = [in_]
        kw = {}
        if bias is not None:
            kw["bias"] = bias
            if not isinstance(bias, (int, float)):
                reads.append(bias)
        if scale is not None:
            kw["scale"] = scale
            if not isinstance(scale, (int, float)):
                reads.append(scale)
        return self.op("act", lambda e: e.activation(out, in_, func, **kw), reads, [out])

    def tt(self, out, a, b, op, eng="dve"):
        return self.op(eng, lambda e: e.tensor_tensor(out, a, b, op), [a, b], [out])

    def ts(self, out, a, s1, op0, s2=None, op1=None, eng="dve"):
        reads = [a]
        if not isinstance(s1, (int, float)):
            reads.append(s1)
        if s2 is not None and not isinstance(s2, (int, float)):
            reads.append(s2)
        if op1 is None:
            return self.op(eng, lambda e: e.tensor_scalar(out, a, s1, None, op0), reads, [out])
        return self.op(eng, lambda e: e.tensor_scalar(out, a, s1, s2, op0, op1), reads, [out])

    def stt(self, out, a, s, b, op0, op1):
        reads = [a, b]
        if not isinstance(s, (int, float)):
            reads.append(s)
        return self.op("dve", lambda e: e.scalar_tensor_tensor(out, a, s, b, op0, op1), reads, [out])

    def scan(self, out, d0, d1, init, op0, op1):
        reads = [d0, d1]
        if not isinstance(init, (int, float)):
            reads.append(init)
        return self.op("dve", lambda e: e.tensor_tensor_scan(out, d0, d1, init, op0, op1), reads, [out])

    def copy(self, out, in_, eng="dve"):
        if eng == "act":
            return self.op("act", lambda e: e.copy(out, in_), [in_], [out])
        return self.op(eng, lambda e: e.tensor_copy(out, in_), [in_], [out])

    def memset(self, ap, val, eng="dve"):
        return self.op(eng, lambda e: e.memset(ap, val), [], [ap])

    def recip(self, out, in_):
        return self.op("dve", lambda e: e.reciprocal(out, in_), [in_], [out])

    def dma(self, out, in_, q="sp", is_output=False, **kw):
        o = self.op(q, lambda e: e.dma_start(out=out, in_=in_, **kw), [in_], [out], is_dma=True)
        if is_output:
            self.out_dmas.append((q, o.idx))
        return o

    def emit(self):
        nc = self.nc
        self.op("sp", None, extra_deps=list(self.out_dmas))
        for e in ENGS:
            for o in self.ops[e]:
                for (e2, i2) in o.deps:
                    self.ops[e2][i2].needed = True
        for e in ENGS:
            c = 0
            for o in self.ops[e]:
                if o.needed and not o.is_dma:
                    c += 1
                    o.signal = c
        with contextlib.ExitStack() as st:
            sems = {e: st.enter_context(nc.semaphore("s_" + e)) for e in ENGS}
            rings = {e: [st.enter_context(nc.semaphore("r_%s_%d" % (e, i))) for i in range(NRING)] for e in ENGS if self.ndma[e]}
            block = st.enter_context(nc.Block())

            def run(engname, eng):
                for o in self.ops[engname]:
                    for (e2, i2) in o.deps:
                        od = self.ops[e2][i2]
                        if od.is_dma:
                            eng.wait_ge(rings[e2][od.ring % NRING], 16 * (od.ring // NRING + 1))
                        else:
                            eng.wait_ge(sems[e2], od.signal)
                    if o.fn is None:
                        continue
                    inst = o.fn(eng)
                    if o.is_dma:
                        inst.then_inc(rings[engname][o.ring % NRING], 16)
                    elif o.signal is not None:
                        inst.then_inc(sems[engname], 1)

            @block.tensor
            def _(eng):
                run("pe", eng)

            @block.scalar
            def _(eng):
                run("act", eng)

            @block.vector
            def _(eng):
                run("dve", eng)

            @block.gpsimd
            def _(eng):
                run("pool", eng)

            @block.sync
            def _(eng):
                run("sp", eng)


class Rot:
    def __init__(self, items):
        self.items = list(items)
        self.i = 0

    def next(self):
        x = self.items[self.i % len(self.items)]
        self.i += 1
        return x


class Arena:
    def __init__(self, tens, nwords):
        self.t = tens
        self.n = nwords
        self.off = 0

    def reset(self):
        self.off = 0

    def get(self, shape, dtype=F32, parts=128):
        n = 1
        for s in shape:
            n *= s
        words = (n * _isz(dtype) + 3) // 4
        words += words & 1
        assert self.off + words <= self.n, ("arena overflow", self.off, words, self.n)
        v = self.t[0:parts, self.off:self.off + words]
        self.off += words
        if dtype != F32:
            v = v.bitcast(dtype)
        v = v[:, 0:n]
        if len(shape) == 2:
            v = v.rearrange("p (a b) -> p a b", a=shape[0])
        elif len(shape) == 3:
            v = v.rearrange("p (a b c) -> p a b c", a=shape[0], b=shape[1])
        return v


def bcast(ap, shape):
    return ap.broadcast_to(list(shape))


def build_consts():
    c = np.zeros((128, NCST), np.float32)
    r = np.arange(128)
    c[:, 0:128] = np.eye(128)
    U = (r[:, None] <= r[None, :]).astype(np.float32)
    c[:, 128:256] = U
    c[:, 256:384] = -U
    sel = np.zeros((128, 128), np.float32)
    sel[127, :] = 1.0
    c[:, 384:512] = sel
    c[:, 512:640] = np.where(r[None, :] < r[:, None], NEG, 0.0)
    r64 = np.arange(64)
    same = (r64[:, None] // 4) == (r64[None, :] // 4)
    Us = (same & (r64[:, None] <= r64[None, :])).astype(np.float32)
    c[0:64, 640:704] = Us
    c[0:64, 704:768] = -Us
    sels = np.zeros((64, 64), np.float32)
    for s in range(64):
        sels[4 * (s // 4) + 3, s] = 1.0
    c[0:64, 768:832] = sels
    c[0:64, 832:896] = np.where(same & (r64[None, :] >= r64[:, None]), 0.0, NEG)
    bi = np.zeros((64, 16), np.float32)
    bi[r64, r64 // 4] = 1.0
    c[0:64, 896:912] = bi
    for g, w in enumerate((2, 4, 8, 16)):
        for t in range(16):
            c[:, 912 + g * 16 + t] = 1.0 / min(t + 1, w)
    c[:, 976:1104] = 1.0
    return c


IN_SPECS = [
    ("x_p", (NPT, D)), ("x_s", (NS, D)), ("mem", (256, D)),
    ("st_pool", (DEPTH, NSQ, 15, 512)), ("st_sconv", (DEPTH, NSQ, 3, 1536)), ("st_ssd", (DEPTH, NSQ, 16, 64, 128)),
    ("st_lconv", (DEPTH, NSQ, 3, 512)), ("st_lru", (DEPTH, NSQ, 512)),
    ("ck", (DEPTH, NSQ, 256, D)), ("cv", (DEPTH, NSQ, 256, D)),
    ("norm_mix", (DEPTH, D)), ("w_in", (DEPTH, D, N_IN)), ("pool_w", (DEPTH, 4, 128, 128)), ("pool_scale", (DEPTH, 512)),
    ("ssd_conv_w", (DEPTH, 4, 1536)), ("ssd_conv_b", (DEPTH, 1536)), ("ssd_dt_bias", (DEPTH, 16)), ("ssd_a_log", (DEPTH, 16)),
    ("ssd_d", (DEPTH, 16)), ("ssd_norm", (DEPTH, D)), ("lru_conv_w", (DEPTH, 4, 512)), ("lru_conv_b", (DEPTH, 512)),
    ("lru_wa", (DEPTH, 8, 64, 64)), ("lru_ba", (DEPTH, 8, 64)), ("lru_wx", (DEPTH, 8, 64, 64)), ("lru_bx", (DEPTH, 8, 64)),
    ("lru_lambda", (DEPTH, 512)), ("w_out", (DEPTH, 2048, D)), ("norm_mem", (DEPTH, D)), ("w_mem_q", (DEPTH, D, D)),
    ("w_mem_k", (DEPTH, D, D)), ("w_mem_v", (DEPTH, D, D)), ("w_mem_o", (DEPTH, D, D)), ("norm_ffn", (DEPTH, D)),
    ("w_ffn_gate", (DEPTH, D, DFF)), ("w_ffn_up", (DEPTH, D, DFF)), ("w_ffn_down", (DEPTH, DFF, D)), ("norm_final", (D,)),
    ("cst", (128, NCST)), ("cst2", (128, 1024)),
]
OUT_SPECS = [
    ("y_p", (NPT, D)), ("y_s", (NS, D)), ("p_pool", (DEPTH, 15, 512)), ("p_sconv", (DEPTH, 3, 1536)),
    ("p_ssd", (DEPTH, 16, 64, 128)), ("p_lconv", (DEPTH, 3, 512)), ("p_lru", (DEPTH, 512)),
    ("p_mk", (DEPTH, 256, D)), ("p_mv", (DEPTH, 256, D)),
    ("s_pool", (DEPTH, NSQ, 15, 512)), ("s_sconv", (DEPTH, NSQ, 3, 1536)), ("s_ssd", (DEPTH, NSQ, 16, 64, 128)),
    ("s_lconv", (DEPTH, NSQ, 3, 512)), ("s_lru", (DEPTH, NSQ, 512)),
]


class _Stop(Exception):
    pass


def build_program(n_layers=DEPTH, dbg=False, stop_at=None):
    nc = bass.Bass("TRN2", target_bir_lowering=False)
    I = {n: nc.dram_tensor(n, list(s), F32, kind="ExternalInput").ap() for n, s in IN_SPECS}
    O = {n: nc.dram_tensor(n, list(s), F32, kind="ExternalOutput").ap() for n, s in OUT_SPECS}
    spill = nc.dram_tensor("ssd_spill", [DEPTH, 128, D], F32, kind="Internal").ap()
    if dbg:
        O["dbg_h"] = nc.dram_tensor("dbg_h", [NPASS, DEPTH, 4, 128, KC * NT], F32, kind="ExternalOutput").ap()
    P = Prog(nc)
    with contextlib.ExitStack() as st:
        def sb(name, shape, dt=F32):
            return st.enter_context(nc.sbuf_tensor(name, list(shape), dt))

        def pst(name, shape, dt=F32):
            return st.enter_context(nc.psum_tensor(name, list(shape), dt))

        hT = sb("hT", [128, KC, NT])
        uT = sb("uT", [128, KC, NT], BF16)
        A = sb("A", [128, FC, NT], BF16)
        wbufs = [sb("wb%d" % i, [128, WBUF], BF16) for i in range(2)]
        cst_f = sb("cst_f", [128, NCST])
        cst_b = sb("cst_b", [128, NCST], BF16)
        PAR = sb("PAR", [128, 640])
        RB = sb("RB", [128, 3 * 64])
        DCOL = sb("DCOL", [128, DEPTH, 8])
        C1 = sb("C1", [128, 16])
        memT = sb("memT", [128, KC, 256], BF16)
        POOLW = sb("POOLW", [128, 4, 128], BF16)
        WA = sb("WA", [128, 4, 128], BF16)
        WX = sb("WX", [128, 4, 128], BF16)
        WDT = sb("WDT", [128, KC, 16], BF16)
        pool_tail = sb("pool_tail", [128, DEPTH, 4, 15])
        sconv_tail = sb("sconv_tail", [128, DEPTH, 12, 3])
        lconv_tail = sb("lconv_tail", [128, DEPTH, 4, 3])
        lru_h = sb("lru_h", [128, DEPTH, 4])
        ssdT = sb("ssdT", [128, D])
        ssdT_b = sb("ssdT_b", [128, D], BF16)
        DIAGD = sb("DIAGD", [128, 8, 128], BF16)
        ARW = 16384
        TMP = sb("TMP", [128, ARW])
        ar = Arena(TMP, ARW)

        psA = pst("psA", [128, 1024])
        pbs = [pst("pb%d" % i, [128, 512]) for i in range(6)]
        mmrot = Rot(pbs[0:3])
        smrot = Rot(pbs[3:5])
        psY = pbs[5]
        wrot = Rot(wbufs)

        ident_f = cst_f[:, 0:128]
        ident_b = cst_b[:, 0:128]
        ones_f = cst_f[:, 976:1104]
        ones_b = cst_b[:, 976:1104]
        CP = dict(U=cst_f[:, 128:256], negU=cst_f[:, 256:384], sel=cst_f[:, 384:512], negm=cst_b[:, 512:640], L=128,
                  Ub=cst_b[:, 128:256], negUb=cst_b[:, 256:384])
        CS = dict(U=cst_f[0:64, 640:704], negU=cst_f[0:64, 704:768], sel=cst_f[0:64, 768:832], negm=cst_b[0:64, 832:896], L=64,
                  Ub=cst_b[0:64, 640:704], negUb=cst_b[0:64, 704:768])
        blockind_f = cst_f[0:64, 896:912]
        blockind_b = cst_b[0:64, 896:912]
        rc_tab = cst_f[:, 912:976].rearrange("p (g t) -> p g t", g=4)

        P.dma(cst_f[:], I["cst"], q="sp")
        P.dma(cst_b[:], I["cst"], q="pool")
        BMASK = sb("BMASK", [128, 16, 64], BF16)
        P.dma(BMASK[:], I["cst2"].rearrange("p (b l) -> p b l", b=16), q="pool")
        prow = {}
        plist = [
            ("nm", I["norm_mix"].rearrange("l (j p) -> (l j) p", p=128)),
            ("nmem", I["norm_mem"].rearrange("l (j p) -> (l j) p", p=128)),
            ("nffn", I["norm_ffn"].rearrange("l (j p) -> (l j) p", p=128)),
            ("nfin", I["norm_final"].rearrange("(j p) -> j p", p=128)),
            ("pscale", I["pool_scale"].rearrange("l (j p) -> (l j) p", p=128)),
            ("scw", I["ssd_conv_w"].rearrange("l k (j p) -> (l k j) p", p=128)),
            ("scb", I["ssd_conv_b"].rearrange("l (j p) -> (l j) p", p=128)),
            ("snorm", I["ssd_norm"].rearrange("l (j p) -> (l j) p", p=128)),
            ("lcw", I["lru_conv_w"].rearrange("l k (j p) -> (l k j) p", p=128)),
            ("lcb", I["lru_conv_b"].rearrange("l (j p) -> (l j) p", p=128)),
            ("lam", I["lru_lambda"].rearrange("l (j p) -> (l j) p", p=128)),
            ("ba", I["lru_ba"].rearrange("l h i -> l (h i)").rearrange("l (j p) -> (l j) p", p=128)),
            ("bx", I["lru_bx"].rearrange("l h i -> l (h i)").rearrange("l (j p) -> (l j) p", p=128)),
        ]
        PST = ar.get([5, 128])
        P.memset(PST, 0.0)
        r0 = 0
        for name, ap2 in plist:
            nr = ap2.shape[0]
            prow[name] = r0
            done = 0
            while done < nr:
                t, rr = divmod(r0 + done, 128)
                n = min(nr - done, 128 - rr)
                P.dma(PST[rr:rr + n, t, :], ap2[done:done + n, :], q="sp")
                done += n
            r0 += nr
        assert r0 <= 640
        for t in range(5):
            ps = smrot.next()
            P.tr(ps[:, 0:128], PST[:, t, :], ident_f)
            P.copy(PAR[:, t * 128:(t + 1) * 128], ps[:, 0:128], eng="act")

        def pcol(name, idx):
            c = prow[name] + idx
            return PAR[:, c:c + 1]

        P.dma(RB[:, 0:64], I["ssd_dt_bias"].rearrange("l h -> (l h)").partition_broadcast(128), q="sp")
        P.dma(RB[:, 64:128], I["ssd_a_log"].rearrange("l h -> (l h)").partition_broadcast(128), q="sp")
        P.dma(RB[:, 128:192], I["ssd_d"].rearrange("l h -> (l h)").partition_broadcast(128), q="sp")
        P.act(RB[:, 64:128], RB[:, 64:128], AF.Exp)
        P.ts(RB[:, 64:128], RB[:, 64:128], -1.0, ALU.mult)
        dview = RB[:, 128:192].rearrange("p (l hp two) -> p l hp two", l=DEPTH, two=2)
        P.copy(DCOL[0:64, :, :], dview[0:64, :, :, 0])
        P.copy(DCOL[64:128, :, :], dview[64:128, :, :, 1])
        lam0 = prow["lam"]
        P.act(C1[:], PAR[:, lam0:lam0 + 16], AF.Exp, scale=-1.0)
        P.act(C1[:], C1[:], AF.Ln, bias=1.0)
        P.ts(C1[:], C1[:], -LRU_C, ALU.mult)
        P.memset(WA[:], 0.0)
        P.memset(WX[:], 0.0)
        for mc in range(2):
            mt = ar.get([D])
            P.dma(mt, I["mem"][mc * 128:(mc + 1) * 128, :], q="sp")
            for k in range(KC):
                P.tr(psA[:, k * 128:(k + 1) * 128], mt[:, k * 128:(k + 1) * 128], ident_f)
            P.copy(memT[:, 0:4, mc * 128:(mc + 1) * 128], psA[:, 0:512].rearrange("p (k n) -> p k n", k=4), eng="act")
            P.copy(memT[:, 4:8, mc * 128:(mc + 1) * 128], psA[:, 512:1024].rearrange("p (k n) -> p k n", k=4), eng="dve")
        for tl in (pool_tail, sconv_tail, lconv_tail, lru_h):
            P.memset(tl[:], 0.0)
        if dbg:
            P.memset(hT[:], 0.0)

        def stream(groups):
            for g in groups:
                wb = wrot.next()
                for dst_fn, src in g["dmas"]:
                    P.dma(dst_fn(wb), src, q="pool")
                g["body"](wb)

        def wview(wb, koff, kc, MG):
            return wb[:, koff * MG:(koff + kc) * MG].rearrange("p (k m) -> p k m", k=kc)

        def linear(wsrcs, m_total, MG, tiles, rhs_fn, consume):
            KCt = sum(kc for _, kc in wsrcs)
            groups = []
            ng = (m_total + MG - 1) // MG
            for gi in range(ng):
                mg = min(MG, m_total - gi * MG)

                def body(wb, gi=gi, mg=mg):
                    wv = wview(wb, 0, KCt, mg)
                    for mc in range(mg // 128):
                        mglob = gi * (MG // 128) + mc
                        for ti, (n0, n1) in enumerate(tiles):
                            ps = mmrot.next()
                            for k in range(KCt):
                                P.mm(ps[:, 0:n1 - n0], wv[:, k, mc * 128:(mc + 1) * 128], rhs_fn(k, n0, n1),
                                     start=(k == 0), stop=(k == KCt - 1))
                            consume(mglob, ti, n0, n1, ps)

                dmas = []
                koff = 0
                for src, kc in wsrcs:
                    dmas.append((lambda wb, koff=koff, kc=kc, mg=mg: wview(wb, koff, kc, mg),
                                 src[:, gi * MG:gi * MG + mg].rearrange("(k p) m -> p k m", p=128)))
                    koff += kc
                groups.append(dict(dmas=dmas, body=body))
            stream(groups)

        def rmsnorm(gname, l, tiles):
            for (n0, n1) in tiles:
                n = n1 - n0
                ps = mmrot.next()
                for k in range(KC):
                    sq = sqrot.next()
                    P.act(sq[:, 0:n], hT[:, k, n0:n1], AF.Square)
                    P.mm(ps[:, 0:n], ones_b, sq[:, 0:n], start=(k == 0), stop=(k == KC - 1))
                P.act(rstd[:, 0:n], ps[:, 0:n], AF.Sqrt, scale=1.0 / D, bias=epsc[:, 0:1])
                P.recip(rstd[:, 0:n], rstd[:, 0:n])
                for k in range(KC):
                    P.stt(uT[:, k, n0:n1], hT[:, k, n0:n1], pcol(gname, l * KC + k), rstd[:, 0:n], ALU.mult, ALU.mult)

        def add_resid(m, ti, n0, n1, ps):
            P.tt(hT[:, m, n0:n1], hT[:, m, n0:n1], ps[:, 0:n1 - n0], ALU.add)

        epsc = sb("epsc", [128, 1])
        P.memset(epsc[:], EPS)

        def stage(name):
            P.tag = name
            if stop_at == name:
                raise _Stop()

        stage("setup")
        try:
          for pas in range(NPASS):
              t0 = pas * NPP
              has_s = pas == NPASS - 1
              first = pas == 0
              last = pas == NPASS - 1
              ncol = NPP + (NS if has_s else 0)
              tiles = [(i * 512, (i + 1) * 512) for i in range(NPP // 512)] + ([(NPP, NT)] if has_s else [])
              nchunk = NPP // 128

              ar.reset()
              xrot = Rot([ar.get([D]) for _ in range(2)])
              for blk in range(nchunk):
                  xt = xrot.next()
                  P.dma(xt, I["x_p"][t0 + blk * 128:t0 + (blk + 1) * 128, :], q="sp")
                  for k in range(KC):
                      P.tr(psA[:, k * 128:(k + 1) * 128], xt[:, k * 128:(k + 1) * 128], ident_f)
                  P.copy(hT[:, 0:4, blk * 128:(blk + 1) * 128], psA[:, 0:512].rearrange("p (k n) -> p k n", k=4), eng="act")
                  P.copy(hT[:, 4:8, blk * 128:(blk + 1) * 128], psA[:, 512:1024].rearrange("p (k n) -> p k n", k=4), eng="dve")
              if has_s:
                  xt = xrot.next()
                  P.dma(xt[0:64, :], I["x_s"], q="sp")
                  for k in range(KC):
                      P.tr(psA[:, k * 64:(k + 1) * 64], xt[0:64, k * 128:(k + 1) * 128], ident_f[0:64, 0:64])
                  P.copy(hT[:, :, NPP:NT], psA[:, 0:512].rearrange("p (k n) -> p k n", k=8), eng="act")
              stage("loadx")

              for l in range(n_layers):
                  ar.reset()
                  sqrot = Rot([ar.get([512], BF16) for _ in range(2)])
                  rstd = ar.get([512])
                  arena_base = ar.off
                  rmsnorm("nm", l, tiles)
                  stage("norm1")
                  P.dma(POOLW[:], I["pool_w"][l].rearrange("g c d -> c g d"), q="pool")
                  for h2 in range(2):
                      P.dma(WA[h2 * 64:(h2 + 1) * 64, :, h2 * 64:(h2 + 1) * 64],
                            I["lru_wa"][l].rearrange("(j two) i o -> two i j o", two=2)[h2], q="pool")
                      P.dma(WX[h2 * 64:(h2 + 1) * 64, :, h2 * 64:(h2 + 1) * 64],
                            I["lru_wx"][l].rearrange("(j two) i o -> two i j o", two=2)[h2], q="pool")
                  P.dma(WDT[:], I["w_in"][l][:, OFF_DT:OFF_DT + 16].rearrange("(k p) m -> p k m", p=128), q="pool")
                  w_in_l = I["w_in"][l]
                  w_out_l = I["w_out"][l]
                  rhs_u = lambda k, n0, n1: uT[:, k, n0:n1]
                  deferred = []

                  def flush():
                      while deferred:
                          deferred.pop(0)()

                  if has_s:
                      SPin = ar.get([2, 512], parts=120)
                      P.dma(SPin, I["st_pool"][l].rearrange("(h b) r c -> (b r) h c", h=2), q="sp")
                      SPout = ar.get([2, 512], parts=120)
                      LCin = ar.get([512], parts=48)
                      P.dma(LCin, I["st_lconv"][l].rearrange("b r c -> (b r) c"), q="sp")
                      LCout = ar.get([512], parts=48)
                      LRin = ar.get([512], parts=16)
                      P.dma(LRin, I["st_lru"][l], q="sp")
                      LRout = ar.get([512], parts=16)
                      lru_h0 = ar.get([4, 16])
                  sbase = ar.off

                  RAWP = 15 + NPP + 16 * 19 + 2
                  rawrot = Rot([ar.get([RAWP]) for _ in range(2)])
                  tA = ar.get([RAWP])
                  tB = ar.get([RAWP])
                  plrot = Rot([ar.get([NT], BF16) for _ in range(2)])
                  tmp15 = ar.get([16])
                  tmp240 = ar.get([240])
                  cur_raw = [None]
                  SOFFP = 15 + NPP
                  ya = A[:, 0:4, :]

                  def svw(t, off, nb, w):
                      return t[:, off:off + nb * w].rearrange("p (b r) -> p b r", b=nb)

                  def pool_consume(j, ti, n0, n1, ps):
                      if ti == 0:
                          cur_raw[0] = rawrot.next()
                          raw = cur_raw[0]
                          P.copy(raw[:, 0:15], pool_tail[:, l, j, :], eng="dve")
                          if has_s:
                              pp = smrot.next()
                              for h in range(2):
                                  P.tr(pp[:, h * 120:(h + 1) * 120], SPin[:, h, j * 128:(j + 1) * 128], ident_f[0:120, 0:120])
                              P.copy(svw(raw, SOFFP, 16, 19)[:, :, 0:15], pp[:, 0:240].rearrange("p (b r) -> p b r", b=16), eng="dve")
                      raw = cur_raw[0]
                      if n0 < NPP:
                          P.copy(raw[:, 15 + n0:15 + n1], ps[:, 0:n1 - n0], eng="act")
                      else:
                          P.copy(svw(raw, SOFFP, 16, 19)[:, :, 15:19], ps[:, 0:64].rearrange("p (b r) -> p b r", b=16), eng="act")
                      if ti != len(tiles) - 1:
                          return
                      flush()
                      w = 2 << j
                      Wd = 15 + NPP
                      cur = raw
                      bufs = [tA, tB]
                      for lev in range(j + 1):
                          sh = 1 << lev
                          lo = (2 << lev) - 1
                          dst = bufs[lev % 2]
                          P.tt(dst[:, lo:Wd], cur[:, lo:Wd], cur[:, lo - sh:Wd - sh], ALU.add)
                          if has_s:
                              P.tt(svw(dst, SOFFP, 16, 19)[:, :, lo:19], svw(cur, SOFFP, 16, 19)[:, :, lo:19],
                                   svw(cur, SOFFP, 16, 19)[:, :, lo - sh:19 - sh], ALU.add)
                          cur = dst
                      pl = plrot.next()
                      P.stt(pl[:, 0:NPP], cur[:, 15:15 + NPP], 1.0 / w, raw[:, 15:15 + NPP], ALU.mult, ALU.subtract)
                      if first:
                          P.tt(tmp15[:, 0:15], cur[:, 15:30], rc_tab[:, j, 0:15], ALU.mult)
                          P.tt(pl[:, 0:15], tmp15[:, 0:15], raw[:, 15:30], ALU.subtract)
                      if has_s:
                          P.stt(pl[:, NPP:NT].rearrange("p (b r) -> p b r", b=16), svw(cur, SOFFP, 16, 19)[:, :, 15:19], 1.0 / w,
                                svw(raw, SOFFP, 16, 19)[:, :, 15:19], ALU.mult, ALU.subtract)
                      P.copy(pool_tail[:, l, j, :], raw[:, NPP:NPP + 15], eng="dve")
                      if has_s:
                          P.copy(tmp240.rearrange("p (b r) -> p b r", b=16), svw(raw, SOFFP, 16, 19)[:, :, 4:19], eng="dve")

                      def pe_part(j=j, pl=pl):
                          for (m0, m1) in tiles:
                              ps2 = mmrot.next()
                              P.mm(ps2[:, 0:m1 - m0], POOLW[:, j, :], pl[:, m0:m1])
                              P.act(ya[:, j, m0:m1], ps2[:, 0:m1 - m0], AF.Identity, scale=pcol("pscale", l * 4 + j))
                          if has_s:
                              pp = smrot.next()
                              for h in range(2):
                                  P.tr(pp[0:120, h * 128:(h + 1) * 128], tmp240[:, h * 120:(h + 1) * 120], ident_f)
                              P.copy(SPout[:, :, j * 128:(j + 1) * 128], pp[0:120, 0:256].rearrange("p (h c) -> p h c", h=2), eng="act")

                      deferred.append(pe_part)

                  linear([(w_in_l[:, OFF_POOL:OFF_POOL + 512], KC)], 512, 512, tiles, rhs_u, pool_consume)
                  flush()
                  stage("pool")
                  if has_s:
                      P.dma(O["s_pool"][l].rearrange("(h b) r c -> (b r) h c", h=2), SPout, q="sp", is_output=True)

                  ar.off = sbase
                  RAWL = 3 + NPP + 16 * 7 + 1
                  SOFFL = 3 + NPP
                  rawrot = Rot([ar.get([RAWL]) for _ in range(2)])
                  acc = ar.get([NT])
                  xcb = ar.get([NT], BF16)
                  rr_ = ar.get([NT])
                  ii_ = ar.get([NT])
                  mm_ = ar.get([NT])
                  tmp48 = ar.get([48])
                  tmp16 = ar.get([16])
                  gel = A[:, 8:12, :]
                  yc = A[:, 4:8, :]
                  if has_s:
                      pp = smrot.next()
                      for j in range(4):
                          P.tr(pp[:, j * 16:(j + 1) * 16], LRin[:, j * 128:(j + 1) * 128], ident_f[0:16, 0:16])
                      P.copy(lru_h0, pp[:, 0:64].rearrange("p (j b) -> p j b", j=4), eng="act")

                  def lru_consume(m, ti, n0, n1, ps):
                      n = n1 - n0
                      if m < 4:
                          P.act(gel[:, m, n0:n1], ps[:, 0:n], AF.Gelu_apprx_tanh)
                          return
                      j = m - 4
                      if ti == 0:
                          cur_raw[0] = rawrot.next()
                          raw = cur_raw[0]
                          P.copy(raw[:, 0:3], lconv_tail[:, l, j, :], eng="dve")
                          if has_s:
                              pp = smrot.next()
                              P.tr(pp[:, 0:48], LCin[:, j * 128:(j + 1) * 128], ident_f[0:48, 0:48])
                              P.copy(svw(raw, SOFFL, 16, 7)[:, :, 0:3], pp[:, 0:48].rearrange("p (b r) -> p b r", b=16), eng="dve")
                      raw = cur_raw[0]
                      if n0 < NPP:
                          P.copy(raw[:, 3 + n0:3 + n1], ps[:, 0:n], eng="act")
                      else:
                          P.copy(svw(raw, SOFFL, 16, 7)[:, :, 3:7], ps[:, 0:64].rearrange("p (b r) -> p b r", b=16), eng="act")
                      if ti != len(tiles) - 1:
                          return
                      flush()
                      for k in range(4):
                          wk = pcol("lcw", (l * 4 + k) * 4 + j)
                          if k == 0:
                              P.ts(acc[:, 0:NPP], raw[:, 0:NPP], wk, ALU.mult, pcol("lcb", l * 4 + j), ALU.add)
                              if has_s:
                                  P.ts(acc[:, NPP:NT].rearrange("p (b r) -> p b r", b=16), svw(raw, SOFFL, 16, 7)[:, :, 0:4], wk, ALU.mult,
                                       pcol("lcb", l * 4 + j), ALU.add)
                          else:
                              P.stt(acc[:, 0:NPP], raw[:, k:k + NPP], wk, acc[:, 0:NPP], ALU.mult, ALU.add)
                              if has_s:
                                  av = acc[:, NPP:NT].rearrange("p (b r) -> p b r", b=16)
                                  P.stt(av, svw(raw, SOFFL, 16, 7)[:, :, k:k + 4], wk, av, ALU.mult, ALU.add)
                      P.copy(lconv_tail[:, l, j, :], raw[:, NPP:NPP + 3], eng="dve")
                      if has_s:
                          P.copy(tmp48.rearrange("p (b r) -> p b r", b=16), svw(raw, SOFFL, 16, 7)[:, :, 4:7], eng="dve")
                      P.copy(xcb[:, 0:ncol], acc[:, 0:ncol], eng="act")

                      def pe_part(j=j):
                          for (m0, m1) in tiles:
                              psr = mmrot.next()
                              P.mm(psr[:, 0:m1 - m0], WA[:, j, :], xcb[:, m0:m1])
                              P.act(rr_[:, m0:m1], psr[:, 0:m1 - m0], AF.Sigmoid, bias=pcol("ba", l * 4 + j))
                              psi = mmrot.next()
                              P.mm(psi[:, 0:m1 - m0], WX[:, j, :], xcb[:, m0:m1])
                              P.act(ii_[:, m0:m1], psi[:, 0:m1 - m0], AF.Sigmoid, bias=pcol("bx", l * 4 + j))
                          if has_s:
                              pp = smrot.next()
                              P.tr(pp[0:48, 0:128], tmp48[:, 0:48], ident_f)
                              P.copy(LCout[:, j * 128:(j + 1) * 128], pp[0:48, 0:128], eng="act")
                          nn = ncol
                          P.act(rr_[:, 0:nn], rr_[:, 0:nn], AF.Exp, scale=C1[:, l * 4 + j:l * 4 + j + 1])
                          P.tt(mm_[:, 0:nn], rr_[:, 0:nn], rr_[:, 0:nn], ALU.mult)
                          P.act(mm_[:, 0:nn], mm_[:, 0:nn], AF.Sqrt, scale=-1.0, bias=1.0)
                          if first:
                              P.memset(mm_[:, 0:1], 1.0)
                          P.tt(ii_[:, 0:nn], ii_[:, 0:nn], mm_[:, 0:nn], ALU.mult)
                          P.tt(ii_[:, 0:nn], ii_[:, 0:nn], acc[:, 0:nn], ALU.mult)
                          if has_s:
                              a_s = rr_[:, NPP:NT].rearrange("p (b r) -> p b r", b=16)
                              b_s = ii_[:, NPP:NT].rearrange("p (b r) -> p b r", b=16)
                              t16 = tmp16.rearrange("p (b o) -> p b o", o=1)
                              P.tt(t16, a_s[:, :, 0:1], lru_h0[:, j, :].rearrange("p (b o) -> p b o", o=1), ALU.mult)
                              P.tt(b_s[:, :, 0:1], b_s[:, :, 0:1], t16, ALU.add)
                              P.memset(a_s[:, :, 0:1], 0.0)
                          P.scan(mm_[:, 0:NPP], rr_[:, 0:NPP], ii_[:, 0:NPP], lru_h[:, l, j:j + 1], ALU.mult, ALU.add)
                          if has_s:
                              P.scan(mm_[:, NPP:NT], rr_[:, NPP:NT], ii_[:, NPP:NT], 0.0, ALU.mult, ALU.add)
                          P.copy(lru_h[:, l, j:j + 1], mm_[:, NPP - 1:NPP], eng="dve")
                          if has_s:
                              P.copy(tmp16.rearrange("p (b o) -> p b o", o=1), mm_[:, NPP:NT].rearrange("p (b r) -> p b r", b=16)[:, :, 3:4], eng="dve")
                              pp = smrot.next()
                              P.tr(pp[0:16, 0:128], tmp16[:, 0:16], ident_f)
                              P.copy(LRout[:, j * 128:(j + 1) * 128], pp[0:16, 0:128], eng="act")
                          P.tt(yc[:, j, 0:nn], mm_[:, 0:nn], gel[:, j, 0:nn], ALU.mult)

                      deferred.append(pe_part)

                  linear([(w_in_l[:, OFF_GATE:OFF_GATE + 1024], KC)], 1024, 512, tiles, rhs_u, lru_consume)
                  flush()
                  stage("lru")
                  if has_s:
                      P.dma(O["s_lconv"][l].rearrange("b r c -> (b r) c"), LCout, q="sp", is_output=True)
                      P.dma(O["s_lru"][l], LRout, q="sp", is_output=True)

                  zs = A[:, 12:20, :]

                  def z_consume(m, ti, n0, n1, ps):
                      P.act(zs[:, m, n0:n1], ps[:, 0:n1 - n0], AF.Silu)

                  linear([(w_in_l[:, OFF_Z:OFF_Z + 1024], KC)], 1024, 512, tiles, rhs_u, z_consume)

                  linear([(w_out_l[0:512, :], 4), (w_out_l[1536:2048, :], 4)], D, 512, tiles,
                         lambda k, n0, n1: A[:, k, n0:n1], add_resid)
                  stage("wout1")

                  ar.off = arena_base
                  xbc = A[:, 0:12, :]
                  zs = A[:, 12:20, :]
                  dtb = RB[:, l * 16:(l + 1) * 16]
                  Abc = RB[:, 64 + l * 16:64 + (l + 1) * 16]
                  dt_all = ar.get([nchunk + 1, 16])
                  dA_all = ar.get([nchunk + 1, 16])
                  dAh = ar.get([nchunk + 1, 16], BF16)
                  dAl = ar.get([nchunk + 1, 16], BF16)
                  dAt = ar.get([nchunk + 1, 16])
                  acs_all = ar.get([nchunk, 16])
                  decst_all = ar.get([nchunk, 16])
                  eacs_all = ar.get([nchunk, 16])
                  cd_all = ar.get([nchunk, 16])
                  dtdec_all = ar.get([nchunk, 16])
                  cc_s = dict(acs_tok=ar.get([16]), decst=ar.get([16]), eacs=ar.get([16]), cdb=ar.get([16]), dtdec=ar.get([16]))
                  Rhr = Rot([ar.get([8, 128], BF16) for _ in range(3)])
                  Rlr = Rot([ar.get([8, 128], BF16) for _ in range(3)])
                  Er = Rot([ar.get([8, 128], BF16) for _ in range(2)])
                  xdtr = Rot([ar.get([512], BF16) for _ in range(2)])
                  xddr = Rot([ar.get([512], BF16) for _ in range(2)])
                  yofr = Rot([ar.get([512], BF16) for _ in range(2)])
                  Btr = Rot([ar.get([128], BF16) for _ in range(2)])
                  CBr = Rot([ar.get([128], BF16) for _ in range(2)])

                  sbase3 = ar.off
                  if has_s:
                      SCin = ar.get([1536], parts=48)
                      P.dma(SCin, I["st_sconv"][l].rearrange("b r c -> (b r) c"), q="sp")
                      SCout = ar.get([1536], parts=48)
                  sbase2 = ar.off
                  RAWS = 3 + NPP + 16 * 7 + 1
                  rawrot = Rot([ar.get([RAWS]) for _ in range(2)])
                  accr = Rot([ar.get([NT]) for _ in range(2)])
                  tmp48 = ar.get([48])

                  def xbc_consume(j, ti, n0, n1, ps):
                      n = n1 - n0
                      if ti == 0:
                          cur_raw[0] = rawrot.next()
                          raw = cur_raw[0]
                          P.copy(raw[:, 0:3], sconv_tail[:, l, j, :], eng="dve")
                          if has_s:
                              pp = smrot.next()
                              P.tr(pp[:, 0:48], SCin[:, j * 128:(j + 1) * 128], ident_f[0:48, 0:48])
                              P.copy(svw(raw, SOFFL, 16, 7)[:, :, 0:3], pp[:, 0:48].rearrange("p (b r) -> p b r", b=16), eng="dve")
                      raw = cur_raw[0]
                      if n0 < NPP:
                          P.copy(raw[:, 3 + n0:3 + n1], ps[:, 0:n], eng="act")
                      else:
                          P.copy(svw(raw, SOFFL, 16, 7)[:, :, 3:7], ps[:, 0:64].rearrange("p (b r) -> p b r", b=16), eng="act")
                      if ti != len(tiles) - 1:
                          return
                      flush()
                      acc = accr.next()
                      for k in range(4):
                          wk = pcol("scw", (l * 4 + k) * 12 + j)
                          if k == 0:
                              P.ts(acc[:, 0:NPP], raw[:, 0:NPP], wk, ALU.mult, pcol("scb", l * 12 + j), ALU.add)
                              if has_s:
                                  P.ts(acc[:, NPP:NT].rearrange("p (b r) -> p b r", b=16), svw(raw, SOFFL, 16, 7)[:, :, 0:4], wk, ALU.mult,
                                       pcol("scb", l * 12 + j), ALU.add)
                          else:
                              P.stt(acc[:, 0:NPP], raw[:, k:k + NPP], wk, acc[:, 0:NPP], ALU.mult, ALU.add)
                              if has_s:
                                  av = acc[:, NPP:NT].rearrange("p (b r) -> p b r", b=16)
                                  P.stt(av, svw(raw, SOFFL, 16, 7)[:, :, k:k + 4], wk, av, ALU.mult, ALU.add)
                      P.copy(sconv_tail[:, l, j, :], raw[:, NPP:NPP + 3], eng="dve")
                      deferred.append(lambda j=j, acc=acc: P.act(xbc[:, j, 0:ncol], acc[:, 0:ncol], AF.Silu))
                      if has_s:
                          P.copy(tmp48.rearrange("p (b r) -> p b r", b=16), svw(raw, SOFFL, 16, 7)[:, :, 4:7], eng="dve")
                          pp = smrot.next()
                          P.tr(pp[0:48, 0:128], tmp48[:, 0:48], ident_f)
                          P.copy(SCout[:, j * 128:(j + 1) * 128], pp[0:48, 0:128], eng="act")

                  linear([(w_in_l[:, OFF_XBC:OFF_XBC + 1536], KC)], 1536, 512, tiles, rhs_u, xbc_consume)
                  flush()
                  if has_s:
                      P.dma(O["s_sconv"][l].rearrange("b r c -> (b r) c"), SCout, q="sp", is_output=True)

                  stage("xbcz")

                  psd = smrot.next()
                  for c in range(nchunk):
                      for k in range(KC):
                          P.mm(psd[:, c * 16:(c + 1) * 16], uT[:, k, c * 128:(c + 1) * 128], WDT[:, k, :], start=(k == 0), stop=(k == KC - 1))
                  P.tt(dt_all[:, 0:nchunk, :], psd[:, 0:nchunk * 16].rearrange("p (c h) -> p c h", h=16),
                       bcast(dtb.rearrange("p (o h) -> p o h", o=1), [128, nchunk, 16]), ALU.add)
                  if has_s:
                      psd2 = smrot.next()
                      for k in range(KC):
                          P.mm(psd2[0:64, 0:16], uT[:, k, NPP:NT], WDT[:, k, :], start=(k == 0), stop=(k == KC - 1))
                      P.memset(dt_all[:, nchunk, :], 0.0)
                      P.tt(dt_all[0:64, nchunk, :], psd2[0:64, 0:16], dtb[0:64, :], ALU.add)
                  nch_all = nchunk + (1 if has_s else 0)
                  P.act(dt_all[:, 0:nch_all, :], dt_all[:, 0:nch_all, :], AF.Exp)
                  P.act(dt_all[:, 0:nch_all, :], dt_all[:, 0:nch_all, :], AF.Ln, bias=1.0)
                  P.tt(dA_all[:, 0:nch_all, :], dt_all[:, 0:nch_all, :], bcast(Abc.rearrange("p (o h) -> p o h", o=1), [128, nch_all, 16]), ALU.mult)
                  P.copy(dAh[:, 0:nch_all, :], dA_all[:, 0:nch_all, :], eng="dve")
                  P.copy(dAt[:, 0:nch_all, :], dAh[:, 0:nch_all, :], eng="dve")
                  P.tt(dAl[:, 0:nch_all, :], dA_all[:, 0:nch_all, :], dAt[:, 0:nch_all, :], ALU.subtract)
                  for hp in range(8):
                      P.ts(DIAGD[:, hp, :], ident_b, DCOL[:, l, hp:hp + 1], ALU.mult)

                  if first:
                      P.memset(ssdT[:], 0.0)
                      P.memset(ssdT_b[:], 0.0)
                  else:
                      P.dma(ssdT[:], spill[l], q="sp")
                      P.copy(ssdT_b[:], ssdT[:], eng="act")
                  stage("dt")

                  def chunk_common(c, cols, K):
                      L = K["L"]
                      CC = cc_s
                      acs_tok, decst, eacs, cdb, dtdec = CC["acs_tok"], CC["decst"], CC["eacs"], CC["cdb"], CC["dtdec"]
                      dA_c = dA_all[0:L, c, :]
                      ps1 = smrot.next()
                      P.mm(ps1[0:L, 0:16], K["U"], dA_c)
                      P.copy(acs_tok[0:L, :], ps1[0:L, 0:16], eng="dve")
                      P.mm(ps1[0:L, 16:32], K["sel"], acs_tok[0:L, :])
                      P.tt(decst[0:L, :], ps1[0:L, 16:32], acs_tok[0:L, :], ALU.subtract)
                      P.act(decst[0:L, :], decst[0:L, :], AF.Exp)
                      P.act(eacs[0:L, :], acs_tok[0:L, :], AF.Exp)
                      P.tt(dtdec[0:L, :], dt_all[0:L, c, :], decst[0:L, :], ALU.mult)
                      return CC

                  def common_all():
                      nq = nchunk * 16
                      fl = lambda t: t.rearrange("p c h -> p (c h)")
                      ps = smrot.next()
                      P.mm(ps[:, 0:nq], CP["U"], fl(dA_all[:, 0:nchunk, :]))
                      P.copy(fl(acs_all), ps[:, 0:nq], eng="dve")
                      ps2 = smrot.next()
                      P.mm(ps2[:, 0:nq], CP["sel"], fl(acs_all))
                      P.tt(fl(decst_all), ps2[:, 0:nq], fl(acs_all), ALU.subtract)
                      P.act(fl(decst_all), fl(decst_all), AF.Exp)
                      P.act(fl(eacs_all), fl(acs_all), AF.Exp)
                      ps3 = smrot.next()
                      P.mm(ps3[:, 0:nq], ones_f, fl(dA_all[:, 0:nchunk, :]))
                      P.copy(fl(cd_all), ps3[:, 0:nq], eng="dve")
                      P.act(fl(cd_all), fl(cd_all), AF.Exp)
                      P.tt(fl(dtdec_all), fl(dt_all[:, 0:nchunk, :]), fl(decst_all), ALU.mult)

                  def cc_of(c):
                      return dict(acs_tok=acs_all[:, c, :], decst=decst_all[:, c, :], eacs=eacs_all[:, c, :], cdb=cd_all[:, c, :],
                                  dtdec=dtdec_all[:, c, :])

                  def make_r(c, g, K):
                      L = K["L"]
                      Rh = Rhr.next()
                      Rl = Rlr.next()
                      Ub = bcast(K["Ub"].rearrange("p (o l) -> p o l", o=1), [L, 8, L])
                      P.tt(Rh[0:L, :, 0:L], Ub, bcast(dAh[0:L, c, 8 * g:8 * g + 8].rearrange("p (h o) -> p h o", o=1), [L, 8, L]), ALU.mult)
                      P.tt(Rl[0:L, :, 0:L], Ub, bcast(dAl[0:L, c, 8 * g:8 * g + 8].rearrange("p (h o) -> p h o", o=1), [L, 8, L]), ALU.mult, eng="pool")
                      return Rh, Rl

                  def unit_pre(c, g, cols, K, CC, RR):
                      L = K["L"]
                      dtdec = CC["dtdec"]
                      c0, c1 = cols
                      Rh, Rl = RR
                      pxt = smrot.next()
                      pxb = pxt[:, 0:256].bitcast(BF16)
                      for j in range(4):
                          P.tr(pxb[0:L, j * 128:(j + 1) * 128], xbc[:, 4 * g + j, c0:c1], ident_b)
                      xdt = xdtr.next()
                      xdd = xddr.next()
                      pv = pxb[0:L, :].rearrange("p (h q) -> p h q", h=8)
                      P.tt(xdt[0:L, :].rearrange("p (h q) -> p h q", h=8), pv,
                           bcast(dt_all[0:L, c, 8 * g:8 * g + 8].rearrange("p (h o) -> p h o", o=1), [L, 8, 64]), ALU.mult)
                      P.tt(xdd[0:L, :].rearrange("p (h q) -> p h q", h=8), pv,
                           bcast(dtdec[0:L, 8 * g:8 * g + 8].rearrange("p (h o) -> p h o", o=1), [L, 8, 64]), ALU.mult)
                      pbt = smrot.next()
                      pbb = pbt[:, 0:256].bitcast(BF16)
                      P.tr(pbb[0:L, 0:128], xbc[:, 8 + g, c0:c1], ident_b)
                      Bt = Btr.next()
                      P.copy(Bt[0:L, :], pbb[0:L, 0:128], eng="act")
                      E = Er.next()
                      nh = 512 // L
                      for half in range(8 // nh):
                          seg = psA[0:L, half * 512:half * 512 + nh * L].rearrange("p (h l) -> p h l", h=nh)
                          hs = slice(half * nh, (half + 1) * nh)
                          dh = bcast(dAh[0:L, c, 8 * g:8 * g + 8][:, hs].rearrange("p (h o) -> p h o", o=1), [L, nh, L])
                          dl = bcast(dAl[0:L, c, 8 * g:8 * g + 8][:, hs].rearrange("p (h o) -> p h o", o=1), [L, nh, L])
                          P.mm(seg, ones_b[0:L, 0:L], Rh[0:L, hs, 0:L], start=True, stop=False)
                          P.mm(seg, ones_b[0:L, 0:L], Rl[0:L, hs, 0:L], start=False, stop=False)
                          P.mm(seg, K["negUb"], dh, start=False, stop=False)
                          P.mm(seg, K["negUb"], dl, start=False, stop=False)
                          P.mm(seg, ident_b[0:L, 0:L], bcast(K["negm"].rearrange("p (o l) -> p o l", o=1), [L, nh, L]), start=False, stop=True)
                          P.act(E[0:L, hs, 0:L], seg, AF.Exp)
                      pcb = smrot.next()
                      P.mm(pcb[0:L, 0:L], xbc[:, 8 + g, c0:c1], xbc[:, 10 + g, c0:c1])
                      CBs = CBr.next()
                      P.copy(CBs[0:L, 0:L], pcb[0:L, 0:L], eng="act")
                      return dict(xdt=xdt, xdd=xdd, Bt=Bt, M=E, E=E, CBs=CBs, CC=CC)

                  def unit_m(H, K):
                      L = K["L"]
                      P.tt(H["M"][0:L, :, 0:L], H["E"][0:L, :, 0:L], bcast(H["CBs"][0:L, 0:L].rearrange("p (o l) -> p o l", o=1), [L, 8, L]),
                           ALU.mult, eng="pool")

                  def unit_y(c, g, cols, K, H, pyo):
                      L = K["L"]
                      c0, c1 = cols
                      eacs = H["CC"]["eacs"]
                      yof = yofr.next()
                      P.tt(yof[0:L, :].rearrange("p (h q) -> p h q", h=8), pyo[0:L, 0:512].rearrange("p (h q) -> p h q", h=8),
                           bcast(eacs[0:L, 8 * g:8 * g + 8].rearrange("p (h o) -> p h o", o=1), [L, 8, 64]), ALU.mult)
                      Y = psY[:, 0:4 * L].rearrange("p (j l) -> p j l", j=4)
                      for k in range(8):
                          out = Y[64 * (k % 2):64 * (k % 2) + 64, k // 2, :]
                          hp = 4 * g + k // 2
                          P.mm(out, H["xdt"][0:L, 64 * k:64 * k + 64], H["M"][0:L, k, 0:L], start=True, stop=False)
                          P.mm(out, yof[0:L, 64 * k:64 * k + 64], ident_b[0:L, 0:L], start=False, stop=False)
                          P.mm(out, DIAGD[:, hp, 64 * (k % 2):64 * (k % 2) + 64], xbc[:, hp, c0:c1], start=False, stop=True)
                      P.copy(xbc[:, 4 * g:4 * g + 4, c0:c1], Y, eng="act")

                  units = [(c, g) for c in range(nchunk) for g in range(2)]
                  nu = len(units)
                  HH = {}
                  RT = {}
                  common_all()

                  def pre_r(u):
                      c, g = units[u]
                      RT[u] = make_r(c, g, CP)

                  def pre_a(u):
                      c, g = units[u]
                      cols = (c * 128, (c + 1) * 128)
                      HH[u] = unit_pre(c, g, cols, CP, cc_of(c), RT.pop(u))

                  def y_part(u):
                      c, g = units[u]
                      cols = (c * 128, (c + 1) * 128)
                      H = HH.pop(u)
                      cdb = H["CC"]["cdb"]
                      pyo = mmrot.next()
                      P.mm(pyo[:, 0:512], xbc[:, 10 + g, cols[0]:cols[1]], ssdT_b[:, 512 * g:512 * (g + 1)])
                      unit_y(c, g, cols, CP, H, pyo)
                      pdl = mmrot.next()
                      P.mm(pdl[:, 0:512], H["Bt"][:, :], H["xdd"][:, :])
                      sv = ssdT[:, 512 * g:512 * (g + 1)].rearrange("p (h q) -> p h q", h=8)
                      P.tt(sv, sv, bcast(cdb[:, 8 * g:8 * g + 8].rearrange("p (h o) -> p h o", o=1), [128, 8, 64]), ALU.mult)
                      P.tt(ssdT[:, 512 * g:512 * (g + 1)], ssdT[:, 512 * g:512 * (g + 1)], pdl[:, 0:512], ALU.add)
                      P.copy(ssdT_b[:, 512 * g:512 * (g + 1)], ssdT[:, 512 * g:512 * (g + 1)], eng="act")

                  pre_r(0)
                  if nu > 1:
                      pre_r(1)
                  pre_a(0)
                  unit_m(HH[0], CP)
                  for u in range(nu):
                      if u + 2 < nu:
                          pre_r(u + 2)
                      if u + 1 < nu:
                          pre_a(u + 1)
                      y_part(u)
                      if u + 1 < nu:
                          unit_m(HH[u + 1], CP)
                      stage("scan1")
                  if not last:
                      P.dma(spill[l], ssdT[:], q="sp")
                      stage("scanp")
                  else:
                      ar.off = sbase3
                      for half in range(2):
                          for j in range(4):
                              P.tr(psA[:, j * 128:(j + 1) * 128], ssdT[:, (half * 4 + j) * 128:(half * 4 + j + 1) * 128], ident_f)
                          so = ar.get([4, 128])
                          P.copy(so, psA[:, 0:512].rearrange("p (j n) -> p j n", j=4), eng="act")
                          P.dma(O["p_ssd"][l].rearrange("(hp two) q n -> (two q) hp n", two=2)[:, half * 4:(half + 1) * 4, :], so, q="sp", is_output=True)

                  if has_s:
                      ar.off = sbase3
                      c = nchunk
                      cols = (NPP, NT)
                      CCs = chunk_common(c, cols, CS)
                      Hs = [unit_pre(c, g, cols, CS, CCs, make_r(c, g, CS)) for g in range(2)]
                      for g in range(2):
                          unit_m(Hs[g], CS)
                      rblk = ar.get([16, 16], parts=64)
                      P.tt(rblk, bcast(dA_all[0:64, c, :].rearrange("p (o h) -> p o h", o=1), [64, 16, 16]),
                           bcast(blockind_f.rearrange("p (b o) -> p b o", o=1), [64, 16, 16]), ALU.mult)
                      pda = smrot.next()
                      P.mm(pda[:, 0:256], ones_f[0:64, :], rblk.rearrange("p b h -> p (b h)"))
                      dec_all = ar.get([16, 16])
                      P.act(dec_all.rearrange("p b h -> p (b h)"), pda[:, 0:256], AF.Exp)
                      Cm = []
                      Bblk = []
                      for g in range(2):
                          cm = ar.get([16, 64], BF16)
                          P.tt(cm, bcast(xbc[:, 10 + g, NPP:NT].rearrange("p (o l) -> p o l", o=1), [128, 16, 64]), BMASK[:], ALU.mult)
                          Cm.append(cm)
                          bb = ar.get([16, 128], BF16, parts=64)
                          P.tt(bb, bcast(Hs[g]["Bt"][0:64, :].rearrange("p (o n) -> p o n", o=1), [64, 16, 128]),
                               bcast(blockind_b.rearrange("p (b o) -> p b o", o=1), [64, 16, 128]), ALU.mult)
                          Bblk.append(bb)
                      S0r = Rot([ar.get([8, 128]) for _ in range(2)])
                      h0Tr = Rot([ar.get([D], BF16) for _ in range(2)])
                      pyo = [mmrot.next(), mmrot.next()]
                      for b in range(NSQ):
                          S0 = S0r.next()
                          P.dma(S0, I["st_ssd"][l, b].rearrange("(hp two) q n -> (two q) hp n", two=2), q="sp")
                          h0T = h0Tr.next()
                          for half in range(2):
                              for j in range(4):
                                  P.tr(psA[:, j * 128:(j + 1) * 128] if half == 0 else psA[:, 512 + j * 128:512 + (j + 1) * 128],
                                       S0[:, half * 4 + j, :], ident_f)
                          P.copy(h0T[:, 0:512], psA[:, 0:512], eng="act")
                          P.copy(h0T[:, 512:1024], psA[:, 512:1024], eng="dve")
                          for g in range(2):
                              P.mm(pyo[g][0:64, 0:512], Cm[g][:, b, :], h0T[:, 512 * g:512 * (g + 1)], start=(b == 0), stop=(b == NSQ - 1))
                          hn = S0
                          for h2 in range(2):
                              dsl = dec_all[h2 * 64:(h2 + 1) * 64, b, :].rearrange("p (hp two) -> p hp two", two=2)[:, :, h2:h2 + 1]
                              P.tt(hn[h2 * 64:(h2 + 1) * 64, :, :], S0[h2 * 64:(h2 + 1) * 64, :, :], bcast(dsl, [64, 8, 128]), ALU.mult)
                          for half in range(2):
                              pdl = smrot.next()
                              for j in range(4):
                                  hp = half * 4 + j
                                  g = hp // 4
                                  P.mm(pdl[:, j * 128:(j + 1) * 128], Hs[g]["xdd"][0:64, (hp % 4) * 128:(hp % 4 + 1) * 128], Bblk[g][:, b, :])
                              P.tt(hn[:, half * 4:(half + 1) * 4, :], hn[:, half * 4:(half + 1) * 4, :],
                                   pdl[:, 0:512].rearrange("p (j n) -> p j n", j=4), ALU.add)
                          P.dma(O["s_ssd"][l, b].rearrange("(hp two) q n -> (two q) hp n", two=2), hn, q="sp", is_output=True)
                      for g in range(2):
                          unit_y(c, g, cols, CS, Hs[g], pyo[g])

                  ar.off = arena_base
                  yg = A[:, 0:8, :]
                  for (n0, n1) in tiles:
                      n = n1 - n0
                      ps = mmrot.next()
                      for k in range(KC):
                          P.tt(yg[:, k, n0:n1], yg[:, k, n0:n1], zs[:, k, n0:n1], ALU.mult)
                          sq = sqrot.next()
                          P.act(sq[:, 0:n], yg[:, k, n0:n1], AF.Square)
                          P.mm(ps[:, 0:n], ones_b, sq[:, 0:n], start=(k == 0), stop=(k == KC - 1))
                      P.act(rstd[:, 0:n], ps[:, 0:n], AF.Sqrt, scale=1.0 / D, bias=epsc[:, 0:1])
                      P.recip(rstd[:, 0:n], rstd[:, 0:n])
                      for k in range(KC):
                          P.stt(yg[:, k, n0:n1], yg[:, k, n0:n1], pcol("snorm", l * KC + k), rstd[:, 0:n], ALU.mult, ALU.mult)
                  linear([(w_out_l[512:1536, :], 8)], D, 512, tiles, lambda k, n0, n1: A[:, k, n0:n1], add_resid)
                  stage("gate")
                  if dbg:
                      P.dma(O["dbg_h"][pas, l, 0], hT[:].rearrange("p k n -> p (k n)"), q="sp", is_output=True)

                  ar.off = arena_base
                  rmsnorm("nmem", l, tiles)
                  stage("norm2")
                  qT = A[:, 0:8, :]
                  oT = A[:, 8:16, :]
                  kvst = Rot([ar.get([512]) for _ in range(2)])
                  KT = ar.get([KC, 256], BF16)
                  VT = ar.get([2, D], BF16)

                  def kt_consume(m, ti, n0, n1, ps):
                      P.copy(KT[:, m, :], ps[:, 0:256], eng="act")

                  linear([(I["w_mem_k"][l], KC)], D, 512, [(0, 256)], lambda k, n0, n1: memT[:, k, n0:n1], kt_consume)
                  stage("kt")

                  def tokmajor_proj(wname, oname, keep):
                      groups = []
                      for gi in range(2):
                          def body(wb, gi=gi):
                              wv = wview(wb, 0, KC, 512)
                              for mc in range(2):
                                  ps = mmrot.next()
                                  for k in range(KC):
                                      P.mm(ps[:, 0:512], memT[:, k, mc * 128:(mc + 1) * 128], wv[:, k, :], start=(k == 0), stop=(k == KC - 1))
                                  if first:
                                      stg = kvst.next()
                                      P.copy(stg, ps[:, 0:512], eng="dve")
                                      if keep is not None:
                                          P.copy(keep[:, mc, gi * 512:(gi + 1) * 512], stg, eng="act")
                                      P.dma(O[oname][l, mc * 128:(mc + 1) * 128, gi * 512:(gi + 1) * 512], stg, q="sp", is_output=True)
                                  elif keep is not None:
                                      P.copy(keep[:, mc, gi * 512:(gi + 1) * 512], ps[:, 0:512], eng="act")
                          groups.append(dict(dmas=[(lambda wb: wview(wb, 0, KC, 512), I[wname][l][:, gi * 512:(gi + 1) * 512].rearrange("(k p) m -> p k m", p=128))], body=body))
                      stream(groups)

                  if first:
                      tokmajor_proj("w_mem_k", "p_mk", None)
                      stage("tmk")
                  tokmajor_proj("w_mem_v", "p_mv", VT)
                  stage("attnkv")

                  def q_consume(m, ti, n0, n1, ps):
                      P.copy(qT[:, m, n0:n1], ps[:, 0:n1 - n0], eng="act")

                  linear([(I["w_mem_q"][l], KC)], D, 512, tiles, rhs_u, q_consume)
                  stage("attnq")
                  SCL = 1.0 / 16.0
                  ptr = Rot([ar.get([2, 512], BF16) for _ in range(2)])
                  rcp = ar.get([512])
                  for (n0, n1) in tiles:
                      if n0 >= NPP:
                          continue
                      for h in range(4):
                          pt = ptr.next()
                          for mc in range(2):
                              sc = psA[:, mc * 512:(mc + 1) * 512]
                              for dc in range(2):
                                  P.mm(sc, KT[:, 2 * h + dc, mc * 128:(mc + 1) * 128], qT[:, 2 * h + dc, n0:n1], start=(dc == 0), stop=(dc == 1))
                              P.act(pt[:, mc, :], sc, AF.Exp, scale=SCL)
                          pss = smrot.next()
                          for mc in range(2):
                              P.mm(pss[:, 0:512], ones_b, pt[:, mc, :], start=(mc == 0), stop=(mc == 1))
                          P.recip(rcp, pss[:, 0:512])
                          for dc in range(2):
                              po = mmrot.next()
                              for mc in range(2):
                                  P.mm(po[:, 0:512], VT[:, mc, h * 256 + dc * 128:h * 256 + (dc + 1) * 128], pt[:, mc, :], start=(mc == 0), stop=(mc == 1))
                              P.tt(oT[:, 2 * h + dc, n0:n1], po[:, 0:512], rcp, ALU.mult)
                  if has_s:
                      kbr = Rot([ar.get([2, D], BF16) for _ in range(5)])
                      vbr = kbr
                      ktr = Rot([ar.get([KC, 256], BF16) for _ in range(2)])
                      pts = ar.get([NSQ, 2, 16], BF16)
                      rcs = ar.get([NSQ * 16])
                      vbs = []
                      pscs = mmrot.next()
                      scv = pscs[:, 0:512].rearrange("p (b m x) -> p b m x", b=NSQ, m=2)
                      for b in range(NSQ):
                          kb = kbr.next()
                          P.dma(kb, I["ck"][l, b].rearrange("(mc p) e -> p mc e", p=128), q="pool")
                          ktb = ktr.next()
                          for half in range(2):
                              pkb = psA[:, half * 512:(half + 1) * 512].bitcast(BF16)
                              for ee in range(4):
                                  e = half * 4 + ee
                                  for mc in range(2):
                                      P.tr(pkb[:, ee * 256 + mc * 128:ee * 256 + (mc + 1) * 128], kb[:, mc, e * 128:(e + 1) * 128], ident_b)
                              P.copy(ktb[:, half * 4:(half + 1) * 4, :], pkb.rearrange("p (e m) -> p e m", e=4), eng=("act" if half == 0 else "dve"))
                          for h in range(4):
                              for mc in range(2):
                                  for dc in range(2):
                                      P.mm(scv[:, b, mc, 4 * h:4 * h + 4], ktb[:, 2 * h + dc, mc * 128:(mc + 1) * 128],
                                           qT[:, 2 * h + dc, NPP + 4 * b:NPP + 4 * b + 4], start=(dc == 0), stop=(dc == 1))
                      P.act(pts.rearrange("p b m x -> p (b m x)"), pscs[:, 0:512], AF.Exp, scale=SCL)
                      pos = mmrot.next()
                      pss = smrot.next()
                      for b in range(NSQ):
                          vb = vbr.next()
                          P.dma(vb, I["cv"][l, b].rearrange("(mc p) e -> p mc e", p=128), q="pool")
                          for mc in range(2):
                              P.mm(pss[:, b * 16:(b + 1) * 16], ones_b, pts[:, b, mc, :], start=(mc == 0), stop=(mc == 1))
                          for e in range(KC):
                              h = e // 2
                              for mc in range(2):
                                  P.mm(pos[:, e * 64 + 4 * b:e * 64 + 4 * b + 4], vb[:, mc, e * 128:(e + 1) * 128], pts[:, b, mc, 4 * h:4 * h + 4],
                                       start=(mc == 0), stop=(mc == 1))
                      P.recip(rcs, pss[:, 0:256])
                      rv = rcs.rearrange("p (b h r) -> p b h r", b=NSQ, h=4)
                      for h in range(4):
                          for dc in range(2):
                              e = 2 * h + dc
                              P.tt(oT[:, e, NPP:NT].rearrange("p (b r) -> p b r", b=NSQ), pos[:, e * 64:(e + 1) * 64].rearrange("p (b r) -> p b r", b=NSQ),
                                   rv[:, :, h, :], ALU.mult)
                  linear([(I["w_mem_o"][l], KC)], D, 512, tiles, lambda k, n0, n1: oT[:, k, n0:n1], add_resid)
                  stage("attn")
                  if dbg:
                      P.dma(O["dbg_h"][pas, l, 1], hT[:].rearrange("p k n -> p (k n)"), q="sp", is_output=True)

                  ar.off = arena_base
                  rmsnorm("nffn", l, tiles)
                  sgr = Rot([ar.get([512]) for _ in range(2)])
                  groups = []
                  for gi in range(FC // 2):
                      def body(wb, gi=gi):
                          wv = wview(wb, 0, KC, 512)
                          for fi in range(2):
                              f = gi * 2 + fi
                              for (n0, n1) in tiles:
                                  n = n1 - n0
                                  pg = mmrot.next()
                                  for k in range(KC):
                                      P.mm(pg[:, 0:n], wv[:, k, fi * 128:(fi + 1) * 128], uT[:, k, n0:n1], start=(k == 0), stop=(k == KC - 1))
                                  pu = mmrot.next()
                                  for k in range(KC):
                                      P.mm(pu[:, 0:n], wv[:, k, 256 + fi * 128:256 + (fi + 1) * 128], uT[:, k, n0:n1], start=(k == 0), stop=(k == KC - 1))
                                  sg = sgr.next()
                                  P.act(sg[:, 0:n], pg[:, 0:n], AF.Silu)
                                  P.tt(A[:, f, n0:n1], sg[:, 0:n], pu[:, 0:n], ALU.mult)
                      dmas = [
                          (lambda wb: wview(wb, 0, KC, 512)[:, :, 0:256], I["w_ffn_gate"][l][:, gi * 256:(gi + 1) * 256].rearrange("(k p) m -> p k m", p=128)),
                          (lambda wb: wview(wb, 0, KC, 512)[:, :, 256:512], I["w_ffn_up"][l][:, gi * 256:(gi + 1) * 256].rearrange("(k p) m -> p k m", p=128)),
                      ]
                      groups.append(dict(dmas=dmas, body=body))
                  stream(groups)
                  linear([(I["w_ffn_down"][l], FC)], D, 128, tiles, lambda k, n0, n1: A[:, k, n0:n1], add_resid)
                  stage("ffn")
                  if dbg:
                      P.dma(O["dbg_h"][pas, l, 2], hT[:].rearrange("p k n -> p (k n)"), q="sp", is_output=True)

                  if last:
                      ar.off = arena_base
                      o1 = ar.get([512], parts=15)
                      pp = smrot.next()
                      for j in range(4):
                          P.tr(pp[0:15, j * 128:(j + 1) * 128], pool_tail[:, l, j, :], ident_f)
                      P.copy(o1, pp[0:15, 0:512], eng="act")
                      P.dma(O["p_pool"][l], o1, q="sp", is_output=True)
                      o2 = ar.get([1536], parts=3)
                      for q3 in range(3):
                          pp = smrot.next()
                          for j in range(4):
                              P.tr(pp[0:3, j * 128:(j + 1) * 128], sconv_tail[:, l, q3 * 4 + j, :], ident_f)
                          P.copy(o2[:, q3 * 512:(q3 + 1) * 512], pp[0:3, 0:512], eng="act")
                      P.dma(O["p_sconv"][l], o2, q="sp", is_output=True)
                      o3 = ar.get([512], parts=3)
                      pp = smrot.next()
                      for j in range(4):
                          P.tr(pp[0:3, j * 128:(j + 1) * 128], lconv_tail[:, l, j, :], ident_f)
                      P.copy(o3, pp[0:3, 0:512], eng="act")
                      P.dma(O["p_lconv"][l], o3, q="sp", is_output=True)
                      o4 = ar.get([512], parts=1)
                      pp = smrot.next()
                      for j in range(4):
                          P.tr(pp[0:1, j * 128:(j + 1) * 128], lru_h[:, l, j:j + 1], ident_f)
                      P.copy(o4, pp[0:1, 0:512], eng="act")
                      P.dma(O["p_lru"][l:l + 1, :], o4, q="sp", is_output=True)

              ar.reset()
              sqrot = Rot([ar.get([512], BF16) for _ in range(2)])
              rstd = ar.get([512])
              ytr = Rot([ar.get([KC, 128]) for _ in range(2)])
              yor = Rot([ar.get([D]) for _ in range(2)])
              for (n0, n1) in tiles:
                  n = n1 - n0
                  ps = mmrot.next()
                  for k in range(KC):
                      sq = sqrot.next()
                      P.act(sq[:, 0:n], hT[:, k, n0:n1], AF.Square)
                      P.mm(ps[:, 0:n], ones_b, sq[:, 0:n], start=(k == 0), stop=(k == KC - 1))
                  P.act(rstd[:, 0:n], ps[:, 0:n], AF.Sqrt, scale=1.0 / D, bias=epsc[:, 0:1])
                  P.recip(rstd[:, 0:n], rstd[:, 0:n])
                  bw = 128 if n0 < NPP else 64
                  for bi in range(n // bw):
                      yt = ytr.next()
                      for k in range(KC):
                          P.stt(yt[:, k, 0:bw], hT[:, k, n0 + bi * bw:n0 + (bi + 1) * bw], pcol("nfin", k), rstd[:, bi * bw:(bi + 1) * bw], ALU.mult, ALU.mult)
                      for k in range(KC):
                          P.tr(psA[0:bw, k * 128:(k + 1) * 128], yt[:, k, 0:bw], ident_f)
                      yo = yor.next()
                      P.copy(yo[0:bw, 0:512], psA[0:bw, 0:512], eng="act")
                      P.copy(yo[0:bw, 512:1024], psA[0:bw, 512:1024], eng="dve")
                      if n0 < NPP:
                          P.dma(O["y_p"][t0 + n0 + bi * 128:t0 + n0 + (bi + 1) * 128, :], yo, q="sp", is_output=True)
                      else:
                          P.dma(O["y_s"], yo[0:64, :], q="sp", is_output=True)
        except _Stop:
            pass
        P.emit()
    return nc


LRU_C = 8.0
_NC_CACHE = {}


def make_in_maps(inputs, cores):
    cst = build_consts()
    cst2 = np.zeros((128, 16, 64), np.float32)
    for b in range(16):
        cst2[:, b, 4 * b:4 * b + 4] = 1.0
    cst2 = cst2.reshape(128, 1024)
    maps = []
    for i in cores:
        s = slice(NSQ * i, NSQ * (i + 1))
        m = {
            "x_p": inputs["x_prompt"][i], "x_s": inputs["x_sample"][s].reshape(NS, D), "mem": inputs["mem_prompt"][i],
            "st_pool": inputs["state_pool"][:, s], "st_sconv": inputs["state_ssd_conv"][:, s], "st_ssd": inputs["state_ssd"][:, s],
            "st_lconv": inputs["state_lru_conv"][:, s], "st_lru": inputs["state_lru"][:, s],
            "ck": inputs["cache_mem_k"][:, s].reshape(DEPTH, NSQ, 256, D), "cv": inputs["cache_mem_v"][:, s].reshape(DEPTH, NSQ, 256, D),
            "cst": cst, "cst2": cst2,
        }
        for n, _ in IN_SPECS:
            if n not in m:
                m[n] = inputs[n]
        maps.append({k: np.ascontiguousarray(np.asarray(v, dtype=np.float32)) for k, v in m.items()})
    return maps


def kernel(**inputs):
    inputs = {k: np.asarray(v) for k, v in inputs.items()}
    if "nc" not in _NC_CACHE:
        _NC_CACHE["nc"] = build_program()
    nc = _NC_CACHE["nc"]
    maps = make_in_maps(inputs, list(range(N_CORES)))
    res = run_bass_kernel_spmd(nc, maps, core_ids=list(range(N_CORES)))
    R = res.results

    def cat(name, axis):
        return np.concatenate([np.asarray(r[name]) for r in R], axis=axis)

    y_p = np.stack([np.asarray(r["y_p"]) for r in R], 0)
    y_s = np.concatenate([np.asarray(r["y_s"]).reshape(NSQ, 4, D) for r in R], 0)
    outs = [y_p, y_s]
    for n in ("p_pool", "p_sconv", "p_ssd", "p_lconv", "p_lru"):
        outs.append(np.stack([np.asarray(r[n]) for r in R], 1))
    for n in ("p_mk", "p_mv"):
        outs.append(np.stack([np.asarray(r[n]).reshape(DEPTH, 256, 4, 256) for r in R], 1))
    for n in ("s_pool", "s_sconv", "s_ssd", "s_lconv", "s_lru"):
        outs.append(cat(n, 1))
    return tuple(np.ascontiguousarray(o.astype(np.float32)) for o in outs)
```

```python
import contextlib
import numpy as np
import concourse.bass as bass
import concourse.mybir as mybir
from concourse.bass_utils import run_bass_kernel_spmd

F32 = mybir.dt.float32
BF16 = mybir.dt.bfloat16
AF = mybir.ActivationFunctionType
ALU = mybir.AluOpType

ENGS = ("pe", "act", "dve", "pool", "sp")
NRING = 8

D = 1024
KC = 8
NPT = 2048
NSQ = 16
NS = 64
DEPTH = 4
NPASS = 2
NPP = NPT // NPASS
NT = NPP + NS
DFF = 2816
FC = 22
OFF_POOL, OFF_Z, OFF_XBC, OFF_DT, OFF_GATE, OFF_LRU, N_IN = 0, 512, 1536, 3072, 3088, 3600, 4112
NEG = -30000.0
EPS = 1e-6
NCST = 1104
WBUF = 4096
N_CORES = 8


def _isz(dt):
    return mybir.dt.size(dt)


def _region(ap):
    t = ap.tensor
    pairs = [tuple(x) for x in ap.ap]
    off = int(ap.offset)
    isz = _isz(ap.dtype)
    if type(t).__name__ == "DRamTensorHandle":
        ext = 0
        for st, cnt in pairs:
            ext += (cnt - 1) * abs(st)
        return (t.name, 0, 1, off * isz, (off + ext + 1) * isz)
    pst, pcnt = pairs[0]
    if pst == 0:
        pst = 1 << 40
    p_lo = off // pst
    f_lo = off % pst
    ext = 0
    for st, cnt in pairs[1:]:
        ext += (cnt - 1) * abs(st)
    b_lo, b_hi = f_lo * isz, (f_lo + ext + 1) * isz
    if type(t).__name__ == "PSumTensorHandle":
        b_lo = (b_lo // 2048) * 2048
        b_hi = ((b_hi + 2047) // 2048) * 2048
        return (t.name, (p_lo // 32) * 32, ((p_lo + pcnt + 31) // 32) * 32, b_lo, b_hi)
    return (t.name, p_lo, p_lo + pcnt, b_lo, b_hi)


class Op:
    __slots__ = ("eng", "idx", "fn", "deps", "is_dma", "needed", "signal", "clock", "ring", "tag")

    def __init__(self, eng, idx, fn, is_dma):
        self.eng = eng
        self.idx = idx
        self.fn = fn
        self.deps = []
        self.is_dma = is_dma
        self.needed = False
        self.signal = None
        self.clock = None
        self.ring = None


class Prog:
    def __init__(self, nc):
        self.nc = nc
        self.ops = {e: [] for e in ENGS}
        self.recs = {}
        self.ndma = {e: 0 for e in ENGS}
        self.dma_seq = {e: [] for e in ENGS}
        self.waited_dma = {e: set() for e in ENGS}
        self.out_dmas = []
        self.tag = ""

    def op(self, eng, fn, reads=(), writes=(), is_dma=False, extra_deps=()):
        lst = self.ops[eng]
        o = Op(eng, len(lst), fn, is_dma)
        o.tag = self.tag
        deps = {}
        ops = self.ops

        def add_dep(e2, i2):
            if e2 == eng and eng == "pe":
                return
            od = ops[e2][i2]
            if od.is_dma:
                deps[(e2, i2)] = True
            else:
                k = deps.get(e2)
                if k is None or i2 > k:
                    deps[e2] = i2

        rregs = [_region(ap) for ap in reads]
        wregs = [_region(ap) for ap in writes]
        for (name, pl, ph, fl, fh) in rregs:
            for r in self.recs.get(name, ()):
                if r[4] and r[0] < ph and pl < r[1] and r[2] < fh and fl < r[3]:
                    add_dep(r[5], r[6])
        for (name, pl, ph, fl, fh) in wregs:
            for r in self.recs.get(name, ()):
                if r[0] < ph and pl < r[1] and r[2] < fh and fl < r[3]:
                    add_dep(r[5], r[6])
        for (e2, i2) in extra_deps:
            add_dep(e2, i2)
        prev = lst[-1].clock if lst else {}
        clock = dict(prev)
        final = []
        for k in deps:
            if isinstance(k, tuple):
                if k in self.waited_dma[eng]:
                    continue
                final.append(k)
            else:
                i2 = deps[k]
                if clock.get(k, -1) >= i2:
                    continue
                final.append((k, i2))
        for (e2, i2) in final:
            od = ops[e2][i2]
            if od.is_dma:
                self.waited_dma[eng].add((e2, i2))
            else:
                if clock.get(e2, -1) < i2:
                    clock[e2] = i2
            for k2, v2 in od.clock.items():
                if clock.get(k2, -1) < v2:
                    clock[k2] = v2
        if is_dma:
            n = self.ndma[eng]
            o.ring = n
            self.ndma[eng] = n + 1
            if n >= NRING:
                pd = self.dma_seq[eng][n - NRING]
                if (eng, pd.idx) not in self.waited_dma[eng]:
                    final.append((eng, pd.idx))
                    self.waited_dma[eng].add((eng, pd.idx))
            self.dma_seq[eng].append(o)
        o.deps = final
        o.clock = clock
        lst.append(o)
        for (name, pl, ph, fl, fh) in rregs:
            L = self.recs.setdefault(name, [])
            if not is_dma:
                L[:] = [r for r in L if not ((not r[4]) and r[5] == eng and (not r[7]) and pl <= r[0] and r[1] <= ph and fl <= r[2] and r[3] <= fh)]
            L.append([pl, ph, fl, fh, False, eng, o.idx, is_dma])
        for (name, pl, ph, fl, fh) in wregs:
            L = self.recs.setdefault(name, [])
            L[:] = [r for r in L if not (pl <= r[0] and r[1] <= ph and fl <= r[2] and r[3] <= fh)]
            L.append([pl, ph, fl, fh, True, eng, o.idx, is_dma])
        return o

    def mm(self, out, lhsT, rhs, start=True, stop=True):
        return self.op("pe", lambda e: e.matmul(out, lhsT, rhs, start=start, stop=stop), [lhsT, rhs], [out])

    def tr(self, out, in_, ident):
        return self.op("pe", lambda e: e.transpose(out, in_, ident), [in_, ident], [out])

    def act(self, out, in_, func, bias=None, scale=None):
        reads = [in_]
        kw = {}
        if bias is not None:
            kw["bias"] = bias
            if not isinstance(bias, (int, float)):
                reads.append(bias)
        if scale is not None:
            kw["scale"] = scale
            if not isinstance(scale, (int, float)):
                reads.append(scale)
        return self.op("act", lambda e: e.activation(out, in_, func, **kw), reads, [out])

    def tt(self, out, a, b, op, eng="dve"):
        return self.op(eng, lambda e: e.tensor_tensor(out, a, b, op), [a, b], [out])

    def ts(self, out, a, s1, op0, s2=None, op1=None, eng="dve"):
        reads = [a]
        if not isinstance(s1, (int, float)):
            reads.append(s1)
        if s2 is not None and not isinstance(s2, (int, float)):
            reads.append(s2)
        if op1 is None:
            return self.op(eng, lambda e: e.tensor_scalar(out, a, s1, None, op0), reads, [out])
        return self.op(eng, lambda e: e.tensor_scalar(out, a, s1, s2, op0, op1), reads, [out])

    def stt(self, out, a, s, b, op0, op1):
        reads = [a, b]
        if not isinstance(s, (int, float)):
            reads.append(s)
        return self.op("dve", lambda e: e.scalar_tensor_tensor(out, a, s, b, op0, op1), reads, [out])

    def scan(self, out, d0, d1, init, op0, op1):
        reads = [d0, d1]
        if not isinstance(init, (int, float)):
            reads.append(init)
        return self.op("dve", lambda e: e.tensor_tensor_scan(out, d0, d1, init, op0, op1), reads, [out])

    def copy(self, out, in_, eng="dve"):
        if eng == "act":
            return self.op("act", lambda e: e.copy(out, in_), [in_], [out])
        return self.op(eng, lambda e: e.tensor_copy(out, in_), [in_], [out])

    def memset(self, ap, val, eng="dve"):
        return self.op(eng, lambda e: e.memset(ap, val), [], [ap])

    def recip(self, out, in_):
        return self.op("dve", lambda e: e.reciprocal(out, in_), [in_], [out])

    def dma(self, out, in_, q="sp", is_output=False, **kw):
        o = self.op(q, lambda e: e.dma_start(out=out, in_=in_, **kw), [in_], [out], is_dma=True)
        if is_output:
            self.out_dmas.append((q, o.idx))
        return o

    def emit(self):
        nc = self.nc
        self.op("sp", None, extra_deps=list(self.out_dmas))
        for e in ENGS:
            for o in self.ops[e]:
                for (e2, i2) in o.deps:
                    self.ops[e2][i2].needed = True
        for e in ENGS:
            c = 0
            for o in self.ops[e]:
                if o.needed and not o.is_dma:
                    c += 1
                    o.signal = c
        with contextlib.ExitStack() as st:
            sems = {e: st.enter_context(nc.semaphore("s_" + e)) for e in ENGS}
            rings = {e: [st.enter_context(nc.semaphore("r_%s_%d" % (e, i))) for i in range(NRING)] for e in ENGS if self.ndma[e]}
            block = st.enter_context(nc.Block())

            def run(engname, eng):
                for o in self.ops[engname]:
                    for (e2, i2) in o.deps:
                        od = self.ops[e2][i2]
                        if od.is_dma:
                            eng.wait_ge(rings[e2][od.ring % NRING], 16 * (od.ring // NRING + 1))
                        else:
                            eng.wait_ge(sems[e2], od.signal)
                    if o.fn is None:
                        continue
                    inst = o.fn(eng)
                    if o.is_dma:
                        inst.then_inc(rings[engname][o.ring % NRING], 16)
                    elif o.signal is not None:
                        inst.then_inc(sems[engname], 1)

            @block.tensor
            def _(eng):
                run("pe", eng)

            @block.scalar
            def _(eng):
                run("act", eng)

            @block.vector
            def _(eng):
                run("dve", eng)

            @block.gpsimd
            def _(eng):
                run("pool", eng)

            @block.sync
            def _(eng):
                run("sp", eng)


class Rot:
    def __init__(self, items):
        self.items = list(items)
        self.i = 0

    def next(self):
        x = self.items[self.i % len(self.items)]
        self.i += 1
        return x


class Arena:
    def __init__(self, tens, nwords):
        self.t = tens
        self.n = nwords
        self.off = 0

    def reset(self):
        self.off = 0

    def get(self, shape, dtype=F32, parts=128):
        n = 1
        for s in shape:
            n *= s
        words = (n * _isz(dtype) + 3) // 4
        words += words & 1
        assert self.off + words <= self.n, ("arena overflow", self.off, words, self.n)
        v = self.t[0:parts, self.off:self.off + words]
        self.off += words
        if dtype != F32:
            v = v.bitcast(dtype)
        v = v[:, 0:n]
        if len(shape) == 2:
            v = v.rearrange("p (a b) -> p a b", a=shape[0])
        elif len(shape) == 3:
            v = v.rearrange("p (a b c) -> p a b c", a=shape[0], b=shape[1])
        return v


def bcast(ap, shape):
    return ap.broadcast_to(list(shape))


def build_consts():
    c = np.zeros((128, NCST), np.float32)
    r = np.arange(128)
    c[:, 0:128] = np.eye(128)
    U = (r[:, None] <= r[None, :]).astype(np.float32)
    c[:, 128:256] = U
    c[:, 256:384] = -U
    sel = np.zeros((128, 128), np.float32)
    sel[127, :] = 1.0
    c[:, 384:512] = sel
    c[:, 512:640] = np.where(r[None, :] < r[:, None], NEG, 0.0)
    r64 = np.arange(64)
    same = (r64[:, None] // 4) == (r64[None, :] // 4)
    Us = (same & (r64[:, None] <= r64[None, :])).astype(np.float32)
    c[0:64, 640:704] = Us
    c[0:64, 704:768] = -Us
    sels = np.zeros((64, 64), np.float32)
    for s in range(64):
        sels[4 * (s // 4) + 3, s] = 1.0
    c[0:64, 768:832] = sels
    c[0:64, 832:896] = np.where(same & (r64[None, :] >= r64[:, None]), 0.0, NEG)
    bi = np.zeros((64, 16), np.float32)
    bi[r64, r64 // 4] = 1.0
    c[0:64, 896:912] = bi
    for g, w in enumerate((2, 4, 8, 16)):
        for t in range(16):
            c[:, 912 + g * 16 + t] = 1.0 / min(t + 1, w)
    c[:, 976:1104] = 1.0
    return c


IN_SPECS = [
    ("x_p", (NPT, D)), ("x_s", (NS, D)), ("mem", (256, D)),
    ("st_pool", (DEPTH, NSQ, 15, 512)), ("st_sconv", (DEPTH, NSQ, 3, 1536)), ("st_ssd", (DEPTH, NSQ, 16, 64, 128)),
    ("st_lconv", (DEPTH, NSQ, 3, 512)), ("st_lru", (DEPTH, NSQ, 512)),
    ("ck", (DEPTH, NSQ, 256, D)), ("cv", (DEPTH, NSQ, 256, D)),
    ("norm_mix", (DEPTH, D)), ("w_in", (DEPTH, D, N_IN)), ("pool_w", (DEPTH, 4, 128, 128)), ("pool_scale", (DEPTH, 512)),
    ("ssd_conv_w", (DEPTH, 4, 1536)), ("ssd_conv_b", (DEPTH, 1536)), ("ssd_dt_bias", (DEPTH, 16)), ("ssd_a_log", (DEPTH, 16)),
    ("ssd_d", (DEPTH, 16)), ("ssd_norm", (DEPTH, D)), ("lru_conv_w", (DEPTH, 4, 512)), ("lru_conv_b", (DEPTH, 512)),
    ("lru_wa", (DEPTH, 8, 64, 64)), ("lru_ba", (DEPTH, 8, 64)), ("lru_wx", (DEPTH, 8, 64, 64)), ("lru_bx", (DEPTH, 8, 64)),
    ("lru_lambda", (DEPTH, 512)), ("w_out", (DEPTH, 2048, D)), ("norm_mem", (DEPTH, D)), ("w_mem_q", (DEPTH, D, D)),
    ("w_mem_k", (DEPTH, D, D)), ("w_mem_v", (DEPTH, D, D)), ("w_mem_o", (DEPTH, D, D)), ("norm_ffn", (DEPTH, D)),
    ("w_ffn_gate", (DEPTH, D, DFF)), ("w_ffn_up", (DEPTH, D, DFF)), ("w_ffn_down", (DEPTH, DFF, D)), ("norm_final", (D,)),
    ("cst", (128, NCST)), ("cst2", (128, 1024)),
]
OUT_SPECS = [
    ("y_p", (NPT, D)), ("y_s", (NS, D)), ("p_pool", (DEPTH, 15, 512)), ("p_sconv", (DEPTH, 3, 1536)),
    ("p_ssd", (DEPTH, 16, 64, 128)), ("p_lconv", (DEPTH, 3, 512)), ("p_lru", (DEPTH, 512)),
    ("p_mk", (DEPTH, 256, D)), ("p_mv", (DEPTH, 256, D)),
    ("s_pool", (DEPTH, NSQ, 15, 512)), ("s_sconv", (DEPTH, NSQ, 3, 1536)), ("s_ssd", (DEPTH, NSQ, 16, 64, 128)),
    ("s_lconv", (DEPTH, NSQ, 3, 512)), ("s_lru", (DEPTH, NSQ, 512)),
]


class _Stop(Exception):
    pass


def build_program(n_layers=DEPTH, dbg=False, stop_at=None):
    nc = bass.Bass("TRN2", target_bir_lowering=False)
    I = {n: nc.dram_tensor(n, list(s), F32, kind="ExternalInput").ap() for n, s in IN_SPECS}
    O = {n: nc.dram_tensor(n, list(s), F32, kind="ExternalOutput").ap() for n, s in OUT_SPECS}
    spill = nc.dram_tensor("ssd_spill", [DEPTH, 128, D], F32, kind="Internal").ap()
    if dbg:
        O["dbg_h"] = nc.dram_tensor("dbg_h", [NPASS, DEPTH, 4, 128, KC * NT], F32, kind="ExternalOutput").ap()
    P = Prog(nc)
    with contextlib.ExitStack() as st:
        def sb(name, shape, dt=F32):
            return st.enter_context(nc.sbuf_tensor(name, list(shape), dt))

        def pst(name, shape, dt=F32):
            return st.enter_context(nc.psum_tensor(name, list(shape), dt))

        hT = sb("hT", [128, KC, NT])
        uT = sb("uT", [128, KC, NT], BF16)
        A = sb("A", [128, FC, NT], BF16)
        wbufs = [sb("wb%d" % i, [128, WBUF], BF16) for i in range(2)]
        cst_f = sb("cst_f", [128, NCST])
        cst_b = sb("cst_b", [128, NCST], BF16)
        PAR = sb("PAR", [128, 640])
        RB = sb("RB", [128, 3 * 64])
        DCOL = sb("DCOL", [128, DEPTH, 8])
        C1 = sb("C1", [128, 16])
        memT = sb("memT", [128, KC, 256], BF16)
        POOLW = sb("POOLW", [128, 4, 128], BF16)
        WA = sb("WA", [128, 4, 128], BF16)
        WX = sb("WX", [128, 4, 128], BF16)
        WDT = sb("WDT", [128, KC, 16], BF16)
        pool_tail = sb("pool_tail", [128, DEPTH, 4, 15])
        sconv_tail = sb("sconv_tail", [128, DEPTH, 12, 3])
        lconv_tail = sb("lconv_tail", [128, DEPTH, 4, 3])
        lru_h = sb("lru_h", [128, DEPTH, 4])
        ssdT = sb("ssdT", [128, D])
        ssdT_b = sb("ssdT_b", [128, D], BF16)
        DIAGD = sb("DIAGD", [128, 8, 128], BF16)
        ARW = 16384
        TMP = sb("TMP", [128, ARW])
        ar = Arena(TMP, ARW)

        psA = pst("psA", [128, 1024])
        pbs = [pst("pb%d" % i, [128, 512]) for i in range(6)]
        mmrot = Rot(pbs[0:3])
        smrot = Rot(pbs[3:5])
        psY = pbs[5]
        wrot = Rot(wbufs)

        ident_f = cst_f[:, 0:128]
        ident_b = cst_b[:, 0:128]
        ones_f = cst_f[:, 976:1104]
        ones_b = cst_b[:, 976:1104]
        CP = dict(U=cst_f[:, 128:256], negU=cst_f[:, 256:384], sel=cst_f[:, 384:512], negm=cst_b[:, 512:640], L=128,
                  Ub=cst_b[:, 128:256], negUb=cst_b[:, 256:384])
        CS = dict(U=cst_f[0:64, 640:704], negU=cst_f[0:64, 704:768], sel=cst_f[0:64, 768:832], negm=cst_b[0:64, 832:896], L=64,
                  Ub=cst_b[0:64, 640:704], negUb=cst_b[0:64, 704:768])
        blockind_f = cst_f[0:64, 896:912]
        blockind_b = cst_b[0:64, 896:912]
        rc_tab = cst_f[:, 912:976].rearrange("p (g t) -> p g t", g=4)

        P.dma(cst_f[:], I["cst"], q="sp")
        P.dma(cst_b[:], I["cst"], q="pool")
        BMASK = sb("BMASK", [128, 16, 64], BF16)
        P.dma(BMASK[:], I["cst2"].rearrange("p (b l) -> p b l", b=16), q="pool")
        prow = {}
        plist = [
            ("nm", I["norm_mix"].rearrange("l (j p) -> (l j) p", p=128)),
            ("nmem", I["norm_mem"].rearrange("l (j p) -> (l j) p", p=128)),
            ("nffn", I["norm_ffn"].rearrange("l (j p) -> (l j) p", p=128)),
            ("nfin", I["norm_final"].rearrange("(j p) -> j p", p=128)),
            ("pscale", I["pool_scale"].rearrange("l (j p) -> (l j) p", p=128)),
            ("scw", I["ssd_conv_w"].rearrange("l k (j p) -> (l k j) p", p=128)),
            ("scb", I["ssd_conv_b"].rearrange("l (j p) -> (l j) p", p=128)),
            ("snorm", I["ssd_norm"].rearrange("l (j p) -> (l j) p", p=128)),
            ("lcw", I["lru_conv_w"].rearrange("l k (j p) -> (l k j) p", p=128)),
            ("lcb", I["lru_conv_b"].rearrange("l (j p) -> (l j) p", p=128)),
            ("lam", I["lru_lambda"].rearrange("l (j p) -> (l j) p", p=128)),
            ("ba", I["lru_ba"].rearrange("l h i -> l (h i)").rearrange("l (j p) -> (l j) p", p=128)),
            ("bx", I["lru_bx"].rearrange("l h i -> l (h i)").rearrange("l (j p) -> (l j) p", p=128)),
        ]
        PST = ar.get([5, 128])
        P.memset(PST, 0.0)
        r0 = 0
        for name, ap2 in plist:
            nr = ap2.shape[0]
            prow[name] = r0
            done = 0
            while done < nr:
                t, rr = divmod(r0 + done, 128)
                n = min(nr - done, 128 - rr)
                P.dma(PST[rr:rr + n, t, :], ap2[done:done + n, :], q="sp")
                done += n
            r0 += nr
        assert r0 <= 640
        for t in range(5):
            ps = smrot.next()
            P.tr(ps[:, 0:128], PST[:, t, :], ident_f)
            P.copy(PAR[:, t * 128:(t + 1) * 128], ps[:, 0:128], eng="act")

        def pcol(name, idx):
            c = prow[name] + idx
            return PAR[:, c:c + 1]

        P.dma(RB[:, 0:64], I["ssd_dt_bias"].rearrange("l h -> (l h)").partition_broadcast(128), q="sp")
        P.dma(RB[:, 64:128], I["ssd_a_log"].rearrange("l h -> (l h)").partition_broadcast(128), q="sp")
        P.dma(RB[:, 128:192], I["ssd_d"].rearrange("l h -> (l h)").partition_broadcast(128), q="sp")
        P.act(RB[:, 64:128], RB[:, 64:128], AF.Exp)
        P.ts(RB[:, 64:128], RB[:, 64:128], -1.0, ALU.mult)
        dview = RB[:, 128:192].rearrange("p (l hp two) -> p l hp two", l=DEPTH, two=2)
        P.copy(DCOL[0:64, :, :], dview[0:64, :, :, 0])
        P.copy(DCOL[64:128, :, :], dview[64:128, :, :, 1])
        lam0 = prow["lam"]
        P.act(C1[:], PAR[:, lam0:lam0 + 16], AF.Exp, scale=-1.0)
        P.act(C1[:], C1[:], AF.Ln, bias=1.0)
        P.ts(C1[:], C1[:], -LRU_C, ALU.mult)
        P.memset(WA[:], 0.0)
        P.memset(WX[:], 0.0)
        for mc in range(2):
            mt = ar.get([D])
            P.dma(mt, I["mem"][mc * 128:(mc + 1) * 128, :], q="sp")
            for k in range(KC):
                P.tr(psA[:, k * 128:(k + 1) * 128], mt[:, k * 128:(k + 1) * 128], ident_f)
            P.copy(memT[:, 0:4, mc * 128:(mc + 1) * 128], psA[:, 0:512].rearrange("p (k n) -> p k n", k=4), eng="act")
            P.copy(memT[:, 4:8, mc * 128:(mc + 1) * 128], psA[:, 512:1024].rearrange("p (k n) -> p k n", k=4), eng="dve")
        for tl in (pool_tail, sconv_tail, lconv_tail, lru_h):
            P.memset(tl[:], 0.0)
        if dbg:
            P.memset(hT[:], 0.0)

        def stream(groups):
            for g in groups:
                wb = wrot.next()
                for dst_fn, src in g["dmas"]:
                    P.dma(dst_fn(wb), src, q="pool")
                g["body"](wb)

        def wview(wb, koff, kc, MG):
            return wb[:, koff * MG:(koff + kc) * MG].rearrange("p (k m) -> p k m", k=kc)

        def linear(wsrcs, m_total, MG, tiles, rhs_fn, consume):
            KCt = sum(kc for _, kc in wsrcs)
            groups = []
            ng = (m_total + MG - 1) // MG
            for gi in range(ng):
                mg = min(MG, m_total - gi * MG)

                def body(wb, gi=gi, mg=mg):
                    wv = wview(wb, 0, KCt, mg)
                    for mc in range(mg // 128):
                        mglob = gi * (MG // 128) + mc
                        for ti, (n0, n1) in enumerate(tiles):
                            ps = mmrot.next()
                            for k in range(KCt):
                                P.mm(ps[:, 0:n1 - n0], wv[:, k, mc * 128:(mc + 1) * 128], rhs_fn(k, n0, n1),
                                     start=(k == 0), stop=(k == KCt - 1))
                            consume(mglob, ti, n0, n1, ps)

                dmas = []
                koff = 0
                for src, kc in wsrcs:
                    dmas.append((lambda wb, koff=koff, kc=kc, mg=mg: wview(wb, koff, kc, mg),
                                 src[:, gi * MG:gi * MG + mg].rearrange("(k p) m -> p k m", p=128)))
                    koff += kc
                groups.append(dict(dmas=dmas, body=body))
            stream(groups)

        def rmsnorm(gname, l, tiles):
            for (n0, n1) in tiles:
                n = n1 - n0
                ps = mmrot.next()
                for k in range(KC):
                    sq = sqrot.next()
                    P.act(sq[:, 0:n], hT[:, k, n0:n1], AF.Square)
                    P.mm(ps[:, 0:n], ones_b, sq[:, 0:n], start=(k == 0), stop=(k == KC - 1))
                P.act(rstd[:, 0:n], ps[:, 0:n], AF.Sqrt, scale=1.0 / D, bias=epsc[:, 0:1])
                P.recip(rstd[:, 0:n], rstd[:, 0:n])
                for k in range(KC):
                    P.stt(uT[:, k, n0:n1], hT[:, k, n0:n1], pcol(gname, l * KC + k), rstd[:, 0:n], ALU.mult, ALU.mult)

        def add_resid(m, ti, n0, n1, ps):
            P.tt(hT[:, m, n0:n1], hT[:, m, n0:n1], ps[:, 0:n1 - n0], ALU.add)

        epsc = sb("epsc", [128, 1])
        P.memset(epsc[:], EPS)

        def stage(name):
            P.tag = name
            if stop_at == name:
                raise _Stop()

        stage("setup")
        try:
          for pas in range(NPASS):
              t0 = pas * NPP
              has_s = pas == NPASS - 1
              first = pas == 0
              last = pas == NPASS - 1
              ncol = NPP + (NS if has_s else 0)
              tiles = [(i * 512, (i + 1) * 512) for i in range(NPP // 512)] + ([(NPP, NT)] if has_s else [])
              nchunk = NPP // 128
              tiles_lin = tiles if not has_s else [(0, 384), (384, 768), (768, NT)]

              ar.reset()
              xrot = Rot([ar.get([D]) for _ in range(2)])
              for blk in range(nchunk):
                  xt = xrot.next()
                  P.dma(xt, I["x_p"][t0 + blk * 128:t0 + (blk + 1) * 128, :], q="sp")
                  for k in range(KC):
                      P.tr(psA[:, k * 128:(k + 1) * 128], xt[:, k * 128:(k + 1) * 128], ident_f)
                  P.copy(hT[:, 0:4, blk * 128:(blk + 1) * 128], psA[:, 0:512].rearrange("p (k n) -> p k n", k=4), eng="act")
                  P.copy(hT[:, 4:8, blk * 128:(blk + 1) * 128], psA[:, 512:1024].rearrange("p (k n) -> p k n", k=4), eng="dve")
              if has_s:
                  xt = xrot.next()
                  P.dma(xt[0:64, :], I["x_s"], q="sp")
                  for k in range(KC):
                      P.tr(psA[:, k * 64:(k + 1) * 64], xt[0:64, k * 128:(k + 1) * 128], ident_f[0:64, 0:64])
                  P.copy(hT[:, :, NPP:NT], psA[:, 0:512].rearrange("p (k n) -> p k n", k=8), eng="act")
              stage("loadx")

              for l in range(n_layers):
                  ar.reset()
                  sqrot = Rot([ar.get([512], BF16) for _ in range(2)])
                  rstd = ar.get([512])
                  arena_base = ar.off
                  rmsnorm("nm", l, tiles)
                  stage("norm1")
                  P.dma(POOLW[:], I["pool_w"][l].rearrange("g c d -> c g d"), q="pool")
                  for h2 in range(2):
                      P.dma(WA[h2 * 64:(h2 + 1) * 64, :, h2 * 64:(h2 + 1) * 64],
                            I["lru_wa"][l].rearrange("(j two) i o -> two i j o", two=2)[h2], q="pool")
                      P.dma(WX[h2 * 64:(h2 + 1) * 64, :, h2 * 64:(h2 + 1) * 64],
                            I["lru_wx"][l].rearrange("(j two) i o -> two i j o", two=2)[h2], q="pool")
                  P.dma(WDT[:], I["w_in"][l][:, OFF_DT:OFF_DT + 16].rearrange("(k p) m -> p k m", p=128), q="pool")
                  w_in_l = I["w_in"][l]
                  w_out_l = I["w_out"][l]
                  rhs_u = lambda k, n0, n1: uT[:, k, n0:n1]
                  deferred = []

                  def flush():
                      while deferred:
                          deferred.pop(0)()

                  if has_s:
                      SPin = ar.get([2, 512], parts=120)
                      P.dma(SPin, I["st_pool"][l].rearrange("(h b) r c -> (b r) h c", h=2), q="sp")
                      SPout = ar.get([2, 512], parts=120)
                      LCin = ar.get([512], parts=48)
                      P.dma(LCin, I["st_lconv"][l].rearrange("b r c -> (b r) c"), q="sp")
                      LCout = ar.get([512], parts=48)
                      LRin = ar.get([512], parts=16)
                      P.dma(LRin, I["st_lru"][l], q="sp")
                      LRout = ar.get([512], parts=16)
                      lru_h0 = ar.get([4, 16])
                  sbase = ar.off

                  RAWP = 15 + NPP + 16 * 19 + 2
                  rawrot = Rot([ar.get([RAWP]) for _ in range(2)])
                  tA = ar.get([RAWP])
                  tB = ar.get([RAWP])
                  plrot = Rot([ar.get([NT], BF16) for _ in range(2)])
                  tmp15 = ar.get([16])
                  tmp240 = ar.get([240])
                  cur_raw = [None]
                  SOFFP = 15 + NPP
                  ya = A[:, 0:4, :]

                  def svw(t, off, nb, w):
                      return t[:, off:off + nb * w].rearrange("p (b r) -> p b r", b=nb)

                  def pool_consume(j, ti, n0, n1, ps):
                      if ti == 0:
                          cur_raw[0] = rawrot.next()
                          raw = cur_raw[0]
                          P.copy(raw[:, 0:15], pool_tail[:, l, j, :], eng="dve")
                          if has_s:
                              pp = smrot.next()
                              for h in range(2):
                                  P.tr(pp[:, h * 120:(h + 1) * 120], SPin[:, h, j * 128:(j + 1) * 128], ident_f[0:120, 0:120])
                              P.copy(svw(raw, SOFFP, 16, 19)[:, :, 0:15], pp[:, 0:240].rearrange("p (b r) -> p b r", b=16), eng="dve")
                      raw = cur_raw[0]
                      if n0 < NPP:
                          P.copy(raw[:, 15 + n0:15 + n1], ps[:, 0:n1 - n0], eng="act")
                      else:
                          P.copy(svw(raw, SOFFP, 16, 19)[:, :, 15:19], ps[:, 0:64].rearrange("p (b r) -> p b r", b=16), eng="act")
                      if ti != len(tiles) - 1:
                          return
                      flush()
                      w = 2 << j
                      Wd = 15 + NPP
                      cur = raw
                      bufs = [tA, tB]
                      for lev in range(j + 1):
                          sh = 1 << lev
                          lo = (2 << lev) - 1
                          dst = bufs[lev % 2]
                          P.tt(dst[:, lo:Wd], cur[:, lo:Wd], cur[:, lo - sh:Wd - sh], ALU.add)
                          if has_s:
                              P.tt(svw(dst, SOFFP, 16, 19)[:, :, lo:19], svw(cur, SOFFP, 16, 19)[:, :, lo:19],
                                   svw(cur, SOFFP, 16, 19)[:, :, lo - sh:19 - sh], ALU.add)
                          cur = dst
                      pl = plrot.next()
                      P.stt(pl[:, 0:NPP], cur[:, 15:15 + NPP], 1.0 / w, raw[:, 15:15 + NPP], ALU.mult, ALU.subtract)
                      if first:
                          P.tt(tmp15[:, 0:15], cur[:, 15:30], rc_tab[:, j, 0:15], ALU.mult)
                          P.tt(pl[:, 0:15], tmp15[:, 0:15], raw[:, 15:30], ALU.subtract)
                      if has_s:
                          P.stt(pl[:, NPP:NT].rearrange("p (b r) -> p b r", b=16), svw(cur, SOFFP, 16, 19)[:, :, 15:19], 1.0 / w,
                                svw(raw, SOFFP, 16, 19)[:, :, 15:19], ALU.mult, ALU.subtract)
                      P.copy(pool_tail[:, l, j, :], raw[:, NPP:NPP + 15], eng="dve")
                      if has_s:
                          P.copy(tmp240.rearrange("p (b r) -> p b r", b=16), svw(raw, SOFFP, 16, 19)[:, :, 4:19], eng="dve")

                      def pe_part(j=j, pl=pl):
                          for (m0, m1) in tiles:
                              ps2 = mmrot.next()
                              P.mm(ps2[:, 0:m1 - m0], POOLW[:, j, :], pl[:, m0:m1])
                              P.act(ya[:, j, m0:m1], ps2[:, 0:m1 - m0], AF.Identity, scale=pcol("pscale", l * 4 + j))
                          if has_s:
                              pp = smrot.next()
                              for h in range(2):
                                  P.tr(pp[0:120, h * 128:(h + 1) * 128], tmp240[:, h * 120:(h + 1) * 120], ident_f)
                              P.copy(SPout[:, :, j * 128:(j + 1) * 128], pp[0:120, 0:256].rearrange("p (h c) -> p h c", h=2), eng="act")

                      deferred.append(pe_part)

                  linear([(w_in_l[:, OFF_POOL:OFF_POOL + 512], KC)], 512, 512, tiles, rhs_u, pool_consume)
                  flush()
                  stage("pool")
                  if has_s:
                      P.dma(O["s_pool"][l].rearrange("(h b) r c -> (b r) h c", h=2), SPout, q="sp", is_output=True)

                  ar.off = sbase
                  RAWL = 3 + NPP + 16 * 7 + 1
                  SOFFL = 3 + NPP
                  rawrot = Rot([ar.get([RAWL]) for _ in range(2)])
                  acc = ar.get([NT])
                  xcb = ar.get([NT], BF16)
                  rr_ = ar.get([NT])
                  ii_ = ar.get([NT])
                  mm_ = ar.get([NT])
                  tmp48 = ar.get([48])
                  tmp16 = ar.get([16])
                  gel = A[:, 8:12, :]
                  yc = A[:, 4:8, :]
                  if has_s:
                      pp = smrot.next()
                      for j in range(4):
                          P.tr(pp[:, j * 16:(j + 1) * 16], LRin[:, j * 128:(j + 1) * 128], ident_f[0:16, 0:16])
                      P.copy(lru_h0, pp[:, 0:64].rearrange("p (j b) -> p j b", j=4), eng="act")

                  def lru_consume(m, ti, n0, n1, ps):
                      n = n1 - n0
                      if m < 4:
                          P.act(gel[:, m, n0:n1], ps[:, 0:n], AF.Gelu_apprx_tanh)
                          return
                      j = m - 4
                      if ti == 0:
                          cur_raw[0] = rawrot.next()
                          raw = cur_raw[0]
                          P.copy(raw[:, 0:3], lconv_tail[:, l, j, :], eng="dve")
                          if has_s:
                              pp = smrot.next()
                              P.tr(pp[:, 0:48], LCin[:, j * 128:(j + 1) * 128], ident_f[0:48, 0:48])
                              P.copy(svw(raw, SOFFL, 16, 7)[:, :, 0:3], pp[:, 0:48].rearrange("p (b r) -> p b r", b=16), eng="dve")
                      raw = cur_raw[0]
                      if n0 < NPP:
                          P.copy(raw[:, 3 + n0:3 + n1], ps[:, 0:n], eng="act")
                      else:
                          P.copy(svw(raw, SOFFL, 16, 7)[:, :, 3:7], ps[:, 0:64].rearrange("p (b r) -> p b r", b=16), eng="act")
                      if ti != len(tiles) - 1:
                          return
                      flush()
                      for k in range(4):
                          wk = pcol("lcw", (l * 4 + k) * 4 + j)
                          if k == 0:
                              P.ts(acc[:, 0:NPP], raw[:, 0:NPP], wk, ALU.mult, pcol("lcb", l * 4 + j), ALU.add)
                              if has_s:
                                  P.ts(acc[:, NPP:NT].rearrange("p (b r) -> p b r", b=16), svw(raw, SOFFL, 16, 7)[:, :, 0:4], wk, ALU.mult,
                                       pcol("lcb", l * 4 + j), ALU.add)
                          else:
                              P.stt(acc[:, 0:NPP], raw[:, k:k + NPP], wk, acc[:, 0:NPP], ALU.mult, ALU.add)
                              if has_s:
                                  av = acc[:, NPP:NT].rearrange("p (b r) -> p b r", b=16)
                                  P.stt(av, svw(raw, SOFFL, 16, 7)[:, :, k:k + 4], wk, av, ALU.mult, ALU.add)
                      P.copy(lconv_tail[:, l, j, :], raw[:, NPP:NPP + 3], eng="dve")
                      if has_s:
                          P.copy(tmp48.rearrange("p (b r) -> p b r", b=16), svw(raw, SOFFL, 16, 7)[:, :, 4:7], eng="dve")
                      P.copy(xcb[:, 0:ncol], acc[:, 0:ncol], eng="act")

                      def pe_part(j=j):
                          for (m0, m1) in tiles:
                              psr = mmrot.next()
                              P.mm(psr[:, 0:m1 - m0], WA[:, j, :], xcb[:, m0:m1])
                              P.act(rr_[:, m0:m1], psr[:, 0:m1 - m0], AF.Sigmoid, bias=pcol("ba", l * 4 + j))
                              psi = mmrot.next()
                              P.mm(psi[:, 0:m1 - m0], WX[:, j, :], xcb[:, m0:m1])
                              P.act(ii_[:, m0:m1], psi[:, 0:m1 - m0], AF.Sigmoid, bias=pcol("bx", l * 4 + j))
                          if has_s:
                              pp = smrot.next()
                              P.tr(pp[0:48, 0:128], tmp48[:, 0:48], ident_f)
                              P.copy(LCout[:, j * 128:(j + 1) * 128], pp[0:48, 0:128], eng="act")
                          nn = ncol
                          P.act(rr_[:, 0:nn], rr_[:, 0:nn], AF.Exp, scale=C1[:, l * 4 + j:l * 4 + j + 1])
                          P.tt(mm_[:, 0:nn], rr_[:, 0:nn], rr_[:, 0:nn], ALU.mult)
                          P.act(mm_[:, 0:nn], mm_[:, 0:nn], AF.Sqrt, scale=-1.0, bias=1.0)
                          if first:
                              P.memset(mm_[:, 0:1], 1.0)
                          P.tt(ii_[:, 0:nn], ii_[:, 0:nn], mm_[:, 0:nn], ALU.mult)
                          P.tt(ii_[:, 0:nn], ii_[:, 0:nn], acc[:, 0:nn], ALU.mult)
                          if has_s:
                              a_s = rr_[:, NPP:NT].rearrange("p (b r) -> p b r", b=16)
                              b_s = ii_[:, NPP:NT].rearrange("p (b r) -> p b r", b=16)
                              t16 = tmp16.rearrange("p (b o) -> p b o", o=1)
                              P.tt(t16, a_s[:, :, 0:1], lru_h0[:, j, :].rearrange("p (b o) -> p b o", o=1), ALU.mult)
                              P.tt(b_s[:, :, 0:1], b_s[:, :, 0:1], t16, ALU.add)
                              P.memset(a_s[:, :, 0:1], 0.0)
                          P.scan(mm_[:, 0:NPP], rr_[:, 0:NPP], ii_[:, 0:NPP], lru_h[:, l, j:j + 1], ALU.mult, ALU.add)
                          if has_s:
                              P.scan(mm_[:, NPP:NT], rr_[:, NPP:NT], ii_[:, NPP:NT], 0.0, ALU.mult, ALU.add)
                          P.copy(lru_h[:, l, j:j + 1], mm_[:, NPP - 1:NPP], eng="dve")
                          if has_s:
                              P.copy(tmp16.rearrange("p (b o) -> p b o", o=1), mm_[:, NPP:NT].rearrange("p (b r) -> p b r", b=16)[:, :, 3:4], eng="dve")
                              pp = smrot.next()
                              P.tr(pp[0:16, 0:128], tmp16[:, 0:16], ident_f)
                              P.copy(LRout[:, j * 128:(j + 1) * 128], pp[0:16, 0:128], eng="act")
                          P.tt(yc[:, j, 0:nn], mm_[:, 0:nn], gel[:, j, 0:nn], ALU.mult)

                      deferred.append(pe_part)

                  linear([(w_in_l[:, OFF_GATE:OFF_GATE + 1024], KC)], 1024, 512, tiles, rhs_u, lru_consume)
                  flush()
                  stage("lru")
                  if has_s:
                      P.dma(O["s_lconv"][l].rearrange("b r c -> (b r) c"), LCout, q="sp", is_output=True)
                      P.dma(O["s_lru"][l], LRout, q="sp", is_output=True)

                  zs = A[:, 12:20, :]

                  def z_consume(m, ti, n0, n1, ps):
                      P.act(zs[:, m, n0:n1], ps[:, 0:n1 - n0], AF.Silu)

                  linear([(w_in_l[:, OFF_Z:OFF_Z + 1024], KC)], 1024, 512, tiles_lin, rhs_u, z_consume)

                  linear([(w_out_l[0:512, :], 4), (w_out_l[1536:2048, :], 4)], D, 512, tiles_lin,
                         lambda k, n0, n1: A[:, k, n0:n1], add_resid)
                  stage("wout1")

                  ar.off = arena_base
                  xbc = A[:, 0:12, :]
                  zs = A[:, 12:20, :]
                  dtb = RB[:, l * 16:(l + 1) * 16]
                  Abc = RB[:, 64 + l * 16:64 + (l + 1) * 16]
                  dt_all = ar.get([nchunk + 1, 16])
                  dA_all = ar.get([nchunk + 1, 16])
                  dAh = ar.get([nchunk + 1, 16], BF16)
                  dAl = ar.get([nchunk + 1, 16], BF16)
                  dAt = ar.get([nchunk + 1, 16])
                  acs_all = ar.get([nchunk, 16])
                  decst_all = ar.get([nchunk, 16])
                  eacs_all = ar.get([nchunk, 16])
                  cd_all = ar.get([nchunk, 16])
                  dtdec_all = ar.get([nchunk, 16])
                  cc_s = dict(acs_tok=ar.get([16]), decst=ar.get([16]), eacs=ar.get([16]), cdb=ar.get([16]), dtdec=ar.get([16]))
                  Rhr = Rot([ar.get([8, 128], BF16) for _ in range(3)])
                  Rlr = Rot([ar.get([8, 128], BF16) for _ in range(3)])
                  Er = Rot([ar.get([8, 128], BF16) for _ in range(2)])
                  xdtr = Rot([ar.get([512], BF16) for _ in range(2)])
                  xddr = Rot([ar.get([512], BF16) for _ in range(2)])
                  yofr = Rot([ar.get([512], BF16) for _ in range(2)])
                  Btr = Rot([ar.get([128], BF16) for _ in range(2)])
                  CBr = Rot([ar.get([128], BF16) for _ in range(2)])
                  hcd = ar.get([512])

                  sbase3 = ar.off
                  if has_s:
                      SCin = ar.get([1536], parts=48)
                      P.dma(SCin, I["st_sconv"][l].rearrange("b r c -> (b r) c"), q="sp")
                      SCout = ar.get([1536], parts=48)
                  sbase2 = ar.off
                  RAWS = 3 + NPP + 16 * 7 + 1
                  rawrot = Rot([ar.get([RAWS]) for _ in range(2)])
                  accr = Rot([ar.get([NT]) for _ in range(2)])
                  tmp48 = ar.get([48])

                  def xbc_consume(j, ti, n0, n1, ps):
                      n = n1 - n0
                      if ti == 0:
                          cur_raw[0] = rawrot.next()
                          raw = cur_raw[0]
                          P.copy(raw[:, 0:3], sconv_tail[:, l, j, :], eng="dve")
                          if has_s:
                              pp = smrot.next()
                              P.tr(pp[:, 0:48], SCin[:, j * 128:(j + 1) * 128], ident_f[0:48, 0:48])
                              P.copy(svw(raw, SOFFL, 16, 7)[:, :, 0:3], pp[:, 0:48].rearrange("p (b r) -> p b r", b=16), eng="dve")
                      raw = cur_raw[0]
                      if n0 < NPP:
                          P.copy(raw[:, 3 + n0:3 + n1], ps[:, 0:n], eng="act")
                      else:
                          P.copy(svw(raw, SOFFL, 16, 7)[:, :, 3:7], ps[:, 0:64].rearrange("p (b r) -> p b r", b=16), eng="act")
                      if ti != len(tiles) - 1:
                          return
                      flush()
                      acc = accr.next()
                      for k in range(4):
                          wk = pcol("scw", (l * 4 + k) * 12 + j)
                          if k == 0:
                              P.ts(acc[:, 0:NPP], raw[:, 0:NPP], wk, ALU.mult, pcol("scb", l * 12 + j), ALU.add)
                              if has_s:
                                  P.ts(acc[:, NPP:NT].rearrange("p (b r) -> p b r", b=16), svw(raw, SOFFL, 16, 7)[:, :, 0:4], wk, ALU.mult,
                                       pcol("scb", l * 12 + j), ALU.add)
                          else:
                              P.stt(acc[:, 0:NPP], raw[:, k:k + NPP], wk, acc[:, 0:NPP], ALU.mult, ALU.add)
                              if has_s:
                                  av = acc[:, NPP:NT].rearrange("p (b r) -> p b r", b=16)
                                  P.stt(av, svw(raw, SOFFL, 16, 7)[:, :, k:k + 4], wk, av, ALU.mult, ALU.add)
                      P.copy(sconv_tail[:, l, j, :], raw[:, NPP:NPP + 3], eng="dve")
                      deferred.append(lambda j=j, acc=acc: P.act(xbc[:, j, 0:ncol], acc[:, 0:ncol], AF.Silu))
                      if has_s:
                          P.copy(tmp48.rearrange("p (b r) -> p b r", b=16), svw(raw, SOFFL, 16, 7)[:, :, 4:7], eng="dve")
                          pp = smrot.next()
                          P.tr(pp[0:48, 0:128], tmp48[:, 0:48], ident_f)
                          P.copy(SCout[:, j * 128:(j + 1) * 128], pp[0:48, 0:128], eng="act")

                  linear([(w_in_l[:, OFF_XBC:OFF_XBC + 1536], KC)], 1536, 512, tiles, rhs_u, xbc_consume)
                  flush()
                  if has_s:
                      P.dma(O["s_sconv"][l].rearrange("b r c -> (b r) c"), SCout, q="sp", is_output=True)

                  stage("xbcz")

                  psd = smrot.next()
                  for c in range(nchunk):
                      for k in range(KC):
                          P.mm(psd[:, c * 16:(c + 1) * 16], uT[:, k, c * 128:(c + 1) * 128], WDT[:, k, :], start=(k == 0), stop=(k == KC - 1))
                  P.tt(dt_all[:, 0:nchunk, :], psd[:, 0:nchunk * 16].rearrange("p (c h) -> p c h", h=16),
                       bcast(dtb.rearrange("p (o h) -> p o h", o=1), [128, nchunk, 16]), ALU.add)
                  if has_s:
                      psd2 = smrot.next()
                      for k in range(KC):
                          P.mm(psd2[0:64, 0:16], uT[:, k, NPP:NT], WDT[:, k, :], start=(k == 0), stop=(k == KC - 1))
                      P.memset(dt_all[:, nchunk, :], 0.0)
                      P.tt(dt_all[0:64, nchunk, :], psd2[0:64, 0:16], dtb[0:64, :], ALU.add)
                  nch_all = nchunk + (1 if has_s else 0)
                  P.act(dt_all[:, 0:nch_all, :], dt_all[:, 0:nch_all, :], AF.Exp)
                  P.act(dt_all[:, 0:nch_all, :], dt_all[:, 0:nch_all, :], AF.Ln, bias=1.0)
                  P.tt(dA_all[:, 0:nch_all, :], dt_all[:, 0:nch_all, :], bcast(Abc.rearrange("p (o h) -> p o h", o=1), [128, nch_all, 16]), ALU.mult)
                  P.copy(dAh[:, 0:nch_all, :], dA_all[:, 0:nch_all, :], eng="dve")
                  P.copy(dAt[:, 0:nch_all, :], dAh[:, 0:nch_all, :], eng="dve")
                  P.tt(dAl[:, 0:nch_all, :], dA_all[:, 0:nch_all, :], dAt[:, 0:nch_all, :], ALU.subtract)
                  for hp in range(8):
                      P.ts(DIAGD[:, hp, :], ident_b, DCOL[:, l, hp:hp + 1], ALU.mult)

                  if first:
                      P.memset(ssdT[:], 0.0)
                      P.memset(ssdT_b[:], 0.0)
                  else:
                      P.dma(ssdT[:], spill[l], q="sp")
                      P.copy(ssdT_b[:], ssdT[:], eng="act")
                  stage("dt")

                  def chunk_common(c, cols, K):
                      L = K["L"]
                      CC = cc_s
                      acs_tok, decst, eacs, cdb, dtdec = CC["acs_tok"], CC["decst"], CC["eacs"], CC["cdb"], CC["dtdec"]
                      dA_c = dA_all[0:L, c, :]
                      ps1 = smrot.next()
                      P.mm(ps1[0:L, 0:16], K["U"], dA_c)
                      P.copy(acs_tok[0:L, :], ps1[0:L, 0:16], eng="dve")
                      P.mm(ps1[0:L, 16:32], K["sel"], acs_tok[0:L, :])
                      P.tt(decst[0:L, :], ps1[0:L, 16:32], acs_tok[0:L, :], ALU.subtract)
                      P.act(decst[0:L, :], decst[0:L, :], AF.Exp)
                      P.act(eacs[0:L, :], acs_tok[0:L, :], AF.Exp)
                      P.tt(dtdec[0:L, :], dt_all[0:L, c, :], decst[0:L, :], ALU.mult)
                      return CC

                  def common_all():
                      nq = nchunk * 16
                      fl = lambda t: t.rearrange("p c h -> p (c h)")
                      ps = smrot.next()
                      P.mm(ps[:, 0:nq], CP["U"], fl(dA_all[:, 0:nchunk, :]))
                      P.copy(fl(acs_all), ps[:, 0:nq], eng="dve")
                      ps2 = smrot.next()
                      P.mm(ps2[:, 0:nq], CP["sel"], fl(acs_all))
                      P.tt(fl(decst_all), ps2[:, 0:nq], fl(acs_all), ALU.subtract)
                      P.act(fl(decst_all), fl(decst_all), AF.Exp)
                      P.act(fl(eacs_all), fl(acs_all), AF.Exp)
                      ps3 = smrot.next()
                      P.mm(ps3[:, 0:nq], ones_f, fl(dA_all[:, 0:nchunk, :]))
                      P.copy(fl(cd_all), ps3[:, 0:nq], eng="dve")
                      P.act(fl(cd_all), fl(cd_all), AF.Exp)
                      P.tt(fl(dtdec_all), fl(dt_all[:, 0:nchunk, :]), fl(decst_all), ALU.mult)

                  def cc_of(c):
                      return dict(acs_tok=acs_all[:, c, :], decst=decst_all[:, c, :], eacs=eacs_all[:, c, :], cdb=cd_all[:, c, :],
                                  dtdec=dtdec_all[:, c, :])

                  def make_r(c, g, K):
                      L = K["L"]
                      Rh = Rhr.next()
                      Rl = Rlr.next()
                      Ub = bcast(K["Ub"].rearrange("p (o l) -> p o l", o=1), [L, 8, L])
                      for k in range(8):
                          P.ts(Rh[0:L, k, 0:L], K["Ub"], dAh[0:L, c, 8 * g + k:8 * g + k + 1], ALU.mult)
                      P.tt(Rl[0:L, :, 0:L], Ub, bcast(dAl[0:L, c, 8 * g:8 * g + 8].rearrange("p (h o) -> p h o", o=1), [L, 8, L]), ALU.mult, eng="pool")
                      return Rh, Rl

                  def unit_pre(c, g, cols, K, CC, RR):
                      L = K["L"]
                      dtdec = CC["dtdec"]
                      c0, c1 = cols
                      Rh, Rl = RR
                      pxt = smrot.next()
                      pxb = pxt[:, 0:256].bitcast(BF16)
                      for j in range(4):
                          P.tr(pxb[0:L, j * 128:(j + 1) * 128], xbc[:, 4 * g + j, c0:c1], ident_b)
                      xdt = xdtr.next()
                      xdd = xddr.next()
                      pv = pxb[0:L, :].rearrange("p (h q) -> p h q", h=8)
                      P.tt(xdt[0:L, :].rearrange("p (h q) -> p h q", h=8), pv,
                           bcast(dt_all[0:L, c, 8 * g:8 * g + 8].rearrange("p (h o) -> p h o", o=1), [L, 8, 64]), ALU.mult)
                      P.tt(xdd[0:L, :].rearrange("p (h q) -> p h q", h=8), pv,
                           bcast(dtdec[0:L, 8 * g:8 * g + 8].rearrange("p (h o) -> p h o", o=1), [L, 8, 64]), ALU.mult)
                      pbt = smrot.next()
                      pbb = pbt[:, 0:256].bitcast(BF16)
                      P.tr(pbb[0:L, 0:128], xbc[:, 8 + g, c0:c1], ident_b)
                      Bt = Btr.next()
                      P.copy(Bt[0:L, :], pbb[0:L, 0:128], eng="act")
                      E = Er.next()
                      nh = 512 // L
                      for half in range(8 // nh):
                          seg = psA[0:L, half * 512:half * 512 + nh * L].rearrange("p (h l) -> p h l", h=nh)
                          hs = slice(half * nh, (half + 1) * nh)
                          dh = bcast(dAh[0:L, c, 8 * g:8 * g + 8][:, hs].rearrange("p (h o) -> p h o", o=1), [L, nh, L])
                          dl = bcast(dAl[0:L, c, 8 * g:8 * g + 8][:, hs].rearrange("p (h o) -> p h o", o=1), [L, nh, L])
                          P.mm(seg, ones_b[0:L, 0:L], Rh[0:L, hs, 0:L], start=True, stop=False)
                          P.mm(seg, ones_b[0:L, 0:L], Rl[0:L, hs, 0:L], start=False, stop=False)
                          P.mm(seg, K["negUb"], dh, start=False, stop=False)
                          P.mm(seg, K["negUb"], dl, start=False, stop=False)
                          P.mm(seg, ident_b[0:L, 0:L], bcast(K["negm"].rearrange("p (o l) -> p o l", o=1), [L, nh, L]), start=False, stop=True)
                          P.act(E[0:L, hs, 0:L], seg, AF.Exp)
                      pcb = smrot.next()
                      P.mm(pcb[0:L, 0:L], xbc[:, 8 + g, c0:c1], xbc[:, 10 + g, c0:c1])
                      CBs = CBr.next()
                      P.copy(CBs[0:L, 0:L], pcb[0:L, 0:L], eng="act")
                      return dict(xdt=xdt, xdd=xdd, Bt=Bt, M=E, E=E, CBs=CBs, CC=CC)

                  def unit_m(H, K):
                      L = K["L"]
                      P.tt(H["M"][0:L, :, 0:L], H["E"][0:L, :, 0:L], bcast(H["CBs"][0:L, 0:L].rearrange("p (o l) -> p o l", o=1), [L, 8, L]),
                           ALU.mult, eng="pool")

                  def unit_y(c, g, cols, K, H, pyo):
                      L = K["L"]
                      c0, c1 = cols
                      eacs = H["CC"]["eacs"]
                      yof = yofr.next()
                      P.tt(yof[0:L, :].rearrange("p (h q) -> p h q", h=8), pyo[0:L, 0:512].rearrange("p (h q) -> p h q", h=8),
                           bcast(eacs[0:L, 8 * g:8 * g + 8].rearrange("p (h o) -> p h o", o=1), [L, 8, 64]), ALU.mult)
                      Y = psY[:, 0:4 * L].rearrange("p (j l) -> p j l", j=4)
                      for k in range(8):
                          out = Y[64 * (k % 2):64 * (k % 2) + 64, k // 2, :]
                          hp = 4 * g + k // 2
                          P.mm(out, H["xdt"][0:L, 64 * k:64 * k + 64], H["M"][0:L, k, 0:L], start=True, stop=False)
                          P.mm(out, yof[0:L, 64 * k:64 * k + 64], ident_b[0:L, 0:L], start=False, stop=False)
                          P.mm(out, DIAGD[:, hp, 64 * (k % 2):64 * (k % 2) + 64], xbc[:, hp, c0:c1], start=False, stop=True)
                      P.copy(xbc[:, 4 * g:4 * g + 4, c0:c1], Y, eng="act")

                  units = [(c, g) for c in range(nchunk) for g in range(2)]
                  nu = len(units)
                  HH = {}
                  RT = {}
                  common_all()

                  def pre_r(u):
                      c, g = units[u]
                      RT[u] = make_r(c, g, CP)

                  def pre_a(u):
                      c, g = units[u]
                      cols = (c * 128, (c + 1) * 128)
                      HH[u] = unit_pre(c, g, cols, CP, cc_of(c), RT.pop(u))

                  def y_part(u):
                      c, g = units[u]
                      cols = (c * 128, (c + 1) * 128)
                      H = HH.pop(u)
                      cdb = H["CC"]["cdb"]
                      pyo = mmrot.next()
                      P.mm(pyo[:, 0:512], xbc[:, 10 + g, cols[0]:cols[1]], ssdT_b[:, 512 * g:512 * (g + 1)])
                      unit_y(c, g, cols, CP, H, pyo)
                      pdl = mmrot.next()
                      P.mm(pdl[:, 0:512], H["Bt"][:, :], H["xdd"][:, :])
                      sv = ssdT[:, 512 * g:512 * (g + 1)].rearrange("p (h q) -> p h q", h=8)
                      P.tt(hcd.rearrange("p (h q) -> p h q", h=8), sv, bcast(cdb[:, 8 * g:8 * g + 8].rearrange("p (h o) -> p h o", o=1), [128, 8, 64]), ALU.mult)
                      P.tt(ssdT[:, 512 * g:512 * (g + 1)], hcd, pdl[:, 0:512], ALU.add)
                      P.copy(ssdT_b[:, 512 * g:512 * (g + 1)], ssdT[:, 512 * g:512 * (g + 1)], eng="act")

                  pre_r(0)
                  if nu > 1:
                      pre_r(1)
                  pre_a(0)
                  unit_m(HH[0], CP)
                  for u in range(nu):
                      if u + 2 < nu:
                          pre_r(u + 2)
                      if u + 1 < nu:
                          pre_a(u + 1)
                      y_part(u)
                      if u + 1 < nu:
                          unit_m(HH[u + 1], CP)
                      stage("scan1")
                  if not last:
                      P.dma(spill[l], ssdT[:], q="sp")
                      stage("scanp")
                  else:
                      ar.off = sbase3
                      for half in range(2):
                          for j in range(4):
                              P.tr(psA[:, j * 128:(j + 1) * 128], ssdT[:, (half * 4 + j) * 128:(half * 4 + j + 1) * 128], ident_f)
                          so = ar.get([4, 128])
                          P.copy(so, psA[:, 0:512].rearrange("p (j n) -> p j n", j=4), eng="act")
                          P.dma(O["p_ssd"][l].rearrange("(hp two) q n -> (two q) hp n", two=2)[:, half * 4:(half + 1) * 4, :], so, q="sp", is_output=True)

                  if has_s:
                      ar.off = sbase3
                      c = nchunk
                      cols = (NPP, NT)
                      CCs = chunk_common(c, cols, CS)
                      Hs = [unit_pre(c, g, cols, CS, CCs, make_r(c, g, CS)) for g in range(2)]
                      for g in range(2):
                          unit_m(Hs[g], CS)
                      rblk = ar.get([16, 16], parts=64)
                      P.tt(rblk, bcast(dA_all[0:64, c, :].rearrange("p (o h) -> p o h", o=1), [64, 16, 16]),
                           bcast(blockind_f.rearrange("p (b o) -> p b o", o=1), [64, 16, 16]), ALU.mult)
                      pda = smrot.next()
                      P.mm(pda[:, 0:256], ones_f[0:64, :], rblk.rearrange("p b h -> p (b h)"))
                      dec_all = ar.get([16, 16])
                      P.act(dec_all.rearrange("p b h -> p (b h)"), pda[:, 0:256], AF.Exp)
                      Cm = []
                      Bblk = []
                      for g in range(2):
                          cm = ar.get([16, 64], BF16)
                          P.tt(cm, bcast(xbc[:, 10 + g, NPP:NT].rearrange("p (o l) -> p o l", o=1), [128, 16, 64]), BMASK[:], ALU.mult)
                          Cm.append(cm)
                          bb = ar.get([16, 128], BF16, parts=64)
                          P.tt(bb, bcast(Hs[g]["Bt"][0:64, :].rearrange("p (o n) -> p o n", o=1), [64, 16, 128]),
                               bcast(blockind_b.rearrange("p (b o) -> p b o", o=1), [64, 16, 128]), ALU.mult)
                          Bblk.append(bb)
                      S0r = Rot([ar.get([8, 128]) for _ in range(2)])
                      h0Tr = Rot([ar.get([D], BF16) for _ in range(2)])
                      pyo = [mmrot.next(), mmrot.next()]
                      for b in range(NSQ):
                          S0 = S0r.next()
                          P.dma(S0, I["st_ssd"][l, b].rearrange("(hp two) q n -> (two q) hp n", two=2), q="sp")
                          h0T = h0Tr.next()
                          for half in range(2):
                              for j in range(4):
                                  P.tr(psA[:, j * 128:(j + 1) * 128] if half == 0 else psA[:, 512 + j * 128:512 + (j + 1) * 128],
                                       S0[:, half * 4 + j, :], ident_f)
                          P.copy(h0T[:, 0:512], psA[:, 0:512], eng="act")
                          P.copy(h0T[:, 512:1024], psA[:, 512:1024], eng="dve")
                          for g in range(2):
                              P.mm(pyo[g][0:64, 0:512], Cm[g][:, b, :], h0T[:, 512 * g:512 * (g + 1)], start=(b == 0), stop=(b == NSQ - 1))
                          hn = S0
                          for h2 in range(2):
                              dsl = dec_all[h2 * 64:(h2 + 1) * 64, b, :].rearrange("p (hp two) -> p hp two", two=2)[:, :, h2:h2 + 1]
                              P.tt(hn[h2 * 64:(h2 + 1) * 64, :, :], S0[h2 * 64:(h2 + 1) * 64, :, :], bcast(dsl, [64, 8, 128]), ALU.mult)
                          for half in range(2):
                              pdl = smrot.next()
                              for j in range(4):
                                  hp = half * 4 + j
                                  g = hp // 4
                                  P.mm(pdl[:, j * 128:(j + 1) * 128], Hs[g]["xdd"][0:64, (hp % 4) * 128:(hp % 4 + 1) * 128], Bblk[g][:, b, :])
                              P.tt(hn[:, half * 4:(half + 1) * 4, :], hn[:, half * 4:(half + 1) * 4, :],
                                   pdl[:, 0:512].rearrange("p (j n) -> p j n", j=4), ALU.add)
                          P.dma(O["s_ssd"][l, b].rearrange("(hp two) q n -> (two q) hp n", two=2), hn, q="sp", is_output=True)
                      for g in range(2):
                          unit_y(c, g, cols, CS, Hs[g], pyo[g])

                  ar.off = arena_base
                  yg = A[:, 0:8, :]
                  for (n0, n1) in tiles_lin:
                      n = n1 - n0
                      ps = mmrot.next()
                      for k in range(KC):
                          P.tt(yg[:, k, n0:n1], yg[:, k, n0:n1], zs[:, k, n0:n1], ALU.mult)
                          sq = sqrot.next()
                          P.act(sq[:, 0:n], yg[:, k, n0:n1], AF.Square)
                          P.mm(ps[:, 0:n], ones_b, sq[:, 0:n], start=(k == 0), stop=(k == KC - 1))
                      P.act(rstd[:, 0:n], ps[:, 0:n], AF.Sqrt, scale=1.0 / D, bias=epsc[:, 0:1])
                      P.recip(rstd[:, 0:n], rstd[:, 0:n])
                      for k in range(KC):
                          P.stt(yg[:, k, n0:n1], yg[:, k, n0:n1], pcol("snorm", l * KC + k), rstd[:, 0:n], ALU.mult, ALU.mult)
                  linear([(w_out_l[512:1536, :], 8)], D, 512, tiles_lin, lambda k, n0, n1: A[:, k, n0:n1], add_resid)
                  stage("gate")
                  if dbg:
                      P.dma(O["dbg_h"][pas, l, 0], hT[:].rearrange("p k n -> p (k n)"), q="sp", is_output=True)

                  ar.off = arena_base
                  rmsnorm("nmem", l, tiles)
                  stage("norm2")
                  qT = A[:, 0:8, :]
                  oT = A[:, 8:16, :]
                  kvst = Rot([ar.get([512]) for _ in range(2)])
                  KT = ar.get([KC, 256], BF16)
                  VT = ar.get([2, D], BF16)

                  def kt_consume(m, ti, n0, n1, ps):
                      P.copy(KT[:, m, :], ps[:, 0:256], eng="act")

                  linear([(I["w_mem_k"][l], KC)], D, 512, [(0, 256)], lambda k, n0, n1: memT[:, k, n0:n1], kt_consume)
                  stage("kt")

                  def tokmajor_proj(wname, oname, keep):
                      groups = []
                      for gi in range(2):
                          def body(wb, gi=gi):
                              wv = wview(wb, 0, KC, 512)
                              for mc in range(2):
                                  ps = mmrot.next()
                                  for k in range(KC):
                                      P.mm(ps[:, 0:512], memT[:, k, mc * 128:(mc + 1) * 128], wv[:, k, :], start=(k == 0), stop=(k == KC - 1))
                                  if first:
                                      stg = kvst.next()
                                      P.copy(stg, ps[:, 0:512], eng="dve")
                                      if keep is not None:
                                          P.copy(keep[:, mc, gi * 512:(gi + 1) * 512], stg, eng="act")
                                      P.dma(O[oname][l, mc * 128:(mc + 1) * 128, gi * 512:(gi + 1) * 512], stg, q="sp", is_output=True)
                                  elif keep is not None:
                                      P.copy(keep[:, mc, gi * 512:(gi + 1) * 512], ps[:, 0:512], eng="act")
                          groups.append(dict(dmas=[(lambda wb: wview(wb, 0, KC, 512), I[wname][l][:, gi * 512:(gi + 1) * 512].rearrange("(k p) m -> p k m", p=128))], body=body))
                      stream(groups)

                  if first:
                      tokmajor_proj("w_mem_k", "p_mk", None)
                      stage("tmk")
                  tokmajor_proj("w_mem_v", "p_mv", VT)
                  stage("attnkv")

                  def q_consume(m, ti, n0, n1, ps):
                      P.copy(qT[:, m, n0:n1], ps[:, 0:n1 - n0], eng="act")

                  linear([(I["w_mem_q"][l], KC)], D, 512, tiles, rhs_u, q_consume)
                  stage("attnq")
                  SCL = 1.0 / 16.0
                  ptr = Rot([ar.get([2, 512], BF16) for _ in range(2)])
                  rcp = ar.get([512])
                  for (n0, n1) in tiles:
                      if n0 >= NPP:
                          continue
                      for h in range(4):
                          pt = ptr.next()
                          for mc in range(2):
                              sc = psA[:, mc * 512:(mc + 1) * 512]
                              for dc in range(2):
                                  P.mm(sc, KT[:, 2 * h + dc, mc * 128:(mc + 1) * 128], qT[:, 2 * h + dc, n0:n1], start=(dc == 0), stop=(dc == 1))
                              P.act(pt[:, mc, :], sc, AF.Exp, scale=SCL)
                          pss = smrot.next()
                          for mc in range(2):
                              P.mm(pss[:, 0:512], ones_b, pt[:, mc, :], start=(mc == 0), stop=(mc == 1))
                          P.recip(rcp, pss[:, 0:512])
                          for dc in range(2):
                              po = mmrot.next()
                              for mc in range(2):
                                  P.mm(po[:, 0:512], VT[:, mc, h * 256 + dc * 128:h * 256 + (dc + 1) * 128], pt[:, mc, :], start=(mc == 0), stop=(mc == 1))
                              P.tt(oT[:, 2 * h + dc, n0:n1], po[:, 0:512], rcp, ALU.mult)
                  if has_s:
                      kbr = Rot([ar.get([2, D], BF16) for _ in range(5)])
                      vbr = kbr
                      ktr = Rot([ar.get([KC, 256], BF16) for _ in range(2)])
                      pts = ar.get([NSQ, 2, 16], BF16)
                      rcs = ar.get([NSQ * 16])
                      vbs = []
                      pscs = mmrot.next()
                      scv = pscs[:, 0:512].rearrange("p (b m x) -> p b m x", b=NSQ, m=2)
                      for b in range(NSQ):
                          kb = kbr.next()
                          P.dma(kb, I["ck"][l, b].rearrange("(mc p) e -> p mc e", p=128), q="pool")
                          ktb = ktr.next()
                          for half in range(2):
                              pkb = psA[:, half * 512:(half + 1) * 512].bitcast(BF16)
                              for ee in range(4):
                                  e = half * 4 + ee
                                  for mc in range(2):
                                      P.tr(pkb[:, ee * 256 + mc * 128:ee * 256 + (mc + 1) * 128], kb[:, mc, e * 128:(e + 1) * 128], ident_b)
                              P.copy(ktb[:, half * 4:(half + 1) * 4, :], pkb.rearrange("p (e m) -> p e m", e=4), eng=("act" if half == 0 else "dve"))
                          for h in range(4):
                              for mc in range(2):
                                  for dc in range(2):
                                      P.mm(scv[:, b, mc, 4 * h:4 * h + 4], ktb[:, 2 * h + dc, mc * 128:(mc + 1) * 128],
                                           qT[:, 2 * h + dc, NPP + 4 * b:NPP + 4 * b + 4], start=(dc == 0), stop=(dc == 1))
                      P.act(pts.rearrange("p b m x -> p (b m x)"), pscs[:, 0:512], AF.Exp, scale=SCL)
                      pos = mmrot.next()
                      pss = smrot.next()
                      for b in range(NSQ):
                          vb = vbr.next()
                          P.dma(vb, I["cv"][l, b].rearrange("(mc p) e -> p mc e", p=128), q="pool")
                          for mc in range(2):
                              P.mm(pss[:, b * 16:(b + 1) * 16], ones_b, pts[:, b, mc, :], start=(mc == 0), stop=(mc == 1))
                          for e in range(KC):
                              h = e // 2
                              for mc in range(2):
                                  P.mm(pos[:, e * 64 + 4 * b:e * 64 + 4 * b + 4], vb[:, mc, e * 128:(e + 1) * 128], pts[:, b, mc, 4 * h:4 * h + 4],
                                       start=(mc == 0), stop=(mc == 1))
                      P.recip(rcs, pss[:, 0:256])
                      rv = rcs.rearrange("p (b h r) -> p b h r", b=NSQ, h=4)
                      for h in range(4):
                          for dc in range(2):
                              e = 2 * h + dc
                              P.tt(oT[:, e, NPP:NT].rearrange("p (b r) -> p b r", b=NSQ), pos[:, e * 64:(e + 1) * 64].rearrange("p (b r) -> p b r", b=NSQ),
                                   rv[:, :, h, :], ALU.mult)
                  linear([(I["w_mem_o"][l], KC)], D, 512, tiles_lin, lambda k, n0, n1: oT[:, k, n0:n1], add_resid)
                  stage("attn")
                  if dbg:
                      P.dma(O["dbg_h"][pas, l, 1], hT[:].rearrange("p k n -> p (k n)"), q="sp", is_output=True)

                  ar.off = arena_base
                  rmsnorm("nffn", l, tiles_lin)
                  sgr = Rot([ar.get([512]) for _ in range(2)])
                  groups = []
                  for gi in range(FC // 2):
                      def body(wb, gi=gi):
                          wv = wview(wb, 0, KC, 512)
                          for fi in range(2):
                              f = gi * 2 + fi
                              for (n0, n1) in tiles_lin:
                                  n = n1 - n0
                                  pg = mmrot.next()
                                  for k in range(KC):
                                      P.mm(pg[:, 0:n], wv[:, k, fi * 128:(fi + 1) * 128], uT[:, k, n0:n1], start=(k == 0), stop=(k == KC - 1))
                                  pu = mmrot.next()
                                  for k in range(KC):
                                      P.mm(pu[:, 0:n], wv[:, k, 256 + fi * 128:256 + (fi + 1) * 128], uT[:, k, n0:n1], start=(k == 0), stop=(k == KC - 1))
                                  sg = sgr.next()
                                  P.act(sg[:, 0:n], pg[:, 0:n], AF.Silu)
                                  P.tt(A[:, f, n0:n1], sg[:, 0:n], pu[:, 0:n], ALU.mult)
                      dmas = [
                          (lambda wb: wview(wb, 0, KC, 512)[:, :, 0:256], I["w_ffn_gate"][l][:, gi * 256:(gi + 1) * 256].rearrange("(k p) m -> p k m", p=128)),
                          (lambda wb: wview(wb, 0, KC, 512)[:, :, 256:512], I["w_ffn_up"][l][:, gi * 256:(gi + 1) * 256].rearrange("(k p) m -> p k m", p=128)),
                      ]
                      groups.append(dict(dmas=dmas, body=body))
                  stream(groups)
                  linear([(I["w_ffn_down"][l], FC)], D, 128, tiles_lin, lambda k, n0, n1: A[:, k, n0:n1], add_resid)
                  stage("ffn")
                  if dbg:
                      P.dma(O["dbg_h"][pas, l, 2], hT[:].rearrange("p k n -> p (k n)"), q="sp", is_output=True)

                  if last:
                      ar.off = arena_base
                      o1 = ar.get([512], parts=15)
                      pp = smrot.next()
                      for j in range(4):
                          P.tr(pp[0:15, j * 128:(j + 1) * 128], pool_tail[:, l, j, :], ident_f)
                      P.copy(o1, pp[0:15, 0:512], eng="act")
                      P.dma(O["p_pool"][l], o1, q="sp", is_output=True)
                      o2 = ar.get([1536], parts=3)
                      for q3 in range(3):
                          pp = smrot.next()
                          for j in range(4):
                              P.tr(pp[0:3, j * 128:(j + 1) * 128], sconv_tail[:, l, q3 * 4 + j, :], ident_f)
                          P.copy(o2[:, q3 * 512:(q3 + 1) * 512], pp[0:3, 0:512], eng="act")
                      P.dma(O["p_sconv"][l], o2, q="sp", is_output=True)
                      o3 = ar.get([512], parts=3)
                      pp = smrot.next()
                      for j in range(4):
                          P.tr(pp[0:3, j * 128:(j + 1) * 128], lconv_tail[:, l, j, :], ident_f)
                      P.copy(o3, pp[0:3, 0:512], eng="act")
                      P.dma(O["p_lconv"][l], o3, q="sp", is_output=True)
                      o4 = ar.get([512], parts=1)
                      pp = smrot.next()
                      for j in range(4):
                          P.tr(pp[0:1, j * 128:(j + 1) * 128], lru_h[:, l, j:j + 1], ident_f)
                      P.copy(o4, pp[0:1, 0:512], eng="act")
                      P.dma(O["p_lru"][l:l + 1, :], o4, q="sp", is_output=True)

              ar.reset()
              sqrot = Rot([ar.get([512], BF16) for _ in range(2)])
              rstd = ar.get([512])
              ytr = Rot([ar.get([KC, 128]) for _ in range(2)])
              yor = Rot([ar.get([D]) for _ in range(2)])
              for (n0, n1) in tiles:
                  n = n1 - n0
                  ps = mmrot.next()
                  for k in range(KC):
                      sq = sqrot.next()
                      P.act(sq[:, 0:n], hT[:, k, n0:n1], AF.Square)
                      P.mm(ps[:, 0:n], ones_b, sq[:, 0:n], start=(k == 0), stop=(k == KC - 1))
                  P.act(rstd[:, 0:n], ps[:, 0:n], AF.Sqrt, scale=1.0 / D, bias=epsc[:, 0:1])
                  P.recip(rstd[:, 0:n], rstd[:, 0:n])
                  bw = 128 if n0 < NPP else 64
                  for bi in range(n // bw):
                      yt = ytr.next()
                      for k in range(KC):
                          P.stt(yt[:, k, 0:bw], hT[:, k, n0 + bi * bw:n0 + (bi + 1) * bw], pcol("nfin", k), rstd[:, bi * bw:(bi + 1) * bw], ALU.mult, ALU.mult)
                      for k in range(KC):
                          P.tr(psA[0:bw, k * 128:(k + 1) * 128], yt[:, k, 0:bw], ident_f)
                      yo = yor.next()
                      P.copy(yo[0:bw, 0:512], psA[0:bw, 0:512], eng="act")
                      P.copy(yo[0:bw, 512:1024], psA[0:bw, 512:1024], eng="dve")
                      if n0 < NPP:
                          P.dma(O["y_p"][t0 + n0 + bi * 128:t0 + n0 + (bi + 1) * 128, :], yo, q="sp", is_output=True)
                      else:
                          P.dma(O["y_s"], yo[0:64, :], q="sp", is_output=True)
        except _Stop:
            pass
        P.emit()
    return nc


LRU_C = 8.0
_NC_CACHE = {}


def make_in_maps(inputs, cores):
    cst = build_consts()
    cst2 = np.zeros((128, 16, 64), np.float32)
    for b in range(16):
        cst2[:, b, 4 * b:4 * b + 4] = 1.0
    cst2 = cst2.reshape(128, 1024)
    maps = []
    for i in cores:
        s = slice(NSQ * i, NSQ * (i + 1))
        m = {
            "x_p": inputs["x_prompt"][i], "x_s": inputs["x_sample"][s].reshape(NS, D), "mem": inputs["mem_prompt"][i],
            "st_pool": inputs["state_pool"][:, s], "st_sconv": inputs["state_ssd_conv"][:, s], "st_ssd": inputs["state_ssd"][:, s],
            "st_lconv": inputs["state_lru_conv"][:, s], "st_lru": inputs["state_lru"][:, s],
            "ck": inputs["cache_mem_k"][:, s].reshape(DEPTH, NSQ, 256, D), "cv": inputs["cache_mem_v"][:, s].reshape(DEPTH, NSQ, 256, D),
            "cst": cst, "cst2": cst2,
        }
        for n, _ in IN_SPECS:
            if n not in m:
                m[n] = inputs[n]
        maps.append({k: np.ascontiguousarray(np.asarray(v, dtype=np.float32)) for k, v in m.items()})
    return maps


def kernel(**inputs):
    inputs = {k: np.asarray(v) for k, v in inputs.items()}
    if "nc" not in _NC_CACHE:
        _NC_CACHE["nc"] = build_program()
    nc = _NC_CACHE["nc"]
    maps = make_in_maps(inputs, list(range(N_CORES)))
    res = run_bass_kernel_spmd(nc, maps, core_ids=list(range(N_CORES)))
    R = res.results

    def cat(name, axis):
        return np.concatenate([np.asarray(r[name]) for r in R], axis=axis)

    y_p = np.stack([np.asarray(r["y_p"]) for r in R], 0)
    y_s = np.concatenate([np.asarray(r["y_s"]).reshape(NSQ, 4, D) for r in R], 0)
    outs = [y_p, y_s]
    for n in ("p_pool", "p_sconv", "p_ssd", "p_lconv", "p_lru"):
        outs.append(np.stack([np.asarray(r[n]) for r in R], 1))
    for n in ("p_mk", "p_mv"):
        outs.append(np.stack([np.asarray(r[n]).reshape(DEPTH, 256, 4, 256) for r in R], 1))
    for n in ("s_pool", "s_sconv", "s_ssd", "s_lconv", "s_lru"):
        outs.append(cat(n, 1))
    return tuple(np.ascontiguousarray(o.astype(np.float32)) for o in outs)
```

```python
import contextlib
import numpy as np
import concourse.bass as bass
import concourse.mybir as mybir
from concourse.bass_utils import run_bass_kernel_spmd

F32 = mybir.dt.float32
BF16 = mybir.dt.bfloat16
AF = mybir.ActivationFunctionType
ALU = mybir.AluOpType

ENGS = ("pe", "act", "dve", "pool", "sp")
NRING = 8

D = 1024
KC = 8
NPT = 2048
NSQ = 16
NS = 64
DEPTH = 4
NPASS = 2
NPP = NPT // NPASS
NT = NPP + NS
DFF = 2816
FC = 22
OFF_POOL, OFF_Z, OFF_XBC, OFF_DT, OFF_GATE, OFF_LRU, N_IN = 0, 512, 1536, 3072, 3088, 3600, 4112
NEG = -30000.0
EPS = 1e-6
NCST = 1104
WBUF = 4096
N_CORES = 8


def _isz(dt):
    return mybir.dt.size(dt)


def _region(ap):
    t = ap.tensor
    pairs = [tuple(x) for x in ap.ap]
    off = int(ap.offset)
    isz = _isz(ap.dtype)
    if type(t).__name__ == "DRamTensorHandle":
        ext = 0
        for st, cnt in pairs:
            ext += (cnt - 1) * abs(st)
        return (t.name, 0, 1, off * isz, (off + ext + 1) * isz)
    pst, pcnt = pairs[0]
    if pst == 0:
        pst = 1 << 40
    p_lo = off // pst
    f_lo = off % pst
    ext = 0
    for st, cnt in pairs[1:]:
        ext += (cnt - 1) * abs(st)
    b_lo, b_hi = f_lo * isz, (f_lo + ext + 1) * isz
    if type(t).__name__ == "PSumTensorHandle":
        b_lo = (b_lo // 2048) * 2048
        b_hi = ((b_hi + 2047) // 2048) * 2048
        return (t.name, (p_lo // 32) * 32, ((p_lo + pcnt + 31) // 32) * 32, b_lo, b_hi)
    return (t.name, p_lo, p_lo + pcnt, b_lo, b_hi)


class Op:
    __slots__ = ("eng", "idx", "fn", "deps", "is_dma", "needed", "signal", "clock", "ring", "tag")

    def __init__(self, eng, idx, fn, is_dma):
        self.eng = eng
        self.idx = idx
        self.fn = fn
        self.deps = []
        self.is_dma = is_dma
        self.needed = False
        self.signal = None
        self.clock = None
        self.ring = None


class Prog:
    def __init__(self, nc):
        self.nc = nc
        self.ops = {e: [] for e in ENGS}
        self.recs = {}
        self.ndma = {e: 0 for e in ENGS}
        self.dma_seq = {e: [] for e in ENGS}
        self.waited_dma = {e: set() for e in ENGS}
        self.out_dmas = []
        self.tag = ""

    def op(self, eng, fn, reads=(), writes=(), is_dma=False, extra_deps=()):
        lst = self.ops[eng]
        o = Op(eng, len(lst), fn, is_dma)
        o.tag = self.tag
        deps = {}
        ops = self.ops

        def add_dep(e2, i2):
            if e2 == eng and eng == "pe":
                return
            od = ops[e2][i2]
            if od.is_dma:
                deps[(e2, i2)] = True
            else:
                k = deps.get(e2)
                if k is None or i2 > k:
                    deps[e2] = i2

        rregs = [_region(ap) for ap in reads]
        wregs = [_region(ap) for ap in writes]
        for (name, pl, ph, fl, fh) in rregs:
            for r in self.recs.get(name, ()):
                if r[4] and r[0] < ph and pl < r[1] and r[2] < fh and fl < r[3]:
                    add_dep(r[5], r[6])
        for (name, pl, ph, fl, fh) in wregs:
            for r in self.recs.get(name, ()):
                if r[0] < ph and pl < r[1] and r[2] < fh and fl < r[3]:
                    add_dep(r[5], r[6])
        for (e2, i2) in extra_deps:
            add_dep(e2, i2)
        prev = lst[-1].clock if lst else {}
        clock = dict(prev)
        final = []
        for k in deps:
            if isinstance(k, tuple):
                if k in self.waited_dma[eng]:
                    continue
                final.append(k)
            else:
                i2 = deps[k]
                if clock.get(k, -1) >= i2:
                    continue
                final.append((k, i2))
        for (e2, i2) in final:
            od = ops[e2][i2]
            if od.is_dma:
                self.waited_dma[eng].add((e2, i2))
            else:
                if clock.get(e2, -1) < i2:
                    clock[e2] = i2
            for k2, v2 in od.clock.items():
                if clock.get(k2, -1) < v2:
                    clock[k2] = v2
        if is_dma:
            n = self.ndma[eng]
            o.ring = n
            self.ndma[eng] = n + 1
            if n >= NRING:
                pd = self.dma_seq[eng][n - NRING]
                if (eng, pd.idx) not in self.waited_dma[eng]:
                    final.append((eng, pd.idx))
                    self.waited_dma[eng].add((eng, pd.idx))
            self.dma_seq[eng].append(o)
        o.deps = final
        o.clock = clock
        lst.append(o)
        for (name, pl, ph, fl, fh) in rregs:
            L = self.recs.setdefault(name, [])
            if not is_dma:
                L[:] = [r for r in L if not ((not r[4]) and r[5] == eng and (not r[7]) and pl <= r[0] and r[1] <= ph and fl <= r[2] and r[3] <= fh)]
            L.append([pl, ph, fl, fh, False, eng, o.idx, is_dma])
        for (name, pl, ph, fl, fh) in wregs:
            L = self.recs.setdefault(name, [])
            L[:] = [r for r in L if not (pl <= r[0] and r[1] <= ph and fl <= r[2] and r[3] <= fh)]
            L.append([pl, ph, fl, fh, True, eng, o.idx, is_dma])
        return o

    def mm(self, out, lhsT, rhs, start=True, stop=True):
        return self.op("pe", lambda e: e.matmul(out, lhsT, rhs, start=start, stop=stop), [lhsT, rhs], [out])

    def tr(self, out, in_, ident):
        return self.op("pe", lambda e: e.transpose(out, in_, ident), [in_, ident], [out])

    def act(self, out, in_, func, bias=None, scale=None):
        reads = [in_]
        kw = {}
        if bias is not None:
            kw["bias"] = bias
            if not isinstance(bias, (int, float)):
                reads.append(bias)
        if scale is not None:
            kw["scale"] = scale
            if not isinstance(scale, (int, float)):
                reads.append(scale)
        return self.op("act", lambda e: e.activation(out, in_, func, **kw), reads, [out])

    def tt(self, out, a, b, op, eng="dve"):
        return self.op(eng, lambda e: e.tensor_tensor(out, a, b, op), [a, b], [out])

    def ts(self, out, a, s1, op0, s2=None, op1=None, eng="dve"):
        reads = [a]
        if not isinstance(s1, (int, float)):
            reads.append(s1)
        if s2 is not None and not isinstance(s2, (int, float)):
            reads.append(s2)
        if op1 is None:
            return self.op(eng, lambda e: e.tensor_scalar(out, a, s1, None, op0), reads, [out])
        return self.op(eng, lambda e: e.tensor_scalar(out, a, s1, s2, op0, op1), reads, [out])

    def stt(self, out, a, s, b, op0, op1):
        reads = [a, b]
        if not isinstance(s, (int, float)):
            reads.append(s)
        return self.op("dve", lambda e: e.scalar_tensor_tensor(out, a, s, b, op0, op1), reads, [out])

    def scan(self, out, d0, d1, init, op0, op1):
        reads = [d0, d1]
        if not isinstance(init, (int, float)):
            reads.append(init)
        return self.op("dve", lambda e: e.tensor_tensor_scan(out, d0, d1, init, op0, op1), reads, [out])

    def copy(self, out, in_, eng="dve"):
        if eng == "act":
            return self.op("act", lambda e: e.copy(out, in_), [in_], [out])
        return self.op(eng, lambda e: e.tensor_copy(out, in_), [in_], [out])

    def memset(self, ap, val, eng="dve"):
        return self.op(eng, lambda e: e.memset(ap, val), [], [ap])

    def recip(self, out, in_):
        return self.op("dve", lambda e: e.reciprocal(out, in_), [in_], [out])

    def dma(self, out, in_, q="sp", is_output=False, **kw):
        o = self.op(q, lambda e: e.dma_start(out=out, in_=in_, **kw), [in_], [out], is_dma=True)
        if is_output:
            self.out_dmas.append((q, o.idx))
        return o

    def emit(self):
        nc = self.nc
        self.op("sp", None, extra_deps=list(self.out_dmas))
        for e in ENGS:
            for o in self.ops[e]:
                for (e2, i2) in o.deps:
                    self.ops[e2][i2].needed = True
        for e in ENGS:
            c = 0
            for o in self.ops[e]:
                if o.needed and not o.is_dma:
                    c += 1
                    o.signal = c
        with contextlib.ExitStack() as st:
            sems = {e: st.enter_context(nc.semaphore("s_" + e)) for e in ENGS}
            rings = {e: [st.enter_context(nc.semaphore("r_%s_%d" % (e, i))) for i in range(NRING)] for e in ENGS if self.ndma[e]}
            block = st.enter_context(nc.Block())

            def run(engname, eng):
                for o in self.ops[engname]:
                    for (e2, i2) in o.deps:
                        od = self.ops[e2][i2]
                        if od.is_dma:
                            eng.wait_ge(rings[e2][od.ring % NRING], 16 * (od.ring // NRING + 1))
                        else:
                            eng.wait_ge(sems[e2], od.signal)
                    if o.fn is None:
                        continue
                    inst = o.fn(eng)
                    if o.is_dma:
                        inst.then_inc(rings[engname][o.ring % NRING], 16)
                    elif o.signal is not None:
                        inst.then_inc(sems[engname], 1)

            @block.tensor
            def _(eng):
                run("pe", eng)

            @block.scalar
            def _(eng):
                run("act", eng)

            @block.vector
            def _(eng):
                run("dve", eng)

            @block.gpsimd
            def _(eng):
                run("pool", eng)

            @block.sync
            def _(eng):
                run("sp", eng)


class Rot:
    def __init__(self, items):
        self.items = list(items)
        self.i = 0

    def next(self):
        x = self.items[self.i % len(self.items)]
        self.i += 1
        return x


class Arena:
    def __init__(self, tens, nwords):
        self.t = tens
        self.n = nwords
        self.off = 0

    def reset(self):
        self.off = 0

    def get(self, shape, dtype=F32, parts=128):
        n = 1
        for s in shape:
            n *= s
        words = (n * _isz(dtype) + 3) // 4
        words += words & 1
        assert self.off + words <= self.n, ("arena overflow", self.off, words, self.n)
        v = self.t[0:parts, self.off:self.off + words]
        self.off += words
        if dtype != F32:
            v = v.bitcast(dtype)
        v = v[:, 0:n]
        if len(shape) == 2:
            v = v.rearrange("p (a b) -> p a b", a=shape[0])
        elif len(shape) == 3:
            v = v.rearrange("p (a b c) -> p a b c", a=shape[0], b=shape[1])
        return v


def bcast(ap, shape):
    return ap.broadcast_to(list(shape))


def build_consts():
    c = np.zeros((128, NCST), np.float32)
    r = np.arange(128)
    c[:, 0:128] = np.eye(128)
    U = (r[:, None] <= r[None, :]).astype(np.float32)
    c[:, 128:256] = U
    c[:, 256:384] = -U
    sel = np.zeros((128, 128), np.float32)
    sel[127, :] = 1.0
    c[:, 384:512] = sel
    c[:, 512:640] = np.where(r[None, :] < r[:, None], NEG, 0.0)
    r64 = np.arange(64)
    same = (r64[:, None] // 4) == (r64[None, :] // 4)
    Us = (same & (r64[:, None] <= r64[None, :])).astype(np.float32)
    c[0:64, 640:704] = Us
    c[0:64, 704:768] = -Us
    sels = np.zeros((64, 64), np.float32)
    for s in range(64):
        sels[4 * (s // 4) + 3, s] = 1.0
    c[0:64, 768:832] = sels
    c[0:64, 832:896] = np.where(same & (r64[None, :] >= r64[:, None]), 0.0, NEG)
    bi = np.zeros((64, 16), np.float32)
    bi[r64, r64 // 4] = 1.0
    c[0:64, 896:912] = bi
    for g, w in enumerate((2, 4, 8, 16)):
        for t in range(16):
            c[:, 912 + g * 16 + t] = 1.0 / min(t + 1, w)
    c[:, 976:1104] = 1.0
    return c


IN_SPECS = [
    ("x_p", (NPT, D)), ("x_s", (NS, D)), ("mem", (256, D)),
    ("st_pool", (DEPTH, NSQ, 15, 512)), ("st_sconv", (DEPTH, NSQ, 3, 1536)), ("st_ssd", (DEPTH, NSQ, 16, 64, 128)),
    ("st_lconv", (DEPTH, NSQ, 3, 512)), ("st_lru", (DEPTH, NSQ, 512)),
    ("ck", (DEPTH, NSQ, 256, D)), ("cv", (DEPTH, NSQ, 256, D)),
    ("norm_mix", (DEPTH, D)), ("w_in", (DEPTH, D, N_IN)), ("pool_w", (DEPTH, 4, 128, 128)), ("pool_scale", (DEPTH, 512)),
    ("ssd_conv_w", (DEPTH, 4, 1536)), ("ssd_conv_b", (DEPTH, 1536)), ("ssd_dt_bias", (DEPTH, 16)), ("ssd_a_log", (DEPTH, 16)),
    ("ssd_d", (DEPTH, 16)), ("ssd_norm", (DEPTH, D)), ("lru_conv_w", (DEPTH, 4, 512)), ("lru_conv_b", (DEPTH, 512)),
    ("lru_wa", (DEPTH, 8, 64, 64)), ("lru_ba", (DEPTH, 8, 64)), ("lru_wx", (DEPTH, 8, 64, 64)), ("lru_bx", (DEPTH, 8, 64)),
    ("lru_lambda", (DEPTH, 512)), ("w_out", (DEPTH, 2048, D)), ("norm_mem", (DEPTH, D)), ("w_mem_q", (DEPTH, D, D)),
    ("w_mem_k", (DEPTH, D, D)), ("w_mem_v", (DEPTH, D, D)), ("w_mem_o", (DEPTH, D, D)), ("norm_ffn", (DEPTH, D)),
    ("w_ffn_gate", (DEPTH, D, DFF)), ("w_ffn_up", (DEPTH, D, DFF)), ("w_ffn_down", (DEPTH, DFF, D)), ("norm_final", (D,)),
    ("cst", (128, NCST)), ("cst2", (128, 1024)),
]
OUT_SPECS = [
    ("y_p", (NPT, D)), ("y_s", (NS, D)), ("p_pool", (DEPTH, 15, 512)), ("p_sconv", (DEPTH, 3, 1536)),
    ("p_ssd", (DEPTH, 16, 64, 128)), ("p_lconv", (DEPTH, 3, 512)), ("p_lru", (DEPTH, 512)),
    ("p_mk", (DEPTH, 256, D)), ("p_mv", (DEPTH, 256, D)),
    ("s_pool", (DEPTH, NSQ, 15, 512)), ("s_sconv", (DEPTH, NSQ, 3, 1536)), ("s_ssd", (DEPTH, NSQ, 16, 64, 128)),
    ("s_lconv", (DEPTH, NSQ, 3, 512)), ("s_lru", (DEPTH, NSQ, 512)),
]


class _Stop(Exception):
    pass


def build_program(n_layers=DEPTH, dbg=False, stop_at=None):
    nc = bass.Bass("TRN2", target_bir_lowering=False)
    I = {n: nc.dram_tensor(n, list(s), F32, kind="ExternalInput").ap() for n, s in IN_SPECS}
    O = {n: nc.dram_tensor(n, list(s), F32, kind="ExternalOutput").ap() for n, s in OUT_SPECS}
    spill = nc.dram_tensor("ssd_spill", [DEPTH, 128, D], F32, kind="Internal").ap()
    kvspill = nc.dram_tensor("kv_spill", [DEPTH, 2, 128, 2048], BF16, kind="Internal").ap()
    if dbg:
        O["dbg_h"] = nc.dram_tensor("dbg_h", [NPASS, DEPTH, 4, 128, KC * NT], F32, kind="ExternalOutput").ap()
    P = Prog(nc)
    with contextlib.ExitStack() as st:
        def sb(name, shape, dt=F32):
            return st.enter_context(nc.sbuf_tensor(name, list(shape), dt))

        def pst(name, shape, dt=F32):
            return st.enter_context(nc.psum_tensor(name, list(shape), dt))

        hT = sb("hT", [128, KC, NT])
        uT = sb("uT", [128, KC, NT], BF16)
        A = sb("A", [128, FC, NT], BF16)
        wbufs = [sb("wb%d" % i, [128, WBUF], BF16) for i in range(2)]
        cst_f = sb("cst_f", [128, NCST])
        cst_b = sb("cst_b", [128, NCST], BF16)
        PAR = sb("PAR", [128, 640])
        RB = sb("RB", [128, 3 * 64])
        DCOL = sb("DCOL", [128, DEPTH, 8])
        C1 = sb("C1", [128, 16])
        memT = sb("memT", [128, KC, 256], BF16)
        POOLW = sb("POOLW", [128, 4, 128], BF16)
        WA = sb("WA", [128, 4, 128], BF16)
        WX = sb("WX", [128, 4, 128], BF16)
        WDT = sb("WDT", [128, KC, 16], BF16)
        pool_tail = sb("pool_tail", [128, DEPTH, 4, 15])
        sconv_tail = sb("sconv_tail", [128, DEPTH, 12, 3])
        lconv_tail = sb("lconv_tail", [128, DEPTH, 4, 3])
        lru_h = sb("lru_h", [128, DEPTH, 4])
        ssdT = sb("ssdT", [128, D])
        ssdT_b = sb("ssdT_b", [128, D], BF16)
        DIAGD = sb("DIAGD", [128, 8, 128], BF16)
        ARW = 16384
        TMP = sb("TMP", [128, ARW])
        ar = Arena(TMP, ARW)

        psA = pst("psA", [128, 1024])
        pbs = [pst("pb%d" % i, [128, 512]) for i in range(6)]
        mmrot = Rot(pbs[0:3])
        smrot = Rot(pbs[3:5])
        psY = pbs[5]
        wrot = Rot(wbufs)

        ident_f = cst_f[:, 0:128]
        ident_b = cst_b[:, 0:128]
        ones_f = cst_f[:, 976:1104]
        ones_b = cst_b[:, 976:1104]
        CP = dict(U=cst_f[:, 128:256], negU=cst_f[:, 256:384], sel=cst_f[:, 384:512], negm=cst_b[:, 512:640], L=128,
                  Ub=cst_b[:, 128:256], negUb=cst_b[:, 256:384])
        CS = dict(U=cst_f[0:64, 640:704], negU=cst_f[0:64, 704:768], sel=cst_f[0:64, 768:832], negm=cst_b[0:64, 832:896], L=64,
                  Ub=cst_b[0:64, 640:704], negUb=cst_b[0:64, 704:768])
        blockind_f = cst_f[0:64, 896:912]
        blockind_b = cst_b[0:64, 896:912]
        rc_tab = cst_f[:, 912:976].rearrange("p (g t) -> p g t", g=4)

        P.dma(cst_f[:], I["cst"], q="sp")
        P.dma(cst_b[:], I["cst"], q="pool")
        BMASK = sb("BMASK", [128, 16, 64], BF16)
        P.dma(BMASK[:], I["cst2"].rearrange("p (b l) -> p b l", b=16), q="pool")
        prow = {}
        plist = [
            ("nm", I["norm_mix"].rearrange("l (j p) -> (l j) p", p=128)),
            ("nmem", I["norm_mem"].rearrange("l (j p) -> (l j) p", p=128)),
            ("nffn", I["norm_ffn"].rearrange("l (j p) -> (l j) p", p=128)),
            ("nfin", I["norm_final"].rearrange("(j p) -> j p", p=128)),
            ("pscale", I["pool_scale"].rearrange("l (j p) -> (l j) p", p=128)),
            ("scw", I["ssd_conv_w"].rearrange("l k (j p) -> (l k j) p", p=128)),
            ("scb", I["ssd_conv_b"].rearrange("l (j p) -> (l j) p", p=128)),
            ("snorm", I["ssd_norm"].rearrange("l (j p) -> (l j) p", p=128)),
            ("lcw", I["lru_conv_w"].rearrange("l k (j p) -> (l k j) p", p=128)),
            ("lcb", I["lru_conv_b"].rearrange("l (j p) -> (l j) p", p=128)),
            ("lam", I["lru_lambda"].rearrange("l (j p) -> (l j) p", p=128)),
            ("ba", I["lru_ba"].rearrange("l h i -> l (h i)").rearrange("l (j p) -> (l j) p", p=128)),
            ("bx", I["lru_bx"].rearrange("l h i -> l (h i)").rearrange("l (j p) -> (l j) p", p=128)),
        ]
        PST = ar.get([5, 128])
        P.memset(PST, 0.0)
        r0 = 0
        for name, ap2 in plist:
            nr = ap2.shape[0]
            prow[name] = r0
            done = 0
            while done < nr:
                t, rr = divmod(r0 + done, 128)
                n = min(nr - done, 128 - rr)
                P.dma(PST[rr:rr + n, t, :], ap2[done:done + n, :], q="sp")
                done += n
            r0 += nr
        assert r0 <= 640
        for t in range(5):
            ps = smrot.next()
            P.tr(ps[:, 0:128], PST[:, t, :], ident_f)
            P.copy(PAR[:, t * 128:(t + 1) * 128], ps[:, 0:128], eng="act")

        def pcol(name, idx):
            c = prow[name] + idx
            return PAR[:, c:c + 1]

        P.dma(RB[:, 0:64], I["ssd_dt_bias"].rearrange("l h -> (l h)").partition_broadcast(128), q="sp")
        P.dma(RB[:, 64:128], I["ssd_a_log"].rearrange("l h -> (l h)").partition_broadcast(128), q="sp")
        P.dma(RB[:, 128:192], I["ssd_d"].rearrange("l h -> (l h)").partition_broadcast(128), q="sp")
        P.act(RB[:, 64:128], RB[:, 64:128], AF.Exp)
        P.ts(RB[:, 64:128], RB[:, 64:128], -1.0, ALU.mult)
        dview = RB[:, 128:192].rearrange("p (l hp two) -> p l hp two", l=DEPTH, two=2)
        P.copy(DCOL[0:64, :, :], dview[0:64, :, :, 0])
        P.copy(DCOL[64:128, :, :], dview[64:128, :, :, 1])
        lam0 = prow["lam"]
        P.act(C1[:], PAR[:, lam0:lam0 + 16], AF.Exp, scale=-1.0)
        P.act(C1[:], C1[:], AF.Ln, bias=1.0)
        P.ts(C1[:], C1[:], -LRU_C, ALU.mult)
        P.memset(WA[:], 0.0)
        P.memset(WX[:], 0.0)
        for mc in range(2):
            mt = ar.get([D])
            P.dma(mt, I["mem"][mc * 128:(mc + 1) * 128, :], q="sp")
            for k in range(KC):
                P.tr(psA[:, k * 128:(k + 1) * 128], mt[:, k * 128:(k + 1) * 128], ident_f)
            P.copy(memT[:, 0:4, mc * 128:(mc + 1) * 128], psA[:, 0:512].rearrange("p (k n) -> p k n", k=4), eng="act")
            P.copy(memT[:, 4:8, mc * 128:(mc + 1) * 128], psA[:, 512:1024].rearrange("p (k n) -> p k n", k=4), eng="dve")
        for tl in (pool_tail, sconv_tail, lconv_tail, lru_h):
            P.memset(tl[:], 0.0)
        if dbg:
            P.memset(hT[:], 0.0)

        def stream(groups):
            for g in groups:
                wb = wrot.next()
                for dst_fn, src in g["dmas"]:
                    P.dma(dst_fn(wb), src, q="pool")
                g["body"](wb)

        def wview(wb, koff, kc, MG):
            return wb[:, koff * MG:(koff + kc) * MG].rearrange("p (k m) -> p k m", k=kc)

        def linear(wsrcs, m_total, MG, tiles, rhs_fn, consume):
            KCt = sum(kc for _, kc in wsrcs)
            groups = []
            ng = (m_total + MG - 1) // MG
            for gi in range(ng):
                mg = min(MG, m_total - gi * MG)

                def body(wb, gi=gi, mg=mg):
                    wv = wview(wb, 0, KCt, mg)
                    for mc in range(mg // 128):
                        mglob = gi * (MG // 128) + mc
                        for ti, (n0, n1) in enumerate(tiles):
                            ps = mmrot.next()
                            for k in range(KCt):
                                P.mm(ps[:, 0:n1 - n0], wv[:, k, mc * 128:(mc + 1) * 128], rhs_fn(k, n0, n1),
                                     start=(k == 0), stop=(k == KCt - 1))
                            consume(mglob, ti, n0, n1, ps)

                dmas = []
                koff = 0
                for src, kc in wsrcs:
                    dmas.append((lambda wb, koff=koff, kc=kc, mg=mg: wview(wb, koff, kc, mg),
                                 src[:, gi * MG:gi * MG + mg].rearrange("(k p) m -> p k m", p=128)))
                    koff += kc
                groups.append(dict(dmas=dmas, body=body))
            stream(groups)

        def rmsnorm(gname, l, tiles):
            for (n0, n1) in tiles:
                n = n1 - n0
                ps = mmrot.next()
                for k in range(KC):
                    sq = sqrot.next()
                    P.act(sq[:, 0:n], hT[:, k, n0:n1], AF.Square)
                    P.mm(ps[:, 0:n], ones_b, sq[:, 0:n], start=(k == 0), stop=(k == KC - 1))
                P.act(rstd[:, 0:n], ps[:, 0:n], AF.Sqrt, scale=1.0 / D, bias=epsc[:, 0:1])
                P.recip(rstd[:, 0:n], rstd[:, 0:n])
                for k in range(KC):
                    P.stt(uT[:, k, n0:n1], hT[:, k, n0:n1], pcol(gname, l * KC + k), rstd[:, 0:n], ALU.mult, ALU.mult)

        def add_resid(m, ti, n0, n1, ps):
            P.tt(hT[:, m, n0:n1], hT[:, m, n0:n1], ps[:, 0:n1 - n0], ALU.add)

        epsc = sb("epsc", [128, 1])
        P.memset(epsc[:], EPS)

        def stage(name):
            P.tag = name
            if stop_at == name:
                raise _Stop()

        stage("setup")
        try:
          for pas in range(NPASS):
              t0 = pas * NPP
              has_s = pas == NPASS - 1
              first = pas == 0
              last = pas == NPASS - 1
              ncol = NPP + (NS if has_s else 0)
              tiles = [(i * 512, (i + 1) * 512) for i in range(NPP // 512)] + ([(NPP, NT)] if has_s else [])
              nchunk = NPP // 128
              tiles_lin = tiles if not has_s else [(0, 384), (384, 768), (768, NT)]

              ar.reset()
              xrot = Rot([ar.get([D]) for _ in range(2)])
              for blk in range(nchunk):
                  xt = xrot.next()
                  P.dma(xt, I["x_p"][t0 + blk * 128:t0 + (blk + 1) * 128, :], q="sp")
                  for k in range(KC):
                      P.tr(psA[:, k * 128:(k + 1) * 128], xt[:, k * 128:(k + 1) * 128], ident_f)
                  P.copy(hT[:, 0:4, blk * 128:(blk + 1) * 128], psA[:, 0:512].rearrange("p (k n) -> p k n", k=4), eng="act")
                  P.copy(hT[:, 4:8, blk * 128:(blk + 1) * 128], psA[:, 512:1024].rearrange("p (k n) -> p k n", k=4), eng="dve")
              if has_s:
                  xt = xrot.next()
                  P.dma(xt[0:64, :], I["x_s"], q="sp")
                  for k in range(KC):
                      P.tr(psA[:, k * 64:(k + 1) * 64], xt[0:64, k * 128:(k + 1) * 128], ident_f[0:64, 0:64])
                  P.copy(hT[:, :, NPP:NT], psA[:, 0:512].rearrange("p (k n) -> p k n", k=8), eng="act")
              stage("loadx")

              for l in range(n_layers):
                  ar.reset()
                  sqrot = Rot([ar.get([512], BF16) for _ in range(2)])
                  rstd = ar.get([512])
                  arena_base = ar.off
                  rmsnorm("nm", l, tiles)
                  stage("norm1")
                  P.dma(POOLW[:], I["pool_w"][l].rearrange("g c d -> c g d"), q="pool")
                  for h2 in range(2):
                      P.dma(WA[h2 * 64:(h2 + 1) * 64, :, h2 * 64:(h2 + 1) * 64],
                            I["lru_wa"][l].rearrange("(j two) i o -> two i j o", two=2)[h2], q="pool")
                      P.dma(WX[h2 * 64:(h2 + 1) * 64, :, h2 * 64:(h2 + 1) * 64],
                            I["lru_wx"][l].rearrange("(j two) i o -> two i j o", two=2)[h2], q="pool")
                  P.dma(WDT[:], I["w_in"][l][:, OFF_DT:OFF_DT + 16].rearrange("(k p) m -> p k m", p=128), q="pool")
                  w_in_l = I["w_in"][l]
                  w_out_l = I["w_out"][l]
                  rhs_u = lambda k, n0, n1: uT[:, k, n0:n1]
                  deferred = []

                  def flush():
                      while deferred:
                          deferred.pop(0)()

                  if has_s:
                      SPin = ar.get([2, 512], parts=120)
                      P.dma(SPin, I["st_pool"][l].rearrange("(h b) r c -> (b r) h c", h=2), q="sp")
                      SPout = ar.get([2, 512], parts=120)
                      LCin = ar.get([512], parts=48)
                      P.dma(LCin, I["st_lconv"][l].rearrange("b r c -> (b r) c"), q="sp")
                      LCout = ar.get([512], parts=48)
                      LRin = ar.get([512], parts=16)
                      P.dma(LRin, I["st_lru"][l], q="sp")
                      LRout = ar.get([512], parts=16)
                      lru_h0 = ar.get([4, 16])
                  sbase = ar.off

                  RAWP = 15 + NPP + 16 * 19 + 2
                  rawrot = Rot([ar.get([RAWP]) for _ in range(2)])
                  tA = ar.get([RAWP])
                  tB = ar.get([RAWP])
                  plrot = Rot([ar.get([NT], BF16) for _ in range(2)])
                  tmp15 = ar.get([16])
                  tmp240 = ar.get([240])
                  cur_raw = [None]
                  SOFFP = 15 + NPP
                  ya = A[:, 0:4, :]

                  def svw(t, off, nb, w):
                      return t[:, off:off + nb * w].rearrange("p (b r) -> p b r", b=nb)

                  def pool_consume(j, ti, n0, n1, ps):
                      if ti == 0:
                          cur_raw[0] = rawrot.next()
                          raw = cur_raw[0]
                          P.copy(raw[:, 0:15], pool_tail[:, l, j, :], eng="dve")
                          if has_s:
                              pp = smrot.next()
                              for h in range(2):
                                  P.tr(pp[:, h * 120:(h + 1) * 120], SPin[:, h, j * 128:(j + 1) * 128], ident_f[0:120, 0:120])
                              P.copy(svw(raw, SOFFP, 16, 19)[:, :, 0:15], pp[:, 0:240].rearrange("p (b r) -> p b r", b=16), eng="dve")
                      raw = cur_raw[0]
                      if n0 < NPP:
                          P.copy(raw[:, 15 + n0:15 + n1], ps[:, 0:n1 - n0], eng="act")
                      else:
                          P.copy(svw(raw, SOFFP, 16, 19)[:, :, 15:19], ps[:, 0:64].rearrange("p (b r) -> p b r", b=16), eng="act")
                      if ti != len(tiles) - 1:
                          return
                      flush()
                      w = 2 << j
                      Wd = 15 + NPP
                      cur = raw
                      bufs = [tA, tB]
                      for lev in range(j + 1):
                          sh = 1 << lev
                          lo = (2 << lev) - 1
                          dst = bufs[lev % 2]
                          P.tt(dst[:, lo:Wd], cur[:, lo:Wd], cur[:, lo - sh:Wd - sh], ALU.add)
                          if has_s:
                              P.tt(svw(dst, SOFFP, 16, 19)[:, :, lo:19], svw(cur, SOFFP, 16, 19)[:, :, lo:19],
                                   svw(cur, SOFFP, 16, 19)[:, :, lo - sh:19 - sh], ALU.add)
                          cur = dst
                      pl = plrot.next()
                      P.stt(pl[:, 0:NPP], cur[:, 15:15 + NPP], 1.0 / w, raw[:, 15:15 + NPP], ALU.mult, ALU.subtract)
                      if first:
                          P.tt(tmp15[:, 0:15], cur[:, 15:30], rc_tab[:, j, 0:15], ALU.mult)
                          P.tt(pl[:, 0:15], tmp15[:, 0:15], raw[:, 15:30], ALU.subtract)
                      if has_s:
                          P.stt(pl[:, NPP:NT].rearrange("p (b r) -> p b r", b=16), svw(cur, SOFFP, 16, 19)[:, :, 15:19], 1.0 / w,
                                svw(raw, SOFFP, 16, 19)[:, :, 15:19], ALU.mult, ALU.subtract)
                      P.copy(pool_tail[:, l, j, :], raw[:, NPP:NPP + 15], eng="dve")
                      if has_s:
                          P.copy(tmp240.rearrange("p (b r) -> p b r", b=16), svw(raw, SOFFP, 16, 19)[:, :, 4:19], eng="dve")

                      def pe_part(j=j, pl=pl):
                          for (m0, m1) in tiles:
                              ps2 = mmrot.next()
                              P.mm(ps2[:, 0:m1 - m0], POOLW[:, j, :], pl[:, m0:m1])
                              P.act(ya[:, j, m0:m1], ps2[:, 0:m1 - m0], AF.Identity, scale=pcol("pscale", l * 4 + j))
                          if has_s:
                              pp = smrot.next()
                              for h in range(2):
                                  P.tr(pp[0:120, h * 128:(h + 1) * 128], tmp240[:, h * 120:(h + 1) * 120], ident_f)
                              P.copy(SPout[:, :, j * 128:(j + 1) * 128], pp[0:120, 0:256].rearrange("p (h c) -> p h c", h=2), eng="act")

                      deferred.append(pe_part)

                  linear([(w_in_l[:, OFF_POOL:OFF_POOL + 512], KC)], 512, 512, tiles, rhs_u, pool_consume)
                  flush()
                  stage("pool")
                  if has_s:
                      P.dma(O["s_pool"][l].rearrange("(h b) r c -> (b r) h c", h=2), SPout, q="sp", is_output=True)

                  ar.off = sbase
                  RAWL = 3 + NPP + 16 * 7 + 1
                  SOFFL = 3 + NPP
                  rawrot = Rot([ar.get([RAWL]) for _ in range(2)])
                  acc = ar.get([NT])
                  xcb = ar.get([NT], BF16)
                  rr_ = ar.get([NT])
                  ii_ = ar.get([NT])
                  mm_ = ar.get([NT])
                  tmp48 = ar.get([48])
                  tmp16 = ar.get([16])
                  gel = A[:, 8:12, :]
                  yc = A[:, 4:8, :]
                  if has_s:
                      pp = smrot.next()
                      for j in range(4):
                          P.tr(pp[:, j * 16:(j + 1) * 16], LRin[:, j * 128:(j + 1) * 128], ident_f[0:16, 0:16])
                      P.copy(lru_h0, pp[:, 0:64].rearrange("p (j b) -> p j b", j=4), eng="act")

                  def lru_consume(m, ti, n0, n1, ps):
                      n = n1 - n0
                      if m < 4:
                          P.act(gel[:, m, n0:n1], ps[:, 0:n], AF.Gelu_apprx_tanh)
                          return
                      j = m - 4
                      if ti == 0:
                          cur_raw[0] = rawrot.next()
                          raw = cur_raw[0]
                          P.copy(raw[:, 0:3], lconv_tail[:, l, j, :], eng="dve")
                          if has_s:
                              pp = smrot.next()
                              P.tr(pp[:, 0:48], LCin[:, j * 128:(j + 1) * 128], ident_f[0:48, 0:48])
                              P.copy(svw(raw, SOFFL, 16, 7)[:, :, 0:3], pp[:, 0:48].rearrange("p (b r) -> p b r", b=16), eng="dve")
                      raw = cur_raw[0]
                      if n0 < NPP:
                          P.copy(raw[:, 3 + n0:3 + n1], ps[:, 0:n], eng="act")
                      else:
                          P.copy(svw(raw, SOFFL, 16, 7)[:, :, 3:7], ps[:, 0:64].rearrange("p (b r) -> p b r", b=16), eng="act")
                      if ti != len(tiles) - 1:
                          return
                      flush()
                      for k in range(4):
                          wk = pcol("lcw", (l * 4 + k) * 4 + j)
                          if k == 0:
                              P.ts(acc[:, 0:NPP], raw[:, 0:NPP], wk, ALU.mult, pcol("lcb", l * 4 + j), ALU.add)
                              if has_s:
                                  P.ts(acc[:, NPP:NT].rearrange("p (b r) -> p b r", b=16), svw(raw, SOFFL, 16, 7)[:, :, 0:4], wk, ALU.mult,
                                       pcol("lcb", l * 4 + j), ALU.add)
                          else:
                              P.stt(acc[:, 0:NPP], raw[:, k:k + NPP], wk, acc[:, 0:NPP], ALU.mult, ALU.add)
                              if has_s:
                                  av = acc[:, NPP:NT].rearrange("p (b r) -> p b r", b=16)
                                  P.stt(av, svw(raw, SOFFL, 16, 7)[:, :, k:k + 4], wk, av, ALU.mult, ALU.add)
                      P.copy(lconv_tail[:, l, j, :], raw[:, NPP:NPP + 3], eng="dve")
                      if has_s:
                          P.copy(tmp48.rearrange("p (b r) -> p b r", b=16), svw(raw, SOFFL, 16, 7)[:, :, 4:7], eng="dve")
                      P.copy(xcb[:, 0:ncol], acc[:, 0:ncol], eng="act")

                      def pe_part(j=j):
                          for (m0, m1) in tiles:
                              psr = mmrot.next()
                              P.mm(psr[:, 0:m1 - m0], WA[:, j, :], xcb[:, m0:m1])
                              P.act(rr_[:, m0:m1], psr[:, 0:m1 - m0], AF.Sigmoid, bias=pcol("ba", l * 4 + j))
                              psi = mmrot.next()
                              P.mm(psi[:, 0:m1 - m0], WX[:, j, :], xcb[:, m0:m1])
                              P.act(ii_[:, m0:m1], psi[:, 0:m1 - m0], AF.Sigmoid, bias=pcol("bx", l * 4 + j))
                          if has_s:
                              pp = smrot.next()
                              P.tr(pp[0:48, 0:128], tmp48[:, 0:48], ident_f)
                              P.copy(LCout[:, j * 128:(j + 1) * 128], pp[0:48, 0:128], eng="act")
                          nn = ncol
                          P.act(rr_[:, 0:nn], rr_[:, 0:nn], AF.Exp, scale=C1[:, l * 4 + j:l * 4 + j + 1])
                          P.tt(mm_[:, 0:nn], rr_[:, 0:nn], rr_[:, 0:nn], ALU.mult)
                          P.act(mm_[:, 0:nn], mm_[:, 0:nn], AF.Sqrt, scale=-1.0, bias=1.0)
                          if first:
                              P.memset(mm_[:, 0:1], 1.0)
                          P.tt(ii_[:, 0:nn], ii_[:, 0:nn], mm_[:, 0:nn], ALU.mult)
                          P.tt(ii_[:, 0:nn], ii_[:, 0:nn], acc[:, 0:nn], ALU.mult)
                          if has_s:
                              a_s = rr_[:, NPP:NT].rearrange("p (b r) -> p b r", b=16)
                              b_s = ii_[:, NPP:NT].rearrange("p (b r) -> p b r", b=16)
                              t16 = tmp16.rearrange("p (b o) -> p b o", o=1)
                              P.tt(t16, a_s[:, :, 0:1], lru_h0[:, j, :].rearrange("p (b o) -> p b o", o=1), ALU.mult)
                              P.tt(b_s[:, :, 0:1], b_s[:, :, 0:1], t16, ALU.add)
                              P.memset(a_s[:, :, 0:1], 0.0)
                          P.scan(mm_[:, 0:NPP], rr_[:, 0:NPP], ii_[:, 0:NPP], lru_h[:, l, j:j + 1], ALU.mult, ALU.add)
                          if has_s:
                              P.scan(mm_[:, NPP:NT], rr_[:, NPP:NT], ii_[:, NPP:NT], 0.0, ALU.mult, ALU.add)
                          P.copy(lru_h[:, l, j:j + 1], mm_[:, NPP - 1:NPP], eng="dve")
                          if has_s:
                              P.copy(tmp16.rearrange("p (b o) -> p b o", o=1), mm_[:, NPP:NT].rearrange("p (b r) -> p b r", b=16)[:, :, 3:4], eng="dve")
                              pp = smrot.next()
                              P.tr(pp[0:16, 0:128], tmp16[:, 0:16], ident_f)
                              P.copy(LRout[:, j * 128:(j + 1) * 128], pp[0:16, 0:128], eng="act")
                          P.tt(yc[:, j, 0:nn], mm_[:, 0:nn], gel[:, j, 0:nn], ALU.mult)

                      deferred.append(pe_part)

                  linear([(w_in_l[:, OFF_GATE:OFF_GATE + 1024], KC)], 1024, 512, tiles, rhs_u, lru_consume)
                  flush()
                  stage("lru")
                  if has_s:
                      P.dma(O["s_lconv"][l].rearrange("b r c -> (b r) c"), LCout, q="sp", is_output=True)
                      P.dma(O["s_lru"][l], LRout, q="sp", is_output=True)

                  zs = A[:, 12:20, :]

                  def z_consume(m, ti, n0, n1, ps):
                      P.act(zs[:, m, n0:n1], ps[:, 0:n1 - n0], AF.Silu)

                  linear([(w_in_l[:, OFF_Z:OFF_Z + 1024], KC)], 1024, 512, tiles_lin, rhs_u, z_consume)

                  linear([(w_out_l[0:512, :], 4), (w_out_l[1536:2048, :], 4)], D, 512, tiles_lin,
                         lambda k, n0, n1: A[:, k, n0:n1], add_resid)
                  stage("wout1")

                  ar.off = arena_base
                  xbc = A[:, 0:12, :]
                  zs = A[:, 12:20, :]
                  dtb = RB[:, l * 16:(l + 1) * 16]
                  Abc = RB[:, 64 + l * 16:64 + (l + 1) * 16]
                  dt_all = ar.get([nchunk + 1, 16])
                  dA_all = ar.get([nchunk + 1, 16])
                  dAh = ar.get([nchunk + 1, 16], BF16)
                  dAl = ar.get([nchunk + 1, 16], BF16)
                  dAt = ar.get([nchunk + 1, 16])
                  acs_all = ar.get([nchunk, 16])
                  decst_all = ar.get([nchunk, 16])
                  eacs_all = ar.get([nchunk, 16])
                  cd_all = ar.get([nchunk, 16])
                  dtdec_all = ar.get([nchunk, 16])
                  cc_s = dict(acs_tok=ar.get([16]), decst=ar.get([16]), eacs=ar.get([16]), cdb=ar.get([16]), dtdec=ar.get([16]))
                  Rhr = Rot([ar.get([8, 128], BF16) for _ in range(3)])
                  Rlr = Rot([ar.get([8, 128], BF16) for _ in range(3)])
                  Er = Rot([ar.get([8, 128], BF16) for _ in range(2)])
                  xdtr = Rot([ar.get([512], BF16) for _ in range(2)])
                  xddr = Rot([ar.get([512], BF16) for _ in range(2)])
                  yofr = Rot([ar.get([512], BF16) for _ in range(2)])
                  Btr = Rot([ar.get([128], BF16) for _ in range(2)])
                  CBr = Rot([ar.get([128], BF16) for _ in range(2)])
                  hcd = ar.get([512])

                  sbase3 = ar.off
                  if has_s:
                      SCin = ar.get([1536], parts=48)
                      P.dma(SCin, I["st_sconv"][l].rearrange("b r c -> (b r) c"), q="sp")
                      SCout = ar.get([1536], parts=48)
                  sbase2 = ar.off
                  RAWS = 3 + NPP + 16 * 7 + 1
                  rawrot = Rot([ar.get([RAWS]) for _ in range(2)])
                  accr = Rot([ar.get([NT]) for _ in range(2)])
                  tmp48 = ar.get([48])

                  def xbc_consume(j, ti, n0, n1, ps):
                      n = n1 - n0
                      if ti == 0:
                          cur_raw[0] = rawrot.next()
                          raw = cur_raw[0]
                          P.copy(raw[:, 0:3], sconv_tail[:, l, j, :], eng="dve")
                          if has_s:
                              pp = smrot.next()
                              P.tr(pp[:, 0:48], SCin[:, j * 128:(j + 1) * 128], ident_f[0:48, 0:48])
                              P.copy(svw(raw, SOFFL, 16, 7)[:, :, 0:3], pp[:, 0:48].rearrange("p (b r) -> p b r", b=16), eng="dve")
                      raw = cur_raw[0]
                      if n0 < NPP:
                          P.copy(raw[:, 3 + n0:3 + n1], ps[:, 0:n], eng="act")
                      else:
                          P.copy(svw(raw, SOFFL, 16, 7)[:, :, 3:7], ps[:, 0:64].rearrange("p (b r) -> p b r", b=16), eng="act")
                      if ti != len(tiles) - 1:
                          return
                      flush()
                      acc = accr.next()
                      for k in range(4):
                          wk = pcol("scw", (l * 4 + k) * 12 + j)
                          if k == 0:
                              P.ts(acc[:, 0:NPP], raw[:, 0:NPP], wk, ALU.mult, pcol("scb", l * 12 + j), ALU.add)
                              if has_s:
                                  P.ts(acc[:, NPP:NT].rearrange("p (b r) -> p b r", b=16), svw(raw, SOFFL, 16, 7)[:, :, 0:4], wk, ALU.mult,
                                       pcol("scb", l * 12 + j), ALU.add)
                          else:
                              P.stt(acc[:, 0:NPP], raw[:, k:k + NPP], wk, acc[:, 0:NPP], ALU.mult, ALU.add)
                              if has_s:
                                  av = acc[:, NPP:NT].rearrange("p (b r) -> p b r", b=16)
                                  P.stt(av, svw(raw, SOFFL, 16, 7)[:, :, k:k + 4], wk, av, ALU.mult, ALU.add)
                      P.copy(sconv_tail[:, l, j, :], raw[:, NPP:NPP + 3], eng="dve")
                      deferred.append(lambda j=j, acc=acc: P.act(xbc[:, j, 0:ncol], acc[:, 0:ncol], AF.Silu))
                      if has_s:
                          P.copy(tmp48.rearrange("p (b r) -> p b r", b=16), svw(raw, SOFFL, 16, 7)[:, :, 4:7], eng="dve")
                          pp = smrot.next()
                          P.tr(pp[0:48, 0:128], tmp48[:, 0:48], ident_f)
                          P.copy(SCout[:, j * 128:(j + 1) * 128], pp[0:48, 0:128], eng="act")

                  linear([(w_in_l[:, OFF_XBC:OFF_XBC + 1536], KC)], 1536, 512, tiles, rhs_u, xbc_consume)
                  flush()
                  if has_s:
                      P.dma(O["s_sconv"][l].rearrange("b r c -> (b r) c"), SCout, q="sp", is_output=True)

                  stage("xbcz")

                  psd = smrot.next()
                  for c in range(nchunk):
                      for k in range(KC):
                          P.mm(psd[:, c * 16:(c + 1) * 16], uT[:, k, c * 128:(c + 1) * 128], WDT[:, k, :], start=(k == 0), stop=(k == KC - 1))
                  P.tt(dt_all[:, 0:nchunk, :], psd[:, 0:nchunk * 16].rearrange("p (c h) -> p c h", h=16),
                       bcast(dtb.rearrange("p (o h) -> p o h", o=1), [128, nchunk, 16]), ALU.add)
                  if has_s:
                      psd2 = smrot.next()
                      for k in range(KC):
                          P.mm(psd2[0:64, 0:16], uT[:, k, NPP:NT], WDT[:, k, :], start=(k == 0), stop=(k == KC - 1))
                      P.memset(dt_all[:, nchunk, :], 0.0)
                      P.tt(dt_all[0:64, nchunk, :], psd2[0:64, 0:16], dtb[0:64, :], ALU.add)
                  nch_all = nchunk + (1 if has_s else 0)
                  P.act(dt_all[:, 0:nch_all, :], dt_all[:, 0:nch_all, :], AF.Exp)
                  P.act(dt_all[:, 0:nch_all, :], dt_all[:, 0:nch_all, :], AF.Ln, bias=1.0)
                  P.tt(dA_all[:, 0:nch_all, :], dt_all[:, 0:nch_all, :], bcast(Abc.rearrange("p (o h) -> p o h", o=1), [128, nch_all, 16]), ALU.mult)
                  P.copy(dAh[:, 0:nch_all, :], dA_all[:, 0:nch_all, :], eng="dve")
                  P.copy(dAt[:, 0:nch_all, :], dAh[:, 0:nch_all, :], eng="dve")
                  P.tt(dAl[:, 0:nch_all, :], dA_all[:, 0:nch_all, :], dAt[:, 0:nch_all, :], ALU.subtract)
                  for hp in range(8):
                      P.ts(DIAGD[:, hp, :], ident_b, DCOL[:, l, hp:hp + 1], ALU.mult)

                  if first:
                      P.memset(ssdT[:], 0.0)
                      P.memset(ssdT_b[:], 0.0)
                  else:
                      P.dma(ssdT[:], spill[l], q="sp")
                      P.copy(ssdT_b[:], ssdT[:], eng="act")
                  stage("dt")

                  def chunk_common(c, cols, K):
                      L = K["L"]
                      CC = cc_s
                      acs_tok, decst, eacs, cdb, dtdec = CC["acs_tok"], CC["decst"], CC["eacs"], CC["cdb"], CC["dtdec"]
                      dA_c = dA_all[0:L, c, :]
                      ps1 = smrot.next()
                      P.mm(ps1[0:L, 0:16], K["U"], dA_c)
                      P.copy(acs_tok[0:L, :], ps1[0:L, 0:16], eng="dve")
                      P.mm(ps1[0:L, 16:32], K["sel"], acs_tok[0:L, :])
                      P.tt(decst[0:L, :], ps1[0:L, 16:32], acs_tok[0:L, :], ALU.subtract)
                      P.act(decst[0:L, :], decst[0:L, :], AF.Exp)
                      P.act(eacs[0:L, :], acs_tok[0:L, :], AF.Exp)
                      P.tt(dtdec[0:L, :], dt_all[0:L, c, :], decst[0:L, :], ALU.mult)
                      return CC

                  def common_all():
                      nq = nchunk * 16
                      fl = lambda t: t.rearrange("p c h -> p (c h)")
                      ps = smrot.next()
                      P.mm(ps[:, 0:nq], CP["U"], fl(dA_all[:, 0:nchunk, :]))
                      P.copy(fl(acs_all), ps[:, 0:nq], eng="dve")
                      ps2 = smrot.next()
                      P.mm(ps2[:, 0:nq], CP["sel"], fl(acs_all))
                      P.tt(fl(decst_all), ps2[:, 0:nq], fl(acs_all), ALU.subtract)
                      P.act(fl(decst_all), fl(decst_all), AF.Exp)
                      P.act(fl(eacs_all), fl(acs_all), AF.Exp)
                      ps3 = smrot.next()
                      P.mm(ps3[:, 0:nq], ones_f, fl(dA_all[:, 0:nchunk, :]))
                      P.copy(fl(cd_all), ps3[:, 0:nq], eng="dve")
                      P.act(fl(cd_all), fl(cd_all), AF.Exp)
                      P.tt(fl(dtdec_all), fl(dt_all[:, 0:nchunk, :]), fl(decst_all), ALU.mult)

                  def cc_of(c):
                      return dict(acs_tok=acs_all[:, c, :], decst=decst_all[:, c, :], eacs=eacs_all[:, c, :], cdb=cd_all[:, c, :],
                                  dtdec=dtdec_all[:, c, :])

                  def make_r(c, g, K):
                      L = K["L"]
                      Rh = Rhr.next()
                      Rl = Rlr.next()
                      Ub = bcast(K["Ub"].rearrange("p (o l) -> p o l", o=1), [L, 8, L])
                      for k in range(8):
                          P.ts(Rh[0:L, k, 0:L], K["Ub"], dAh[0:L, c, 8 * g + k:8 * g + k + 1], ALU.mult)
                      P.tt(Rl[0:L, :, 0:L], Ub, bcast(dAl[0:L, c, 8 * g:8 * g + 8].rearrange("p (h o) -> p h o", o=1), [L, 8, L]), ALU.mult, eng="pool")
                      return Rh, Rl

                  def unit_pre(c, g, cols, K, CC, RR):
                      L = K["L"]
                      dtdec = CC["dtdec"]
                      c0, c1 = cols
                      Rh, Rl = RR
                      pxt = smrot.next()
                      pxb = pxt[:, 0:256].bitcast(BF16)
                      for j in range(4):
                          P.tr(pxb[0:L, j * 128:(j + 1) * 128], xbc[:, 4 * g + j, c0:c1], ident_b)
                      xdt = xdtr.next()
                      xdd = xddr.next()
                      pv = pxb[0:L, :].rearrange("p (h q) -> p h q", h=8)
                      P.tt(xdt[0:L, :].rearrange("p (h q) -> p h q", h=8), pv,
                           bcast(dt_all[0:L, c, 8 * g:8 * g + 8].rearrange("p (h o) -> p h o", o=1), [L, 8, 64]), ALU.mult)
                      P.tt(xdd[0:L, :].rearrange("p (h q) -> p h q", h=8), pv,
                           bcast(dtdec[0:L, 8 * g:8 * g + 8].rearrange("p (h o) -> p h o", o=1), [L, 8, 64]), ALU.mult)
                      pbt = smrot.next()
                      pbb = pbt[:, 0:256].bitcast(BF16)
                      P.tr(pbb[0:L, 0:128], xbc[:, 8 + g, c0:c1], ident_b)
                      Bt = Btr.next()
                      P.copy(Bt[0:L, :], pbb[0:L, 0:128], eng="act")
                      E = Er.next()
                      nh = 512 // L
                      for half in range(8 // nh):
                          seg = psA[0:L, half * 512:half * 512 + nh * L].rearrange("p (h l) -> p h l", h=nh)
                          hs = slice(half * nh, (half + 1) * nh)
                          dh = bcast(dAh[0:L, c, 8 * g:8 * g + 8][:, hs].rearrange("p (h o) -> p h o", o=1), [L, nh, L])
                          dl = bcast(dAl[0:L, c, 8 * g:8 * g + 8][:, hs].rearrange("p (h o) -> p h o", o=1), [L, nh, L])
                          P.mm(seg, ones_b[0:L, 0:L], Rh[0:L, hs, 0:L], start=True, stop=False)
                          P.mm(seg, ones_b[0:L, 0:L], Rl[0:L, hs, 0:L], start=False, stop=False)
                          P.mm(seg, K["negUb"], dh, start=False, stop=False)
                          P.mm(seg, K["negUb"], dl, start=False, stop=False)
                          P.mm(seg, ident_b[0:L, 0:L], bcast(K["negm"].rearrange("p (o l) -> p o l", o=1), [L, nh, L]), start=False, stop=True)
                          P.act(E[0:L, hs, 0:L], seg, AF.Exp)
                      pcb = smrot.next()
                      P.mm(pcb[0:L, 0:L], xbc[:, 8 + g, c0:c1], xbc[:, 10 + g, c0:c1])
                      CBs = CBr.next()
                      P.copy(CBs[0:L, 0:L], pcb[0:L, 0:L], eng="act")
                      return dict(xdt=xdt, xdd=xdd, Bt=Bt, M=E, E=E, CBs=CBs, CC=CC)

                  def unit_m(H, K):
                      L = K["L"]
                      P.tt(H["M"][0:L, :, 0:L], H["E"][0:L, :, 0:L], bcast(H["CBs"][0:L, 0:L].rearrange("p (o l) -> p o l", o=1), [L, 8, L]),
                           ALU.mult, eng="pool")

                  def unit_y(c, g, cols, K, H, pyo):
                      L = K["L"]
                      c0, c1 = cols
                      eacs = H["CC"]["eacs"]
                      yof = yofr.next()
                      P.tt(yof[0:L, :].rearrange("p (h q) -> p h q", h=8), pyo[0:L, 0:512].rearrange("p (h q) -> p h q", h=8),
                           bcast(eacs[0:L, 8 * g:8 * g + 8].rearrange("p (h o) -> p h o", o=1), [L, 8, 64]), ALU.mult)
                      Y = psY[:, 0:4 * L].rearrange("p (j l) -> p j l", j=4)
                      for k in range(8):
                          out = Y[64 * (k % 2):64 * (k % 2) + 64, k // 2, :]
                          hp = 4 * g + k // 2
                          P.mm(out, H["xdt"][0:L, 64 * k:64 * k + 64], H["M"][0:L, k, 0:L], start=True, stop=False)
                          P.mm(out, yof[0:L, 64 * k:64 * k + 64], ident_b[0:L, 0:L], start=False, stop=False)
                          P.mm(out, DIAGD[:, hp, 64 * (k % 2):64 * (k % 2) + 64], xbc[:, hp, c0:c1], start=False, stop=True)
                      P.copy(xbc[:, 4 * g:4 * g + 4, c0:c1], Y, eng="act")

                  units = [(c, g) for c in range(nchunk) for g in range(2)]
                  nu = len(units)
                  HH = {}
                  RT = {}
                  common_all()

                  def pre_r(u):
                      c, g = units[u]
                      RT[u] = make_r(c, g, CP)

                  def pre_a(u):
                      c, g = units[u]
                      cols = (c * 128, (c + 1) * 128)
                      HH[u] = unit_pre(c, g, cols, CP, cc_of(c), RT.pop(u))

                  def y_part(u):
                      c, g = units[u]
                      cols = (c * 128, (c + 1) * 128)
                      H = HH.pop(u)
                      cdb = H["CC"]["cdb"]
                      pyo = mmrot.next()
                      P.mm(pyo[:, 0:512], xbc[:, 10 + g, cols[0]:cols[1]], ssdT_b[:, 512 * g:512 * (g + 1)])
                      unit_y(c, g, cols, CP, H, pyo)
                      pdl = mmrot.next()
                      P.mm(pdl[:, 0:512], H["Bt"][:, :], H["xdd"][:, :])
                      sv = ssdT[:, 512 * g:512 * (g + 1)].rearrange("p (h q) -> p h q", h=8)
                      P.tt(hcd.rearrange("p (h q) -> p h q", h=8), sv, bcast(cdb[:, 8 * g:8 * g + 8].rearrange("p (h o) -> p h o", o=1), [128, 8, 64]), ALU.mult)
                      P.tt(ssdT[:, 512 * g:512 * (g + 1)], hcd, pdl[:, 0:512], ALU.add)
                      P.copy(ssdT_b[:, 512 * g:512 * (g + 1)], ssdT[:, 512 * g:512 * (g + 1)], eng="act")

                  pre_r(0)
                  if nu > 1:
                      pre_r(1)
                  pre_a(0)
                  unit_m(HH[0], CP)
                  for u in range(nu):
                      if u + 2 < nu:
                          pre_r(u + 2)
                      if u + 1 < nu:
                          pre_a(u + 1)
                      y_part(u)
                      if u + 1 < nu:
                          unit_m(HH[u + 1], CP)
                      stage("scan1")
                  if not last:
                      P.dma(spill[l], ssdT[:], q="sp")
                      stage("scanp")
                  else:
                      ar.off = sbase3
                      for half in range(2):
                          for j in range(4):
                              P.tr(psA[:, j * 128:(j + 1) * 128], ssdT[:, (half * 4 + j) * 128:(half * 4 + j + 1) * 128], ident_f)
                          so = ar.get([4, 128])
                          P.copy(so, psA[:, 0:512].rearrange("p (j n) -> p j n", j=4), eng="act")
                          P.dma(O["p_ssd"][l].rearrange("(hp two) q n -> (two q) hp n", two=2)[:, half * 4:(half + 1) * 4, :], so, q="sp", is_output=True)

                  if has_s:
                      ar.off = sbase3
                      c = nchunk
                      cols = (NPP, NT)
                      CCs = chunk_common(c, cols, CS)
                      Hs = [unit_pre(c, g, cols, CS, CCs, make_r(c, g, CS)) for g in range(2)]
                      for g in range(2):
                          unit_m(Hs[g], CS)
                      rblk = ar.get([16, 16], parts=64)
                      P.tt(rblk, bcast(dA_all[0:64, c, :].rearrange("p (o h) -> p o h", o=1), [64, 16, 16]),
                           bcast(blockind_f.rearrange("p (b o) -> p b o", o=1), [64, 16, 16]), ALU.mult)
                      pda = smrot.next()
                      P.mm(pda[:, 0:256], ones_f[0:64, :], rblk.rearrange("p b h -> p (b h)"))
                      dec_all = ar.get([16, 16])
                      P.act(dec_all.rearrange("p b h -> p (b h)"), pda[:, 0:256], AF.Exp)
                      Cm = []
                      Bblk = []
                      for g in range(2):
                          cm = ar.get([16, 64], BF16)
                          P.tt(cm, bcast(xbc[:, 10 + g, NPP:NT].rearrange("p (o l) -> p o l", o=1), [128, 16, 64]), BMASK[:], ALU.mult)
                          Cm.append(cm)
                          bb = ar.get([16, 128], BF16, parts=64)
                          P.tt(bb, bcast(Hs[g]["Bt"][0:64, :].rearrange("p (o n) -> p o n", o=1), [64, 16, 128]),
                               bcast(blockind_b.rearrange("p (b o) -> p b o", o=1), [64, 16, 128]), ALU.mult)
                          Bblk.append(bb)
                      S0r = Rot([ar.get([8, 128]) for _ in range(2)])
                      h0Tr = Rot([ar.get([D], BF16) for _ in range(2)])
                      pyo = [mmrot.next(), mmrot.next()]
                      for b in range(NSQ):
                          S0 = S0r.next()
                          P.dma(S0, I["st_ssd"][l, b].rearrange("(hp two) q n -> (two q) hp n", two=2), q="sp")
                          h0T = h0Tr.next()
                          for half in range(2):
                              for j in range(4):
                                  P.tr(psA[:, j * 128:(j + 1) * 128] if half == 0 else psA[:, 512 + j * 128:512 + (j + 1) * 128],
                                       S0[:, half * 4 + j, :], ident_f)
                          P.copy(h0T[:, 0:512], psA[:, 0:512], eng="act")
                          P.copy(h0T[:, 512:1024], psA[:, 512:1024], eng="dve")
                          for g in range(2):
                              P.mm(pyo[g][0:64, 0:512], Cm[g][:, b, :], h0T[:, 512 * g:512 * (g + 1)], start=(b == 0), stop=(b == NSQ - 1))
                          hn = S0
                          for h2 in range(2):
                              dsl = dec_all[h2 * 64:(h2 + 1) * 64, b, :].rearrange("p (hp two) -> p hp two", two=2)[:, :, h2:h2 + 1]
                              P.tt(hn[h2 * 64:(h2 + 1) * 64, :, :], S0[h2 * 64:(h2 + 1) * 64, :, :], bcast(dsl, [64, 8, 128]), ALU.mult)
                          for half in range(2):
                              pdl = smrot.next()
                              for j in range(4):
                                  hp = half * 4 + j
                                  g = hp // 4
                                  P.mm(pdl[:, j * 128:(j + 1) * 128], Hs[g]["xdd"][0:64, (hp % 4) * 128:(hp % 4 + 1) * 128], Bblk[g][:, b, :])
                              P.tt(hn[:, half * 4:(half + 1) * 4, :], hn[:, half * 4:(half + 1) * 4, :],
                                   pdl[:, 0:512].rearrange("p (j n) -> p j n", j=4), ALU.add)
                          P.dma(O["s_ssd"][l, b].rearrange("(hp two) q n -> (two q) hp n", two=2), hn, q="sp", is_output=True)
                      for g in range(2):
                          unit_y(c, g, cols, CS, Hs[g], pyo[g])

                  ar.off = arena_base
                  yg = A[:, 0:8, :]
                  for (n0, n1) in tiles_lin:
                      n = n1 - n0
                      ps = mmrot.next()
                      for k in range(KC):
                          P.tt(yg[:, k, n0:n1], yg[:, k, n0:n1], zs[:, k, n0:n1], ALU.mult)
                          sq = sqrot.next()
                          P.act(sq[:, 0:n], yg[:, k, n0:n1], AF.Square)
                          P.mm(ps[:, 0:n], ones_b, sq[:, 0:n], start=(k == 0), stop=(k == KC - 1))
                      P.act(rstd[:, 0:n], ps[:, 0:n], AF.Sqrt, scale=1.0 / D, bias=epsc[:, 0:1])
                      P.recip(rstd[:, 0:n], rstd[:, 0:n])
                      for k in range(KC):
                          P.stt(yg[:, k, n0:n1], yg[:, k, n0:n1], pcol("snorm", l * KC + k), rstd[:, 0:n], ALU.mult, ALU.mult)
                  linear([(w_out_l[512:1536, :], 8)], D, 512, tiles_lin, lambda k, n0, n1: A[:, k, n0:n1], add_resid)
                  stage("gate")
                  if dbg:
                      P.dma(O["dbg_h"][pas, l, 0], hT[:].rearrange("p k n -> p (k n)"), q="sp", is_output=True)

                  ar.off = arena_base
                  rmsnorm("nmem", l, tiles)
                  stage("norm2")
                  qT = A[:, 0:8, :]
                  oT = A[:, 8:16, :]
                  kvst = Rot([ar.get([512]) for _ in range(2)])
                  KT = ar.get([KC, 256], BF16)
                  VT = ar.get([2, D], BF16)

                  if first:
                      groups = []
                      for gi in range(2):
                          def body(wb, gi=gi):
                              wv = wview(wb, 0, KC, 512)
                              for mc in range(4):
                                  ps = mmrot.next()
                                  for k in range(KC):
                                      P.mm(ps[:, 0:256], wv[:, k, mc * 128:(mc + 1) * 128], memT[:, k, :], start=(k == 0), stop=(k == KC - 1))
                                  P.copy(KT[:, gi * 4 + mc, :], ps[:, 0:256], eng="act")
                              for mc in range(2):
                                  ps = mmrot.next()
                                  for k in range(KC):
                                      P.mm(ps[:, 0:512], memT[:, k, mc * 128:(mc + 1) * 128], wv[:, k, :], start=(k == 0), stop=(k == KC - 1))
                                  stg = kvst.next()
                                  P.copy(stg, ps[:, 0:512], eng="dve")
                                  P.dma(O["p_mk"][l, mc * 128:(mc + 1) * 128, gi * 512:(gi + 1) * 512], stg, q="sp", is_output=True)
                          groups.append(dict(dmas=[(lambda wb: wview(wb, 0, KC, 512), I["w_mem_k"][l][:, gi * 512:(gi + 1) * 512].rearrange("(k p) m -> p k m", p=128))], body=body))
                      stream(groups)
                      stage("kt")
                      groups = []
                      for gi in range(2):
                          def body(wb, gi=gi):
                              wv = wview(wb, 0, KC, 512)
                              for mc in range(2):
                                  ps = mmrot.next()
                                  for k in range(KC):
                                      P.mm(ps[:, 0:512], memT[:, k, mc * 128:(mc + 1) * 128], wv[:, k, :], start=(k == 0), stop=(k == KC - 1))
                                  stg = kvst.next()
                                  P.copy(stg, ps[:, 0:512], eng="dve")
                                  P.copy(VT[:, mc, gi * 512:(gi + 1) * 512], stg, eng="act")
                                  P.dma(O["p_mv"][l, mc * 128:(mc + 1) * 128, gi * 512:(gi + 1) * 512], stg, q="sp", is_output=True)
                          groups.append(dict(dmas=[(lambda wb: wview(wb, 0, KC, 512), I["w_mem_v"][l][:, gi * 512:(gi + 1) * 512].rearrange("(k p) m -> p k m", p=128))], body=body))
                      stream(groups)
                      if NPASS > 1:
                          P.dma(kvspill[l, 0], KT.rearrange("p e m -> p (e m)"), q="sp")
                          P.dma(kvspill[l, 1], VT.rearrange("p c e -> p (c e)"), q="sp")
                  else:
                      P.dma(KT.rearrange("p e m -> p (e m)"), kvspill[l, 0], q="sp")
                      P.dma(VT.rearrange("p c e -> p (c e)"), kvspill[l, 1], q="sp")
                  stage("attnkv")

                  def q_consume(m, ti, n0, n1, ps):
                      P.copy(qT[:, m, n0:n1], ps[:, 0:n1 - n0], eng="act")

                  linear([(I["w_mem_q"][l], KC)], D, 512, tiles, rhs_u, q_consume)
                  stage("attnq")
                  SCL = 1.0 / 16.0
                  ptr = Rot([ar.get([2, 512], BF16) for _ in range(2)])
                  rcp = ar.get([512])
                  for (n0, n1) in tiles:
                      if n0 >= NPP:
                          continue
                      for h in range(4):
                          pt = ptr.next()
                          for mc in range(2):
                              sc = psA[:, mc * 512:(mc + 1) * 512]
                              for dc in range(2):
                                  P.mm(sc, KT[:, 2 * h + dc, mc * 128:(mc + 1) * 128], qT[:, 2 * h + dc, n0:n1], start=(dc == 0), stop=(dc == 1))
                              P.act(pt[:, mc, :], sc, AF.Exp, scale=SCL)
                          pss = smrot.next()
                          for mc in range(2):
                              P.mm(pss[:, 0:512], ones_b, pt[:, mc, :], start=(mc == 0), stop=(mc == 1))
                          P.recip(rcp, pss[:, 0:512])
                          for dc in range(2):
                              po = mmrot.next()
                              for mc in range(2):
                                  P.mm(po[:, 0:512], VT[:, mc, h * 256 + dc * 128:h * 256 + (dc + 1) * 128], pt[:, mc, :], start=(mc == 0), stop=(mc == 1))
                              P.tt(oT[:, 2 * h + dc, n0:n1], po[:, 0:512], rcp, ALU.mult)
                  if has_s:
                      kbr = Rot([ar.get([2, D], BF16) for _ in range(5)])
                      vbr = kbr
                      ktr = Rot([ar.get([KC, 256], BF16) for _ in range(2)])
                      pts = ar.get([NSQ, 2, 16], BF16)
                      rcs = ar.get([NSQ * 16])
                      vbs = []
                      pscs = mmrot.next()
                      scv = pscs[:, 0:512].rearrange("p (b m x) -> p b m x", b=NSQ, m=2)
                      for b in range(NSQ):
                          kb = kbr.next()
                          P.dma(kb, I["ck"][l, b].rearrange("(mc p) e -> p mc e", p=128), q="pool")
                          ktb = ktr.next()
                          for half in range(2):
                              pkb = psA[:, half * 512:(half + 1) * 512].bitcast(BF16)
                              for ee in range(4):
                                  e = half * 4 + ee
                                  for mc in range(2):
                                      P.tr(pkb[:, ee * 256 + mc * 128:ee * 256 + (mc + 1) * 128], kb[:, mc, e * 128:(e + 1) * 128], ident_b)
                              P.copy(ktb[:, half * 4:(half + 1) * 4, :], pkb.rearrange("p (e m) -> p e m", e=4), eng=("act" if half == 0 else "dve"))
                          for h in range(4):
                              for mc in range(2):
                                  for dc in range(2):
                                      P.mm(scv[:, b, mc, 4 * h:4 * h + 4], ktb[:, 2 * h + dc, mc * 128:(mc + 1) * 128],
                                           qT[:, 2 * h + dc, NPP + 4 * b:NPP + 4 * b + 4], start=(dc == 0), stop=(dc == 1))
                      P.act(pts.rearrange("p b m x -> p (b m x)"), pscs[:, 0:512], AF.Exp, scale=SCL)
                      pos = mmrot.next()
                      pss = smrot.next()
                      for b in range(NSQ):
                          vb = vbr.next()
                          P.dma(vb, I["cv"][l, b].rearrange("(mc p) e -> p mc e", p=128), q="pool")
                          for mc in range(2):
                              P.mm(pss[:, b * 16:(b + 1) * 16], ones_b, pts[:, b, mc, :], start=(mc == 0), stop=(mc == 1))
                          for e in range(KC):
                              h = e // 2
                              for mc in range(2):
                                  P.mm(pos[:, e * 64 + 4 * b:e * 64 + 4 * b + 4], vb[:, mc, e * 128:(e + 1) * 128], pts[:, b, mc, 4 * h:4 * h + 4],
                                       start=(mc == 0), stop=(mc == 1))
                      P.recip(rcs, pss[:, 0:256])
                      rv = rcs.rearrange("p (b h r) -> p b h r", b=NSQ, h=4)
                      for h in range(4):
                          for dc in range(2):
                              e = 2 * h + dc
                              P.tt(oT[:, e, NPP:NT].rearrange("p (b r) -> p b r", b=NSQ), pos[:, e * 64:(e + 1) * 64].rearrange("p (b r) -> p b r", b=NSQ),
                                   rv[:, :, h, :], ALU.mult)
                  linear([(I["w_mem_o"][l], KC)], D, 512, tiles_lin, lambda k, n0, n1: oT[:, k, n0:n1], add_resid)
                  stage("attn")
                  if dbg:
                      P.dma(O["dbg_h"][pas, l, 1], hT[:].rearrange("p k n -> p (k n)"), q="sp", is_output=True)

                  ar.off = arena_base
                  rmsnorm("nffn", l, tiles_lin)
                  sgr = Rot([ar.get([512]) for _ in range(2)])
                  groups = []
                  for gi in range(FC // 2):
                      def body(wb, gi=gi):
                          wv = wview(wb, 0, KC, 512)
                          for fi in range(2):
                              f = gi * 2 + fi
                              for (n0, n1) in tiles_lin:
                                  n = n1 - n0
                                  pg = mmrot.next()
                                  for k in range(KC):
                                      P.mm(pg[:, 0:n], wv[:, k, fi * 128:(fi + 1) * 128], uT[:, k, n0:n1], start=(k == 0), stop=(k == KC - 1))
                                  pu = mmrot.next()
                                  for k in range(KC):
                                      P.mm(pu[:, 0:n], wv[:, k, 256 + fi * 128:256 + (fi + 1) * 128], uT[:, k, n0:n1], start=(k == 0), stop=(k == KC - 1))
                                  sg = sgr.next()
                                  P.act(sg[:, 0:n], pg[:, 0:n], AF.Silu)
                                  P.tt(A[:, f, n0:n1], sg[:, 0:n], pu[:, 0:n], ALU.mult)
                      dmas = [
                          (lambda wb: wview(wb, 0, KC, 512)[:, :, 0:256], I["w_ffn_gate"][l][:, gi * 256:(gi + 1) * 256].rearrange("(k p) m -> p k m", p=128)),
                          (lambda wb: wview(wb, 0, KC, 512)[:, :, 256:512], I["w_ffn_up"][l][:, gi * 256:(gi + 1) * 256].rearrange("(k p) m -> p k m", p=128)),
                      ]
                      groups.append(dict(dmas=dmas, body=body))
                  stream(groups)
                  linear([(I["w_ffn_down"][l], FC)], D, 128, tiles_lin, lambda k, n0, n1: A[:, k, n0:n1], add_resid)
                  stage("ffn")
                  if dbg:
                      P.dma(O["dbg_h"][pas, l, 2], hT[:].rearrange("p k n -> p (k n)"), q="sp", is_output=True)

                  if last:
                      ar.off = arena_base
                      o1 = ar.get([512], parts=15)
                      pp = smrot.next()
                      for j in range(4):
                          P.tr(pp[0:15, j * 128:(j + 1) * 128], pool_tail[:, l, j, :], ident_f)
                      P.copy(o1, pp[0:15, 0:512], eng="act")
                      P.dma(O["p_pool"][l], o1, q="sp", is_output=True)
                      o2 = ar.get([1536], parts=3)
                      for q3 in range(3):
                          pp = smrot.next()
                          for j in range(4):
                              P.tr(pp[0:3, j * 128:(j + 1) * 128], sconv_tail[:, l, q3 * 4 + j, :], ident_f)
                          P.copy(o2[:, q3 * 512:(q3 + 1) * 512], pp[0:3, 0:512], eng="act")
                      P.dma(O["p_sconv"][l], o2, q="sp", is_output=True)
                      o3 = ar.get([512], parts=3)
                      pp = smrot.next()
                      for j in range(4):
                          P.tr(pp[0:3, j * 128:(j + 1) * 128], lconv_tail[:, l, j, :], ident_f)
                      P.copy(o3, pp[0:3, 0:512], eng="act")
                      P.dma(O["p_lconv"][l], o3, q="sp", is_output=True)
                      o4 = ar.get([512], parts=1)
                      pp = smrot.next()
                      for j in range(4):
                          P.tr(pp[0:1, j * 128:(j + 1) * 128], lru_h[:, l, j:j + 1], ident_f)
                      P.copy(o4, pp[0:1, 0:512], eng="act")
                      P.dma(O["p_lru"][l:l + 1, :], o4, q="sp", is_output=True)

              ar.reset()
              sqrot = Rot([ar.get([512], BF16) for _ in range(2)])
              rstd = ar.get([512])
              ytr = Rot([ar.get([KC, 128]) for _ in range(2)])
              yor = Rot([ar.get([D]) for _ in range(2)])
              for (n0, n1) in tiles:
                  n = n1 - n0
                  ps = mmrot.next()
                  for k in range(KC):
                      sq = sqrot.next()
                      P.act(sq[:, 0:n], hT[:, k, n0:n1], AF.Square)
                      P.mm(ps[:, 0:n], ones_b, sq[:, 0:n], start=(k == 0), stop=(k == KC - 1))
                  P.act(rstd[:, 0:n], ps[:, 0:n], AF.Sqrt, scale=1.0 / D, bias=epsc[:, 0:1])
                  P.recip(rstd[:, 0:n], rstd[:, 0:n])
                  bw = 128 if n0 < NPP else 64
                  for bi in range(n // bw):
                      yt = ytr.next()
                      for k in range(KC):
                          P.stt(yt[:, k, 0:bw], hT[:, k, n0 + bi * bw:n0 + (bi + 1) * bw], pcol("nfin", k), rstd[:, bi * bw:(bi + 1) * bw], ALU.mult, ALU.mult)
                      for k in range(KC):
                          P.tr(psA[0:bw, k * 128:(k + 1) * 128], yt[:, k, 0:bw], ident_f)
                      yo = yor.next()
                      P.copy(yo[0:bw, 0:512], psA[0:bw, 0:512], eng="act")
                      P.copy(yo[0:bw, 512:1024], psA[0:bw, 512:1024], eng="dve")
                      if n0 < NPP:
                          P.dma(O["y_p"][t0 + n0 + bi * 128:t0 + n0 + (bi + 1) * 128, :], yo, q="sp", is_output=True)
                      else:
                          P.dma(O["y_s"], yo[0:64, :], q="sp", is_output=True)
        except _Stop:
            pass
        P.emit()
    return nc


LRU_C = 8.0
_NC_CACHE = {}


def make_in_maps(inputs, cores):
    cst = build_consts()
    cst2 = np.zeros((128, 16, 64), np.float32)
    for b in range(16):
        cst2[:, b, 4 * b:4 * b + 4] = 1.0
    cst2 = cst2.reshape(128, 1024)
    maps = []
    for i in cores:
        s = slice(NSQ * i, NSQ * (i + 1))
        m = {
            "x_p": inputs["x_prompt"][i], "x_s": inputs["x_sample"][s].reshape(NS, D), "mem": inputs["mem_prompt"][i],
            "st_pool": inputs["state_pool"][:, s], "st_sconv": inputs["state_ssd_conv"][:, s], "st_ssd": inputs["state_ssd"][:, s],
            "st_lconv": inputs["state_lru_conv"][:, s], "st_lru": inputs["state_lru"][:, s],
            "ck": inputs["cache_mem_k"][:, s].reshape(DEPTH, NSQ, 256, D), "cv": inputs["cache_mem_v"][:, s].reshape(DEPTH, NSQ, 256, D),
            "cst": cst, "cst2": cst2,
        }
        for n, _ in IN_SPECS:
            if n not in m:
                m[n] = inputs[n]
        maps.append({k: np.ascontiguousarray(np.asarray(v, dtype=np.float32)) for k, v in m.items()})
    return maps


def kernel(**inputs):
    inputs = {k: np.asarray(v) for k, v in inputs.items()}
    if "nc" not in _NC_CACHE:
        _NC_CACHE["nc"] = build_program()
    nc = _NC_CACHE["nc"]
    maps = make_in_maps(inputs, list(range(N_CORES)))
    res = run_bass_kernel_spmd(nc, maps, core_ids=list(range(N_CORES)))
    R = res.results

    def cat(name, axis):
        return np.concatenate([np.asarray(r[name]) for r in R], axis=axis)

    y_p = np.stack([np.asarray(r["y_p"]) for r in R], 0)
    y_s = np.concatenate([np.asarray(r["y_s"]).reshape(NSQ, 4, D) for r in R], 0)
    outs = [y_p, y_s]
    for n in ("p_pool", "p_sconv", "p_ssd", "p_lconv", "p_lru"):
        outs.append(np.stack([np.asarray(r[n]) for r in R], 1))
    for n in ("p_mk", "p_mv"):
        outs.append(np.stack([np.asarray(r[n]).reshape(DEPTH, 256, 4, 256) for r in R], 1))
    for n in ("s_pool", "s_sconv", "s_ssd", "s_lconv", "s_lru"):
        outs.append(cat(n, 1))
    return tuple(np.ascontiguousarray(o.astype(np.float32)) for o in outs)
```
